# Optimizing a Trainium2 kernel written in Bass

```python
import jax, jax.numpy as jnp
from jax import lax
import numpy as np

D_MODEL = 4096
BATCH = 4
SEQ = 4096
DEPTH = 2

MIX_WIDTH = D_MODEL
PLE_DIM = 256
RMS_EPS = 1e-6

RET_HEAD_DIM = 128
RET_WIDTH = MIX_WIDTH // 4
RET_HEADS = RET_WIDTH // RET_HEAD_DIM
RET_CHUNK = 128
ROPE_BASE = 10000.0
RET_GN_EPS = 1e-5

RWKV_HEAD_DIM = 64
RWKV_WIDTH = MIX_WIDTH // 4
RWKV_HEADS = RWKV_WIDTH // RWKV_HEAD_DIM
RWKV_LORA = 64
RWKV_GN_EPS = 64e-5

GDN_HEAD_DIM = 128
GDN_WIDTH = MIX_WIDTH - RET_WIDTH - RWKV_WIDTH
GDN_HEADS = GDN_WIDTH // GDN_HEAD_DIM
GDN_CONV = 4
GDN_CHUNK = 64

RET_COLS = 4 * RET_WIDTH
RWKV_COLS = 4 * RWKV_WIDTH + 2 * RWKV_LORA
GDN_COLS = 4 * GDN_WIDTH + 2 * GDN_HEADS
IN_WIDTH = RET_COLS + RWKV_COLS + GDN_COLS

kernel_name = "hybrid_retention_rwkv7_gdn_block"


def rms_norm(x, w):
    x32 = x.astype(jnp.float32)
    y = x32 * lax.rsqrt(jnp.mean(x32 * x32, axis=-1, keepdims=True) + RMS_EPS)
    return (y * w.astype(jnp.float32)).astype(x.dtype)


def head_standardize(y, eps):
    yc = y - jnp.mean(y, axis=-1, keepdims=True)
    return yc * lax.rsqrt(jnp.mean(yc * yc, axis=-1, keepdims=True) + eps)


def l2_normalize(t, eps=1e-6):
    return t * lax.rsqrt(jnp.sum(t * t, axis=-1, keepdims=True) + eps)


def token_shift(u):
    return jnp.pad(u, ((0, 0), (1, 0), (0, 0)))[:, :-1]


def apply_rope(t, positions):
    half = t.shape[-1] // 2
    inv_freq = ROPE_BASE ** (-jnp.arange(half, dtype=jnp.float32) / half)
    ang = positions.astype(jnp.float32)[..., None] * inv_freq
    cos = jnp.cos(ang)[:, :, None, :]
    sin = jnp.sin(ang)[:, :, None, :]
    t1, t2 = t[..., :half], t[..., half:]
    return jnp.concatenate([t1 * cos - t2 * sin, t2 * cos + t1 * sin], axis=-1)


def retention_chunked(q, k, v):
    b, s, h, dk = q.shape
    dv = v.shape[-1]
    c = RET_CHUNK
    n = s // c
    log_gamma = jnp.log1p(-(2.0 ** (-5.0 - jnp.arange(h, dtype=jnp.float32))))
    idx = jnp.arange(c, dtype=jnp.float32)
    rel = idx[:, None] - idx[None, :]
    mask = jnp.where(rel >= 0, jnp.exp(jnp.maximum(rel, 0.0)[None] * log_gamma[:, None, None]), 0.0)
    xi = jnp.exp((idx + 1.0)[:, None] * log_gamma[None, :])
    zeta = jnp.exp((c - 1.0 - idx)[:, None] * log_gamma[None, :])
    chunk_decay = jnp.exp(c * log_gamma)
    qc = q.reshape(b, n, c, h, dk)
    kc = k.reshape(b, n, c, h, dk)
    vc = v.reshape(b, n, c, h, dv)
    scores = jnp.einsum('bnihd,bnjhd->bnhij', qc, kc) * mask
    inner = jnp.einsum('bnhij,bnjhe->bnihe', scores, vc)
    kv = jnp.einsum('bnjhd,bnjhe->nbhde', kc * zeta[None, None, :, :, None], vc)

    def step(state, kv_n):
        return state * chunk_decay[None, :, None, None] + kv_n, state

    _, state_in = lax.scan(step, jnp.zeros((b, h, dk, dv), jnp.float32), kv)
    cross = jnp.einsum('bnihd,nbhde->bnihe', qc * xi[None, None, :, :, None], state_in)
    return (inner + cross).reshape(b, s, h, dv)


def retention_branch(seg, positions, gn_w):
    b, s, _ = seg.shape
    q, k, v, z = jnp.split(seg, 4, axis=-1)
    shp = (b, s, RET_HEADS, RET_HEAD_DIM)
    q = apply_rope(q.reshape(shp), positions)
    k = apply_rope(k.reshape(shp), positions) * (RET_HEAD_DIM ** -0.5)
    y = retention_chunked(q, k, v.reshape(shp))
    y = head_standardize(y, RET_GN_EPS).reshape(b, s, RET_WIDTH) * gn_w
    return y * jax.nn.silu(z)


def rwkv7_scan(r, decay, k, v, kk, kka):
    b, s, h, n = r.shape
    xs = tuple(jnp.moveaxis(t, 1, 0) for t in (r, decay, k, v, kk, kka))

    def step(state, inp):
        r_t, w_t, k_t, v_t, kk_t, kka_t = inp
        sa = jnp.einsum('bhvk,bhk->bhv', state, kk_t)
        state = (state * w_t[:, :, None, :]
                 - sa[..., None] * kka_t[:, :, None, :]
                 + v_t[..., None] * k_t[:, :, None, :])
        return state, jnp.einsum('bhvk,bhk->bhv', state, r_t)

    _, y = lax.scan(step, jnp.zeros((b, h, n, n), jnp.float32), xs)
    return jnp.moveaxis(y, 0, 1)


def rwkv7_branch(seg, mu, w0, w2, a0, a2, k_k, k_a, r_k, ln_w, ln_b):
    b, s, _ = seg.shape
    wd = RWKV_WIDTH
    seg = seg + (token_shift(seg) - seg) * mu
    r, k, v, z, dw, da = jnp.split(seg, [wd, 2 * wd, 3 * wd, 4 * wd, 4 * wd + RWKV_LORA], axis=-1)
    w_log = -jax.nn.softplus(-(w0 + jnp.matmul(jnp.tanh(dw), w2))) - 0.5
    decay = jnp.exp(-jnp.exp(w_log))
    a = jax.nn.sigmoid(a0 + jnp.matmul(da, a2))
    shp = (b, s, RWKV_HEADS, RWKV_HEAD_DIM)
    kk = l2_normalize((k * k_k).reshape(shp))
    k = k * (1.0 + (a - 1.0) * k_a)
    r_h, k_h, v_h = r.reshape(shp), k.reshape(shp), v.reshape(shp)
    y = rwkv7_scan(r_h, decay.reshape(shp), k_h, v_h, kk, kk * a.reshape(shp))
    y = head_standardize(y, RWKV_GN_EPS).reshape(b, s, wd) * ln_w + ln_b
    bonus = jnp.sum(r_h * k_h * r_k.reshape(RWKV_HEADS, RWKV_HEAD_DIM), axis=-1, keepdims=True) * v_h
    y = y + bonus.reshape(b, s, wd)
    return y * jax.nn.silu(z)


def causal_depthwise_conv(u, w):
    kw, ch = w.shape
    return lax.conv_general_dilated(u, w[:, None, :], window_strides=(1,), padding=[(kw - 1, 0)],
                                    dimension_numbers=('NWC', 'WIO', 'NWC'), feature_group_count=ch)


def gated_delta_chunked(q, k, v, g, beta):
    b, s, h, dk = q.shape
    dv = v.shape[-1]
    c = GDN_CHUNK
    n = s // c

    def chunks(t):
        return jnp.moveaxis(t.reshape((b, n, c, h) + t.shape[3:]), 3, 2)

    q, k, v, g, beta = chunks(q * (dk ** -0.5)), chunks(k), chunks(v), chunks(g), chunks(beta)
    gc = jnp.cumsum(g, axis=-1)
    causal = jnp.tril(jnp.ones((c, c), dtype=bool))
    strict = jnp.tril(jnp.ones((c, c), dtype=bool), -1)
    decay = jnp.exp(jnp.where(causal, gc[..., :, None] - gc[..., None, :], -jnp.inf))
    kb = k * beta[..., None]
    lower = jnp.where(strict, jnp.einsum('bnhid,bnhjd->bnhij', kb, k) * decay, 0.0)
    eye = jnp.eye(c, dtype=jnp.float32)
    rhs = jnp.concatenate([v * beta[..., None], kb * jnp.exp(gc)[..., None]], axis=-1)
    sol = lax.linalg.triangular_solve(eye + lower, rhs, left_side=True, lower=True, unit_diagonal=True)
    u, wk = sol[..., :dv], sol[..., dv:]
    attn = jnp.einsum('bnhid,bnhjd->bnhij', q, k) * decay
    q_dec = q * jnp.exp(gc)[..., None]
    k_tail = k * jnp.exp(gc[..., -1:] - gc)[..., None]
    last = jnp.exp(gc[..., -1])
    xs = tuple(jnp.moveaxis(t, 1, 0) for t in (q_dec, attn, u, wk, k_tail, last))

    def step(state, inp):
        qd, at, u_n, w_n, kt, ld = inp
        v_new = u_n - jnp.einsum('bhcd,bhde->bhce', w_n, state)
        o = jnp.einsum('bhcd,bhde->bhce', qd, state) + jnp.einsum('bhij,bhje->bhie', at, v_new)
        state = state * ld[..., None, None] + jnp.einsum('bhcd,bhce->bhde', kt, v_new)
        return state, o

    _, o = lax.scan(step, jnp.zeros((b, h, dk, dv), jnp.float32), xs)
    o = jnp.moveaxis(o, 0, 1)
    return jnp.moveaxis(o, 2, 3).reshape(b, s, h, dv)


def gdn_branch(seg, conv_w, a_log, dt_bias, norm_w):
    b, s, _ = seg.shape
    gw = GDN_WIDTH
    qkv, z, a, beta_logit = jnp.split(seg, [3 * gw, 4 * gw, 4 * gw + GDN_HEADS], axis=-1)
    qkv = jax.nn.silu(causal_depthwise_conv(qkv, conv_w.astype(jnp.float32)))
    q, k, v = jnp.split(qkv, 3, axis=-1)
    shp = (b, s, GDN_HEADS, GDN_HEAD_DIM)
    q = l2_normalize(q.reshape(shp))
    k = l2_normalize(k.reshape(shp))
    g = -jnp.exp(a_log.astype(jnp.float32)) * jax.nn.softplus(a + dt_bias)
    beta = jax.nn.sigmoid(beta_logit)
    o = gated_delta_chunked(q, k, v.reshape(shp), g, beta)
    o = o * lax.rsqrt(jnp.mean(o * o, axis=-1, keepdims=True) + RMS_EPS) * norm_w
    return o.reshape(b, s, gw) * jax.nn.silu(z)


def setup_inputs(seed: int = 0) -> dict:
    key = jax.random.key(seed)
    ks = jax.random.split(key, 26)
    f32 = jnp.float32

    def nrm(k, shape, scale):
        return jax.random.normal(k, shape, f32) * scale

    x = nrm(ks[0], (BATCH, SEQ, D_MODEL), 1.0)
    p = nrm(ks[1], (DEPTH, BATCH, SEQ, PLE_DIM), 1.0)
    offsets = jax.random.randint(ks[2], (BATCH, 1), 0, 1024, dtype=jnp.int32)
    positions = (offsets + jnp.arange(SEQ, dtype=jnp.int32)[None, :]).astype(jnp.int32)
    norm_w = 1.0 + nrm(ks[3], (DEPTH, D_MODEL), 0.02)
    w_in = nrm(ks[4], (DEPTH, D_MODEL, IN_WIDTH), D_MODEL ** -0.5)
    ret_gn = 1.0 + nrm(ks[5], (DEPTH, RET_WIDTH), 0.02)
    rwkv_mu = jax.random.uniform(ks[6], (DEPTH, RWKV_COLS), f32, 0.0, 1.0)
    ratio = jnp.arange(RWKV_WIDTH, dtype=f32) / (RWKV_WIDTH - 1)
    rwkv_w0 = (-7.0 + 5.0 * ratio ** 0.85 + 0.5)[None, :] + nrm(ks[7], (DEPTH, RWKV_WIDTH), 0.05)
    rwkv_w2 = nrm(ks[8], (DEPTH, RWKV_LORA, RWKV_WIDTH), 0.5 * RWKV_LORA ** -0.5)
    rwkv_a0 = nrm(ks[9], (DEPTH, RWKV_WIDTH), 0.1)
    rwkv_a2 = nrm(ks[10], (DEPTH, RWKV_LORA, RWKV_WIDTH), 0.5 * RWKV_LORA ** -0.5)
    rwkv_k_k = 0.85 + nrm(ks[11], (DEPTH, RWKV_WIDTH), 0.02)
    rwkv_k_a = 1.0 + nrm(ks[12], (DEPTH, RWKV_WIDTH), 0.02)
    rwkv_r_k = nrm(ks[13], (DEPTH, RWKV_WIDTH), 0.1)
    rwkv_ln_w = 1.0 + nrm(ks[14], (DEPTH, RWKV_WIDTH), 0.02)
    rwkv_ln_b = nrm(ks[15], (DEPTH, RWKV_WIDTH), 0.02)
    gdn_conv = nrm(ks[16], (DEPTH, GDN_CONV, 3 * GDN_WIDTH), GDN_CONV ** -0.5)
    gdn_a_log = jnp.log(jax.random.uniform(ks[17], (DEPTH, GDN_HEADS), f32, 1.0, 16.0))
    dt = jnp.exp(jax.random.uniform(ks[18], (DEPTH, GDN_HEADS), f32, float(np.log(1e-3)), float(np.log(1e-1))))
    gdn_dt_bias = dt + jnp.log(-jnp.expm1(-dt))
    gdn_norm = 1.0 + nrm(ks[19], (DEPTH, GDN_HEAD_DIM), 0.02)
    w_out = nrm(ks[20], (DEPTH, MIX_WIDTH, D_MODEL), MIX_WIDTH ** -0.5)
    w_ple = nrm(ks[21], (DEPTH, PLE_DIM, D_MODEL), PLE_DIM ** -0.5)
    ple_norm = 1.0 + nrm(ks[22], (DEPTH, D_MODEL), 0.02)
    w_ple_gate = nrm(ks[23], (DEPTH, D_MODEL, D_MODEL), D_MODEL ** -0.5)
    final_norm = 1.0 + nrm(ks[24], (D_MODEL,), 0.02)
    return {"x": x, "p": p, "positions": positions, "norm_w": norm_w, "w_in": w_in,
            "ret_gn": ret_gn, "rwkv_mu": rwkv_mu, "rwkv_w0": rwkv_w0, "rwkv_w2": rwkv_w2,
            "rwkv_a0": rwkv_a0, "rwkv_a2": rwkv_a2, "rwkv_k_k": rwkv_k_k, "rwkv_k_a": rwkv_k_a,
            "rwkv_r_k": rwkv_r_k, "rwkv_ln_w": rwkv_ln_w, "rwkv_ln_b": rwkv_ln_b,
            "gdn_conv": gdn_conv, "gdn_a_log": gdn_a_log, "gdn_dt_bias": gdn_dt_bias,
            "gdn_norm": gdn_norm, "w_out": w_out, "w_ple": w_ple, "ple_norm": ple_norm,
            "w_ple_gate": w_ple_gate, "final_norm": final_norm}


def reference(x, p, positions, norm_w, w_in, ret_gn, rwkv_mu, rwkv_w0, rwkv_w2, rwkv_a0, rwkv_a2,
              rwkv_k_k, rwkv_k_a, rwkv_r_k, rwkv_ln_w, rwkv_ln_b, gdn_conv, gdn_a_log, gdn_dt_bias,
              gdn_norm, w_out, w_ple, ple_norm, w_ple_gate, final_norm):
    for i in range(DEPTH):
        h = rms_norm(x, norm_w[i])
        proj = jnp.matmul(h, w_in[i]).astype(jnp.float32)
        seg_ret, seg_rwkv, seg_gdn = jnp.split(proj, [RET_COLS, RET_COLS + RWKV_COLS], axis=-1)
        y_ret = retention_branch(seg_ret, positions, ret_gn[i])
        y_rwkv = rwkv7_branch(seg_rwkv, rwkv_mu[i], rwkv_w0[i], rwkv_w2[i], rwkv_a0[i], rwkv_a2[i],
                              rwkv_k_k[i], rwkv_k_a[i], rwkv_r_k[i], rwkv_ln_w[i], rwkv_ln_b[i])
        y_gdn = gdn_branch(seg_gdn, gdn_conv[i], gdn_a_log[i], gdn_dt_bias[i], gdn_norm[i])
        y = jnp.concatenate([y_ret, y_rwkv, y_gdn], axis=-1).astype(x.dtype)
        x = x + jnp.matmul(y, w_out[i])
        ple = jnp.matmul(p[i], w_ple[i]).astype(jnp.float32)
        gate = jax.nn.sigmoid(jnp.matmul(rms_norm(x, ple_norm[i]), w_ple_gate[i]).astype(jnp.float32))
        x = x + (ple * gate).astype(x.dtype)
    return rms_norm(x, final_norm)
```

```python
import contextlib, math
from collections import defaultdict
import numpy as np
import concourse.bass as bass
import concourse.mybir as mybir
from concourse.bass_utils import run_bass_kernel_spmd


F32 = mybir.dt.float32
BF16 = mybir.dt.bfloat16
I32 = mybir.dt.int32
ALU = mybir.AluOpType
AF = mybir.ActivationFunctionType
AX = mybir.AxisListType

ENGS = ("pe", "act", "dve", "pool", "sp")
EPOCH = 20000
N_EPOCHS = {"pe": 16, "act": 10, "dve": 12, "pool": 10, "sp": 1}
N_DMA_SEM = 12


_UID = [0]


def _uid():
    return f"u{_UID[0]}_"


class Res:
    __slots__ = ("name", "w", "r")

    def __init__(self, name=""):
        self.name = name
        self.w = None
        self.r = []


class Prog:
    def __init__(self, nc, stack):
        self.nc = nc
        self.stack = stack
        self.sems = {}
        for e in ENGS:
            self.sems[e] = [stack.enter_context(nc.semaphore(f"s_{e}_{i}")) for i in range(N_EPOCHS[e])]
        self.dsems = {}
        for e in ("sp", "pool"):
            self.dsems[e] = [stack.enter_context(nc.semaphore(f"d_{e}_{i}")) for i in range(N_DMA_SEM)]
        self.dcount = {e: [0] * N_DMA_SEM for e in self.dsems}
        self.dnext = {e: 0 for e in self.dsems}
        self.seq = {e: 0 for e in ENGS}
        self.known = {e: {} for e in ENGS}
        self.ops = {e: [] for e in ENGS}
        self.last = {e: None for e in ENGS}
        self.outstanding = []
        self.n_ops = 0
        self.pe_needed = set()
        self.pe_map = {}
        self.pe_count = 0

    def _waits_for(self, eng, reads, writes, extra=()):
        toks = list(extra)
        for r in reads:
            if r.w is not None:
                toks.append(r.w)
        for w in writes:
            if w.w is not None:
                toks.append(w.w)
            toks.extend(w.r)
        best = {}
        for (sem, val, te, raw) in toks:
            pass
        return toks

    def _filter(self, eng, toks):
        best = {}
        for tok, is_raw in toks:
            sem, val, te = tok
            if te == eng and eng == "pe":
                continue
            k = "PE" if te == "pe" else id(sem)
            if k not in best or best[k][1] < val:
                best[k] = (sem, val)
        out = []
        kn = self.known[eng]
        for k, (sem, val) in best.items():
            if kn.get(k, 0) >= val:
                continue
            kn[k] = val
            if k == "PE":
                self.pe_needed.add(val)
            out.append((sem, val))
        return out

    def op(self, eng, fn, reads=(), writes=(), extra=()):
        toks = [(t, True) for t in extra]
        for r in reads:
            if r.w is not None:
                toks.append((r.w, True))
        for w in writes:
            if w.w is not None:
                toks.append((w.w, False))
            toks.extend((t, False) for t in w.r)
        waits = self._filter(eng, toks)
        self.seq[eng] += 1
        s = self.seq[eng]
        if eng == "pe":
            tok = ("PE", s, "pe")
            self.ops[eng].append((waits, fn, s, 1))
        else:
            ep = (s - 1) // EPOCH
            tok = (self.sems[eng][ep], s - ep * EPOCH, eng)
            self.ops[eng].append((waits, fn, tok[0], 1))
        self.last[eng] = tok
        for r in reads:
            r.r.append(tok)
        for w in writes:
            w.w = tok
            w.r = []
        self.n_ops += 1
        return tok

    def dma(self, q, out, in_, reads=(), writes=(), **kw):
        i = self.dnext[q]
        self.dnext[q] = (i + 1) % N_DMA_SEM
        sem = self.dsems[q][i]
        toks = []
        if self.dcount[q][i] > 0:
            toks.append(((sem, self.dcount[q][i], None), True))
        for r in reads:
            if r.w is not None:
                toks.append((r.w, True))
        for w in writes:
            if w.w is not None:
                toks.append((w.w, False))
            toks.extend((t, False) for t in w.r)
        waits = self._filter(q, toks)
        self.dcount[q][i] += 16
        tok = (sem, self.dcount[q][i], None)

        def fn(e, out=out, in_=in_, kw=kw):
            return e.dma_start(out=out, in_=in_, **kw)
        self.ops[q].append((waits, fn, sem, 16))
        for r in reads:
            r.r.append(tok)
        for w in writes:
            w.w = tok
            w.r = []
        self.outstanding.append(tok)
        self.n_ops += 1
        return tok

    def barrier(self):
        toks = [(t, True) for t in self.outstanding]
        for e in ENGS:
            if self.last[e] is not None:
                toks.append((self.last[e], True))
        for e in ENGS:
            waits = self._filter(e, [(t, r) for (t, r) in toks if t[2] != e])
            if waits:
                self.ops[e].append((waits, None, None, 0))
        self.outstanding = []

    def flush(self):
        _UID[0] += 1
        nc = self.nc
        ops = self.ops
        for (waits, fn, idx, inc) in ops["pe"]:
            if fn is not None and idx in self.pe_needed:
                self.pe_count += 1
                c = self.pe_count
                ep = (c - 1) // EPOCH
                self.pe_map[idx] = (self.sems["pe"][ep], c - ep * EPOCH)
        pe_map = self.pe_map

        def rw(w):
            s_, v_ = w
            if isinstance(s_, str):
                return pe_map[v_]
            return w
        with nc.Block() as block:
            def run(handle, lst, is_pe=False):
                for waits, fn, sem, inc in lst:
                    for w in waits:
                        s_, v_ = rw(w)
                        handle.wait_ge(s_, v_)
                    if fn is not None:
                        ins = fn(handle)
                        if is_pe:
                            if sem in pe_map:
                                ins.then_inc(pe_map[sem][0], 1)
                        else:
                            ins.then_inc(sem, inc)

            @block.tensor
            def _(e):
                run(e, ops["pe"], True)

            @block.scalar
            def _(e):
                run(e, ops["act"])

            @block.vector
            def _(e):
                run(e, ops["dve"])

            @block.gpsimd
            def _(e):
                run(e, ops["pool"])

            @block.sync
            def _(e):
                run(e, ops["sp"])
        self.ops = {e: [] for e in ENGS}


RET_H = 8


def ret_consts():
    h = np.arange(8, dtype=np.float64)
    lg = np.log1p(-(2.0 ** (-5.0 - h)))
    i = np.arange(128, dtype=np.float64)
    Gq = np.exp((i[None, :] + 1) * lg[:, None])
    Gk = np.exp(-(i[None, :] + 1) * lg[:, None]) * 128 ** -0.5
    GC = np.exp(128 * lg)
    c = {}
    c["ret_gq"] = np.broadcast_to(Gq.reshape(1, 8 * 128), (128, 1024)).astype(np.float32).copy()
    c["ret_gk"] = np.broadcast_to(Gk.reshape(1, 8 * 128), (128, 1024)).astype(np.float32).copy()
    c["ret_gc"] = np.broadcast_to(np.repeat(GC, 128).reshape(1, 1024), (128, 1024)).astype(np.float32).copy()
    jj, ii = np.meshgrid(np.arange(128), np.arange(128), indexing="ij")
    c["mask_ui"] = (ii >= jj).astype(np.float32)
    c["ident"] = np.eye(128, dtype=np.float32)
    return c


def dma_rows(P, q, out_tile, dram, row0, nh, t0, tl, res_w, hs=4):
    for h0 in range(0, nh, hs):
        src = dram[row0 + h0 * 128: row0 + (h0 + hs) * 128, t0:t0 + tl].rearrange("(h p) t -> p h t", p=128)
        P.dma(q, out_tile[:, h0:h0 + hs, 0:tl], src, writes=[res_w])


def phase_ret(P, nc, projT, yT, C, gnwT_dram, T, eps=1e-5):
    H = RET_H
    NCH = T // 128
    with contextlib.ExitStack() as st:
        def sb(name, shape, dt):
            return st.enter_context(nc.sbuf_tensor(_uid() + "rt_" + name, shape, dt))

        def pst(name, shape, dt):
            return st.enter_context(nc.psum_tensor(_uid() + "rt_" + name, shape, dt))
        qf = [sb(f"qf{i}", [128, H, 128], F32) for i in range(2)]
        kf = [sb(f"kf{i}", [128, H, 128], F32) for i in range(2)]
        vf = [sb(f"vf{i}", [128, H, 128], F32) for i in range(2)]
        zf = [sb(f"zf{i}", [128, H, 128], F32) for i in range(2)]
        gq = sb("gq", [128, H, 128], F32); gk = sb("gk", [128, H, 128], F32); gc = sb("gc", [128, H, 128], F32)
        maskf = sb("maskf", [128, 128], F32)
        identf = sb("identf", [128, 128], F32); ident = sb("ident", [128, 128], BF16)
        gnw = sb("gnw", [128, H], F32)
        qd = sb("qd", [128, H, 128], BF16); kd = sb("kd", [128, H, 128], BF16); vb = sb("vb", [128, H, 128], BF16)
        scT = sb("scT", [128, H, 128], BF16)
        vtok = sb("vtok", [128, H, 128], BF16); kdtok = sb("kdtok", [128, H, 128], BF16)
        state = sb("state", [128, H, 128], F32); stmp = sb("stmp", [128, H, 128], F32); state_bf = sb("state_bf", [128, H, 128], BF16)
        y_sb = sb("y_sb", [128, H, 128], F32); sq = sb("sq", [128, H, 128], F32)
        s1 = sb("s1", [128, H], F32); s2 = sb("s2", [128, H], F32); mean = sb("mean", [128, H], F32)
        var = sb("var", [128, H], F32); sd = sb("sd", [128, H], F32); rstd = sb("rstd", [128, H], F32)
        epsc = sb("epsc", [128, 1], F32); mh = sb("mh", [128, H], F32)
        yc = sb("yc", [128, H, 128], F32); yn = sb("yn", [128, H, 128], BF16)
        sz = sb("sz", [128, H, 128], F32); yg = sb("yg", [128, H, 128], F32); yfin = [sb(f"yfin{i}", [128, H, 128], BF16) for i in range(2)]
        sc_ps = pst("sc_ps", [128, H, 128], F32)
        vt_ps = pst("vt_ps", [128, H, 128], BF16)
        kt_ps = pst("kt_ps", [128, H, 128], BF16)
        y_ps = pst("y_ps", [128, H, 128], F32)
        kv_ps = pst("kv_ps", [128, H, 128], F32)
        r = {n: Res(n) for n in ["gq", "gk", "gc", "mask", "identf", "ident", "gnw", "qd", "kd", "vb", "scT", "vtok", "kdtok",
                                 "state", "stmp", "state_bf", "y_sb", "sq", "s1", "s2", "mean", "var", "sd", "rstd", "eps",
                                 "yc", "yn", "sz", "yg", "mh", "sc_ps", "vt_ps", "kt_ps", "y_ps", "kv_ps", "out"]}
        r_qf = [Res(), Res()]; r_kf = [Res(), Res()]; r_vf = [Res(), Res()]; r_zf = [Res(), Res()]; r_yfin = [Res(), Res()]

        flat = lambda t: t[:, :, :].rearrange("p h t -> p (h t)")
        P.dma("sp", flat(gq), C["ret_gq"][:, :], writes=[r["gq"]])
        P.dma("sp", flat(gk), C["ret_gk"][:, :], writes=[r["gk"]])
        P.dma("sp", flat(gc), C["ret_gc"][:, :], writes=[r["gc"]])
        P.dma("sp", maskf[:, :], C["mask_ui"][:, :], writes=[r["mask"]])
        P.dma("sp", identf[:, :], C["ident"][:, :], writes=[r["identf"]])
        P.dma("sp", gnw[:, :], gnwT_dram[:, :], writes=[r["gnw"]])
        P.op("pool", lambda e: e.tensor_copy(ident[:, :], identf[:, :]), reads=[r["identf"]], writes=[r["ident"]])
        P.op("pool", lambda e: e.memset(epsc[:, :], eps), writes=[r["eps"]])
        P.op("pool", lambda e: e.memset(mh[:, :], -0.5), writes=[r["mh"]])
        P.op("pool", lambda e: e.memset(flat(state), 0.0), writes=[r["state"]])
        P.op("pool", lambda e: e.memset(flat(state_bf), 0.0), writes=[r["state_bf"]])

        def bc(t):
            return t[:, :].unsqueeze(2).to_broadcast([128, H, 128])

        for n in range(NCH):
            s = n % 2
            t0 = n * 128
            dma_rows(P, "sp", qf[s], projT, 0, H, t0, 128, r_qf[s])
            dma_rows(P, "sp", kf[s], projT, 1024, H, t0, 128, r_kf[s])
            dma_rows(P, "sp", vf[s], projT, 2048, H, t0, 128, r_vf[s])
            dma_rows(P, "sp", zf[s], projT, 3072, H, t0, 128, r_zf[s])
            P.op("dve", lambda e, s=s: e.tensor_tensor(out=flat(qd), in0=flat(qf[s]), in1=flat(gq), op=ALU.mult),
                 reads=[r_qf[s], r["gq"]], writes=[r["qd"]])
            P.op("dve", lambda e, s=s: e.tensor_tensor(out=flat(kd), in0=flat(kf[s]), in1=flat(gk), op=ALU.mult),
                 reads=[r_kf[s], r["gk"]], writes=[r["kd"]])
            P.op("pool", lambda e, s=s: e.tensor_copy(flat(vb), flat(vf[s])), reads=[r_vf[s]], writes=[r["vb"]])
            P.op("act", lambda e, s=s: e.activation(out=flat(sz), in_=flat(zf[s]), func=AF.Silu), reads=[r_zf[s]], writes=[r["sz"]])
            for h in range(H):
                P.op("pe", lambda e, h=h: e.matmul(sc_ps[:, h, :], kd[:, h, :], qd[:, h, :], start=True, stop=True),
                     reads=[r["kd"], r["qd"]], writes=[r["sc_ps"]])
            for h in range(H):
                P.op("pe", lambda e, h=h: e.transpose(vt_ps[:, h, :], vb[:, h, :], ident[:, :]),
                     reads=[r["vb"], r["ident"]], writes=[r["vt_ps"]])
            for h in range(H):
                P.op("pe", lambda e, h=h: e.transpose(kt_ps[:, h, :], kd[:, h, :], ident[:, :]),
                     reads=[r["kd"], r["ident"]], writes=[r["kt_ps"]])
            P.op("dve", lambda e: e.tensor_tensor(out=scT[:, :, :], in0=sc_ps[:, :, :],
                                                  in1=maskf[:, :].unsqueeze(1).to_broadcast([128, H, 128]), op=ALU.mult),
                 reads=[r["sc_ps"], r["mask"]], writes=[r["scT"]])
            P.op("act", lambda e: e.copy(flat(vtok), flat(vt_ps)), reads=[r["vt_ps"]], writes=[r["vtok"]])
            P.op("act", lambda e: e.copy(flat(kdtok), flat(kt_ps)), reads=[r["kt_ps"]], writes=[r["kdtok"]])
            for h in range(H):
                P.op("pe", lambda e, h=h: e.matmul(y_ps[:, h, :], scT[:, h, :], vtok[:, h, :], start=True, stop=False),
                     reads=[r["scT"], r["vtok"]], writes=[r["y_ps"]])
                P.op("pe", lambda e, h=h: e.matmul(y_ps[:, h, :], qd[:, h, :], state_bf[:, h, :], start=False, stop=True),
                     reads=[r["qd"], r["state_bf"]], writes=[r["y_ps"]])
            for h in range(H):
                P.op("pe", lambda e, h=h: e.matmul(kv_ps[:, h, :], kdtok[:, h, :], vtok[:, h, :], start=True, stop=True),
                     reads=[r["kdtok"], r["vtok"]], writes=[r["kv_ps"]])
            P.op("dve", lambda e: e.tensor_tensor(out=flat(stmp), in0=flat(kv_ps), in1=flat(state), op=ALU.add),
                 reads=[r["kv_ps"], r["state"]], writes=[r["stmp"]])
            P.op("dve", lambda e: e.tensor_tensor(out=flat(state), in0=flat(stmp), in1=flat(gc), op=ALU.mult),
                 reads=[r["stmp"], r["gc"]], writes=[r["state"]])
            P.op("act", lambda e: e.copy(flat(state_bf), flat(state)), reads=[r["state"]], writes=[r["state_bf"]])
            P.op("act", lambda e: e.copy(flat(y_sb), flat(y_ps)), reads=[r["y_ps"]], writes=[r["y_sb"]])
            P.op("dve", lambda e: e.tensor_reduce(out=s1[:, :], in_=y_sb[:, :, :], axis=AX.X, op=ALU.add),
                 reads=[r["y_sb"]], writes=[r["s1"]])
            P.op("pool", lambda e: e.tensor_tensor(out=flat(sq), in0=flat(y_sb), in1=flat(y_sb), op=ALU.mult),
                 reads=[r["y_sb"]], writes=[r["sq"]])
            P.op("dve", lambda e: e.tensor_reduce(out=s2[:, :], in_=sq[:, :, :], axis=AX.X, op=ALU.add),
                 reads=[r["sq"]], writes=[r["s2"]])
            P.op("dve", lambda e: e.tensor_scalar(out=mean[:, :], in0=s1[:, :], scalar1=1.0 / 128, scalar2=None, op0=ALU.mult),
                 reads=[r["s1"]], writes=[r["mean"]])
            P.op("dve", lambda e: e.tensor_tensor(out=var[:, :], in0=mean[:, :], in1=mean[:, :], op=ALU.mult),
                 reads=[r["mean"]], writes=[r["var"]])
            P.op("dve", lambda e: e.scalar_tensor_tensor(out=var[:, :], in0=s2[:, :], scalar=1.0 / 128, in1=var[:, :],
                                                         op0=ALU.mult, op1=ALU.subtract),
                 reads=[r["s2"], r["var"]], writes=[r["var"]])
            P.op("dve", lambda e: e.tensor_scalar(out=sd[:, :], in0=var[:, :], scalar1=1.0, scalar2=eps, op0=ALU.mult, op1=ALU.add),
                 reads=[r["var"]], writes=[r["sd"]])
            P.op("pool", lambda e: e.tensor_tensor(out=rstd[:, :], in0=sd[:, :], in1=mh[:, :], op=ALU.pow), reads=[r["sd"], r["mh"]], writes=[r["rstd"]])
            P.op("dve", lambda e: e.tensor_tensor(out=yc[:, :, :], in0=y_sb[:, :, :], in1=bc(mean), op=ALU.subtract),
                 reads=[r["y_sb"], r["mean"]], writes=[r["yc"]])
            P.op("dve", lambda e: e.tensor_tensor(out=yn[:, :, :], in0=yc[:, :, :], in1=bc(rstd), op=ALU.mult),
                 reads=[r["yc"], r["rstd"]], writes=[r["yn"]])
            for h in range(H):
                P.op("pe", lambda e, h=h: e.transpose(kt_ps[:, h, :], yn[:, h, :], ident[:, :]),
                     reads=[r["yn"], r["ident"]], writes=[r["kt_ps"]])
            P.op("dve", lambda e: e.tensor_tensor(out=yg[:, :, :], in0=kt_ps[:, :, :], in1=bc(gnw), op=ALU.mult),
                 reads=[r["kt_ps"], r["gnw"]], writes=[r["yg"]])
            P.op("pool", lambda e, s=s: e.tensor_tensor(out=flat(yfin[s]), in0=flat(yg), in1=flat(sz), op=ALU.mult),
                 reads=[r["yg"], r["sz"]], writes=[r_yfin[s]])
            P.dma("pool", yT[0:H, :, t0:t0 + 128].rearrange("k p t -> p k t"), yfin[s][:, :, :],
                  reads=[r_yfin[s]], writes=[r["out"]])
        P.barrier()
        P.flush()


GH = 16


def gdn_consts():
    c = {}
    k, i = np.meshgrid(np.arange(128), np.arange(128), indexing="ij")
    c["triu"] = (k <= i).astype(np.float32)
    c["ones"] = np.ones((128, 128), np.float32)
    c["negones"] = -np.ones((128, 128), np.float32)
    c["mask_ls"] = (k > i).astype(np.float32)
    c["mask_li"] = (k >= i).astype(np.float32)
    c["mask_ui"] = (i >= k).astype(np.float32)
    c["mask_us"] = (i > k).astype(np.float32)
    c["ident"] = np.eye(128, dtype=np.float32)
    sel = np.zeros((16, 16, 128), np.float32)
    for h in range(16):
        sel[h, h, :] = 1.0
    c["sel16"] = sel.reshape(16, 16 * 128)
    cm = np.ones((128, 512), np.float32)
    cm[:, ::128] = 0.0
    c["cmask128"] = cm
    return c


def neumann(P, nc, L, LT, rL, rLT, W, r, nlev=6, G=4):
    identb = W["identb"]
    Pk = W["Pk"]; nL = W["nL"]; nLT = W["nLT"]
    pa, pb, pp = W["pa"], W["pb"], W["pp"]
    P.op("pool", lambda e: e.tensor_tensor(out=Pk[0][:, :, :], in0=identb[:, :].unsqueeze(1).to_broadcast([128, G, 128]),
                                           in1=LT[:, :, :], op=ALU.subtract),
         reads=[r["identb"], rLT], writes=[r["Pk0"]])
    curL, curLT, rcL, rcLT = L, LT, rL, rLT
    pi = 0
    for lev in range(nlev):
        s = lev % 2
        for g in range(G):
            P.op("pe", lambda e, g=g, a=curLT, b=curL: e.matmul(pa[:, g, :], a[:, g, :], b[:, g, :], start=True, stop=True),
                 reads=[rcLT, rcL], writes=[r["pa"]])
        if lev < nlev - 1:
            for g in range(G):
                P.op("pe", lambda e, g=g, a=curL, b=curLT: e.matmul(pb[:, g, :], a[:, g, :], b[:, g, :], start=True, stop=True),
                     reads=[rcLT, rcL], writes=[r["pb"]])
        P.op("act", lambda e, s=s: e.copy(nL[s][:, :, :], pa[:, :, :]), reads=[r["pa"]], writes=[r[f"nL{s}"]])
        if lev < nlev - 1:
            P.op("dve", lambda e, s=s: e.tensor_copy(nLT[s][:, :, :], pb[:, :, :]), reads=[r["pb"]], writes=[r[f"nLT{s}"]])
        for g in range(G):
            P.op("pe", lambda e, g=g, s=s, pi=pi: e.matmul(pp[:, g, :], nL[s][:, g, :], Pk[pi][:, g, :], start=True, stop=True),
                 reads=[r[f"nL{s}"], r[f"Pk{pi}"]], writes=[r["pp"]])
        P.op("dve", lambda e, pi=pi: e.tensor_tensor(out=Pk[1 - pi][:, :, :], in0=pp[:, :, :], in1=Pk[pi][:, :, :], op=ALU.add),
             reads=[r["pp"], r[f"Pk{pi}"]], writes=[r[f"Pk{1 - pi}"]])
        pi = 1 - pi
        curL, curLT, rcL, rcLT = nL[s], nLT[s], r[f"nL{s}"], r[f"nLT{s}"]
    return Pk[pi], r[f"Pk{pi}"]


def phase_gdn_pre(P, nc, projT, S, C, prm, T, rows):
    NB = T // 512
    with contextlib.ExitStack() as st:
        def sb(name, shape, dt):
            return st.enter_context(nc.sbuf_tensor(_uid() + "gp_" + name, shape, dt))

        def pst(name, shape, dt):
            return st.enter_context(nc.psum_tensor(_uid() + "gp_" + name, shape, dt))
        r = defaultdict(Res)
        convw = sb("convw", [128, 48 * 4], F32)
        alog = sb("alog", [16, 1], F32); dtb = sb("dtb", [16, 1], F32); nega = sb("nega", [16, 1], F32)
        onesf = sb("onesf", [128, 128], F32); identf = sb("identf", [128, 128], F32)
        sel = sb("sel", [16, 16 * 128], F32); cmask = sb("cmask", [16, 512], F32)
        epsc = sb("epsc", [128, 1], F32)
        at = sb("at", [16, 512], F32); bt = sb("bt", [16, 512], F32)
        e1 = sb("e1", [16, 512], F32); spt = sb("spt", [16, 512], F32); gt = sb("gt", [16, 512], F32)
        beta = sb("beta", [16, 512], F32); gcT = sb("gcT", [16, 512], F32); egcT = sb("egcT", [16, 512], F32)
        gbs = sb("gbs", [128, 4, 32], F32)
        u = [sb(f"u{i}", [128, 515], F32) for i in range(2)]
        acc = sb("acc", [128, 512], F32); sl = sb("sl", [128, 512], F32); sq = sb("sq", [128, 512], F32)
        sd = sb("sd", [128, 512], F32); rn = sb("rn", [128, 512], F32); kn = sb("kn", [128, 512], F32)
        ob = [sb(f"ob{i}", [128, 512], BF16) for i in range(2)]
        ob2 = [sb(f"ob2{i}", [128, 512], BF16) for i in range(2)]
        ss_ps = pst("ss_ps", [128, 512], F32)
        bc_ps = pst("bc_ps", [128, 512], F32)
        tr_ps_full = pst("tr_ps", [128, 512], F32)
        tr_ps = tr_ps_full[:, 0:128].rearrange("p (c x) -> p c x", x=32)
        P.dma("sp", convw[:, :], prm["convT"][:, :], writes=[r["convw"]])
        P.dma("sp", alog[:, :], prm["alog"][:, :], writes=[r["alog"]])
        P.dma("sp", dtb[:, :], prm["dtb"][:, :], writes=[r["dtb"]])
        P.dma("sp", onesf[:, :], C["ones"][:, :], writes=[r["onesf"]])
        P.dma("sp", identf[:, :], C["ident"][:, :], writes=[r["identf"]])
        P.dma("sp", sel[:, :], C["sel16"][:, :], writes=[r["sel"]])
        P.dma("sp", cmask[:, :], C["cmask128"][0:16, :], writes=[r["cmask"]])
        P.op("pool", lambda e: e.memset(epsc[:, :], 1e-6), writes=[r["eps"]])
        P.op("act", lambda e: e.activation(out=nega[:, :], in_=alog[:, :], func=AF.Exp), reads=[r["alog"]], writes=[r["nega"]])
        P.op("dve", lambda e: e.tensor_scalar(out=nega[:, :], in0=nega[:, :], scalar1=-1.0, scalar2=None, op0=ALU.mult),
             reads=[r["nega"]], writes=[r["nega"]])
        cnt = 0
        acc2 = [acc, sb("acc_b", [128, 512], F32)]; sl2 = [sl, sb("sl_b", [128, 512], F32)]; sq2 = [sq, sb("sq_b", [128, 512], F32)]
        sd2 = [sd, sb("sd_b", [128, 512], F32)]; rn2 = [rn, sb("rn_b", [128, 512], F32)]; kn2 = [kn, sb("kn_b", [128, 512], F32)]
        ss2 = [ss_ps, pst("ss_ps_b", [128, 512], F32)]; bc2 = [bc_ps, pst("bc_ps_b", [128, 512], F32)]

        def do_tile(kind, rbase, dst, ti, tb, s):
            t0 = tb * 512
            acc, sl, sq, sd, rn, kn, ss_ps, bc_ps = acc2[s], sl2[s], sq2[s], sd2[s], rn2[s], kn2[s], ss2[s], bc2[s]
            ra, rsl, rsq, rsd, rrn, rkn, rss, rbc = (r[f"acc{s}"], r[f"sl{s}"], r[f"sq{s}"], r[f"sd{s}"], r[f"rn{s}"], r[f"kn{s}"],
                                                     r[f"ss_ps{s}"], r[f"bc_ps{s}"])
            wi = {"q": 0, "k": 16, "v": 32}[kind] + ti
            row0 = rbase + ti * 128
            if tb == 0:
                P.op("pool", lambda e: e.memset(u[s][:, 0:3], 0.0), writes=[r[f"u{s}"]])
                P.dma("sp", u[s][:, 3:515], projT[row0:row0 + 128, 0:512], writes=[r[f"u{s}"]])
            else:
                P.dma("sp", u[s][:, :], projT[row0:row0 + 128, t0 - 3:t0 + 512], writes=[r[f"u{s}"]])
            P.op("dve", lambda e: e.tensor_scalar(out=acc[:, :], in0=u[s][:, 3:515], scalar1=convw[:, wi * 4 + 3:wi * 4 + 4],
                                                  scalar2=None, op0=ALU.mult),
                 reads=[r[f"u{s}"], r["convw"]], writes=[ra])
            for j in (2, 1, 0):
                P.op("dve", lambda e, j=j: e.scalar_tensor_tensor(out=acc[:, :], in0=u[s][:, j:j + 512],
                                                                  scalar=convw[:, wi * 4 + j:wi * 4 + j + 1],
                                                                  in1=acc[:, :], op0=ALU.mult, op1=ALU.add),
                     reads=[r[f"u{s}"], r["convw"], ra], writes=[ra])
            yield
            if kind == "v":
                P.op("act", lambda e: e.activation(out=ob[s][:, :], in_=acc[:, :], func=AF.Silu), reads=[ra], writes=[r[f"ob{s}"]])
                P.dma("pool", S[dst][ti * 128:(ti + 1) * 128, t0:t0 + 512], ob[s][:, :], reads=[r[f"ob{s}"]], writes=[r[dst]])
                return
            P.op("act", lambda e: e.activation(out=sl[:, :], in_=acc[:, :], func=AF.Silu), reads=[ra], writes=[rsl])
            P.op("pool", lambda e: e.tensor_tensor(out=sq[:, :], in0=sl[:, :], in1=sl[:, :], op=ALU.mult), reads=[rsl], writes=[rsq])
            P.op("pe", lambda e: e.matmul(ss_ps[:, :], onesf[:, :], sq[:, :], start=True, stop=True), reads=[r["onesf"], rsq], writes=[rss])
            P.op("act", lambda e: e.activation(out=sd[:, :], in_=ss_ps[:, :], func=AF.Sqrt, bias=epsc[:, 0:1], scale=1.0),
                 reads=[rss, r["eps"]], writes=[rsd])
            yield
            P.op("dve", lambda e: e.reciprocal(rn[:, :], sd[:, :]), reads=[rsd], writes=[rrn])
            if kind == "k":
                P.op("dve", lambda e: e.tensor_tensor(out=ob[s][:, :], in0=sl[:, :], in1=rn[:, :], op=ALU.mult),
                     reads=[rsl, rrn], writes=[r[f"ob{s}"]])
                P.dma("pool", S[dst][ti * 128:(ti + 1) * 128, t0:t0 + 512], ob[s][:, :], reads=[r[f"ob{s}"]], writes=[r[dst]])
            else:
                P.op("dve", lambda e: e.scalar_tensor_tensor(out=kn[:, :], in0=sl[:, :], scalar=128 ** -0.5, in1=rn[:, :],
                                                             op0=ALU.mult, op1=ALU.mult),
                     reads=[rsl, rrn], writes=[rkn])
                P.op("act", lambda e: e.copy(ob[s][:, :], kn[:, :]), reads=[rkn], writes=[r[f"ob{s}"]])
                P.dma("pool", S[dst][ti * 128:(ti + 1) * 128, t0:t0 + 512], ob[s][:, :], reads=[r[f"ob{s}"]], writes=[r[dst]])
                P.op("pe", lambda e: e.matmul(bc_ps[:, :], sel[:, ti * 128:(ti + 1) * 128], egcT[:, :], start=True, stop=True),
                     reads=[r["sel"], r["egcT"]], writes=[rbc])
                P.op("dve", lambda e: e.tensor_tensor(out=ob2[s][:, :], in0=kn[:, :], in1=bc_ps[:, :], op=ALU.mult),
                     reads=[rkn, rbc], writes=[r[f"ob2{s}"]])
                P.dma("pool", S["gqdT"][ti * 128:(ti + 1) * 128, t0:t0 + 512], ob2[s][:, :], reads=[r[f"ob2{s}"]], writes=[r["gqdT"]])

        for tb in range(NB):
            t0 = tb * 512
            P.dma("sp", at[:, :], projT[rows["a"]:rows["a"] + 16, t0:t0 + 512], writes=[r["at"]])
            P.dma("sp", bt[:, :], projT[rows["b"]:rows["b"] + 16, t0:t0 + 512], writes=[r["bt"]])
            P.op("act", lambda e: e.activation(out=e1[:, :], in_=at[:, :], func=AF.Exp, bias=dtb[:, 0:1], scale=1.0),
                 reads=[r["at"], r["dtb"]], writes=[r["e1"]])
            P.op("dve", lambda e: e.tensor_scalar(out=e1[:, :], in0=e1[:, :], scalar1=1.0, scalar2=None, op0=ALU.add),
                 reads=[r["e1"]], writes=[r["e1"]])
            P.op("act", lambda e: e.activation(out=spt[:, :], in_=e1[:, :], func=AF.Ln), reads=[r["e1"]], writes=[r["spt"]])
            P.op("dve", lambda e: e.tensor_scalar(out=gt[:, :], in0=spt[:, :], scalar1=nega[:, 0:1], scalar2=None, op0=ALU.mult),
                 reads=[r["spt"], r["nega"]], writes=[r["gt"]])
            P.op("act", lambda e: e.activation(out=beta[:, :], in_=bt[:, :], func=AF.Sigmoid), reads=[r["bt"]], writes=[r["beta"]])
            P.op("dve", lambda e: e.tensor_tensor_scan(out=gcT[:, :], data0=cmask[:, :], data1=gt[:, :], initial=0.0,
                                                       op0=ALU.mult, op1=ALU.add),
                 reads=[r["cmask"], r["gt"]], writes=[r["gcT"]])
            P.op("act", lambda e: e.activation(out=egcT[:, :], in_=gcT[:, :], func=AF.Exp), reads=[r["gcT"]], writes=[r["egcT"]])
            for c4 in range(4):
                P.op("pe", lambda e, c4=c4: e.transpose(tr_ps[:, c4, 0:16], gt[:, c4 * 128:(c4 + 1) * 128], identf[0:16, 0:16]),
                     reads=[r["gt"], r["identf"]], writes=[r["tr_ps"]])
                P.op("pe", lambda e, c4=c4: e.transpose(tr_ps[:, c4, 16:32], beta[:, c4 * 128:(c4 + 1) * 128], identf[0:16, 0:16]),
                     reads=[r["beta"], r["identf"]], writes=[r["tr_ps"]])
            P.op("dve", lambda e: e.tensor_copy(gbs[:, :, :], tr_ps[:, :, :]), reads=[r["tr_ps"]], writes=[r["gbs"]])
            P.dma("pool", S["gbt"][t0:t0 + 512, :].rearrange("(c p) x -> p c x", p=128), gbs[:, :, :],
                  reads=[r["gbs"]], writes=[r["gbt_out"]])
            for kind, rbase, dst in (("q", rows["q"], "gqT"), ("k", rows["k"], "gkT"), ("v", rows["v"], "gvT")):
                for ti in range(0, 16, 2):
                    zipper_lag([do_tile(kind, rbase, dst, ti, tb, 0), do_tile(kind, rbase, dst, ti + 1, tb, 1)])
        P.barrier()
        P.flush()


def zipper_lag(gens):
    a, b = gens
    a_done = b_done = False
    try:
        next(a)
    except StopIteration:
        a_done = True
    while not (a_done and b_done):
        if not b_done:
            try:
                next(b)
            except StopIteration:
                b_done = True
        if not a_done:
            try:
                next(a)
            except StopIteration:
                a_done = True


def zipper(gens):
    active = list(gens)
    while active:
        for g in list(active):
            try:
                next(g)
            except StopIteration:
                active.remove(g)


def neumann_gen(P, nc, L, LT, rL, rLT, W, r, nlev=6, G=4):
    identb = W["identb"]
    Pk = W["Pk"]; nL = W["nL"]; nLT = W["nLT"]
    pa, pb = W["pa"], W["pb"]
    P.op("pool", lambda e: e.tensor_tensor(out=Pk[0][:, :, :], in0=identb[:, :].unsqueeze(1).to_broadcast([128, G, 128]),
                                           in1=LT[:, :, :], op=ALU.subtract),
         reads=[W["r_identb"], rLT], writes=[r["Pk0"]])
    yield
    curL, curLT, rcL, rcLT = L, LT, rL, rLT
    pi = 0
    for lev in range(nlev):
        s = lev % 2
        for g in range(G):
            P.op("pe", lambda e, g=g, a=curLT, b=curL: e.matmul(pa[:, g, :], a[:, g, :], b[:, g, :], start=True, stop=True),
                 reads=[rcLT, rcL], writes=[r["pa"]])
        if lev < nlev - 1:
            for g in range(G):
                P.op("pe", lambda e, g=g, a=curL, b=curLT: e.matmul(pb[:, g, :], a[:, g, :], b[:, g, :], start=True, stop=True),
                     reads=[rcLT, rcL], writes=[r["pb"]])
        yield
        P.op("act", lambda e, s=s: e.copy(nL[s][:, :, :], pa[:, :, :]), reads=[r["pa"]], writes=[r[f"nL{s}"]])
        if lev < nlev - 1:
            P.op("dve", lambda e, s=s: e.tensor_copy(nLT[s][:, :, :], pb[:, :, :]), reads=[r["pb"]], writes=[r[f"nLT{s}"]])
        yield
        for g in range(G):
            P.op("pe", lambda e, g=g, s=s, pi=pi: e.matmul(pa[:, g, :], nL[s][:, g, :], Pk[pi][:, g, :], start=True, stop=True),
                 reads=[r[f"nL{s}"], r[f"Pk{pi}"]], writes=[r["pa"]])
        yield
        P.op("dve", lambda e, pi=pi: e.tensor_tensor(out=Pk[1 - pi][:, :, :], in0=pa[:, :, :], in1=Pk[pi][:, :, :], op=ALU.add),
             reads=[r["pa"], r[f"Pk{pi}"]], writes=[r[f"Pk{1 - pi}"]])
        yield
        pi = 1 - pi
        curL, curLT, rcL, rcLT = nL[s], nLT[s], r[f"nL{s}"], r[f"nLT{s}"]
    W["result"] = (Pk[pi], r[f"Pk{pi}"])


def phase_gdn_g1(P, nc, S, C, T):
    NCH = T // 128
    G = 4
    with contextlib.ExitStack() as st:
        def sb(name, shape, dt):
            return st.enter_context(nc.sbuf_tensor(_uid() + "g1_" + name, shape, dt))

        def pst(name, shape, dt):
            return st.enter_context(nc.psum_tensor(_uid() + "g1_" + name, shape, dt))
        r = defaultdict(Res)
        triu = sb("triu", [128, 128], F32); onesf = sb("onesf", [128, 128], F32); negones = sb("negones", [128, 128], F32)
        mls = sb("mls", [128, 128], F32); mli = sb("mli", [128, 128], F32)
        identf = sb("identf", [128, 128], F32); identb = sb("identb", [128, 128], BF16)
        gb = [sb(f"gb{i}", [128, 32], F32) for i in range(2)]
        gcs = sb("gcs", [128, 32], F32)
        egc = sb("egc", [128, 16], F32); dtl = sb("dtl", [128, 16], F32); etail = sb("etail", [128, 16], F32)
        elast = [sb(f"elast{i}", [128, 16], F32) for i in range(2)]
        bgc = sb("bgc", [128, 16], F32)
        Gb = sb("Gb", [128, 16, 128], F32); X = sb("X", [128, 16, 128], F32)
        gc_ps = None
        ST = []
        for q in range(2):
            t = {}
            for n in ("kT", "qT", "vT", "L", "attn", "LT", "attnT", "kbg", "ktail", "vb", "wk_sb", "nL0", "nL1", "nLT0", "nLT1", "Pk0", "Pk1"):
                t[n] = sb(f"{n}_{q}", [128, G, 128], BF16)
            for n in ("M1", "dec", "dec_s", "dec_i", "u_sb"):
                t[n] = sb(f"{n}_{q}", [128, G, 128], F32)
            t["g_ps"] = pst(f"g_ps{q}", [128, G, 128], F32)
            t["tr_ps"] = pst(f"tr_ps{q}", [128, 2, G, 128], BF16)
            t["pa"] = pst(f"pa{q}", [128, G, 128], F32)
            t["pb"] = pst(f"pb{q}", [128, G, 128], F32)
            t["r"] = defaultdict(Res)
            t["W"] = {"identb": identb, "r_identb": r["identb"], "nL": [t["nL0"], t["nL1"]], "nLT": [t["nLT0"], t["nLT1"]],
                      "Pk": [t["Pk0"], t["Pk1"]], "pa": t["pa"], "pb": t["pb"]}
            ST.append(t)
        for nm, t_, src in (("triu", triu, "triu"), ("onesf", onesf, "ones"), ("negones", negones, "negones"),
                            ("mls", mls, "mask_ls"), ("mli", mli, "mask_li"), ("identf", identf, "ident")):
            P.dma("sp", t_[:, :], C[src][:, :], writes=[r[nm]])
        P.op("pool", lambda e: e.tensor_copy(identb[:, :], identf[:, :]), reads=[r["identf"]], writes=[r["identb"]])

        def bcg(t, g):
            return t[:, g * G:(g + 1) * G].unsqueeze(2).to_broadcast([128, G, 128])

        def bcm(m):
            return m[:, :].unsqueeze(1).to_broadcast([128, G, 128])

        def grp(c, g, q, sc):
            t = ST[q]
            rr = t["r"]
            t0 = c * 128
            r0 = g * G * 128
            kT, qT, vT = t["kT"], t["qT"], t["vT"]
            g_ps, tr_ps = t["g_ps"], t["tr_ps"]
            for nm, tl, src in (("kT", kT, "gkT"), ("qT", qT, "gqT"), ("vT", vT, "gvT")):
                P.dma("sp", tl[:, :, :], S[src][r0:r0 + G * 128, t0:t0 + 128].rearrange("(h p) t -> p h t", p=128), writes=[rr[nm]])
            P.op("pe", lambda e: e.matmul(g_ps[:, :, :], triu[:, :], Gb[:, g * G:(g + 1) * G, :], start=True, stop=False),
                 reads=[r["triu"], r["Gb"]], writes=[rr["g_ps"]])
            P.op("pe", lambda e: e.matmul(g_ps[:, :, :], negones[:, :], X[:, g * G:(g + 1) * G, :], start=False, stop=True),
                 reads=[r["negones"], r["X"]], writes=[rr["g_ps"]])
            yield
            P.op("dve", lambda e: e.tensor_scalar(out=t["M1"][:, :, :], in0=g_ps[:, :, :], scalar1=0.0, scalar2=None, op0=ALU.min),
                 reads=[rr["g_ps"]], writes=[rr["M1"]])
            yield
            P.op("act", lambda e: e.activation(out=t["dec"][:, :, :], in_=t["M1"][:, :, :], func=AF.Exp), reads=[rr["M1"]], writes=[rr["dec"]])
            for h in range(G):
                P.op("pe", lambda e, h=h: e.matmul(g_ps[:, h, :], kT[:, h, :], kT[:, h, :], start=True, stop=True),
                     reads=[rr["kT"]], writes=[rr["g_ps"]])
            yield
            P.op("pool", lambda e: e.tensor_tensor(out=t["dec_i"][:, :, :], in0=t["dec"][:, :, :], in1=bcm(mli), op=ALU.mult),
                 reads=[rr["dec"], r["mli"]], writes=[rr["dec_i"]])
            P.op("dve", lambda e: e.tensor_tensor(out=t["dec_s"][:, :, :], in0=t["dec"][:, :, :], in1=bcm(mls), op=ALU.mult),
                 reads=[rr["dec"], r["mls"]], writes=[rr["dec_s"]])
            yield
            P.op("dve", lambda e: e.tensor_tensor(out=t["dec_s"][:, :, :], in0=t["dec_s"][:, :, :],
                                                  in1=gb[sc][:, 16 + g * G:16 + (g + 1) * G].unsqueeze(2).to_broadcast([128, G, 128]), op=ALU.mult),
                 reads=[rr["dec_s"], r[f"gb{sc}"]], writes=[rr["dec_s"]])
            yield
            P.op("dve", lambda e: e.tensor_tensor(out=t["L"][:, :, :], in0=g_ps[:, :, :], in1=t["dec_s"][:, :, :], op=ALU.mult),
                 reads=[rr["g_ps"], rr["dec_s"]], writes=[rr["L"]])
            yield
            for h in range(G):
                P.op("pe", lambda e, h=h: e.matmul(g_ps[:, h, :], qT[:, h, :], kT[:, h, :], start=True, stop=True),
                     reads=[rr["kT"], rr["qT"]], writes=[rr["g_ps"]])
            for h in range(G):
                P.op("pe", lambda e, h=h: e.transpose(tr_ps[:, 0, h, :], t["L"][:, h, :], identb[:, :]),
                     reads=[rr["L"], r["identb"]], writes=[rr["tr_ps"]])
            yield
            P.op("dve", lambda e: e.tensor_tensor(out=t["attn"][:, :, :], in0=g_ps[:, :, :], in1=t["dec_i"][:, :, :], op=ALU.mult),
                 reads=[rr["g_ps"], rr["dec_i"]], writes=[rr["attn"]])
            P.op("act", lambda e: e.copy(t["LT"][:, :, :], tr_ps[:, 0, :, :]), reads=[rr["tr_ps"]], writes=[rr["LT"]])
            yield
            for h in range(G):
                P.op("pe", lambda e, h=h: e.transpose(tr_ps[:, 1, h, :], t["attn"][:, h, :], identb[:, :]),
                     reads=[rr["attn"], r["identb"]], writes=[rr["tr_ps"]])
            yield
            P.op("act", lambda e: e.copy(t["attnT"][:, :, :], tr_ps[:, 1, :, :]), reads=[rr["tr_ps"]], writes=[rr["attnT"]])
            P.dma("pool", S["attnT_d"][c, :, g * G:(g + 1) * G, :], t["attnT"][:, :, :],
                  reads=[rr["attnT"]], writes=[r["attnT_out"]])
            yield
            yield from neumann_gen(P, nc, t["L"], t["LT"], rr["L"], rr["LT"], t["W"], rr)
            Pt, rPt = t["W"]["result"]
            for h in range(G):
                P.op("pe", lambda e, h=h: e.transpose(tr_ps[:, 0, h, :], kT[:, h, :], identb[:, :]),
                     reads=[rr["kT"], r["identb"]], writes=[rr["tr_ps"]])
            for h in range(G):
                P.op("pe", lambda e, h=h: e.transpose(tr_ps[:, 1, h, :], vT[:, h, :], identb[:, :]),
                     reads=[rr["vT"], r["identb"]], writes=[rr["tr_ps"]])
            yield
            P.op("dve", lambda e: e.tensor_tensor(out=t["kbg"][:, :, :], in0=tr_ps[:, 0, :, :], in1=bcg(bgc, g), op=ALU.mult),
                 reads=[rr["tr_ps"], r["bgc"]], writes=[rr["kbg"]])
            P.op("dve", lambda e: e.tensor_tensor(out=t["ktail"][:, :, :], in0=tr_ps[:, 0, :, :], in1=bcg(etail, g), op=ALU.mult),
                 reads=[rr["tr_ps"], r["etail"]], writes=[rr["ktail"]])
            P.op("dve", lambda e: e.tensor_tensor(out=t["vb"][:, :, :], in0=tr_ps[:, 1, :, :],
                                                  in1=gb[sc][:, 16 + g * G:16 + (g + 1) * G].unsqueeze(2).to_broadcast([128, G, 128]), op=ALU.mult),
                 reads=[rr["tr_ps"], r[f"gb{sc}"]], writes=[rr["vb"]])
            P.dma("pool", S["ktl_d"][t0:t0 + 128, r0:r0 + G * 128], t["ktail"][:, :, :].rearrange("p h d -> p (h d)"),
                  reads=[rr["ktail"]], writes=[r["ktl_out"]])
            yield
            for h in range(G):
                P.op("pe", lambda e, h=h: e.matmul(g_ps[:, h, :], Pt[:, h, :], t["vb"][:, h, :], start=True, stop=True),
                     reads=[rPt, rr["vb"]], writes=[rr["g_ps"]])
            yield
            P.op("act", lambda e: e.copy(t["u_sb"][:, :, :], g_ps[:, :, :]), reads=[rr["g_ps"]], writes=[rr["u_sb"]])
            P.dma("pool", S["u_d"][t0:t0 + 128, r0:r0 + G * 128], t["u_sb"][:, :, :].rearrange("p h d -> p (h d)"),
                  reads=[rr["u_sb"]], writes=[r["u_out"]])
            yield
            for h in range(G):
                P.op("pe", lambda e, h=h: e.matmul(g_ps[:, h, :], t["kbg"][:, h, :], Pt[:, h, :], start=True, stop=True),
                     reads=[rPt, rr["kbg"]], writes=[rr["g_ps"]])
            yield
            P.op("act", lambda e: e.copy(t["wk_sb"][:, :, :], g_ps[:, :, :]), reads=[rr["g_ps"]], writes=[rr["wk_sb"]])
            P.dma("pool", S["wkT_d"][r0:r0 + G * 128, t0:t0 + 128].rearrange("(h p) t -> p h t", p=128), t["wk_sb"][:, :, :],
                  reads=[rr["wk_sb"]], writes=[r["wk_out"]])
            yield

        for c in range(NCH):
            t0 = c * 128
            sc = c % 2
            gps0 = ST[0]["g_ps"]
            rg0 = ST[0]["r"]["g_ps"]
            P.dma("sp", gb[sc][:, :], S["gbt"][t0:t0 + 128, :], writes=[r[f"gb{sc}"]])
            P.op("pe", lambda e, sc=sc: e.matmul(gps0[:, 0, 0:16], triu[:, :], gb[sc][:, 0:16], start=True, stop=True),
                 reads=[r["triu"], r[f"gb{sc}"]], writes=[rg0])
            P.op("pe", lambda e, sc=sc: e.matmul(gps0[:, 0, 16:32], onesf[:, :], gb[sc][:, 0:16], start=True, stop=True),
                 reads=[r["onesf"], r[f"gb{sc}"]], writes=[rg0])
            P.op("dve", lambda e: e.tensor_copy(gcs[:, :], gps0[:, 0, 0:32]), reads=[rg0], writes=[r["gcs"]])
            P.op("act", lambda e: e.activation(out=egc[:, :], in_=gcs[:, 0:16], func=AF.Exp), reads=[r["gcs"]], writes=[r["egc"]])
            P.op("dve", lambda e: e.tensor_tensor(out=dtl[:, :], in0=gcs[:, 16:32], in1=gcs[:, 0:16], op=ALU.subtract),
                 reads=[r["gcs"]], writes=[r["dtl"]])
            P.op("act", lambda e: e.activation(out=etail[:, :], in_=dtl[:, :], func=AF.Exp), reads=[r["dtl"]], writes=[r["etail"]])
            P.op("act", lambda e, sc=sc: e.activation(out=elast[sc][:, :], in_=gcs[:, 16:32], func=AF.Exp),
                 reads=[r["gcs"]], writes=[r[f"elast{sc}"]])
            P.dma("pool", S["els_d"][c, :, :], elast[sc][:, :], reads=[r[f"elast{sc}"]], writes=[r["els_out"]])
            P.op("dve", lambda e, sc=sc: e.tensor_tensor(out=bgc[:, :], in0=gb[sc][:, 16:32], in1=egc[:, :], op=ALU.mult),
                 reads=[r[f"gb{sc}"], r["egc"]], writes=[r["bgc"]])
            P.op("pool", lambda e, sc=sc: e.tensor_copy(Gb[:, :, :], gb[sc][:, 0:16].unsqueeze(2).to_broadcast([128, 16, 128])),
                 reads=[r[f"gb{sc}"]], writes=[r["Gb"]])
            P.op("pool", lambda e: e.tensor_tensor(out=X[:, :, :], in0=Gb[:, :, :],
                                                   in1=triu[:, :].unsqueeze(1).to_broadcast([128, 16, 128]), op=ALU.mult),
                 reads=[r["Gb"], r["triu"]], writes=[r["X"]])
            for g0 in (0, 2):
                zipper([grp(c, g0, 0, sc), grp(c, g0 + 1, 1, sc)])
        P.barrier()
        P.flush()


def phase_gdn_g2(P, nc, projT, S, C, yT, normw_dram, T, zrow, kc0=16):
    NCH = T // 128
    G = 4
    H = 16
    with contextlib.ExitStack() as st:
        def sb(name, shape, dt):
            return st.enter_context(nc.sbuf_tensor(_uid() + "g2_" + name, shape, dt))

        def pst(name, shape, dt):
            return st.enter_context(nc.psum_tensor(_uid() + "g2_" + name, shape, dt))
        r = defaultdict(Res)
        identf = sb("identf", [128, 128], F32); identb = sb("identb", [128, 128], BF16)
        nrm = sb("nrm", [128, 1], F32); epsc = sb("epsc", [128, 1], F32); mh = sb("mh", [128, G], F32)
        wkT = [sb(f"wkT{i}", [128, H, 128], BF16) for i in range(2)]
        qdT = [sb(f"qdT{i}", [128, H, 128], BF16) for i in range(2)]
        atT = [sb(f"atT{i}", [128, H, 128], BF16) for i in range(2)]
        ktl = [sb(f"ktl{i}", [128, H, 128], BF16) for i in range(2)]
        uu = [sb(f"uu{i}", [128, H, 128], F32) for i in range(2)]
        zt = [sb(f"zt{i}", [128, H, 128], F32) for i in range(2)]
        els = [sb(f"els{i}", [128, H], F32) for i in range(2)]
        Sf = sb("Sf", [128, H, 128], F32); Sb = sb("Sb", [128, H, 128], BF16)
        TS = []
        for q in range(2):
            d = {"Stmp": sb(f"Stmp{q}", [128, G, 128], F32), "vnew": sb(f"vnew{q}", [128, G, 128], BF16),
                 "o_sb": sb(f"o_sb{q}", [128, G, 128], F32), "osq": sb(f"osq{q}", [128, G, 128], F32),
                 "s2": sb(f"s2{q}", [128, G], F32), "sd": sb(f"sd{q}", [128, G], F32), "rstd": sb(f"rstd{q}", [128, G], F32),
                 "on": sb(f"on{q}", [128, G, 128], BF16), "sz": sb(f"sz{q}", [128, G, 128], F32), "yg": sb(f"yg{q}", [128, G, 128], F32)}
            TS.append(d)
        yfin = [sb(f"yfin{i}", [128, G, 128], BF16) for i in range(2)]
        ws_ps = [pst(f"ws_ps{i}", [128, G, 128], F32) for i in range(2)]
        o_ps = [pst(f"o_ps{i}", [128, G, 128], F32) for i in range(2)]
        kv_ps = [pst(f"kv_ps{i}", [128, G, 128], F32) for i in range(2)]
        tr_ps2 = [pst(f"tr_ps{q}", [128, 2, G, 128], BF16) for q in range(2)]
        P.dma("sp", identf[:, :], C["ident"][:, :], writes=[r["identf"]])
        P.dma("sp", nrm[:, :], normw_dram[:, :], writes=[r["nrm"]])
        P.op("pool", lambda e: e.tensor_copy(identb[:, :], identf[:, :]), reads=[r["identf"]], writes=[r["identb"]])
        P.op("pool", lambda e: e.memset(epsc[:, :], 1e-6), writes=[r["eps"]])
        P.op("pool", lambda e: e.memset(mh[:, :], -0.5), writes=[r["mh"]])
        P.op("pool", lambda e: e.memset(Sf[:, :, :].rearrange("p h e -> p (h e)"), 0.0), writes=[r["Sf"]])
        P.op("pool", lambda e: e.memset(Sb[:, :, :].rearrange("p h e -> p (h e)"), 0.0), writes=[r["Sb"]])
        def grp(c, g, b, s):
            t0 = c * 128
            T_ = TS[b]
            Stmp, vnew, o_sb, osq, s2, sd, rstd, on, sz, yg = [T_[n] for n in ("Stmp", "vnew", "o_sb", "osq", "s2", "sd", "rstd", "on", "sz", "yg")]
            tr_ps = tr_ps2[b]
            hs = slice(g * G, (g + 1) * G)
            rS = r[f"Sf{g}"]; rSb = r[f"Sb{g}"]
            for h in range(G):
                hh = g * G + h
                P.op("pe", lambda e, h=h, hh=hh, s=s, b=b: e.matmul(ws_ps[b][:, h, :], wkT[s][:, hh, :], Sb[:, hh, :], start=True, stop=True),
                     reads=[r[f"wkT{s}"], rSb, r["Sb"]], writes=[r[f"ws_ps{b}"]])
            yield
            P.op("dve", lambda e, s=s, b=b, hs=hs: e.tensor_tensor(out=vnew[:, :, :], in0=uu[s][:, hs, :], in1=ws_ps[b][:, :, :], op=ALU.subtract),
                 reads=[r[f"uu{s}"], r[f"ws_ps{b}"]], writes=[r["vnew" + str(b)]])
            yield
            for h in range(G):
                hh = g * G + h
                P.op("pe", lambda e, h=h, hh=hh, s=s, b=b: e.matmul(o_ps[b][:, h, :], qdT[s][:, hh, :], Sb[:, hh, :], start=True, stop=False),
                     reads=[r[f"qdT{s}"], rSb, r["Sb"]], writes=[r[f"o_ps{b}"]])
                P.op("pe", lambda e, h=h, hh=hh, s=s, b=b: e.matmul(o_ps[b][:, h, :], atT[s][:, hh, :], vnew[:, h, :], start=False, stop=True),
                     reads=[r[f"atT{s}"], r["vnew" + str(b)]], writes=[r[f"o_ps{b}"]])
            for h in range(G):
                hh = g * G + h
                P.op("pe", lambda e, h=h, hh=hh, s=s, b=b: e.matmul(kv_ps[b][:, h, :], ktl[s][:, hh, :], vnew[:, h, :], start=True, stop=True),
                     reads=[r[f"ktl{s}"], r["vnew" + str(b)]], writes=[r[f"kv_ps{b}"]])
            yield
            P.op("dve", lambda e, s=s, hs=hs, g=g: e.tensor_tensor(out=Stmp[:, :, :], in0=Sf[:, hs, :],
                                                                  in1=els[s][:, g * G:(g + 1) * G].unsqueeze(2).to_broadcast([128, G, 128]),
                                                                  op=ALU.mult),
                 reads=[rS, r["Sf"], r[f"els{s}"]], writes=[r["Stmp" + str(b)]])
            P.op("dve", lambda e, hs=hs, b=b: e.tensor_tensor(out=Sf[:, hs, :], in0=Stmp[:, :, :], in1=kv_ps[b][:, :, :], op=ALU.add),
                 reads=[r["Stmp" + str(b)], r[f"kv_ps{b}"]], writes=[rS])
            P.op("act", lambda e, hs=hs: e.copy(Sb[:, hs, :], Sf[:, hs, :]), reads=[rS], writes=[rSb])
            yield
            P.op("act", lambda e, b=b: e.copy(o_sb[:, :, :], o_ps[b][:, :, :]), reads=[r[f"o_ps{b}"]], writes=[r["o_sb" + str(b)]])
            P.op("pool", lambda e: e.tensor_tensor(out=osq[:, :, :], in0=o_sb[:, :, :], in1=o_sb[:, :, :], op=ALU.mult),
                 reads=[r["o_sb" + str(b)]], writes=[r["osq" + str(b)]])
            yield
            P.op("dve", lambda e: e.tensor_reduce(out=s2[:, :], in_=osq[:, :, :], axis=AX.X, op=ALU.add), reads=[r["osq" + str(b)]], writes=[r["s2" + str(b)]])
            P.op("dve", lambda e: e.tensor_scalar(out=sd[:, :], in0=s2[:, :], scalar1=1.0 / 128, scalar2=1e-6, op0=ALU.mult, op1=ALU.add),
                 reads=[r["s2" + str(b)]], writes=[r["sd" + str(b)]])
            yield
            P.op("pool", lambda e: e.tensor_tensor(out=rstd[:, :], in0=sd[:, :], in1=mh[:, :], op=ALU.pow),
                 reads=[r["sd" + str(b)], r["mh"]], writes=[r["rstd" + str(b)]])
            P.op("dve", lambda e: e.tensor_tensor(out=on[:, :, :], in0=o_sb[:, :, :],
                                                  in1=rstd[:, :].unsqueeze(2).to_broadcast([128, G, 128]), op=ALU.mult),
                 reads=[r["o_sb" + str(b)], r["rstd" + str(b)]], writes=[r["on" + str(b)]])
            yield
            for h in range(G):
                P.op("pe", lambda e, h=h: e.transpose(tr_ps[:, 0, h, :], on[:, h, :], identb[:, :]),
                     reads=[r["on" + str(b)], r["identb"]], writes=[r["tr_ps" + str(b)]])
            P.op("act", lambda e, s=s, hs=hs: e.activation(out=sz[:, :, :], in_=zt[s][:, hs, :], func=AF.Silu),
                 reads=[r[f"zt{s}"]], writes=[r["sz" + str(b)]])
            yield
            P.op("dve", lambda e: e.scalar_tensor_tensor(out=yg[:, :, :], in0=tr_ps[:, 0, :, :], scalar=nrm[:, 0:1], in1=sz[:, :, :],
                                                         op0=ALU.mult, op1=ALU.mult),
                 reads=[r["tr_ps" + str(b)], r["nrm"], r["sz" + str(b)]], writes=[r["yg" + str(b)]])
            P.op("pool", lambda e, b=b: e.tensor_copy(yfin[b][:, :, :], yg[:, :, :]), reads=[r["yg" + str(b)]], writes=[r[f"yfin{b}"]])
            P.dma("pool", yT[kc0 + g * G:kc0 + (g + 1) * G, :, t0:t0 + 128].rearrange("k p t -> p k t"), yfin[b][:, :, :],
                  reads=[r[f"yfin{b}"]], writes=[r["y_out"]])

        it = 0
        for c in range(NCH):
            t0 = c * 128
            s = c % 2
            for g in range(4):
                r0 = g * G * 128
                hs = slice(g * G, (g + 1) * G)
                P.dma("sp", wkT[s][:, hs, :], S["wkT_d"][r0:r0 + G * 128, t0:t0 + 128].rearrange("(h p) t -> p h t", p=128),
                      writes=[r[f"wkT{s}"]])
                P.dma("sp", qdT[s][:, hs, :], S["gqdT"][r0:r0 + G * 128, t0:t0 + 128].rearrange("(h p) t -> p h t", p=128),
                      writes=[r[f"qdT{s}"]])
                P.dma("sp", atT[s][:, hs, :], S["attnT_d"][c, :, g * G:(g + 1) * G, :],
                      writes=[r[f"atT{s}"]])
                P.dma("sp", ktl[s][:, hs, :].rearrange("p h d -> p (h d)"), S["ktl_d"][t0:t0 + 128, r0:r0 + G * 128],
                      writes=[r[f"ktl{s}"]])
                P.dma("sp", uu[s][:, hs, :].rearrange("p h d -> p (h d)"), S["u_d"][t0:t0 + 128, r0:r0 + G * 128],
                      writes=[r[f"uu{s}"]])
                P.dma("sp", zt[s][:, hs, :], projT[zrow + r0:zrow + r0 + G * 128, t0:t0 + 128].rearrange("(h p) t -> p h t", p=128),
                      writes=[r[f"zt{s}"]])
            P.dma("sp", els[s][:, :], S["els_d"][c, :, :], writes=[r[f"els{s}"]])
            for g0 in (0, 2):
                zipper([grp(c, g0, 0, s), grp(c, g0 + 1, 1, s)])
        P.barrier()
        P.flush()


def rwkv_consts():
    c = {}
    bo = np.zeros((128, 128), np.float32)
    bo[:64, :64] = 1.0
    bo[64:, 64:] = 1.0
    c["blockones"] = bo
    return c


def phase_rwkv_pre(P, nc, projT, S, C, prm, T, row0):
    NB = T // 512
    with contextlib.ExitStack() as st:
        def sb(name, shape, dt):
            return st.enter_context(nc.sbuf_tensor(_uid() + "wp_" + name, shape, dt))

        def pst(name, shape, dt):
            return st.enter_context(nc.psum_tensor(_uid() + "wp_" + name, shape, dt))
        r = defaultdict(Res)
        mu = sb("mu", [128, 33], F32); omm = sb("omm", [128, 33], F32)
        w0 = sb("w0", [128, 8], F32); nw0 = sb("nw0", [128, 8], F32); a0 = sb("a0", [128, 8], F32)
        kk_ = sb("kk_", [128, 8], F32); ka = sb("ka", [128, 8], F32); omka = sb("omka", [128, 8], F32); rk = sb("rk", [128, 8], F32)
        lw2 = sb("lw2", [128, 1024], F32)
        bones = sb("bones", [128, 128], F32)
        cmask = sb("cmask", [128, 512], F32)
        onec = sb("onec", [128, 1], F32); negh = sb("negh", [128, 1], F32); epsc = sb("epsc", [128, 1], F32)
        ul = sb("ul", [128, 513], F32); ml = sb("ml", [128, 512], F32); th = sb("th", [128, 512], F32)
        TS = []
        for q in range(2):
            d = {}
            for j in range(4):
                d[f"u{j}"] = sb(f"u{j}_{q}", [128, 513], F32); d[f"mx{j}"] = sb(f"mx{j}_{q}", [128, 512], F32)
            for n in ("tmp", "e1", "spt", "e2", "cw", "ecw", "cwm", "ecwm", "einv", "av", "kk0", "sq", "sd", "rn", "kkn", "fac", "k2", "kka", "prod"):
                d[n] = sb(f"{n}_{q}", [128, 512], F32)
            TS.append(d)
        tmp = sb("tmp_l", [128, 512], F32)
        obf = {n: [sb(f"o_{n}{i}", [128, 512], BF16) for i in range(2)] for n in ("rt", "kkt", "kh", "kka", "v")}
        of32 = {n: [sb(f"o_{n}{i}", [128, 512], F32) for i in range(2)] for n in ("bon", "sz")}
        pcs = [sb(f"pcs{i}", [128, 4], F32) for i in range(2)]
        for q in range(2):
            for n in ("wl_ps", "a_ps", "ss_ps", "sb_ps"):
                TS[q][n] = pst(f"{n}_{q}", [128, 512], F32)
        for nm, t_, src in (("mu", mu, "muT"), ("w0", w0, "w0T"), ("a0", a0, "a0T"), ("kk_", kk_, "kkT"), ("ka", ka, "kaT"),
                            ("rk", rk, "rkT"), ("lw2", lw2, "lw2")):
            P.dma("sp", t_[:, :], prm[src][:, :], writes=[r[nm]])
        P.dma("sp", bones[:, :], C["blockones"][:, :], writes=[r["bones"]])
        P.dma("sp", cmask[:, :], C["cmask128"][:, :], writes=[r["cmask"]])
        P.op("pool", lambda e: e.memset(onec[:, :], 1.0), writes=[r["onec"]])
        P.op("pool", lambda e: e.memset(negh[:, :], -0.5), writes=[r["negh"]])
        P.op("pool", lambda e: e.memset(epsc[:, :], 1e-6), writes=[r["eps"]])
        P.op("dve", lambda e: e.tensor_scalar(out=omm[:, :], in0=mu[:, :], scalar1=-1.0, scalar2=1.0, op0=ALU.mult, op1=ALU.add),
             reads=[r["mu"]], writes=[r["omm"]])
        P.op("dve", lambda e: e.tensor_scalar(out=omka[:, :], in0=ka[:, :], scalar1=-1.0, scalar2=1.0, op0=ALU.mult, op1=ALU.add),
             reads=[r["ka"]], writes=[r["omka"]])
        P.op("dve", lambda e: e.tensor_scalar(out=nw0[:, :], in0=w0[:, :], scalar1=-1.0, scalar2=None, op0=ALU.mult),
             reads=[r["w0"]], writes=[r["nw0"]])

        def load_mix(ut, rut, mt, rmt, row, ti, tb, tmp=tmp, rtmp=None):
            rtmp = rtmp if rtmp is not None else r["tmp_l"]
            t0 = tb * 512
            if tb == 0:
                P.op("pool", lambda e: e.memset(ut[:, 0:1], 0.0), writes=[rut])
                P.dma("sp", ut[:, 1:513], projT[row:row + 128, 0:512], writes=[rut])
            else:
                P.dma("sp", ut[:, :], projT[row:row + 128, t0 - 1:t0 + 512], writes=[rut])
            P.op("dve", lambda e: e.tensor_scalar(out=tmp[:, :], in0=ut[:, 1:513], scalar1=omm[:, ti:ti + 1], scalar2=None, op0=ALU.mult),
                 reads=[rut, r["omm"]], writes=[rtmp])
            P.op("dve", lambda e: e.scalar_tensor_tensor(out=mt[:, :], in0=ut[:, 0:512], scalar=mu[:, ti:ti + 1], in1=tmp[:, :],
                                                         op0=ALU.mult, op1=ALU.add),
                 reads=[rut, r["mu"], rtmp], writes=[rmt])
        def do_ct(ct, tb, s):
            t0 = tb * 512
            T_ = TS[s]
            rq = lambda n: r[f"{n}_{s}"]
            (e1, spt, e2, cw, ecw, cwm, ecwm, einv, av, kk0, sq, sd, rn, kkn, fac, k2, kka, prod) = [T_[n] for n in (
                "e1", "spt", "e2", "cw", "ecw", "cwm", "ecwm", "einv", "av", "kk0", "sq", "sd", "rn", "kkn", "fac", "k2", "kka", "prod")]
            wl_ps, a_ps, ss_ps, sb_ps = T_["wl_ps"], T_["a_ps"], T_["ss_ps"], T_["sb_ps"]
            for j in range(4):
                load_mix(T_[f"u{j}"], rq(f"u{j}"), T_[f"mx{j}"], rq(f"mx{j}"), row0 + j * 1024 + ct * 128, j * 8 + ct, tb, T_["tmp"], rq("tmp"))
            rm, km, vm, zm = T_['mx0'], T_['mx1'], T_['mx2'], T_['mx3']
            yield
            P.op("pe", lambda e, ct=ct: e.matmul(wl_ps[:, :], lw2[0:64, ct * 128:(ct + 1) * 128], th[0:64, :], start=True, stop=True),
                 reads=[r["lw2"], r["th"]], writes=[rq("wl_ps")])
            P.op("pe", lambda e, ct=ct: e.matmul(a_ps[:, :], lw2[64:128, ct * 128:(ct + 1) * 128], ml[64:128, :], start=True, stop=True),
                 reads=[r["lw2"], r["ml"]], writes=[rq("a_ps")])
            P.op("act", lambda e, ct=ct: e.activation(out=e1[:, :], in_=wl_ps[:, :], func=AF.Exp, bias=nw0[:, ct:ct + 1], scale=-1.0),
                 reads=[rq("wl_ps"), r["nw0"]], writes=[rq("e1")])
            P.op("act", lambda e: e.activation(out=spt[:, :], in_=e1[:, :], func=AF.Ln, bias=onec[:, 0:1], scale=1.0),
                 reads=[rq("e1"), r["onec"]], writes=[rq("spt")])
            P.op("act", lambda e: e.activation(out=e2[:, :], in_=spt[:, :], func=AF.Exp, bias=negh[:, 0:1], scale=-1.0),
                 reads=[rq("spt"), r["negh"]], writes=[rq("e2")])
            P.op("dve", lambda e: e.tensor_tensor_scan(out=cw[:, :], data0=cmask[:, :], data1=e2[:, :], initial=0.0,
                                                       op0=ALU.mult, op1=ALU.subtract),
                 reads=[r["cmask"], rq("e2")], writes=[rq("cw")])
            P.op("act", lambda e: e.activation(out=ecw[:, :], in_=cw[:, :], func=AF.Exp), reads=[rq("cw")], writes=[rq("ecw")])
            P.op("pool", lambda e: e.tensor_tensor(out=cwm[:, :], in0=cw[:, :], in1=e2[:, :], op=ALU.add),
                 reads=[rq("cw"), rq("e2")], writes=[rq("cwm")])
            P.op("act", lambda e: e.activation(out=ecwm[:, :], in_=cwm[:, :], func=AF.Exp), reads=[rq("cwm")], writes=[rq("ecwm")])
            P.op("act", lambda e: e.activation(out=einv[:, :], in_=cw[:, :], func=AF.Exp, scale=-1.0), reads=[rq("cw")], writes=[rq("einv")])
            P.op("act", lambda e, ct=ct: e.activation(out=av[:, :], in_=a_ps[:, :], func=AF.Sigmoid, bias=a0[:, ct:ct + 1], scale=1.0),
                 reads=[rq("a_ps"), r["a0"]], writes=[rq("av")])
            yield
            P.op("dve", lambda e, ct=ct: e.tensor_scalar(out=kk0[:, :], in0=km[:, :], scalar1=kk_[:, ct:ct + 1], scalar2=None, op0=ALU.mult),
                 reads=[rq("mx1"), r["kk_"]], writes=[rq("kk0")])
            P.op("pool", lambda e: e.tensor_tensor(out=sq[:, :], in0=kk0[:, :], in1=kk0[:, :], op=ALU.mult), reads=[rq("kk0")], writes=[rq("sq")])
            P.op("pe", lambda e: e.matmul(ss_ps[:, :], bones[:, :], sq[:, :], start=True, stop=True),
                 reads=[r["bones"], rq("sq")], writes=[rq("ss_ps")])
            P.op("act", lambda e: e.activation(out=sd[:, :], in_=ss_ps[:, :], func=AF.Sqrt, bias=epsc[:, 0:1], scale=1.0),
                 reads=[rq("ss_ps"), r["eps"]], writes=[rq("sd")])
            P.op("dve", lambda e: e.reciprocal(rn[:, :], sd[:, :]), reads=[rq("sd")], writes=[rq("rn")])
            P.op("dve", lambda e: e.tensor_tensor(out=kkn[:, :], in0=kk0[:, :], in1=rn[:, :], op=ALU.mult),
                 reads=[rq("kk0"), rq("rn")], writes=[rq("kkn")])
            P.op("dve", lambda e, ct=ct: e.tensor_scalar(out=fac[:, :], in0=av[:, :], scalar1=ka[:, ct:ct + 1], scalar2=omka[:, ct:ct + 1],
                                                         op0=ALU.mult, op1=ALU.add),
                 reads=[rq("av"), r["ka"], r["omka"]], writes=[rq("fac")])
            P.op("dve", lambda e: e.tensor_tensor(out=k2[:, :], in0=km[:, :], in1=fac[:, :], op=ALU.mult),
                 reads=[rq("mx1"), rq("fac")], writes=[rq("k2")])
            P.op("pool", lambda e: e.tensor_tensor(out=kka[:, :], in0=kkn[:, :], in1=av[:, :], op=ALU.mult),
                 reads=[rq("kkn"), rq("av")], writes=[rq("kka")])
            yield
            P.op("dve", lambda e, s=s: e.tensor_tensor(out=obf["rt"][s][:, :], in0=rm[:, :], in1=ecw[:, :], op=ALU.mult),
                 reads=[rq("mx0"), rq("ecw")], writes=[r[f"o_rt{s}"]])
            P.op("dve", lambda e, s=s: e.tensor_tensor(out=obf["kkt"][s][:, :], in0=kkn[:, :], in1=ecwm[:, :], op=ALU.mult),
                 reads=[rq("kkn"), rq("ecwm")], writes=[r[f"o_kkt{s}"]])
            P.op("dve", lambda e, s=s: e.tensor_tensor(out=obf["kh"][s][:, :], in0=k2[:, :], in1=einv[:, :], op=ALU.mult),
                 reads=[rq("k2"), rq("einv")], writes=[r[f"o_kh{s}"]])
            P.op("pool", lambda e, s=s: e.tensor_tensor(out=obf["kka"][s][:, :], in0=kka[:, :], in1=einv[:, :], op=ALU.mult),
                 reads=[rq("kka"), rq("einv")], writes=[r[f"o_kka{s}"]])
            P.op("pool", lambda e, s=s: e.tensor_copy(obf["v"][s][:, :], vm[:, :]), reads=[rq("mx2")], writes=[r[f"o_v{s}"]])
            P.op("act", lambda e, s=s: e.activation(out=of32["sz"][s][:, :], in_=zm[:, :], func=AF.Silu), reads=[rq("mx3")], writes=[r[f"o_sz{s}"]])
            yield
            P.op("pool", lambda e: e.tensor_tensor(out=prod[:, :], in0=rm[:, :], in1=k2[:, :], op=ALU.mult),
                 reads=[rq("mx0"), rq("k2")], writes=[rq("prod")])
            P.op("dve", lambda e, ct=ct: e.tensor_scalar(out=prod[:, :], in0=prod[:, :], scalar1=rk[:, ct:ct + 1], scalar2=None, op0=ALU.mult),
                 reads=[rq("prod"), r["rk"]], writes=[rq("prod")])
            P.op("pe", lambda e: e.matmul(sb_ps[:, :], bones[:, :], prod[:, :], start=True, stop=True),
                 reads=[r["bones"], rq("prod")], writes=[rq("sb_ps")])
            P.op("dve", lambda e, s=s: e.tensor_tensor(out=of32["bon"][s][:, :], in0=vm[:, :], in1=sb_ps[:, :], op=ALU.mult),
                 reads=[rq("mx2"), rq("sb_ps")], writes=[r[f"o_bon{s}"]])
            P.op("pool", lambda e, s=s: e.tensor_copy(pcs[s][:, :], ecw[:, 127:512:128]), reads=[rq("ecw")], writes=[r[f"pcs{s}"]])
            rows_ = slice(ct * 128, (ct + 1) * 128)
            for n, dst in (("rt", "rtT"), ("kkt", "kktT"), ("kh", "khT"), ("kka", "kkaT"), ("v", "rvT")):
                P.dma("pool", S[dst][rows_, t0:t0 + 512], obf[n][s][:, :], reads=[r[f"o_{n}{s}"]], writes=[r[dst]])
            for n, dst in (("bon", "bonT"), ("sz", "szT")):
                P.dma("pool", S[dst][rows_, t0:t0 + 512], of32[n][s][:, :], reads=[r[f"o_{n}{s}"]], writes=[r[dst]])
            P.dma("pool", S["pc_d"][ct, :, tb * 4:(tb + 1) * 4], pcs[s][:, :], reads=[r[f"pcs{s}"]], writes=[r["pc_d"]])

        cnt = 0
        for tb in range(NB):
            t0 = tb * 512
            load_mix(ul, r["ul"], ml, r["ml"], row0 + 4096, 32, tb)
            P.op("act", lambda e: e.activation(out=th[0:64, :], in_=ml[0:64, :], func=AF.Tanh), reads=[r["ml"]], writes=[r["th"]])
            for ct in range(0, 8, 2):
                zipper_lag([do_ct(ct, tb, 0), do_ct(ct + 1, tb, 1)])
        P.barrier()
        P.flush()


def phase_rwkv_r1(P, nc, S, C, T):
    NCH = T // 128
    G = 4
    with contextlib.ExitStack() as st:
        def sb(name, shape, dt):
            return st.enter_context(nc.sbuf_tensor(_uid() + "r1_" + name, shape, dt))

        def pst(name, shape, dt):
            return st.enter_context(nc.psum_tensor(_uid() + "r1_" + name, shape, dt))
        r = defaultdict(Res)
        mls = sb("mls", [128, 128], F32); mus = sb("mus", [128, 128], F32); mui = sb("mui", [128, 128], F32); nmui = sb("nmui", [128, 128], F32)
        identf = sb("identf", [128, 128], F32); identb = sb("identb", [128, 128], BF16)
        tl = {n: [sb(f"{n}{i}", [128, 2, 128], BF16) for i in range(2)] for n in ("rt", "kkt", "kh", "kka", "v")}
        tz = {n: [sb(f"z{n}{i}", [128, 2, 2, 128], BF16) for i in range(2)] for n in ("kkt", "kh", "kka")}
        L = sb("L", [128, G, 128], BF16); LT = sb("LT", [128, G, 128], BF16)
        om = {n: [sb(f"{n}{i}", [128, G, 128], BF16) for i in range(2)] for n in ("akv", "bkv", "nbab", "tinv")}
        tk = {n: [sb(f"tk_{n}{i}", [128, 2, 128], BF16) for i in range(2)] for n in ("v", "kh", "kka")}
        W = {"identb": identb,
             "nL": [sb(f"nL{i}", [128, G, 128], BF16) for i in range(2)],
             "nLT": [sb(f"nLT{i}", [128, G, 128], BF16) for i in range(2)],
             "Pk": [sb(f"Pk{i}", [128, G, 128], BF16) for i in range(2)],
             "pa": pst("pa", [128, G, 128], F32), "pb": pst("pb", [128, G, 128], F32), "pp": pst("pp", [128, G, 128], F32)}
        s_ps = [pst(f"s_ps{i}", [128, G, 128], F32) for i in range(3)]
        tr_ps = pst("tr_ps", [128, 8, 128], BF16)
        for nm, t_, src in (("mls", mls, "mask_ls"), ("mus", mus, "mask_us"), ("mui", mui, "mask_ui"), ("identf", identf, "ident")):
            P.dma("sp", t_[:, :], C[src][:, :], writes=[r[nm]])
        P.op("pool", lambda e: e.tensor_copy(identb[:, :], identf[:, :]), reads=[r["identf"]], writes=[r["identb"]])
        P.op("dve", lambda e: e.tensor_scalar(out=nmui[:, :], in0=mui[:, :], scalar1=-1.0, scalar2=None, op0=ALU.mult),
             reads=[r["mui"]], writes=[r["nmui"]])

        def bcm(m):
            return m[:, :].unsqueeze(1).to_broadcast([128, G, 128])
        it = 0
        srcs = {"rt": "rtT", "kkt": "kktT", "kh": "khT", "kka": "kkaT", "v": "rvT"}
        for n in tz:
            for i in range(2):
                P.op("pool", lambda e, n=n, i=i: e.memset(tz[n][i][:, :, :, :].rearrange("p a q t -> p (a q t)"), 0.0), writes=[r[f"z{n}{i}"]])
        for c in range(NCH):
            t0 = c * 128
            for g in range(4):
                s = it % 2
                it += 1
                for n in tl:
                    P.dma("sp", tl[n][s][:, :, :], S[srcs[n]][g * 256:(g + 1) * 256, t0:t0 + 128].rearrange("(q p) t -> p q t", p=128),
                          writes=[r[f"{n}{s}"]])

                for n in tz:
                    srcv = S[srcs[n]][g * 256:(g + 1) * 256, t0:t0 + 128].rearrange("(q p) t -> p q t", p=128)
                    P.dma("sp", tz[n][s][0:64, 0, :, :], srcv[0:64], writes=[r[f"z{n}{s}"]])
                    P.dma("sp", tz[n][s][64:128, 1, :, :], srcv[64:128], writes=[r[f"z{n}{s}"]])

                def hz(n, h):
                    return tz[n][s][:, h % 2, h // 2, :]

                def hv(n, h):
                    return tl[n][s][:, h // 2, :]
                for h in range(G):
                    P.op("pe", lambda e, h=h, a=hz("kkt", h), b=hv("kka", h): e.matmul(s_ps[0][:, h, :], a, b, start=True, stop=True),
                         reads=[r[f"zkkt{s}"], r[f"kka{s}"]], writes=[r["s_ps0"]])
                for h in range(G):
                    P.op("pe", lambda e, h=h, a=hz("kka", h), b=hv("kkt", h): e.matmul(s_ps[1][:, h, :], a, b, start=True, stop=True),
                         reads=[r[f"zkka{s}"], r[f"kkt{s}"]], writes=[r["s_ps1"]])
                P.op("dve", lambda e: e.tensor_tensor(out=L[:, :, :], in0=s_ps[0][:, :, :], in1=bcm(mls), op=ALU.mult),
                     reads=[r["s_ps0"], r["mls"]], writes=[r["L"]])
                P.op("dve", lambda e: e.tensor_tensor(out=LT[:, :, :], in0=s_ps[1][:, :, :], in1=bcm(mus), op=ALU.mult),
                     reads=[r["s_ps1"], r["mus"]], writes=[r["LT"]])
                for h in range(G):
                    P.op("pe", lambda e, h=h, a=hz("kh", h), b=hv("kkt", h): e.matmul(s_ps[2][:, h, :], a, b, start=True, stop=True),
                         reads=[r[f"zkh{s}"], r[f"kkt{s}"]], writes=[r["s_ps2"]])
                P.op("dve", lambda e, s=s: e.tensor_tensor(out=om["akv"][s][:, :, :], in0=s_ps[2][:, :, :], in1=bcm(mus), op=ALU.mult),
                     reads=[r["s_ps2"], r["mus"]], writes=[r[f"akv{s}"]])
                for h in range(G):
                    P.op("pe", lambda e, h=h, a=hz("kh", h), b=hv("rt", h): e.matmul(s_ps[0][:, h, :], a, b, start=True, stop=True),
                         reads=[r[f"zkh{s}"], r[f"rt{s}"]], writes=[r["s_ps0"]])
                P.op("dve", lambda e, s=s: e.tensor_tensor(out=om["bkv"][s][:, :, :], in0=s_ps[0][:, :, :], in1=bcm(mui), op=ALU.mult),
                     reads=[r["s_ps0"], r["mui"]], writes=[r[f"bkv{s}"]])
                for h in range(G):
                    P.op("pe", lambda e, h=h, a=hz("kka", h), b=hv("rt", h): e.matmul(s_ps[1][:, h, :], a, b, start=True, stop=True),
                         reads=[r[f"zkka{s}"], r[f"rt{s}"]], writes=[r["s_ps1"]])
                P.op("dve", lambda e, s=s: e.tensor_tensor(out=om["nbab"][s][:, :, :], in0=s_ps[1][:, :, :], in1=bcm(mui), op=ALU.mult),
                     reads=[r["s_ps1"], r["mui"]], writes=[r[f"nbab{s}"]])
                Pt, rPt = neumann(P, nc, L, LT, r["L"], r["LT"], W, r)
                P.op("pool", lambda e, s=s, Pt=Pt: e.tensor_copy(om["tinv"][s][:, :, :], Pt[:, :, :]), reads=[rPt], writes=[r[f"tinv{s}"]])
                for n, dst in (("akv", "akvT_d"), ("bkv", "bkvT_d"), ("nbab", "nbabT_d"), ("tinv", "tinvT_d")):
                    P.dma("pool", S[dst][c, :, g * G:(g + 1) * G, :], om[n][s][:, :, :],
                          reads=[r[f"{n}{s}"]], writes=[r[dst]])
                for qi, n in enumerate(("v", "kh", "kka")):
                    for q in range(2):
                        P.op("pe", lambda e, qi=qi, q=q, n=n, s=s: e.transpose(tr_ps[:, qi * 2 + q, :], tl[n][s][:, q, :], identb[:, :]),
                             reads=[r[f"{n}{s}"], r["identb"]], writes=[r["tr_ps"]])
                for qi, (n, dst) in enumerate((("v", "vtk_d"), ("kh", "khtk_d"), ("kka", "kkatk_d"))):
                    P.op("act", lambda e, qi=qi, n=n, s=s: e.copy(tk[n][s][:, :, :], tr_ps[:, qi * 2:qi * 2 + 2, :]),
                         reads=[r["tr_ps"]], writes=[r[f"tk_{n}{s}"]])
                    P.dma("pool", S[dst][t0:t0 + 128, g * 256:(g + 1) * 256], tk[n][s][:, :, :].rearrange("p q d -> p (q d)"),
                          reads=[r[f"tk_{n}{s}"]], writes=[r[dst]])
        P.barrier()
        P.flush()


def phase_rwkv_r2(P, nc, S, C, yT, prm, T, kc0=8, eps=64e-5):
    NCH = T // 128
    H = 16
    with contextlib.ExitStack() as st:
        def sb(name, shape, dt):
            return st.enter_context(nc.sbuf_tensor(_uid() + "r2_" + name, shape, dt))

        def pst(name, shape, dt):
            return st.enter_context(nc.psum_tensor(_uid() + "r2_" + name, shape, dt))
        r = defaultdict(Res)
        identf = sb("identf", [128, 128], F32); identb = sb("identb", [128, 128], BF16)
        lnw = sb("lnw", [128, 8], F32); lnb = sb("lnb", [128, 8], F32); epsc = sb("epsc", [128, 1], F32); mh = sb("mh", [128, 8], F32)
        pc = sb("pc", [128, 8, NCH], F32)
        kkt = [sb(f"kkt{i}", [128, 2, 8, 128], BF16) for i in range(2)]
        rt = [sb(f"rt{i}", [128, 2, 8, 128], BF16) for i in range(2)]
        tkz = {n: [sb(f"tkz_{n}{i}", [128, 2, 8, 128], BF16) for i in range(2)] for n in ("kh", "kka")}
        mm = {n: [sb(f"{n}{i}", [128, H, 128], BF16) for i in range(2)] for n in ("akv", "bkv", "nbab", "tinv")}
        tk = {n: [sb(f"tk_{n}{i}", [128, 1024], BF16) for i in range(2)] for n in ("v",)}
        bon = [sb(f"bon{i}", [128, 8, 128], F32) for i in range(2)]
        szt = [sb(f"szt{i}", [128, 8, 128], F32) for i in range(2)]
        Tf = sb("Tf", [128, 8, 64], F32); Tb = sb("Tb", [128, 8, 64], BF16)
        yfin = [sb(f"yfin{i}", [128, 4, 128], BF16) for i in range(2)]
        TS = []
        for q in range(2):
            d = {"Ttmp": sb(f"Ttmp{q}", [128, 4, 64], F32), "rhs0": sb(f"rhs0{q}", [128, 8, 64], BF16), "nU": sb(f"nU{q}", [128, 8, 64], BF16),
                 "y_sb": sb(f"y_sb{q}", [128, 8, 64], F32), "ysq": sb(f"ysq{q}", [128, 8, 64], F32),
                 "yc": sb(f"yc{q}", [128, 8, 64], F32), "yn": sb(f"yn{q}", [128, 8, 64], BF16),
                 "t1": sb(f"t1{q}", [128, 4, 128], F32), "t2": sb(f"t2{q}", [128, 4, 128], F32)}
            for n in ("s1", "s2", "mean", "var", "sd", "rstd"):
                d[n] = sb(f"{n}{q}", [128, 8], F32)
            d["r0_ps"] = pst(f"r0_ps{q}", [128, 8, 64], F32)
            d["u_ps"] = d["r0_ps"]
            d["y_ps"] = pst(f"y_ps{q}", [128, 8, 64], F32)
            d["st_ps"] = pst(f"st_ps{q}", [128, 512], F32)[:, 0:256].rearrange("p (q e) -> p q e", e=64)
            d["tr_ps"] = pst(f"tr_ps{q}", [128, 1024], BF16)[:, 0:512].rearrange("p (q t) -> p q t", t=128)
            TS.append(d)
        P.dma("sp", identf[:, :], C["ident"][:, :], writes=[r["identf"]])
        P.dma("sp", lnw[:, :], prm["lnwT"][:, :], writes=[r["lnw"]])
        P.dma("sp", lnb[:, :], prm["lnbT"][:, :], writes=[r["lnb"]])
        for q in range(8):
            P.dma("sp", pc[:, q, :], S["pc_d"][q, :, :], writes=[r["pc"]])
        P.op("pool", lambda e: e.tensor_copy(identb[:, :], identf[:, :]), reads=[r["identf"]], writes=[r["identb"]])
        P.op("pool", lambda e: e.memset(epsc[:, :], eps), writes=[r["eps"]])
        P.op("pool", lambda e: e.memset(mh[:, :], -0.5), writes=[r["mh"]])
        P.op("pool", lambda e: e.memset(Tf[:, :, :].rearrange("p q e -> p (q e)"), 0.0), writes=[r["Tf"]])
        P.op("pool", lambda e: e.memset(Tb[:, :, :].rearrange("p q e -> p (q e)"), 0.0), writes=[r["Tb"]])
        def grp(c, gi, b, s):
            t0 = c * 128
            T_ = TS[b]
            (Ttmp, rhs0, nU, y_sb, ysq, yc, yn, t1, t2, s1, s2, mean, var, sd, rstd, r0_ps, u_ps, y_ps, st_ps, tr_ps) = [T_[n] for n in (
                "Ttmp", "rhs0", "nU", "y_sb", "ysq", "yc", "yn", "t1", "t2", "s1", "s2", "mean", "var", "sd", "rstd", "r0_ps", "u_ps", "y_ps", "st_ps", "tr_ps")]
            rT = r[f"Tf{gi}"]; rTb = r[f"Tb{gi}"]
            for h in range(8):
                hh = gi * 8 + h
                p_ = hh // 2
                ba = (hh % 2) * 64
                P.op("pe", lambda e, h=h, hh=hh, p_=p_, ba=ba, s=s: e.matmul(r0_ps[:, h, :], kkt[s][:, ba // 64, p_, :], Tb[:, p_, :],
                                                                          start=True, stop=False),
                     reads=[r[f"kkt{s}"], rTb, r["Tb"]], writes=[r["ru_ps" + str(b)]])
                P.op("pe", lambda e, h=h, hh=hh, s=s: e.matmul(r0_ps[:, h, :], mm["akv"][s][:, hh, :], tk["v"][s][:, hh * 64:(hh + 1) * 64],
                                                              start=False, stop=True),
                     reads=[r[f"akv{s}"], r[f"tk_v{s}"]], writes=[r["ru_ps" + str(b)]])
            yield
            P.op("act", lambda e: e.copy(rhs0[:, :, :], r0_ps[:, :, :]), reads=[r["ru_ps" + str(b)]], writes=[r["rhs0" + str(b)]])
            yield
            for h in range(8):
                hh = gi * 8 + h
                P.op("pe", lambda e, h=h, hh=hh, s=s: e.matmul(u_ps[:, h, :], mm["tinv"][s][:, hh, :], rhs0[:, h, :], start=True, stop=True),
                     reads=[r[f"tinv{s}"], r["rhs0" + str(b)]], writes=[r["ru_ps" + str(b)]])
            yield
            P.op("dve", lambda e: e.tensor_scalar(out=nU[:, :, :], in0=u_ps[:, :, :], scalar1=-1.0, scalar2=None, op0=ALU.mult),
                 reads=[r["ru_ps" + str(b)]], writes=[r["nU" + str(b)]])
            for h in range(8):
                hh = gi * 8 + h
                p_ = hh // 2
                ba = (hh % 2) * 64
                P.op("pe", lambda e, h=h, p_=p_, ba=ba, s=s: e.matmul(y_ps[:, h, :], rt[s][:, ba // 64, p_, :], Tb[:, p_, :],
                                                                   start=True, stop=False),
                     reads=[r[f"rt{s}"], rTb, r["Tb"]], writes=[r["y_ps" + str(b)]])
                P.op("pe", lambda e, h=h, hh=hh, s=s: e.matmul(y_ps[:, h, :], mm["bkv"][s][:, hh, :], tk["v"][s][:, hh * 64:(hh + 1) * 64],
                                                              start=False, stop=False),
                     reads=[r[f"bkv{s}"], r[f"tk_v{s}"]], writes=[r["y_ps" + str(b)]])
                P.op("pe", lambda e, h=h, hh=hh, s=s: e.matmul(y_ps[:, h, :], mm["nbab"][s][:, hh, :], nU[:, h, :], start=False, stop=True),
                     reads=[r[f"nbab{s}"], r["nU" + str(b)]], writes=[r["y_ps" + str(b)]])
            yield
            for pl in range(4):
                q = gi * 4 + pl
                seq = [("kh", 0, "v"), ("kh", 1, "v"), ("kka", 0, "u"), ("kka", 1, "u")]
                for i_, (n, a, rk_) in enumerate(seq):
                    hh = 2 * q + a
                    if rk_ == "v":
                        P.op("pe", lambda e, pl=pl, q=q, a=a, n=n, hh=hh, s=s, i_=i_: e.matmul(st_ps[:, pl, :], tkz[n][s][:, a, q, :],
                                                                                           tk["v"][s][:, hh * 64:(hh + 1) * 64],
                                                                                           start=(i_ == 0), stop=(i_ == 3)),
                             reads=[r[f"tkz_{n}{s}"], r[f"tk_v{s}"]], writes=[r["st_ps" + str(b)]])
                    else:
                        P.op("pe", lambda e, pl=pl, q=q, a=a, n=n, hh=hh, s=s, i_=i_, gi=gi: e.matmul(st_ps[:, pl, :], tkz[n][s][:, a, q, :],
                                                                                                  nU[:, hh - gi * 8, :],
                                                                                                  start=(i_ == 0), stop=(i_ == 3)),
                             reads=[r[f"tkz_{n}{s}"], r["nU" + str(b)]], writes=[r["st_ps" + str(b)]])
            yield
            qs = slice(gi * 4, gi * 4 + 4)
            P.op("dve", lambda e, qs=qs: e.tensor_tensor(out=Ttmp[:, :, :], in0=st_ps[:, :, :], in1=Tf[:, qs, :], op=ALU.add),
                 reads=[r["st_ps" + str(b)], rT, r["Tf"]], writes=[r["Ttmp" + str(b)]])
            P.op("dve", lambda e, qs=qs, c=c: e.tensor_tensor(out=Tf[:, qs, :], in0=Ttmp[:, :, :],
                                                              in1=pc[:, qs, c:c + 1].to_broadcast([128, 4, 64]), op=ALU.mult),
                 reads=[r["Ttmp" + str(b)], r["pc"]], writes=[rT])
            P.op("act", lambda e, qs=qs: e.copy(Tb[:, qs, :], Tf[:, qs, :]), reads=[rT], writes=[rTb])
            yield
            P.op("act", lambda e: e.copy(y_sb[:, :, :], y_ps[:, :, :]), reads=[r["y_ps" + str(b)]], writes=[r["y_sb" + str(b)]])
            P.op("dve", lambda e: e.tensor_reduce(out=s1[:, :], in_=y_sb[:, :, :], axis=AX.X, op=ALU.add), reads=[r["y_sb" + str(b)]], writes=[r["s1" + str(b)]])
            P.op("pool", lambda e: e.tensor_tensor(out=ysq[:, :, :], in0=y_sb[:, :, :], in1=y_sb[:, :, :], op=ALU.mult),
                 reads=[r["y_sb" + str(b)]], writes=[r["ysq" + str(b)]])
            yield
            P.op("dve", lambda e: e.tensor_reduce(out=s2[:, :], in_=ysq[:, :, :], axis=AX.X, op=ALU.add), reads=[r["ysq" + str(b)]], writes=[r["s2" + str(b)]])
            P.op("dve", lambda e: e.tensor_scalar(out=mean[:, :], in0=s1[:, :], scalar1=1.0 / 64, scalar2=None, op0=ALU.mult),
                 reads=[r["s1" + str(b)]], writes=[r["mean" + str(b)]])
            P.op("dve", lambda e: e.tensor_tensor(out=var[:, :], in0=mean[:, :], in1=mean[:, :], op=ALU.mult), reads=[r["mean" + str(b)]], writes=[r["var" + str(b)]])
            P.op("dve", lambda e: e.scalar_tensor_tensor(out=var[:, :], in0=s2[:, :], scalar=1.0 / 64, in1=var[:, :],
                                                         op0=ALU.mult, op1=ALU.subtract),
                 reads=[r["s2" + str(b)], r["var" + str(b)]], writes=[r["var" + str(b)]])
            P.op("dve", lambda e: e.tensor_scalar(out=sd[:, :], in0=var[:, :], scalar1=1.0, scalar2=eps, op0=ALU.mult, op1=ALU.add),
                 reads=[r["var" + str(b)]], writes=[r["sd" + str(b)]])
            yield
            P.op("pool", lambda e: e.tensor_tensor(out=rstd[:, :], in0=sd[:, :], in1=mh[:, :], op=ALU.pow),
                 reads=[r["sd" + str(b)], r["mh"]], writes=[r["rstd" + str(b)]])
            P.op("dve", lambda e: e.tensor_tensor(out=yc[:, :, :], in0=y_sb[:, :, :], in1=mean[:, :].unsqueeze(2).to_broadcast([128, 8, 64]),
                                                  op=ALU.subtract),
                 reads=[r["y_sb" + str(b)], r["mean" + str(b)]], writes=[r["yc" + str(b)]])
            P.op("dve", lambda e: e.tensor_tensor(out=yn[:, :, :], in0=yc[:, :, :], in1=rstd[:, :].unsqueeze(2).to_broadcast([128, 8, 64]),
                                                  op=ALU.mult),
                 reads=[r["yc" + str(b)], r["rstd" + str(b)]], writes=[r["yn" + str(b)]])
            yield
            ynp = yn[:, :, :].rearrange("p (q a) e -> p q (a e)", a=2)
            for q in range(4):
                P.op("pe", lambda e, q=q: e.transpose(tr_ps[:, q, :], ynp[:, q, :], identb[:, :]),
                     reads=[r["yn" + str(b)], r["identb"]], writes=[r["tr_ps" + str(b)]])
            yield
            P.op("dve", lambda e, qs=qs: e.tensor_tensor(out=t1[:, :, :], in0=tr_ps[:, :, :],
                                                         in1=lnw[:, qs].unsqueeze(2).to_broadcast([128, 4, 128]), op=ALU.mult),
                 reads=[r["tr_ps" + str(b)], r["lnw"]], writes=[r["t1" + str(b)]])
            P.op("pool", lambda e, qs=qs: e.tensor_tensor(out=t2[:, :, :], in0=t1[:, :, :],
                                                          in1=lnb[:, qs].unsqueeze(2).to_broadcast([128, 4, 128]), op=ALU.add),
                 reads=[r["t1" + str(b)], r["lnb"]], writes=[r["t2" + str(b)]])
            P.op("pool", lambda e, qs=qs, s=s: e.tensor_tensor(out=t1[:, :, :], in0=t2[:, :, :], in1=bon[s][:, qs, :], op=ALU.add),
                 reads=[r["t2" + str(b)], r[f"bon{s}"]], writes=[r["t1" + str(b)]])
            P.op("dve", lambda e, qs=qs, s=s, b=b: e.tensor_tensor(out=yfin[b][:, :, :], in0=t1[:, :, :], in1=szt[s][:, qs, :], op=ALU.mult),
                 reads=[r["t1" + str(b)], r[f"szt{s}"]], writes=[r[f"yfin{b}"]])
            P.dma("pool", yT[kc0 + gi * 4:kc0 + gi * 4 + 4, :, t0:t0 + 128].rearrange("k p t -> p k t"), yfin[b][:, :, :],
                  reads=[r[f"yfin{b}"]], writes=[r["y_out"]])

        it = 0
        for i in range(2):
            P.op("pool", lambda e, i=i: e.memset(kkt[i][:, :, :, :].rearrange("p a q t -> p (a q t)"), 0.0), writes=[r[f"kkt{i}"]])
            P.op("pool", lambda e, i=i: e.memset(rt[i][:, :, :, :].rearrange("p a q t -> p (a q t)"), 0.0), writes=[r[f"rt{i}"]])
            for n in tkz:
                P.op("pool", lambda e, i=i, n=n: e.memset(tkz[n][i][:, :, :, :].rearrange("p a q t -> p (a q t)"), 0.0), writes=[r[f"tkz_{n}{i}"]])
        for c in range(NCH):
            t0 = c * 128
            s = c % 2
            for tl_, nm, src in ((kkt, "kkt", "kktT"), (rt, "rt", "rtT")):
                srcv = S[src][0:1024, t0:t0 + 128].rearrange("(q p) t -> p q t", p=128)
                P.dma("sp", tl_[s][0:64, 0, :, :], srcv[0:64], writes=[r[f"{nm}{s}"]])
                P.dma("sp", tl_[s][64:128, 1, :, :], srcv[64:128], writes=[r[f"{nm}{s}"]])
            for n, src in (("kh", "khtk_d"), ("kka", "kkatk_d")):
                srcv = S[src][t0:t0 + 128, :].rearrange("p (q a d) -> p q a d", a=2, d=64)
                for a in range(2):
                    P.dma("sp", tkz[n][s][:, a, :, a * 64:(a + 1) * 64], srcv[:, :, a, :], writes=[r[f"tkz_{n}{s}"]])
            for q0 in range(0, 8, 4):
                P.dma("sp", bon[s][:, q0:q0 + 4, :], S["bonT"][q0 * 128:(q0 + 4) * 128, t0:t0 + 128].rearrange("(q p) t -> p q t", p=128),
                      writes=[r[f"bon{s}"]])
                P.dma("sp", szt[s][:, q0:q0 + 4, :], S["szT"][q0 * 128:(q0 + 4) * 128, t0:t0 + 128].rearrange("(q p) t -> p q t", p=128),
                      writes=[r[f"szt{s}"]])
            for n, dst in (("akv", "akvT_d"), ("bkv", "bkvT_d"), ("nbab", "nbabT_d"), ("tinv", "tinvT_d")):
                for h0 in range(0, 16, 4):
                    P.dma("sp", mm[n][s][:, h0:h0 + 4, :], S[dst][c, :, h0:h0 + 4, :], writes=[r[f"{n}{s}"]])
            for n, dst in (("v", "vtk_d"),):
                P.dma("sp", tk[n][s][:, :], S[dst][t0:t0 + 128, :], writes=[r[f"tk_{n}{s}"]])
            zipper([grp(c, 0, 0, s), grp(c, 1, 1, s)])
        P.barrier()
        P.flush()


D = 4096
KC = 32
EPS = 1e-6
TWO_PI = 2.0 * math.pi
C1 = 6.28125
C2 = float(np.float32(TWO_PI - C1))


def dense_consts():
    c = {}
    j = np.arange(64, dtype=np.float32)
    invf = (np.float32(10000.0) ** (-(j / np.float32(64.0)))).astype(np.float32)
    c["invf"] = np.concatenate([invf, invf]).reshape(128, 1).astype(np.float32)
    c["sgn"] = np.concatenate([-np.ones(64), np.ones(64)]).reshape(128, 1).astype(np.float32)
    return c


def phase_rope_tables(P, nc, pos_dram, C, cosT, sinT, T):
    with contextlib.ExitStack() as st:
        def sb(name, shape, dt):
            return st.enter_context(nc.sbuf_tensor(_uid() + "rp_" + name, shape, dt))
        r = defaultdict(Res)
        invf = sb("invf", [128, 1], F32); sgn = sb("sgn", [128, 1], F32)
        pi_ = sb("pi", [128, 512], I32)
        ang = sb("ang", [128, 512], F32); kf = sb("kf", [128, 512], F32); kr = sb("kr", [128, 512], F32)
        rr = sb("rr", [128, 512], F32); rc = sb("rc", [128, 512], F32); m = sb("m", [128, 512], F32)
        so = [sb(f"so{i}", [128, 512], F32) for i in range(2)]
        co = [sb(f"co{i}", [128, 512], F32) for i in range(2)]
        P.dma("sp", invf[:, :], C["invf"][:, :], writes=[r["invf"]])
        P.dma("sp", sgn[:, :], C["sgn"][:, :], writes=[r["sgn"]])
        MAGIC = 12582912.0
        for tb in range(T // 512):
            s = tb % 2
            t0 = tb * 512
            P.dma("sp", pi_[:, :], pos_dram[:, t0:t0 + 512], writes=[r["pi"]])
            P.op("dve", lambda e: e.tensor_copy(ang[:, :], pi_[:, :]), reads=[r["pi"]], writes=[r["ang"]])
            P.op("dve", lambda e: e.tensor_scalar(out=ang[:, :], in0=ang[:, :], scalar1=invf[:, 0:1], scalar2=None, op0=ALU.mult),
                 reads=[r["ang"], r["invf"]], writes=[r["ang"]])
            P.op("dve", lambda e: e.tensor_scalar(out=kf[:, :], in0=ang[:, :], scalar1=1.0 / TWO_PI, scalar2=None, op0=ALU.mult),
                 reads=[r["ang"]], writes=[r["kf"]])
            P.op("dve", lambda e: e.tensor_scalar(out=kr[:, :], in0=kf[:, :], scalar1=MAGIC, scalar2=None, op0=ALU.add),
                 reads=[r["kf"]], writes=[r["kr"]])
            P.op("dve", lambda e: e.tensor_scalar(out=kf[:, :], in0=kr[:, :], scalar1=MAGIC, scalar2=None, op0=ALU.subtract),
                 reads=[r["kr"]], writes=[r["kf"]])
            P.op("dve", lambda e: e.scalar_tensor_tensor(out=rr[:, :], in0=kf[:, :], scalar=-C1, in1=ang[:, :], op0=ALU.mult, op1=ALU.add),
                 reads=[r["kf"], r["ang"]], writes=[r["rr"]])
            P.op("dve", lambda e: e.scalar_tensor_tensor(out=rr[:, :], in0=kf[:, :], scalar=-C2, in1=rr[:, :], op0=ALU.mult, op1=ALU.add),
                 reads=[r["kf"], r["rr"]], writes=[r["rr"]])
            P.op("dve", lambda e: e.tensor_scalar(out=rr[:, :], in0=rr[:, :], scalar1=math.pi, scalar2=-math.pi, op0=ALU.min, op1=ALU.max),
                 reads=[r["rr"]], writes=[r["rr"]])
            P.op("dve", lambda e: e.tensor_scalar(out=m[:, :], in0=rr[:, :], scalar1=math.pi / 2, scalar2=-TWO_PI, op0=ALU.is_gt, op1=ALU.mult),
                 reads=[r["rr"]], writes=[r["m"]])
            P.op("dve", lambda e: e.scalar_tensor_tensor(out=rc[:, :], in0=rr[:, :], scalar=math.pi / 2, in1=m[:, :], op0=ALU.add, op1=ALU.add),
                 reads=[r["rr"], r["m"]], writes=[r["rc"]])
            P.op("dve", lambda e: e.tensor_scalar(out=rc[:, :], in0=rc[:, :], scalar1=math.pi, scalar2=-math.pi, op0=ALU.min, op1=ALU.max),
                 reads=[r["rc"]], writes=[r["rc"]])
            P.op("act", lambda e, s=s: e.activation(out=so[s][:, :], in_=rr[:, :], func=AF.Sin), reads=[r["rr"]], writes=[r[f"so{s}"]])
            P.op("act", lambda e, s=s: e.activation(out=co[s][:, :], in_=rc[:, :], func=AF.Sin), reads=[r["rc"]], writes=[r[f"co{s}"]])
            P.op("dve", lambda e, s=s: e.tensor_scalar(out=so[s][:, :], in0=so[s][:, :], scalar1=sgn[:, 0:1], scalar2=None, op0=ALU.mult),
                 reads=[r[f"so{s}"], r["sgn"]], writes=[r[f"so{s}"]])
            P.dma("pool", sinT[:, t0:t0 + 512], so[s][:, :], reads=[r[f"so{s}"]], writes=[r["sinT"]])
            P.dma("pool", cosT[:, t0:t0 + 512], co[s][:, :], reads=[r[f"co{s}"]], writes=[r["cosT"]])
        P.barrier()
        P.flush()


def phase_norm(P, nc, x_dram, normw_bc_dram, hT_dram, T, out_dram=None):
    NT = T // 128
    with contextlib.ExitStack() as st:
        def sb(name, shape, dt):
            return st.enter_context(nc.sbuf_tensor(_uid() + "pn_" + name, shape, dt))
        r = defaultdict(Res)
        xt = [sb(f"x{i}", [128, D], F32) for i in range(2)]
        junk = sb("junk", [128, D], BF16)
        nw = sb("nw", [128, D], F32)
        ss = sb("ss", [128, 2], F32); sd = sb("sd", [128, 2], F32); rs = sb("rs", [128, 2], F32)
        epsc = sb("eps", [128, 1], F32)
        P.dma("sp", nw[:, :], normw_bc_dram[:, :], writes=[r["nw"]])
        P.op("pool", lambda e: e.memset(epsc[:, :], EPS), writes=[r["eps"]])
        if out_dram is None:
            hb = [sb(f"h{i}", [128, D], BF16) for i in range(2)]
            ident = sb("id", [128, 128], BF16)
            stg = [sb(f"stg{i}", [128, KC, 256], BF16) for i in range(2)]
            tp = [st.enter_context(nc.psum_tensor(_uid() + f"pn_tp{i}", [128, 1024], BF16)) for i in range(4)]
            P.op("pool", lambda e: e.memset(ident[:, :], 0.0), writes=[r["id"]])
            P.op("pool", lambda e: e.affine_select(out=ident[:, :], in_=ident[:, :], pattern=[[-1, 128]],
                                                   compare_op=ALU.not_equal, fill=1.0, base=0, channel_multiplier=1),
                 reads=[r["id"]], writes=[r["id"]])
        else:
            of = [sb(f"of{i}", [128, D], F32) for i in range(2)]
        for tt in range(NT):
            s = tt % 2
            P.dma("sp", xt[s][:, :], x_dram[tt * 128:(tt + 1) * 128, :], writes=[r[f"xt{s}"]])
            P.op("dve", lambda e, s=s: e.scalar_tensor_tensor(out=junk[:, :], in0=xt[s][:, :], scalar=1.0, in1=xt[s][:, :],
                                                              op0=ALU.mult, op1=ALU.mult, accum_out=ss[:, s:s + 1]),
                 reads=[r[f"xt{s}"]], writes=[r["junk"], r[f"ss{s}"]])
            P.op("act", lambda e, s=s: e.activation(out=sd[:, s:s + 1], in_=ss[:, s:s + 1], func=AF.Sqrt,
                                                    bias=epsc[:, 0:1], scale=1.0 / D),
                 reads=[r[f"ss{s}"], r["eps"]], writes=[r[f"sd{s}"]])
            P.op("dve", lambda e, s=s: e.reciprocal(rs[:, s:s + 1], sd[:, s:s + 1]), reads=[r[f"sd{s}"]], writes=[r[f"rs{s}"]])
            if out_dram is not None:
                P.op("dve", lambda e, s=s: e.scalar_tensor_tensor(out=of[s][:, :], in0=xt[s][:, :], scalar=rs[:, s:s + 1],
                                                                  in1=nw[:, :], op0=ALU.mult, op1=ALU.mult),
                     reads=[r[f"xt{s}"], r[f"rs{s}"], r["nw"]], writes=[r[f"of{s}"]])
                P.dma("pool", out_dram[tt * 128:(tt + 1) * 128, :], of[s][:, :], reads=[r[f"of{s}"]], writes=[r["out"]])
                continue
            P.op("dve", lambda e, s=s: e.scalar_tensor_tensor(out=hb[s][:, :], in0=xt[s][:, :], scalar=rs[:, s:s + 1],
                                                              in1=nw[:, :], op0=ALU.mult, op1=ALU.mult),
                 reads=[r[f"xt{s}"], r[f"rs{s}"], r["nw"]], writes=[r[f"hb{s}"]])
            sg = (tt // 2) % 2
            off = (tt % 2) * 128
            for b in range(4):
                for j in range(8):
                    kc = b * 8 + j
                    P.op("pe", lambda e, s=s, b=b, j=j, kc=kc: e.transpose(tp[b][:, j * 128:(j + 1) * 128],
                                                                         hb[s][:, kc * 128:(kc + 1) * 128], ident[:, :]),
                         reads=[r[f"hb{s}"], r["id"]], writes=[r[f"tp{b}"]])
                P.op("act", lambda e, b=b, sg=sg, off=off: e.copy(
                    stg[sg][:, b * 8:(b + 1) * 8, off:off + 128],
                    tp[b][:, :].rearrange("p (j t) -> p j t", j=8)),
                     reads=[r[f"tp{b}"]], writes=[r[f"stg{sg}"]])
            if tt % 2 == 1:
                t0 = (tt - 1) * 128
                P.dma("pool", hT_dram[:, :, t0:t0 + 256].rearrange("k p t -> p k t"), stg[sg][:, :, :],
                      reads=[r[f"stg{sg}"]], writes=[r["hT_out"]])
        P.barrier()
        P.flush()


def phase_proj(P, nc, hT_dram, w_tiles_dram, projT_dram, cosT, sinT, T, NCT, n_rope=16, TH=2048):
    TH = min(TH, T)
    NTB = TH // 512
    with contextlib.ExitStack() as st:
        def sb(name, shape, dt):
            return st.enter_context(nc.sbuf_tensor(_uid() + "pp_" + name, shape, dt))
        r = defaultdict(Res)
        hT = sb("hT", [128, KC, TH], BF16)
        wf = [sb(f"wf{i}", [128, KC * 128], F32) for i in range(2)]
        wb = [sb(f"wb{i}", [128, KC, 128], BF16) for i in range(2)]
        wbr = sb("wbr", [128, KC, 128], BF16)
        cs_ = [sb(f"cs{i}", [128, 512], F32) for i in range(2)]
        sn_ = [sb(f"sn{i}", [128, 512], F32) for i in range(2)]
        t1 = sb("t1", [128, 512], F32); t2 = sb("t2", [128, 512], F32)
        ob = [sb(f"ob{i}", [128, 512], F32) for i in range(4)]
        ps = [[st.enter_context(nc.psum_tensor(_uid() + f"pp_ps{a}_{i}", [128, 512], F32)) for i in range(NTB)] for a in range(2)]
        cnt = 0
        rc = 0
        for t0 in range(0, T, TH):
            for kc in range(KC):
                P.dma("sp", hT[:, kc, :], hT_dram[kc, :, t0:t0 + TH], writes=[r["hT"]])
            def prep(ct):
                s = ct % 2
                P.dma("sp", wf[s][:, :], w_tiles_dram[ct, :, :], writes=[r[f"wf{s}"]])
                if ct % 2 == 0:
                    P.op("act", lambda e, s=s: e.copy(wb[s][:, :, :].rearrange("p k c -> p (k c)"), wf[s][:, :]),
                         reads=[r[f"wf{s}"]], writes=[r[f"wb{s}"]])
                else:
                    P.op("dve", lambda e, s=s: e.tensor_copy(wb[s][:, :, :].rearrange("p k c -> p (k c)"), wf[s][:, :]),
                         reads=[r[f"wf{s}"]], writes=[r[f"wb{s}"]])
            prep(0)
            for ct in range(NCT):
                s = ct % 2
                rope = ct < n_rope
                if ct + 1 < NCT:
                    prep(ct + 1)
                if rope:
                    P.op("pool", lambda e, s=s: e.tensor_copy(wbr[:, :, 0:64], wb[s][:, :, 64:128]), reads=[r[f"wb{s}"]], writes=[r["wbr"]])
                    P.op("pool", lambda e, s=s: e.tensor_copy(wbr[:, :, 64:128], wb[s][:, :, 0:64]), reads=[r[f"wb{s}"]], writes=[r["wbr"]])
                for kc in range(KC):
                    for tb in range(NTB):
                        P.op("pe", lambda e, s=s, kc=kc, tb=tb: e.matmul(ps[s][tb][:, :], wb[s][:, kc, :], hT[:, kc, tb * 512:(tb + 1) * 512],
                                                                        start=(kc == 0), stop=(kc == KC - 1)),
                             reads=[r[f"wb{s}"], r["hT"]], writes=[r[f"ps{s}_{tb}"]])
                if rope:
                    for kc in range(KC):
                        for tb in range(NTB):
                            P.op("pe", lambda e, s=s, kc=kc, tb=tb: e.matmul(ps[1 - s][tb][:, :], wbr[:, kc, :], hT[:, kc, tb * 512:(tb + 1) * 512],
                                                                            start=(kc == 0), stop=(kc == KC - 1)),
                                 reads=[r["wbr"], r["hT"]], writes=[r[f"ps{1 - s}_{tb}"]])
                for tb in range(NTB):
                    b = cnt % 4
                    cnt += 1
                    tg = t0 + tb * 512
                    if rope:
                        q = rc % 2
                        rc += 1
                        P.dma("sp", cs_[q][:, :], cosT[:, tg:tg + 512], writes=[r[f"cs{q}"]])
                        P.dma("sp", sn_[q][:, :], sinT[:, tg:tg + 512], writes=[r[f"sn{q}"]])
                        P.op("dve", lambda e, s=s, tb=tb, q=q: e.tensor_tensor(out=t1[:, :], in0=ps[s][tb][:, :], in1=cs_[q][:, :], op=ALU.mult),
                             reads=[r[f"ps{s}_{tb}"], r[f"cs{q}"]], writes=[r["t1"]])
                        P.op("dve", lambda e, s=s, tb=tb, q=q: e.tensor_tensor(out=t2[:, :], in0=ps[1 - s][tb][:, :], in1=sn_[q][:, :], op=ALU.mult),
                             reads=[r[f"ps{1 - s}_{tb}"], r[f"sn{q}"]], writes=[r["t2"]])
                        P.op("pool", lambda e, b=b: e.tensor_tensor(out=ob[b][:, :], in0=t1[:, :], in1=t2[:, :], op=ALU.add),
                             reads=[r["t1"], r["t2"]], writes=[r[f"ob{b}"]])
                    else:
                        if cnt % 2 == 0:
                            P.op("dve", lambda e, b=b, s=s, tb=tb: e.tensor_copy(ob[b][:, :], ps[s][tb][:, :]), reads=[r[f"ps{s}_{tb}"]], writes=[r[f"ob{b}"]])
                        else:
                            P.op("act", lambda e, b=b, s=s, tb=tb: e.copy(ob[b][:, :], ps[s][tb][:, :]), reads=[r[f"ps{s}_{tb}"]], writes=[r[f"ob{b}"]])
                    pd_, pr_ = projT_dram(ct)
                    P.dma("pool", pd_[pr_:pr_ + 128, tg:tg + 512], ob[b][:, :], reads=[r[f"ob{b}"]], writes=[r["out"]])
        P.barrier()
        P.flush()


def _load_wblk(P, r, wf, wb, s, w_dram, cb, wcnt):
    for q4 in range(4):
        ws = wcnt[0] % 2
        wcnt[0] += 1
        P.dma("sp", wf[ws][:, :], w_dram[cb * 4 + q4, :, :], writes=[r[f"wf{ws}"]])
        src = wf[ws][:, :].rearrange("p (k c) -> p k c", c=128)
        if q4 % 2 == 0:
            P.op("act", lambda e, s=s, q4=q4, src=src: e.copy(wb[s][:, :, q4 * 128:(q4 + 1) * 128], src),
                 reads=[r[f"wf{ws}"]], writes=[r[f"wb{s}"]])
        else:
            P.op("dve", lambda e, s=s, q4=q4, src=src: e.tensor_copy(wb[s][:, :, q4 * 128:(q4 + 1) * 128], src),
                 reads=[r[f"wf{ws}"]], writes=[r[f"wb{s}"]])


def phase_out(P, nc, yT_dram, w_blk_dram, x_dram, x1_dram, T, TQ=1024):
    NB = D // 512
    TQ = min(TQ, T)
    with contextlib.ExitStack() as st:
        def sb(name, shape, dt):
            return st.enter_context(nc.sbuf_tensor(_uid() + "po_" + name, shape, dt))
        r = defaultdict(Res)
        yT = sb("yT", [128, KC, TQ], BF16)
        wf = [sb(f"wf{i}", [128, KC * 128], F32) for i in range(2)]
        wb = [sb(f"wb{i}", [128, KC, 512], BF16) for i in range(2)]
        xt = [sb(f"xt{i}", [128, 512], F32) for i in range(4)]
        ot = [sb(f"ot{i}", [128, 512], F32) for i in range(4)]
        ps = [st.enter_context(nc.psum_tensor(_uid() + f"po_ps{i}", [128, 512], F32)) for i in range(4)]
        cnt = 0
        wcnt = [0]
        for t0 in range(0, T, TQ):
            for kc in range(KC):
                P.dma("sp", yT[:, kc, :], yT_dram[kc, :, t0:t0 + TQ], writes=[r["yT"]])
            _load_wblk(P, r, wf, wb, 0, w_blk_dram, 0, wcnt)
            for cb in range(NB):
                s = cb % 2
                if cb + 1 < NB:
                    _load_wblk(P, r, wf, wb, 1 - s, w_blk_dram, cb + 1, wcnt)
                for tt in range(TQ // 128):
                    b = cnt % 4
                    cnt += 1
                    tok = t0 + tt * 128
                    P.dma("sp", xt[b][:, :], x_dram[tok:tok + 128, cb * 512:(cb + 1) * 512], writes=[r[f"xt{b}"]])
                    for kc in range(KC):
                        P.op("pe", lambda e, s=s, b=b, kc=kc, tt=tt: e.matmul(ps[b][:, :], yT[:, kc, tt * 128:(tt + 1) * 128], wb[s][:, kc, :],
                                                                             start=(kc == 0), stop=(kc == KC - 1)),
                             reads=[r[f"wb{s}"], r["yT"]], writes=[r[f"ps{b}"]])
                    P.op("dve", lambda e, b=b: e.tensor_tensor(out=ot[b][:, :], in0=ps[b][:, :], in1=xt[b][:, :], op=ALU.add),
                         reads=[r[f"ps{b}"], r[f"xt{b}"]], writes=[r[f"ot{b}"]])
                    P.dma("pool", x1_dram[tok:tok + 128, cb * 512:(cb + 1) * 512], ot[b][:, :], reads=[r[f"ot{b}"]], writes=[r["out"]])
        P.barrier()
        P.flush()


def phase_gate(P, nc, h2T_dram, wg_blk_dram, p_dram, wple_dram, x1_dram, x2_dram, T, TQ=1024):
    NB = D // 512
    TQ = min(TQ, T)
    with contextlib.ExitStack() as st:
        def sb(name, shape, dt):
            return st.enter_context(nc.sbuf_tensor(_uid() + "pg_" + name, shape, dt))
        r = defaultdict(Res)
        hT = sb("hT", [128, KC, TQ], BF16)
        wf = [sb(f"wf{i}", [128, KC * 128], F32) for i in range(2)]
        wb = [sb(f"wb{i}", [128, KC, 512], BF16) for i in range(2)]
        wpb = sb("wpb", [128, 2, D], BF16)
        ident = sb("ident", [128, 128], BF16)
        pt = [sb(f"pt{i}", [128, 256], F32) for i in range(2)]
        pb = [sb(f"pb{i}", [128, 256], BF16) for i in range(2)]
        pT = sb("pT", [128, 2, TQ], BF16)
        xt = [sb(f"xt{i}", [128, 512], F32) for i in range(2)]
        gt = sb("gt", [128, 512], F32)
        tm = sb("tm", [128, 512], F32)
        ot = [sb(f"ot{i}", [128, 512], F32) for i in range(2)]
        ps = [st.enter_context(nc.psum_tensor(_uid() + f"pg_ps{i}", [128, 512], F32)) for i in range(3)]
        pp = [st.enter_context(nc.psum_tensor(_uid() + f"pg_pp{i}", [128, 512], F32)) for i in range(3)]
        tp = st.enter_context(nc.psum_tensor(_uid() + "pg_tp", [128, 1024], BF16))
        for q2 in range(2):
            P.dma("sp", wf[q2][:, :], wple_dram[:, q2 * D:(q2 + 1) * D], writes=[r[f"wf{q2}"]])
            P.op("act", lambda e, q2=q2: e.copy(wpb[:, q2, :], wf[q2][:, :]), reads=[r[f"wf{q2}"]], writes=[r["wpb"]])
        P.op("pool", lambda e: e.memset(ident[:, :], 0.0), writes=[r["id"]])
        P.op("pool", lambda e: e.affine_select(out=ident[:, :], in_=ident[:, :], pattern=[[-1, 128]],
                                               compare_op=ALU.not_equal, fill=1.0, base=0, channel_multiplier=1),
             reads=[r["id"]], writes=[r["id"]])
        cnt = 0
        wcnt = [0]
        for t0 in range(0, T, TQ):
            for kc in range(KC):
                P.dma("sp", hT[:, kc, :], h2T_dram[kc, :, t0:t0 + TQ], writes=[r["hT"]])
            for tt in range(TQ // 128):
                s = tt % 2
                tok = t0 + tt * 128
                P.dma("sp", pt[s][:, :], p_dram[tok:tok + 128, :], writes=[r[f"pt{s}"]])
                P.op("dve", lambda e, s=s: e.tensor_copy(pb[s][:, :], pt[s][:, :]), reads=[r[f"pt{s}"]], writes=[r[f"pb{s}"]])
                for k2 in range(2):
                    P.op("pe", lambda e, s=s, k2=k2: e.transpose(tp[:, k2 * 128:(k2 + 1) * 128], pb[s][:, k2 * 128:(k2 + 1) * 128], ident[:, :]),
                         reads=[r[f"pb{s}"], r["id"]], writes=[r["tp"]])
                P.op("act", lambda e, tt=tt: e.copy(pT[:, :, tt * 128:(tt + 1) * 128], tp[:, 0:256].rearrange("p (k t) -> p k t", k=2)),
                     reads=[r["tp"]], writes=[r["pT"]])
            _load_wblk(P, r, wf, wb, 0, wg_blk_dram, 0, wcnt)
            for cb in range(NB):
                s = cb % 2
                if cb + 1 < NB:
                    _load_wblk(P, r, wf, wb, 1 - s, wg_blk_dram, cb + 1, wcnt)
                for tt in range(TQ // 128):
                    b3 = cnt % 3
                    b2 = cnt % 2
                    cnt += 1
                    tok = t0 + tt * 128
                    P.dma("sp", xt[b2][:, :], x1_dram[tok:tok + 128, cb * 512:(cb + 1) * 512], writes=[r[f"xt{b2}"]])
                    for kc in range(KC):
                        P.op("pe", lambda e, s=s, b3=b3, kc=kc, tt=tt: e.matmul(ps[b3][:, :], hT[:, kc, tt * 128:(tt + 1) * 128], wb[s][:, kc, :],
                                                                               start=(kc == 0), stop=(kc == KC - 1)),
                             reads=[r[f"wb{s}"], r["hT"]], writes=[r[f"ps{b3}"]])
                    for k2 in range(2):
                        P.op("pe", lambda e, b3=b3, k2=k2, tt=tt, cb=cb: e.matmul(pp[b3][:, :], pT[:, k2, tt * 128:(tt + 1) * 128],
                                                                                 wpb[:, k2, cb * 512:(cb + 1) * 512], start=(k2 == 0), stop=(k2 == 1)),
                             reads=[r["wpb"], r["pT"]], writes=[r[f"pp{b3}"]])
                    P.op("act", lambda e, b3=b3: e.activation(out=gt[:, :], in_=ps[b3][:, :], func=AF.Sigmoid),
                         reads=[r[f"ps{b3}"]], writes=[r["gt"]])
                    P.op("dve", lambda e, b3=b3: e.tensor_tensor(out=tm[:, :], in0=pp[b3][:, :], in1=gt[:, :], op=ALU.mult),
                         reads=[r[f"pp{b3}"], r["gt"]], writes=[r["tm"]])
                    P.op("pool", lambda e, b2=b2: e.tensor_tensor(out=ot[b2][:, :], in0=tm[:, :], in1=xt[b2][:, :], op=ALU.add),
                         reads=[r["tm"], r[f"xt{b2}"]], writes=[r[f"ot{b2}"]])
                    P.dma("pool", x2_dram[tok:tok + 128, cb * 512:(cb + 1) * 512], ot[b2][:, :], reads=[r[f"ot{b2}"]], writes=[r["out"]])
        P.barrier()
        P.flush()


SEQ = 4096
NLAYER = 2
NCT = 130


def make_consts():
    c = {}
    c.update(ret_consts())
    c.update(gdn_consts())
    c.update(rwkv_consts())
    c.update(dense_consts())
    return c


_UIDC = [0]


def build_program(T=SEQ, L=NLAYER):
    nc = bass.Bass("TRN2", target_bir_lowering=False)
    NCH = T // 128
    cs = make_consts()
    ext = lambda n, shp, dt=F32: nc.dram_tensor(n, list(shp), dt, kind="ExternalInput")
    x = ext("x", [T, D])
    pos = ext("pos", [128, T], I32)
    C = {k: ext("c_" + k, v.shape) for k, v in cs.items()}
    Lp = []
    for l in range(L):
        d = {}
        d["nwb"] = ext(f"nwb{l}", [128, D]); d["win"] = ext(f"win{l}", [NCT, 128, KC * 128])
        d["gnwT"] = ext(f"gnwT{l}", [128, 8])
        for n in ("w0T", "a0T", "kkT", "kaT", "rkT", "lnwT", "lnbT"):
            d[n] = ext(f"{n}{l}", [128, 8])
        d["muT"] = ext(f"muT{l}", [128, 33]); d["lw2"] = ext(f"lw2{l}", [128, 1024])
        d["convT"] = ext(f"convT{l}", [128, 192]); d["alog"] = ext(f"alog{l}", [16, 1]); d["dtb"] = ext(f"dtb{l}", [16, 1])
        d["nrm"] = ext(f"nrm{l}", [128, 1])
        d["wout"] = ext(f"wout{l}", [32, 128, KC * 128]); d["wgate"] = ext(f"wgate{l}", [32, 128, KC * 128])
        d["wple"] = ext(f"wple{l}", [128, 2 * D]); d["plnb"] = ext(f"plnb{l}", [128, D]); d["p"] = ext(f"p{l}", [T, 256])
        Lp.append(d)
    fnb = ext("fnb", [128, D])
    out = nc.dram_tensor("out", [T, D], F32, kind="ExternalOutput")
    scr = lambda n, shp, dt: nc.dram_tensor(n, list(shp), dt)
    hT = scr("hT", [KC, 128, T], BF16); yT = scr("yT", [KC, 128, T], BF16)
    projA = scr("projA", [65 * 128, T], F32); projB = scr("projB", [65 * 128, T], F32)
    projT = lambda ct: (projA, ct * 128) if ct < 65 else (projB, (ct - 65) * 128)
    cosT = scr("cosT", [128, T], F32); sinT = scr("sinT", [128, T], F32)
    x1 = scr("x1", [T, D], F32); x2 = scr("x2", [T, D], F32)
    S = {n: scr(n, [2048, T], BF16) for n in ("gqT", "gqdT", "gkT", "gvT", "wkT_d")}
    S["gbt"] = scr("gbt", [T, 32], F32)
    S["attnT_d"] = scr("attnT_d", [NCH, 128, 16, 128], BF16); S["ktl_d"] = scr("ktl_d", [T, 2048], BF16)
    S["u_d"] = scr("u_d", [T, 2048], F32); S["els_d"] = scr("els_d", [NCH, 128, 16], F32)
    S.update({n: scr(n, [1024, T], BF16) for n in ("rtT", "kktT", "khT", "kkaT", "rvT")})
    S.update({n: scr(n, [1024, T], F32) for n in ("bonT", "szT")})
    S["pc_d"] = scr("pc_d", [8, 128, NCH], F32)
    S.update({n: scr(n, [NCH, 128, 16, 128], BF16) for n in ("tinvT_d", "akvT_d", "bkvT_d", "nbabT_d")})
    S.update({n: scr(n, [T, 1024], BF16) for n in ("vtk_d", "khtk_d", "kkatk_d")})
    grows = dict(q=0, k=2048, v=4096, z=6144, a=8192, b=8208)
    import os
    only = os.environ.get("MK_PH")
    only = set(only.split(",")) if only else None

    def on(n):
        return only is None or n in only
    with contextlib.ExitStack() as stack:
        P = Prog(nc, stack)
        if on("rope"):
            phase_rope_tables(P, nc, pos, C, cosT, sinT, T)
        xin = x
        for l in range(L):
            d = Lp[l]
            if on("norm"):
                phase_norm(P, nc, xin, d["nwb"], hT, T)
            if on("proj"):
                phase_proj(P, nc, hT, d["win"], projT, cosT, sinT, T, NCT, n_rope=int(os.environ.get("MK_NROPE", "16")))
            if on("ret"):
                phase_ret(P, nc, projA, yT, C, d["gnwT"], T)
            if on("rwkv"):
                phase_rwkv_pre(P, nc, projA, S, C, d, T, 4096)
            if on("rwkv"):
                phase_rwkv_r1(P, nc, S, C, T)
            if on("rwkv"):
                phase_rwkv_r2(P, nc, S, C, yT, d, T, kc0=8)
            if on("gdn"):
                phase_gdn_pre(P, nc, projB, S, C, d, T, grows)
            if on("gdn"):
                phase_gdn_g1(P, nc, S, C, T)
            if on("gdn"):
                phase_gdn_g2(P, nc, projB, S, C, yT, d["nrm"], T, grows["z"], kc0=16)
            if on("out"):
                phase_out(P, nc, yT, d["wout"], xin, x1, T)
            if on("norm2"):
                phase_norm(P, nc, x1, d["plnb"], hT, T)
            if on("gate"):
                phase_gate(P, nc, hT, d["wgate"], d["p"], d["wple"], x1, x2, T)
            xin = x2
        if on("fin"):
            phase_norm(P, nc, xin, fnb, None, T, out_dram=out)
        n_ops = P.n_ops
    return nc, cs, n_ops


def _tiles(w, ncols_pad=None):
    K, N = w.shape
    if ncols_pad is not None and ncols_pad > N:
        w = np.concatenate([w, np.zeros((K, ncols_pad - N), w.dtype)], axis=1)
        N = ncols_pad
    return np.ascontiguousarray(w.reshape(K // 128, 128, N // 128, 128).transpose(2, 1, 0, 3).reshape(N // 128, 128, (K // 128) * 128))


def prep_shared(inp, L=NLAYER):
    f = np.float32
    sh = {}
    bc = lambda v: np.ascontiguousarray(np.broadcast_to(np.asarray(v, f), (128, v.shape[-1])))
    col8 = lambda v: np.ascontiguousarray(np.asarray(v, f).reshape(8, 128).T)
    for l in range(L):
        sh[f"nwb{l}"] = bc(inp["norm_w"][l])
        sh[f"win{l}"] = _tiles(np.asarray(inp["w_in"][l], f), NCT * 128)
        sh[f"gnwT{l}"] = col8(inp["ret_gn"][l])
        sh[f"w0T{l}"] = col8(inp["rwkv_w0"][l]); sh[f"a0T{l}"] = col8(inp["rwkv_a0"][l])
        sh[f"kkT{l}"] = col8(inp["rwkv_k_k"][l]); sh[f"kaT{l}"] = col8(inp["rwkv_k_a"][l]); sh[f"rkT{l}"] = col8(inp["rwkv_r_k"][l])
        sh[f"lnwT{l}"] = col8(inp["rwkv_ln_w"][l]); sh[f"lnbT{l}"] = col8(inp["rwkv_ln_b"][l])
        sh[f"muT{l}"] = np.ascontiguousarray(np.asarray(inp["rwkv_mu"][l], f).reshape(33, 128).T)
        sh[f"lw2{l}"] = np.ascontiguousarray(np.concatenate([np.asarray(inp["rwkv_w2"][l], f), np.asarray(inp["rwkv_a2"][l], f)], 0))
        sh[f"convT{l}"] = np.ascontiguousarray(np.asarray(inp["gdn_conv"][l], f).reshape(4, 48, 128).transpose(2, 1, 0).reshape(128, 192))
        sh[f"alog{l}"] = np.asarray(inp["gdn_a_log"][l], f).reshape(16, 1).copy()
        sh[f"dtb{l}"] = np.asarray(inp["gdn_dt_bias"][l], f).reshape(16, 1).copy()
        sh[f"nrm{l}"] = np.asarray(inp["gdn_norm"][l], f).reshape(128, 1).copy()
        sh[f"wout{l}"] = _tiles(np.asarray(inp["w_out"][l], f))
        sh[f"wgate{l}"] = _tiles(np.asarray(inp["w_ple_gate"][l], f))
        sh[f"wple{l}"] = np.ascontiguousarray(np.asarray(inp["w_ple"][l], f).reshape(2, 128, D).transpose(1, 0, 2).reshape(128, 2 * D))
        sh[f"plnb{l}"] = bc(inp["ple_norm"][l])
    sh["fnb"] = bc(inp["final_norm"])
    return sh


def kernel(**inp):
    B = inp["x"].shape[0]
    T = inp["x"].shape[1]
    nc, cs, n_ops = build_program(T, NLAYER)
    sh = prep_shared(inp)
    for k, v in cs.items():
        sh["c_" + k] = v
    in_maps = []
    for b in range(B):
        m = dict(sh)
        m["x"] = np.ascontiguousarray(np.asarray(inp["x"][b], np.float32))
        m["pos"] = np.ascontiguousarray(np.broadcast_to(np.asarray(inp["positions"][b], np.int32), (128, T)))
        for l in range(NLAYER):
            m[f"p{l}"] = np.ascontiguousarray(np.asarray(inp["p"][l, b], np.float32))
        in_maps.append(m)
    res = run_bass_kernel_spmd(nc, in_maps, core_ids=list(range(B)))
    return np.stack([np.asarray(r["out"], np.float32) for r in res.results], axis=0)
```

```python
import contextlib, math
from collections import defaultdict
import numpy as np
import concourse.bass as bass
import concourse.mybir as mybir
from concourse.bass_utils import run_bass_kernel_spmd


F32 = mybir.dt.float32
BF16 = mybir.dt.bfloat16
I32 = mybir.dt.int32
ALU = mybir.AluOpType
AF = mybir.ActivationFunctionType
AX = mybir.AxisListType

ENGS = ("pe", "act", "dve", "pool", "sp")
EPOCH = 20000
N_EPOCHS = {"pe": 16, "act": 10, "dve": 12, "pool": 10, "sp": 1}
N_DMA_SEM = 12


_UID = [0]


def _uid():
    return f"u{_UID[0]}_"


class Res:
    __slots__ = ("name", "w", "r")

    def __init__(self, name=""):
        self.name = name
        self.w = None
        self.r = []


class Prog:
    def __init__(self, nc, stack):
        self.nc = nc
        self.stack = stack
        self.sems = {}
        for e in ENGS:
            self.sems[e] = [stack.enter_context(nc.semaphore(f"s_{e}_{i}")) for i in range(N_EPOCHS[e])]
        self.dsems = {}
        for e in ("sp", "pool"):
            self.dsems[e] = [stack.enter_context(nc.semaphore(f"d_{e}_{i}")) for i in range(N_DMA_SEM)]
        self.dcount = {e: [0] * N_DMA_SEM for e in self.dsems}
        self.dnext = {e: 0 for e in self.dsems}
        self.seq = {e: 0 for e in ENGS}
        self.known = {e: {} for e in ENGS}
        self.ops = {e: [] for e in ENGS}
        self.last = {e: None for e in ENGS}
        self.outstanding = []
        self.n_ops = 0
        self.pe_needed = set()
        self.pe_map = {}
        self.pe_count = 0

    def _waits_for(self, eng, reads, writes, extra=()):
        toks = list(extra)
        for r in reads:
            if r.w is not None:
                toks.append(r.w)
        for w in writes:
            if w.w is not None:
                toks.append(w.w)
            toks.extend(w.r)
        best = {}
        for (sem, val, te, raw) in toks:
            pass
        return toks

    def _filter(self, eng, toks):
        best = {}
        for tok, is_raw in toks:
            sem, val, te = tok
            if te == eng and eng == "pe":
                continue
            k = "PE" if te == "pe" else id(sem)
            if k not in best or best[k][1] < val:
                best[k] = (sem, val)
        out = []
        kn = self.known[eng]
        for k, (sem, val) in best.items():
            if kn.get(k, 0) >= val:
                continue
            kn[k] = val
            if k == "PE":
                self.pe_needed.add(val)
            out.append((sem, val))
        return out

    def op(self, eng, fn, reads=(), writes=(), extra=()):
        toks = [(t, True) for t in extra]
        for r in reads:
            if r.w is not None:
                toks.append((r.w, True))
        for w in writes:
            if w.w is not None:
                toks.append((w.w, False))
            toks.extend((t, False) for t in w.r)
        waits = self._filter(eng, toks)
        self.seq[eng] += 1
        s = self.seq[eng]
        if eng == "pe":
            tok = ("PE", s, "pe")
            self.ops[eng].append((waits, fn, s, 1))
        else:
            ep = (s - 1) // EPOCH
            tok = (self.sems[eng][ep], s - ep * EPOCH, eng)
            self.ops[eng].append((waits, fn, tok[0], 1))
        self.last[eng] = tok
        for r in reads:
            r.r.append(tok)
        for w in writes:
            w.w = tok
            w.r = []
        self.n_ops += 1
        return tok

    def dma(self, q, out, in_, reads=(), writes=(), **kw):
        i = self.dnext[q]
        self.dnext[q] = (i + 1) % N_DMA_SEM
        sem = self.dsems[q][i]
        toks = []
        if self.dcount[q][i] > 0:
            toks.append(((sem, self.dcount[q][i], None), True))
        for r in reads:
            if r.w is not None:
                toks.append((r.w, True))
        for w in writes:
            if w.w is not None:
                toks.append((w.w, False))
            toks.extend((t, False) for t in w.r)
        waits = self._filter(q, toks)
        self.dcount[q][i] += 16
        tok = (sem, self.dcount[q][i], None)

        def fn(e, out=out, in_=in_, kw=kw):
            return e.dma_start(out=out, in_=in_, **kw)
        self.ops[q].append((waits, fn, sem, 16))
        for r in reads:
            r.r.append(tok)
        for w in writes:
            w.w = tok
            w.r = []
        self.outstanding.append(tok)
        self.n_ops += 1
        return tok

    def barrier(self):
        toks = [(t, True) for t in self.outstanding]
        for e in ENGS:
            if self.last[e] is not None:
                toks.append((self.last[e], True))
        for e in ENGS:
            waits = self._filter(e, [(t, r) for (t, r) in toks if t[2] != e])
            if waits:
                self.ops[e].append((waits, None, None, 0))
        self.outstanding = []

    def flush(self):
        _UID[0] += 1
        nc = self.nc
        ops = self.ops
        for (waits, fn, idx, inc) in ops["pe"]:
            if fn is not None and idx in self.pe_needed:
                self.pe_count += 1
                c = self.pe_count
                ep = (c - 1) // EPOCH
                self.pe_map[idx] = (self.sems["pe"][ep], c - ep * EPOCH)
        pe_map = self.pe_map

        def rw(w):
            s_, v_ = w
            if isinstance(s_, str):
                return pe_map[v_]
            return w
        with nc.Block() as block:
            def run(handle, lst, is_pe=False):
                for waits, fn, sem, inc in lst:
                    for w in waits:
                        s_, v_ = rw(w)
                        handle.wait_ge(s_, v_)
                    if fn is not None:
                        ins = fn(handle)
                        if is_pe:
                            if sem in pe_map:
                                ins.then_inc(pe_map[sem][0], 1)
                        else:
                            ins.then_inc(sem, inc)

            @block.tensor
            def _(e):
                run(e, ops["pe"], True)

            @block.scalar
            def _(e):
                run(e, ops["act"])

            @block.vector
            def _(e):
                run(e, ops["dve"])

            @block.gpsimd
            def _(e):
                run(e, ops["pool"])

            @block.sync
            def _(e):
                run(e, ops["sp"])
        self.ops = {e: [] for e in ENGS}


RET_H = 8


def ret_consts():
    h = np.arange(8, dtype=np.float64)
    lg = np.log1p(-(2.0 ** (-5.0 - h)))
    i = np.arange(128, dtype=np.float64)
    Gq = np.exp((i[None, :] + 1) * lg[:, None])
    Gk = np.exp(-(i[None, :] + 1) * lg[:, None]) * 128 ** -0.5
    GC = np.exp(128 * lg)
    c = {}
    c["ret_gq"] = np.broadcast_to(Gq.reshape(1, 8 * 128), (128, 1024)).astype(np.float32).copy()
    c["ret_gk"] = np.broadcast_to(Gk.reshape(1, 8 * 128), (128, 1024)).astype(np.float32).copy()
    c["ret_gc"] = np.broadcast_to(np.repeat(GC, 128).reshape(1, 1024), (128, 1024)).astype(np.float32).copy()
    jj, ii = np.meshgrid(np.arange(128), np.arange(128), indexing="ij")
    c["mask_ui"] = (ii >= jj).astype(np.float32)
    c["ident"] = np.eye(128, dtype=np.float32)
    return c


def dma_rows(P, q, out_tile, dram, row0, nh, t0, tl, res_w, hs=4):
    for h0 in range(0, nh, hs):
        src = dram[row0 + h0 * 128: row0 + (h0 + hs) * 128, t0:t0 + tl].rearrange("(h p) t -> p h t", p=128)
        P.dma(q, out_tile[:, h0:h0 + hs, 0:tl], src, writes=[res_w])


def phase_ret(P, nc, projT, yT, C, gnwT_dram, T, eps=1e-5):
    H = RET_H
    NCH = T // 128
    with contextlib.ExitStack() as st:
        def sb(name, shape, dt):
            return st.enter_context(nc.sbuf_tensor(_uid() + "rt_" + name, shape, dt))

        def pst(name, shape, dt):
            return st.enter_context(nc.psum_tensor(_uid() + "rt_" + name, shape, dt))
        qf = [sb(f"qf{i}", [128, H, 128], F32) for i in range(2)]
        kf = [sb(f"kf{i}", [128, H, 128], F32) for i in range(2)]
        vf = [sb(f"vf{i}", [128, H, 128], F32) for i in range(2)]
        zf = [sb(f"zf{i}", [128, H, 128], F32) for i in range(2)]
        gq = sb("gq", [128, H, 128], F32); gk = sb("gk", [128, H, 128], F32); gc = sb("gc", [128, H, 128], F32)
        maskf = sb("maskf", [128, 128], F32)
        identf = sb("identf", [128, 128], F32); ident = sb("ident", [128, 128], BF16)
        gnw = sb("gnw", [128, H], F32)
        qd = sb("qd", [128, H, 128], BF16); kd = sb("kd", [128, H, 128], BF16); vb = sb("vb", [128, H, 128], BF16)
        scT = sb("scT", [128, H, 128], BF16)
        vtok = sb("vtok", [128, H, 128], BF16); kdtok = sb("kdtok", [128, H, 128], BF16)
        state = sb("state", [128, H, 128], F32); stmp = sb("stmp", [128, H, 128], F32); state_bf = sb("state_bf", [128, H, 128], BF16)
        y_sb = sb("y_sb", [128, H, 128], F32); sq = sb("sq", [128, H, 128], F32)
        s1 = sb("s1", [128, H], F32); s2 = sb("s2", [128, H], F32); mean = sb("mean", [128, H], F32)
        var = sb("var", [128, H], F32); sd = sb("sd", [128, H], F32); rstd = sb("rstd", [128, H], F32)
        epsc = sb("epsc", [128, 1], F32); mh = sb("mh", [128, H], F32)
        yc = sb("yc", [128, H, 128], F32); yn = sb("yn", [128, H, 128], BF16)
        sz = sb("sz", [128, H, 128], F32); yg = sb("yg", [128, H, 128], F32); yfin = [sb(f"yfin{i}", [128, H, 128], BF16) for i in range(2)]
        sc_ps = pst("sc_ps", [128, H, 128], F32)
        vt_ps = pst("vt_ps", [128, H, 128], BF16)
        kt_ps = pst("kt_ps", [128, H, 128], BF16)
        y_ps = pst("y_ps", [128, H, 128], F32)
        kv_ps = pst("kv_ps", [128, H, 128], F32)
        r = {n: Res(n) for n in ["gq", "gk", "gc", "mask", "identf", "ident", "gnw", "qd", "kd", "vb", "scT", "vtok", "kdtok",
                                 "state", "stmp", "state_bf", "y_sb", "sq", "s1", "s2", "mean", "var", "sd", "rstd", "eps",
                                 "yc", "yn", "sz", "yg", "mh", "sc_ps", "vt_ps", "kt_ps", "y_ps", "kv_ps", "out"]}
        r_qf = [Res(), Res()]; r_kf = [Res(), Res()]; r_vf = [Res(), Res()]; r_zf = [Res(), Res()]; r_yfin = [Res(), Res()]

        flat = lambda t: t[:, :, :].rearrange("p h t -> p (h t)")
        P.dma("sp", flat(gq), C["ret_gq"][:, :], writes=[r["gq"]])
        P.dma("sp", flat(gk), C["ret_gk"][:, :], writes=[r["gk"]])
        P.dma("sp", flat(gc), C["ret_gc"][:, :], writes=[r["gc"]])
        P.dma("sp", maskf[:, :], C["mask_ui"][:, :], writes=[r["mask"]])
        P.dma("sp", identf[:, :], C["ident"][:, :], writes=[r["identf"]])
        P.dma("sp", gnw[:, :], gnwT_dram[:, :], writes=[r["gnw"]])
        P.op("pool", lambda e: e.tensor_copy(ident[:, :], identf[:, :]), reads=[r["identf"]], writes=[r["ident"]])
        P.op("pool", lambda e: e.memset(epsc[:, :], eps), writes=[r["eps"]])
        P.op("pool", lambda e: e.memset(mh[:, :], -0.5), writes=[r["mh"]])
        P.op("pool", lambda e: e.memset(flat(state), 0.0), writes=[r["state"]])
        P.op("pool", lambda e: e.memset(flat(state_bf), 0.0), writes=[r["state_bf"]])

        def bc(t):
            return t[:, :].unsqueeze(2).to_broadcast([128, H, 128])

        for n in range(NCH):
            s = n % 2
            t0 = n * 128
            dma_rows(P, "sp", qf[s], projT, 0, H, t0, 128, r_qf[s])
            dma_rows(P, "sp", kf[s], projT, 1024, H, t0, 128, r_kf[s])
            dma_rows(P, "sp", vf[s], projT, 2048, H, t0, 128, r_vf[s])
            dma_rows(P, "sp", zf[s], projT, 3072, H, t0, 128, r_zf[s])
            P.op("dve", lambda e, s=s: e.tensor_tensor(out=flat(qd), in0=flat(qf[s]), in1=flat(gq), op=ALU.mult),
                 reads=[r_qf[s], r["gq"]], writes=[r["qd"]])
            P.op("dve", lambda e, s=s: e.tensor_tensor(out=flat(kd), in0=flat(kf[s]), in1=flat(gk), op=ALU.mult),
                 reads=[r_kf[s], r["gk"]], writes=[r["kd"]])
            P.op("pool", lambda e, s=s: e.tensor_copy(flat(vb), flat(vf[s])), reads=[r_vf[s]], writes=[r["vb"]])
            P.op("act", lambda e, s=s: e.activation(out=flat(sz), in_=flat(zf[s]), func=AF.Silu), reads=[r_zf[s]], writes=[r["sz"]])
            for h in range(H):
                P.op("pe", lambda e, h=h: e.matmul(sc_ps[:, h, :], kd[:, h, :], qd[:, h, :], start=True, stop=True),
                     reads=[r["kd"], r["qd"]], writes=[r["sc_ps"]])
            for h in range(H):
                P.op("pe", lambda e, h=h: e.transpose(vt_ps[:, h, :], vb[:, h, :], ident[:, :]),
                     reads=[r["vb"], r["ident"]], writes=[r["vt_ps"]])
            for h in range(H):
                P.op("pe", lambda e, h=h: e.transpose(kt_ps[:, h, :], kd[:, h, :], ident[:, :]),
                     reads=[r["kd"], r["ident"]], writes=[r["kt_ps"]])
            P.op("dve", lambda e: e.tensor_tensor(out=scT[:, :, :], in0=sc_ps[:, :, :],
                                                  in1=maskf[:, :].unsqueeze(1).to_broadcast([128, H, 128]), op=ALU.mult),
                 reads=[r["sc_ps"], r["mask"]], writes=[r["scT"]])
            P.op("act", lambda e: e.copy(flat(vtok), flat(vt_ps)), reads=[r["vt_ps"]], writes=[r["vtok"]])
            P.op("act", lambda e: e.copy(flat(kdtok), flat(kt_ps)), reads=[r["kt_ps"]], writes=[r["kdtok"]])
            for h in range(H):
                P.op("pe", lambda e, h=h: e.matmul(y_ps[:, h, :], scT[:, h, :], vtok[:, h, :], start=True, stop=False),
                     reads=[r["scT"], r["vtok"]], writes=[r["y_ps"]])
                P.op("pe", lambda e, h=h: e.matmul(y_ps[:, h, :], qd[:, h, :], state_bf[:, h, :], start=False, stop=True),
                     reads=[r["qd"], r["state_bf"]], writes=[r["y_ps"]])
            for h in range(H):
                P.op("pe", lambda e, h=h: e.matmul(kv_ps[:, h, :], kdtok[:, h, :], vtok[:, h, :], start=True, stop=True),
                     reads=[r["kdtok"], r["vtok"]], writes=[r["kv_ps"]])
            P.op("dve", lambda e: e.tensor_tensor(out=flat(stmp), in0=flat(kv_ps), in1=flat(state), op=ALU.add),
                 reads=[r["kv_ps"], r["state"]], writes=[r["stmp"]])
            P.op("dve", lambda e: e.tensor_tensor(out=flat(state), in0=flat(stmp), in1=flat(gc), op=ALU.mult),
                 reads=[r["stmp"], r["gc"]], writes=[r["state"]])
            P.op("act", lambda e: e.copy(flat(state_bf), flat(state)), reads=[r["state"]], writes=[r["state_bf"]])
            P.op("act", lambda e: e.copy(flat(y_sb), flat(y_ps)), reads=[r["y_ps"]], writes=[r["y_sb"]])
            P.op("dve", lambda e: e.tensor_reduce(out=s1[:, :], in_=y_sb[:, :, :], axis=AX.X, op=ALU.add),
                 reads=[r["y_sb"]], writes=[r["s1"]])
            P.op("pool", lambda e: e.tensor_tensor(out=flat(sq), in0=flat(y_sb), in1=flat(y_sb), op=ALU.mult),
                 reads=[r["y_sb"]], writes=[r["sq"]])
            P.op("dve", lambda e: e.tensor_reduce(out=s2[:, :], in_=sq[:, :, :], axis=AX.X, op=ALU.add),
                 reads=[r["sq"]], writes=[r["s2"]])
            P.op("dve", lambda e: e.tensor_scalar(out=mean[:, :], in0=s1[:, :], scalar1=1.0 / 128, scalar2=None, op0=ALU.mult),
                 reads=[r["s1"]], writes=[r["mean"]])
            P.op("dve", lambda e: e.tensor_tensor(out=var[:, :], in0=mean[:, :], in1=mean[:, :], op=ALU.mult),
                 reads=[r["mean"]], writes=[r["var"]])
            P.op("dve", lambda e: e.scalar_tensor_tensor(out=var[:, :], in0=s2[:, :], scalar=1.0 / 128, in1=var[:, :],
                                                         op0=ALU.mult, op1=ALU.subtract),
                 reads=[r["s2"], r["var"]], writes=[r["var"]])
            P.op("dve", lambda e: e.tensor_scalar(out=sd[:, :], in0=var[:, :], scalar1=1.0, scalar2=eps, op0=ALU.mult, op1=ALU.add),
                 reads=[r["var"]], writes=[r["sd"]])
            P.op("pool", lambda e: e.tensor_tensor(out=rstd[:, :], in0=sd[:, :], in1=mh[:, :], op=ALU.pow), reads=[r["sd"], r["mh"]], writes=[r["rstd"]])
            P.op("dve", lambda e: e.tensor_tensor(out=yc[:, :, :], in0=y_sb[:, :, :], in1=bc(mean), op=ALU.subtract),
                 reads=[r["y_sb"], r["mean"]], writes=[r["yc"]])
            P.op("dve", lambda e: e.tensor_tensor(out=yn[:, :, :], in0=yc[:, :, :], in1=bc(rstd), op=ALU.mult),
                 reads=[r["yc"], r["rstd"]], writes=[r["yn"]])
            for h in range(H):
                P.op("pe", lambda e, h=h: e.transpose(kt_ps[:, h, :], yn[:, h, :], ident[:, :]),
                     reads=[r["yn"], r["ident"]], writes=[r["kt_ps"]])
            P.op("dve", lambda e: e.tensor_tensor(out=yg[:, :, :], in0=kt_ps[:, :, :], in1=bc(gnw), op=ALU.mult),
                 reads=[r["kt_ps"], r["gnw"]], writes=[r["yg"]])
            P.op("pool", lambda e, s=s: e.tensor_tensor(out=flat(yfin[s]), in0=flat(yg), in1=flat(sz), op=ALU.mult),
                 reads=[r["yg"], r["sz"]], writes=[r_yfin[s]])
            P.dma("pool", yT[0:H, :, t0:t0 + 128].rearrange("k p t -> p k t"), yfin[s][:, :, :],
                  reads=[r_yfin[s]], writes=[r["out"]])
        P.barrier()
        P.flush()


GH = 16


def gdn_consts():
    c = {}
    k, i = np.meshgrid(np.arange(128), np.arange(128), indexing="ij")
    c["triu"] = (k <= i).astype(np.float32)
    c["ones"] = np.ones((128, 128), np.float32)
    c["negones"] = -np.ones((128, 128), np.float32)
    c["mask_ls"] = (k > i).astype(np.float32)
    c["mask_li"] = (k >= i).astype(np.float32)
    c["mask_ui"] = (i >= k).astype(np.float32)
    c["mask_us"] = (i > k).astype(np.float32)
    c["ident"] = np.eye(128, dtype=np.float32)
    sel = np.zeros((16, 16, 128), np.float32)
    for h in range(16):
        sel[h, h, :] = 1.0
    c["sel16"] = sel.reshape(16, 16 * 128)
    cm = np.ones((128, 512), np.float32)
    cm[:, ::128] = 0.0
    c["cmask128"] = cm
    return c


def neumann(P, nc, L, LT, rL, rLT, W, r, nlev=6, G=4):
    identb = W["identb"]
    Pk = W["Pk"]; nL = W["nL"]; nLT = W["nLT"]
    pa, pb, pp = W["pa"], W["pb"], W["pp"]
    P.op("pool", lambda e: e.tensor_tensor(out=Pk[0][:, :, :], in0=identb[:, :].unsqueeze(1).to_broadcast([128, G, 128]),
                                           in1=LT[:, :, :], op=ALU.subtract),
         reads=[r["identb"], rLT], writes=[r["Pk0"]])
    curL, curLT, rcL, rcLT = L, LT, rL, rLT
    pi = 0
    for lev in range(nlev):
        s = lev % 2
        for g in range(G):
            P.op("pe", lambda e, g=g, a=curLT, b=curL: e.matmul(pa[:, g, :], a[:, g, :], b[:, g, :], start=True, stop=True),
                 reads=[rcLT, rcL], writes=[r["pa"]])
        if lev < nlev - 1:
            for g in range(G):
                P.op("pe", lambda e, g=g, a=curL, b=curLT: e.matmul(pb[:, g, :], a[:, g, :], b[:, g, :], start=True, stop=True),
                     reads=[rcLT, rcL], writes=[r["pb"]])
        P.op("act", lambda e, s=s: e.copy(nL[s][:, :, :], pa[:, :, :]), reads=[r["pa"]], writes=[r[f"nL{s}"]])
        if lev < nlev - 1:
            P.op("dve", lambda e, s=s: e.tensor_copy(nLT[s][:, :, :], pb[:, :, :]), reads=[r["pb"]], writes=[r[f"nLT{s}"]])
        for g in range(G):
            P.op("pe", lambda e, g=g, s=s, pi=pi: e.matmul(pp[:, g, :], nL[s][:, g, :], Pk[pi][:, g, :], start=True, stop=True),
                 reads=[r[f"nL{s}"], r[f"Pk{pi}"]], writes=[r["pp"]])
        P.op("dve", lambda e, pi=pi: e.tensor_tensor(out=Pk[1 - pi][:, :, :], in0=pp[:, :, :], in1=Pk[pi][:, :, :], op=ALU.add),
             reads=[r["pp"], r[f"Pk{pi}"]], writes=[r[f"Pk{1 - pi}"]])
        pi = 1 - pi
        curL, curLT, rcL, rcLT = nL[s], nLT[s], r[f"nL{s}"], r[f"nLT{s}"]
    return Pk[pi], r[f"Pk{pi}"]


def phase_gdn_pre(P, nc, projT, S, C, prm, T, rows):
    NB = T // 512
    with contextlib.ExitStack() as st:
        def sb(name, shape, dt):
            return st.enter_context(nc.sbuf_tensor(_uid() + "gp_" + name, shape, dt))

        def pst(name, shape, dt):
            return st.enter_context(nc.psum_tensor(_uid() + "gp_" + name, shape, dt))
        r = defaultdict(Res)
        convw = sb("convw", [128, 48 * 4], F32)
        alog = sb("alog", [16, 1], F32); dtb = sb("dtb", [16, 1], F32); nega = sb("nega", [16, 1], F32)
        onesf = sb("onesf", [128, 128], F32); identf = sb("identf", [128, 128], F32)
        sel = sb("sel", [16, 16 * 128], F32); cmask = sb("cmask", [16, 512], F32)
        epsc = sb("epsc", [128, 1], F32)
        at = sb("at", [16, 512], F32); bt = sb("bt", [16, 512], F32)
        e1 = sb("e1", [16, 512], F32); spt = sb("spt", [16, 512], F32); gt = sb("gt", [16, 512], F32)
        beta = sb("beta", [16, 512], F32); gcT = sb("gcT", [16, 512], F32); egcT = sb("egcT", [16, 512], F32)
        gbs = sb("gbs", [128, 4, 32], F32)
        u = [sb(f"u{i}", [128, 515], F32) for i in range(4)]
        acc = sb("acc", [128, 512], F32); sl = sb("sl", [128, 512], F32); sq = sb("sq", [128, 512], F32)
        sd = sb("sd", [128, 512], F32); rn = sb("rn", [128, 512], F32); kn = sb("kn", [128, 512], F32)
        ob = [sb(f"ob{i}", [128, 512], BF16) for i in range(4)]
        ob2 = [sb(f"ob2{i}", [128, 512], BF16) for i in range(4)]
        ss_ps = pst("ss_ps", [128, 512], F32)
        bc_ps = pst("bc_ps", [128, 512], F32)
        tr_ps_full = pst("tr_ps", [128, 512], F32)
        tr_ps = tr_ps_full[:, 0:128].rearrange("p (c x) -> p c x", x=32)
        P.dma("sp", convw[:, :], prm["convT"][:, :], writes=[r["convw"]])
        P.dma("sp", alog[:, :], prm["alog"][:, :], writes=[r["alog"]])
        P.dma("sp", dtb[:, :], prm["dtb"][:, :], writes=[r["dtb"]])
        P.dma("sp", onesf[:, :], C["ones"][:, :], writes=[r["onesf"]])
        P.dma("sp", identf[:, :], C["ident"][:, :], writes=[r["identf"]])
        P.dma("sp", sel[:, :], C["sel16"][:, :], writes=[r["sel"]])
        P.dma("sp", cmask[:, :], C["cmask128"][0:16, :], writes=[r["cmask"]])
        P.op("pool", lambda e: e.memset(epsc[:, :], 1e-6), writes=[r["eps"]])
        P.op("act", lambda e: e.activation(out=nega[:, :], in_=alog[:, :], func=AF.Exp), reads=[r["alog"]], writes=[r["nega"]])
        P.op("dve", lambda e: e.tensor_scalar(out=nega[:, :], in0=nega[:, :], scalar1=-1.0, scalar2=None, op0=ALU.mult),
             reads=[r["nega"]], writes=[r["nega"]])
        cnt = 0
        acc2 = [acc] + [sb(f"acc_{i}", [128, 512], F32) for i in range(3)]; sl2 = [sl] + [sb(f"sl_{i}", [128, 512], F32) for i in range(3)]
        sq2 = [sq] + [sb(f"sq_{i}", [128, 512], F32) for i in range(3)]; sd2 = [sd] + [sb(f"sd_{i}", [128, 512], F32) for i in range(3)]
        rn2 = [rn] + [sb(f"rn_{i}", [128, 512], F32) for i in range(3)]; kn2 = [kn] + [sb(f"kn_{i}", [128, 512], F32) for i in range(3)]
        ss2 = [ss_ps] + [pst(f"ss_ps_{i}", [128, 512], F32) for i in range(3)]
        bc2 = [bc_ps, tr_ps_full] + [pst(f"bc_ps_{i}", [128, 512], F32) for i in range(2)]

        def do_tile(kind, rbase, dst, ti, tb, s):
            t0 = tb * 512
            acc, sl, sq, sd, rn, kn, ss_ps, bc_ps = acc2[s], sl2[s], sq2[s], sd2[s], rn2[s], kn2[s], ss2[s], bc2[s]
            ra, rsl, rsq, rsd, rrn, rkn, rss, rbc = (r[f"acc{s}"], r[f"sl{s}"], r[f"sq{s}"], r[f"sd{s}"], r[f"rn{s}"], r[f"kn{s}"],
                                                     r[f"ss_ps{s}"], r[f"bc_ps{s}"])
            wi = {"q": 0, "k": 16, "v": 32}[kind] + ti
            row0 = rbase + ti * 128
            if tb == 0:
                P.op("pool", lambda e: e.memset(u[s][:, 0:3], 0.0), writes=[r[f"u{s}"]])
                P.dma("sp", u[s][:, 3:515], projT[row0:row0 + 128, 0:512], writes=[r[f"u{s}"]])
            else:
                P.dma("sp", u[s][:, :], projT[row0:row0 + 128, t0 - 3:t0 + 512], writes=[r[f"u{s}"]])
            P.op("act", lambda e: e.mul(acc[:, :], u[s][:, 3:515], convw[:, wi * 4 + 3:wi * 4 + 4]),
                 reads=[r[f"u{s}"], r["convw"]], writes=[ra])
            for j in (2, 1, 0):
                P.op("dve", lambda e, j=j: e.scalar_tensor_tensor(out=acc[:, :], in0=u[s][:, j:j + 512],
                                                                  scalar=convw[:, wi * 4 + j:wi * 4 + j + 1],
                                                                  in1=acc[:, :], op0=ALU.mult, op1=ALU.add),
                     reads=[r[f"u{s}"], r["convw"], ra], writes=[ra])
            yield
            if kind == "v":
                P.op("act", lambda e: e.activation(out=ob[s][:, :], in_=acc[:, :], func=AF.Silu), reads=[ra], writes=[r[f"ob{s}"]])
                P.dma("pool", S[dst][ti * 128:(ti + 1) * 128, t0:t0 + 512], ob[s][:, :], reads=[r[f"ob{s}"]], writes=[r[dst]])
                return
            P.op("act", lambda e: e.activation(out=sl[:, :], in_=acc[:, :], func=AF.Silu), reads=[ra], writes=[rsl])
            P.op("pool", lambda e: e.tensor_tensor(out=sq[:, :], in0=sl[:, :], in1=sl[:, :], op=ALU.mult), reads=[rsl], writes=[rsq])
            P.op("pe", lambda e: e.matmul(ss_ps[:, :], onesf[:, :], sq[:, :], start=True, stop=True), reads=[r["onesf"], rsq], writes=[rss])
            yield
            P.op("act", lambda e: e.activation(out=sd[:, :], in_=ss_ps[:, :], func=AF.Sqrt, bias=epsc[:, 0:1], scale=1.0),
                 reads=[rss, r["eps"]], writes=[rsd])
            yield
            P.op("dve", lambda e: e.reciprocal(rn[:, :], sd[:, :]), reads=[rsd], writes=[rrn])
            if kind == "k":
                P.op("dve", lambda e: e.tensor_tensor(out=ob[s][:, :], in0=sl[:, :], in1=rn[:, :], op=ALU.mult),
                     reads=[rsl, rrn], writes=[r[f"ob{s}"]])
                P.dma("pool", S[dst][ti * 128:(ti + 1) * 128, t0:t0 + 512], ob[s][:, :], reads=[r[f"ob{s}"]], writes=[r[dst]])
            else:
                P.op("dve", lambda e: e.scalar_tensor_tensor(out=kn[:, :], in0=sl[:, :], scalar=128 ** -0.5, in1=rn[:, :],
                                                             op0=ALU.mult, op1=ALU.mult),
                     reads=[rsl, rrn], writes=[rkn])
                P.op("act", lambda e: e.copy(ob[s][:, :], kn[:, :]), reads=[rkn], writes=[r[f"ob{s}"]])
                P.dma("pool", S[dst][ti * 128:(ti + 1) * 128, t0:t0 + 512], ob[s][:, :], reads=[r[f"ob{s}"]], writes=[r[dst]])
                P.op("pe", lambda e: e.matmul(bc_ps[:, :], sel[:, ti * 128:(ti + 1) * 128], egcT[:, :], start=True, stop=True),
                     reads=[r["sel"], r["egcT"]], writes=[rbc])
                P.op("dve", lambda e: e.tensor_tensor(out=ob2[s][:, :], in0=kn[:, :], in1=bc_ps[:, :], op=ALU.mult),
                     reads=[rkn, rbc], writes=[r[f"ob2{s}"]])
                P.dma("pool", S["gqdT"][ti * 128:(ti + 1) * 128, t0:t0 + 512], ob2[s][:, :], reads=[r[f"ob2{s}"]], writes=[r["gqdT"]])

        for tb in range(NB):
            t0 = tb * 512
            P.dma("sp", at[:, :], projT[rows["a"]:rows["a"] + 16, t0:t0 + 512], writes=[r["at"]])
            P.dma("sp", bt[:, :], projT[rows["b"]:rows["b"] + 16, t0:t0 + 512], writes=[r["bt"]])
            P.op("act", lambda e: e.activation(out=e1[:, :], in_=at[:, :], func=AF.Exp, bias=dtb[:, 0:1], scale=1.0),
                 reads=[r["at"], r["dtb"]], writes=[r["e1"]])
            P.op("dve", lambda e: e.tensor_scalar(out=e1[:, :], in0=e1[:, :], scalar1=1.0, scalar2=None, op0=ALU.add),
                 reads=[r["e1"]], writes=[r["e1"]])
            P.op("act", lambda e: e.activation(out=spt[:, :], in_=e1[:, :], func=AF.Ln), reads=[r["e1"]], writes=[r["spt"]])
            P.op("dve", lambda e: e.tensor_scalar(out=gt[:, :], in0=spt[:, :], scalar1=nega[:, 0:1], scalar2=None, op0=ALU.mult),
                 reads=[r["spt"], r["nega"]], writes=[r["gt"]])
            P.op("act", lambda e: e.activation(out=beta[:, :], in_=bt[:, :], func=AF.Sigmoid), reads=[r["bt"]], writes=[r["beta"]])
            P.op("dve", lambda e: e.tensor_tensor_scan(out=gcT[:, :], data0=cmask[:, :], data1=gt[:, :], initial=0.0,
                                                       op0=ALU.mult, op1=ALU.add),
                 reads=[r["cmask"], r["gt"]], writes=[r["gcT"]])
            P.op("act", lambda e: e.activation(out=egcT[:, :], in_=gcT[:, :], func=AF.Exp), reads=[r["gcT"]], writes=[r["egcT"]])
            for c4 in range(4):
                P.op("pe", lambda e, c4=c4: e.transpose(tr_ps[:, c4, 0:16], gt[:, c4 * 128:(c4 + 1) * 128], identf[0:16, 0:16]),
                     reads=[r["gt"], r["identf"]], writes=[r["bc_ps1"]])
                P.op("pe", lambda e, c4=c4: e.transpose(tr_ps[:, c4, 16:32], beta[:, c4 * 128:(c4 + 1) * 128], identf[0:16, 0:16]),
                     reads=[r["beta"], r["identf"]], writes=[r["bc_ps1"]])
            P.op("dve", lambda e: e.tensor_copy(gbs[:, :, :], tr_ps[:, :, :]), reads=[r["bc_ps1"]], writes=[r["gbs"]])
            P.dma("pool", S["gbt"][t0:t0 + 512, :].rearrange("(c p) x -> p c x", p=128), gbs[:, :, :],
                  reads=[r["gbs"]], writes=[r["gbt_out"]])
            for kind, rbase, dst in (("q", rows["q"], "gqT"), ("k", rows["k"], "gkT"), ("v", rows["v"], "gvT")):
                for ti in range(0, 16, 4):
                    zipper([do_tile(kind, rbase, dst, ti + j, tb, j) for j in range(4)])
        P.barrier()
        P.flush()


def zipper_lag(gens):
    a, b = gens
    a_done = b_done = False
    try:
        next(a)
    except StopIteration:
        a_done = True
    while not (a_done and b_done):
        if not b_done:
            try:
                next(b)
            except StopIteration:
                b_done = True
        if not a_done:
            try:
                next(a)
            except StopIteration:
                a_done = True


def zipper(gens):
    active = list(gens)
    while active:
        for g in list(active):
            try:
                next(g)
            except StopIteration:
                active.remove(g)


def neumann_gen(P, nc, L, LT, rL, rLT, W, r, nlev=6, G=4):
    identb = W["identb"]
    Pk = W["Pk"]; nL = W["nL"]; nLT = W["nLT"]
    pa, pb = W["pa"], W["pb"]
    P.op("pool", lambda e: e.tensor_tensor(out=Pk[0][:, :, :], in0=identb[:, :].unsqueeze(1).to_broadcast([128, G, 128]),
                                           in1=LT[:, :, :], op=ALU.subtract),
         reads=[W["r_identb"], rLT], writes=[r["Pk0"]])
    yield
    curL, curLT, rcL, rcLT = L, LT, rL, rLT
    pi = 0
    for lev in range(nlev):
        s = lev % 2
        for g in range(G):
            P.op("pe", lambda e, g=g, a=curLT, b=curL: e.matmul(pa[:, g, :], a[:, g, :], b[:, g, :], start=True, stop=True),
                 reads=[rcLT, rcL], writes=[r["pa"]])
        if lev < nlev - 1:
            for g in range(G):
                P.op("pe", lambda e, g=g, a=curL, b=curLT: e.matmul(pb[:, g, :], a[:, g, :], b[:, g, :], start=True, stop=True),
                     reads=[rcLT, rcL], writes=[r["pb"]])
        yield
        P.op("act", lambda e, s=s: e.copy(nL[s][:, :, :], pa[:, :, :]), reads=[r["pa"]], writes=[r[f"nL{s}"]])
        if lev < nlev - 1:
            P.op("dve", lambda e, s=s: e.tensor_copy(nLT[s][:, :, :], pb[:, :, :]), reads=[r["pb"]], writes=[r[f"nLT{s}"]])
        yield
        for g in range(G):
            P.op("pe", lambda e, g=g, s=s, pi=pi: e.matmul(pa[:, g, :], nL[s][:, g, :], Pk[pi][:, g, :], start=True, stop=True),
                 reads=[r[f"nL{s}"], r[f"Pk{pi}"]], writes=[r["pa"]])
        yield
        P.op("dve", lambda e, pi=pi: e.tensor_tensor(out=Pk[1 - pi][:, :, :], in0=pa[:, :, :], in1=Pk[pi][:, :, :], op=ALU.add),
             reads=[r["pa"], r[f"Pk{pi}"]], writes=[r[f"Pk{1 - pi}"]])
        yield
        pi = 1 - pi
        curL, curLT, rcL, rcLT = nL[s], nLT[s], r[f"nL{s}"], r[f"nLT{s}"]
    W["result"] = (Pk[pi], r[f"Pk{pi}"])


def phase_gdn_g1(P, nc, S, C, T):
    NCH = T // 128
    G = 4
    with contextlib.ExitStack() as st:
        def sb(name, shape, dt):
            return st.enter_context(nc.sbuf_tensor(_uid() + "g1_" + name, shape, dt))

        def pst(name, shape, dt):
            return st.enter_context(nc.psum_tensor(_uid() + "g1_" + name, shape, dt))
        r = defaultdict(Res)
        triu = sb("triu", [128, 128], F32); onesf = sb("onesf", [128, 128], F32); negones = sb("negones", [128, 128], F32)
        mls = sb("mls", [128, 128], F32); mli = sb("mli", [128, 128], F32)
        identf = sb("identf", [128, 128], F32); identb = sb("identb", [128, 128], BF16)
        gb = [sb(f"gb{i}", [128, 32], F32) for i in range(2)]
        gcs = sb("gcs", [128, 32], F32)
        egc = sb("egc", [128, 16], F32); dtl = sb("dtl", [128, 16], F32); etail = sb("etail", [128, 16], F32)
        elast = [sb(f"elast{i}", [128, 16], F32) for i in range(2)]
        bgc = sb("bgc", [128, 16], F32)
        Gb = sb("Gb", [128, 16, 128], F32); X = sb("X", [128, 16, 128], F32)
        gc_ps = None
        ST = []
        for q in range(2):
            t = {}
            for n in ("kT", "qT", "vT", "L", "attn", "LT", "attnT", "kbg", "ktail", "vb", "wk_sb", "nL0", "nL1", "nLT0", "nLT1", "Pk0", "Pk1"):
                t[n] = sb(f"{n}_{q}", [128, G, 128], BF16)
            for n in ("M1", "dec", "dec_s", "dec_i", "u_sb"):
                t[n] = sb(f"{n}_{q}", [128, G, 128], F32)
            t["g_ps"] = pst(f"g_ps{q}", [128, G, 128], F32)
            t["tr_ps"] = pst(f"tr_ps{q}", [128, 2, G, 128], BF16)
            t["pa"] = pst(f"pa{q}", [128, G, 128], F32)
            t["pb"] = pst(f"pb{q}", [128, G, 128], F32)
            t["r"] = defaultdict(Res)
            t["W"] = {"identb": identb, "r_identb": r["identb"], "nL": [t["nL0"], t["nL1"]], "nLT": [t["nLT0"], t["nLT1"]],
                      "Pk": [t["Pk0"], t["Pk1"]], "pa": t["pa"], "pb": t["pb"]}
            ST.append(t)
        for nm, t_, src in (("triu", triu, "triu"), ("onesf", onesf, "ones"), ("negones", negones, "negones"),
                            ("mls", mls, "mask_ls"), ("mli", mli, "mask_li"), ("identf", identf, "ident")):
            P.dma("sp", t_[:, :], C[src][:, :], writes=[r[nm]])
        P.op("pool", lambda e: e.tensor_copy(identb[:, :], identf[:, :]), reads=[r["identf"]], writes=[r["identb"]])

        def bcg(t, g):
            return t[:, g * G:(g + 1) * G].unsqueeze(2).to_broadcast([128, G, 128])

        def bcm(m):
            return m[:, :].unsqueeze(1).to_broadcast([128, G, 128])

        def grp(c, g, q, sc):
            t = ST[q]
            rr = t["r"]
            t0 = c * 128
            r0 = g * G * 128
            kT, qT, vT = t["kT"], t["qT"], t["vT"]
            g_ps, tr_ps = t["g_ps"], t["tr_ps"]
            for nm, tl, src in (("kT", kT, "gkT"), ("qT", qT, "gqT"), ("vT", vT, "gvT")):
                P.dma("sp", tl[:, :, :], S[src][r0:r0 + G * 128, t0:t0 + 128].rearrange("(h p) t -> p h t", p=128), writes=[rr[nm]])
            P.op("pe", lambda e: e.matmul(g_ps[:, :, :], triu[:, :], Gb[:, g * G:(g + 1) * G, :], start=True, stop=False),
                 reads=[r["triu"], r["Gb"]], writes=[rr["g_ps"]])
            P.op("pe", lambda e: e.matmul(g_ps[:, :, :], negones[:, :], X[:, g * G:(g + 1) * G, :], start=False, stop=True),
                 reads=[r["negones"], r["X"]], writes=[rr["g_ps"]])
            yield
            P.op("dve", lambda e: e.tensor_scalar(out=t["M1"][:, :, :], in0=g_ps[:, :, :], scalar1=0.0, scalar2=None, op0=ALU.min),
                 reads=[rr["g_ps"]], writes=[rr["M1"]])
            yield
            P.op("act", lambda e: e.activation(out=t["dec"][:, :, :], in_=t["M1"][:, :, :], func=AF.Exp), reads=[rr["M1"]], writes=[rr["dec"]])
            for h in range(G):
                P.op("pe", lambda e, h=h: e.matmul(g_ps[:, h, :], kT[:, h, :], kT[:, h, :], start=True, stop=True),
                     reads=[rr["kT"]], writes=[rr["g_ps"]])
            yield
            P.op("pool", lambda e: e.tensor_tensor(out=t["dec_i"][:, :, :], in0=t["dec"][:, :, :], in1=bcm(mli), op=ALU.mult),
                 reads=[rr["dec"], r["mli"]], writes=[rr["dec_i"]])
            P.op("dve", lambda e: e.tensor_tensor(out=t["dec_s"][:, :, :], in0=t["dec"][:, :, :], in1=bcm(mls), op=ALU.mult),
                 reads=[rr["dec"], r["mls"]], writes=[rr["dec_s"]])
            yield
            P.op("dve", lambda e: e.tensor_tensor(out=t["dec_s"][:, :, :], in0=t["dec_s"][:, :, :],
                                                  in1=gb[sc][:, 16 + g * G:16 + (g + 1) * G].unsqueeze(2).to_broadcast([128, G, 128]), op=ALU.mult),
                 reads=[rr["dec_s"], r[f"gb{sc}"]], writes=[rr["dec_s"]])
            yield
            P.op("dve", lambda e: e.tensor_tensor(out=t["L"][:, :, :], in0=g_ps[:, :, :], in1=t["dec_s"][:, :, :], op=ALU.mult),
                 reads=[rr["g_ps"], rr["dec_s"]], writes=[rr["L"]])
            yield
            for h in range(G):
                P.op("pe", lambda e, h=h: e.matmul(g_ps[:, h, :], qT[:, h, :], kT[:, h, :], start=True, stop=True),
                     reads=[rr["kT"], rr["qT"]], writes=[rr["g_ps"]])
            for h in range(G):
                P.op("pe", lambda e, h=h: e.transpose(tr_ps[:, 0, h, :], t["L"][:, h, :], identb[:, :]),
                     reads=[rr["L"], r["identb"]], writes=[rr["tr_ps"]])
            yield
            P.op("dve", lambda e: e.tensor_tensor(out=t["attn"][:, :, :], in0=g_ps[:, :, :], in1=t["dec_i"][:, :, :], op=ALU.mult),
                 reads=[rr["g_ps"], rr["dec_i"]], writes=[rr["attn"]])
            P.op("act", lambda e: e.copy(t["LT"][:, :, :], tr_ps[:, 0, :, :]), reads=[rr["tr_ps"]], writes=[rr["LT"]])
            yield
            for h in range(G):
                P.op("pe", lambda e, h=h: e.transpose(tr_ps[:, 1, h, :], t["attn"][:, h, :], identb[:, :]),
                     reads=[rr["attn"], r["identb"]], writes=[rr["tr_ps"]])
            yield
            P.op("act", lambda e: e.copy(t["attnT"][:, :, :], tr_ps[:, 1, :, :]), reads=[rr["tr_ps"]], writes=[rr["attnT"]])
            P.dma("pool", S["attnT_d"][c, :, g * G:(g + 1) * G, :], t["attnT"][:, :, :],
                  reads=[rr["attnT"]], writes=[r["attnT_out"]])
            yield
            yield from neumann_gen(P, nc, t["L"], t["LT"], rr["L"], rr["LT"], t["W"], rr)
            Pt, rPt = t["W"]["result"]
            for h in range(G):
                P.op("pe", lambda e, h=h: e.transpose(tr_ps[:, 0, h, :], kT[:, h, :], identb[:, :]),
                     reads=[rr["kT"], r["identb"]], writes=[rr["tr_ps"]])
            for h in range(G):
                P.op("pe", lambda e, h=h: e.transpose(tr_ps[:, 1, h, :], vT[:, h, :], identb[:, :]),
                     reads=[rr["vT"], r["identb"]], writes=[rr["tr_ps"]])
            yield
            P.op("dve", lambda e: e.tensor_tensor(out=t["kbg"][:, :, :], in0=tr_ps[:, 0, :, :], in1=bcg(bgc, g), op=ALU.mult),
                 reads=[rr["tr_ps"], r["bgc"]], writes=[rr["kbg"]])
            P.op("dve", lambda e: e.tensor_tensor(out=t["ktail"][:, :, :], in0=tr_ps[:, 0, :, :], in1=bcg(etail, g), op=ALU.mult),
                 reads=[rr["tr_ps"], r["etail"]], writes=[rr["ktail"]])
            P.op("dve", lambda e: e.tensor_tensor(out=t["vb"][:, :, :], in0=tr_ps[:, 1, :, :],
                                                  in1=gb[sc][:, 16 + g * G:16 + (g + 1) * G].unsqueeze(2).to_broadcast([128, G, 128]), op=ALU.mult),
                 reads=[rr["tr_ps"], r[f"gb{sc}"]], writes=[rr["vb"]])
            P.dma("pool", S["ktl_d"][t0:t0 + 128, r0:r0 + G * 128], t["ktail"][:, :, :].rearrange("p h d -> p (h d)"),
                  reads=[rr["ktail"]], writes=[r["ktl_out"]])
            yield
            for h in range(G):
                P.op("pe", lambda e, h=h: e.matmul(g_ps[:, h, :], Pt[:, h, :], t["vb"][:, h, :], start=True, stop=True),
                     reads=[rPt, rr["vb"]], writes=[rr["g_ps"]])
            yield
            P.op("act", lambda e: e.copy(t["u_sb"][:, :, :], g_ps[:, :, :]), reads=[rr["g_ps"]], writes=[rr["u_sb"]])
            P.dma("pool", S["u_d"][t0:t0 + 128, r0:r0 + G * 128], t["u_sb"][:, :, :].rearrange("p h d -> p (h d)"),
                  reads=[rr["u_sb"]], writes=[r["u_out"]])
            yield
            for h in range(G):
                P.op("pe", lambda e, h=h: e.matmul(g_ps[:, h, :], t["kbg"][:, h, :], Pt[:, h, :], start=True, stop=True),
                     reads=[rPt, rr["kbg"]], writes=[rr["g_ps"]])
            yield
            P.op("act", lambda e: e.copy(t["wk_sb"][:, :, :], g_ps[:, :, :]), reads=[rr["g_ps"]], writes=[rr["wk_sb"]])
            P.dma("pool", S["wkT_d"][r0:r0 + G * 128, t0:t0 + 128].rearrange("(h p) t -> p h t", p=128), t["wk_sb"][:, :, :],
                  reads=[rr["wk_sb"]], writes=[r["wk_out"]])
            yield

        for c in range(NCH):
            t0 = c * 128
            sc = c % 2
            gps0 = ST[0]["g_ps"]
            rg0 = ST[0]["r"]["g_ps"]
            P.dma("sp", gb[sc][:, :], S["gbt"][t0:t0 + 128, :], writes=[r[f"gb{sc}"]])
            P.op("pe", lambda e, sc=sc: e.matmul(gps0[:, 0, 0:16], triu[:, :], gb[sc][:, 0:16], start=True, stop=True),
                 reads=[r["triu"], r[f"gb{sc}"]], writes=[rg0])
            P.op("pe", lambda e, sc=sc: e.matmul(gps0[:, 0, 16:32], onesf[:, :], gb[sc][:, 0:16], start=True, stop=True),
                 reads=[r["onesf"], r[f"gb{sc}"]], writes=[rg0])
            P.op("dve", lambda e: e.tensor_copy(gcs[:, :], gps0[:, 0, 0:32]), reads=[rg0], writes=[r["gcs"]])
            P.op("act", lambda e: e.activation(out=egc[:, :], in_=gcs[:, 0:16], func=AF.Exp), reads=[r["gcs"]], writes=[r["egc"]])
            P.op("dve", lambda e: e.tensor_tensor(out=dtl[:, :], in0=gcs[:, 16:32], in1=gcs[:, 0:16], op=ALU.subtract),
                 reads=[r["gcs"]], writes=[r["dtl"]])
            P.op("act", lambda e: e.activation(out=etail[:, :], in_=dtl[:, :], func=AF.Exp), reads=[r["dtl"]], writes=[r["etail"]])
            P.op("act", lambda e, sc=sc: e.activation(out=elast[sc][:, :], in_=gcs[:, 16:32], func=AF.Exp),
                 reads=[r["gcs"]], writes=[r[f"elast{sc}"]])
            P.dma("pool", S["els_d"][c, :, :], elast[sc][:, :], reads=[r[f"elast{sc}"]], writes=[r["els_out"]])
            P.op("dve", lambda e, sc=sc: e.tensor_tensor(out=bgc[:, :], in0=gb[sc][:, 16:32], in1=egc[:, :], op=ALU.mult),
                 reads=[r[f"gb{sc}"], r["egc"]], writes=[r["bgc"]])
            P.op("pool", lambda e, sc=sc: e.tensor_copy(Gb[:, :, :], gb[sc][:, 0:16].unsqueeze(2).to_broadcast([128, 16, 128])),
                 reads=[r[f"gb{sc}"]], writes=[r["Gb"]])
            P.op("pool", lambda e: e.tensor_tensor(out=X[:, :, :], in0=Gb[:, :, :],
                                                   in1=triu[:, :].unsqueeze(1).to_broadcast([128, 16, 128]), op=ALU.mult),
                 reads=[r["Gb"], r["triu"]], writes=[r["X"]])
            for g0 in (0, 2):
                zipper([grp(c, g0, 0, sc), grp(c, g0 + 1, 1, sc)])
        P.barrier()
        P.flush()


def phase_gdn_g2(P, nc, projT, S, C, yT, normw_dram, T, zrow, kc0=16):
    NCH = T // 128
    G = 4
    H = 16
    with contextlib.ExitStack() as st:
        def sb(name, shape, dt):
            return st.enter_context(nc.sbuf_tensor(_uid() + "g2_" + name, shape, dt))

        def pst(name, shape, dt):
            return st.enter_context(nc.psum_tensor(_uid() + "g2_" + name, shape, dt))
        r = defaultdict(Res)
        identf = sb("identf", [128, 128], F32); identb = sb("identb", [128, 128], BF16)
        nrm = sb("nrm", [128, 1], F32); epsc = sb("epsc", [128, 1], F32); mh = sb("mh", [128, G], F32)
        wkT = [sb(f"wkT{i}", [128, H, 128], BF16) for i in range(2)]
        qdT = [sb(f"qdT{i}", [128, H, 128], BF16) for i in range(2)]
        atT = [sb(f"atT{i}", [128, H, 128], BF16) for i in range(2)]
        ktl = [sb(f"ktl{i}", [128, H, 128], BF16) for i in range(2)]
        uu = [sb(f"uu{i}", [128, H, 128], F32) for i in range(2)]
        zt = [sb(f"zt{i}", [128, H, 128], F32) for i in range(2)]
        els = [sb(f"els{i}", [128, H], F32) for i in range(2)]
        Sf = sb("Sf", [128, H, 128], F32); Sb = sb("Sb", [128, H, 128], BF16)
        TS = []
        for q in range(2):
            d = {"Stmp": sb(f"Stmp{q}", [128, G, 128], F32), "vnew": sb(f"vnew{q}", [128, G, 128], BF16),
                 "o_sb": sb(f"o_sb{q}", [128, G, 128], F32), "osq": sb(f"osq{q}", [128, G, 128], F32),
                 "s2": sb(f"s2{q}", [128, G], F32), "sd": sb(f"sd{q}", [128, G], F32), "rstd": sb(f"rstd{q}", [128, G], F32),
                 "on": sb(f"on{q}", [128, G, 128], BF16), "sz": sb(f"sz{q}", [128, G, 128], F32), "yg": sb(f"yg{q}", [128, G, 128], F32)}
            TS.append(d)
        yfin = [sb(f"yfin{i}", [128, G, 128], BF16) for i in range(2)]
        ws_ps = [pst(f"ws_ps{i}", [128, G, 128], F32) for i in range(2)]
        o_ps = [pst(f"o_ps{i}", [128, G, 128], F32) for i in range(2)]
        kv_ps = [pst(f"kv_ps{i}", [128, G, 128], F32) for i in range(2)]
        tr_ps2 = [pst(f"tr_ps{q}", [128, 2, G, 128], BF16) for q in range(2)]
        P.dma("sp", identf[:, :], C["ident"][:, :], writes=[r["identf"]])
        P.dma("sp", nrm[:, :], normw_dram[:, :], writes=[r["nrm"]])
        P.op("pool", lambda e: e.tensor_copy(identb[:, :], identf[:, :]), reads=[r["identf"]], writes=[r["identb"]])
        P.op("pool", lambda e: e.memset(epsc[:, :], 1e-6), writes=[r["eps"]])
        P.op("pool", lambda e: e.memset(mh[:, :], -0.5), writes=[r["mh"]])
        P.op("pool", lambda e: e.memset(Sf[:, :, :].rearrange("p h e -> p (h e)"), 0.0), writes=[r["Sf"]])
        P.op("pool", lambda e: e.memset(Sb[:, :, :].rearrange("p h e -> p (h e)"), 0.0), writes=[r["Sb"]])
        def grp(c, g, b, s):
            t0 = c * 128
            T_ = TS[b]
            Stmp, vnew, o_sb, osq, s2, sd, rstd, on, sz, yg = [T_[n] for n in ("Stmp", "vnew", "o_sb", "osq", "s2", "sd", "rstd", "on", "sz", "yg")]
            tr_ps = tr_ps2[b]
            hs = slice(g * G, (g + 1) * G)
            rS = r[f"Sf{g}"]; rSb = r[f"Sb{g}"]
            for h in range(G):
                hh = g * G + h
                P.op("pe", lambda e, h=h, hh=hh, s=s, b=b: e.matmul(ws_ps[b][:, h, :], wkT[s][:, hh, :], Sb[:, hh, :], start=True, stop=True),
                     reads=[r[f"wkT{s}"], rSb, r["Sb"]], writes=[r[f"ws_ps{b}"]])
            yield
            P.op("dve", lambda e, s=s, b=b, hs=hs: e.tensor_tensor(out=vnew[:, :, :], in0=uu[s][:, hs, :], in1=ws_ps[b][:, :, :], op=ALU.subtract),
                 reads=[r[f"uu{s}"], r[f"ws_ps{b}"]], writes=[r["vnew" + str(b)]])
            yield
            for h in range(G):
                hh = g * G + h
                P.op("pe", lambda e, h=h, hh=hh, s=s, b=b: e.matmul(o_ps[b][:, h, :], qdT[s][:, hh, :], Sb[:, hh, :], start=True, stop=False),
                     reads=[r[f"qdT{s}"], rSb, r["Sb"]], writes=[r[f"o_ps{b}"]])
                P.op("pe", lambda e, h=h, hh=hh, s=s, b=b: e.matmul(o_ps[b][:, h, :], atT[s][:, hh, :], vnew[:, h, :], start=False, stop=True),
                     reads=[r[f"atT{s}"], r["vnew" + str(b)]], writes=[r[f"o_ps{b}"]])
            for h in range(G):
                hh = g * G + h
                P.op("pe", lambda e, h=h, hh=hh, s=s, b=b: e.matmul(kv_ps[b][:, h, :], ktl[s][:, hh, :], vnew[:, h, :], start=True, stop=True),
                     reads=[r[f"ktl{s}"], r["vnew" + str(b)]], writes=[r[f"kv_ps{b}"]])
            yield
            P.op("dve", lambda e, s=s, hs=hs, g=g: e.tensor_tensor(out=Stmp[:, :, :], in0=Sf[:, hs, :],
                                                                  in1=els[s][:, g * G:(g + 1) * G].unsqueeze(2).to_broadcast([128, G, 128]),
                                                                  op=ALU.mult),
                 reads=[rS, r["Sf"], r[f"els{s}"]], writes=[r["Stmp" + str(b)]])
            P.op("dve", lambda e, hs=hs, b=b: e.tensor_tensor(out=Sf[:, hs, :], in0=Stmp[:, :, :], in1=kv_ps[b][:, :, :], op=ALU.add),
                 reads=[r["Stmp" + str(b)], r[f"kv_ps{b}"]], writes=[rS])
            P.op("act", lambda e, hs=hs: e.copy(Sb[:, hs, :], Sf[:, hs, :]), reads=[rS], writes=[rSb])
            yield
            P.op("act", lambda e, b=b: e.copy(o_sb[:, :, :], o_ps[b][:, :, :]), reads=[r[f"o_ps{b}"]], writes=[r["o_sb" + str(b)]])
            P.op("pool", lambda e: e.tensor_tensor(out=osq[:, :, :], in0=o_sb[:, :, :], in1=o_sb[:, :, :], op=ALU.mult),
                 reads=[r["o_sb" + str(b)]], writes=[r["osq" + str(b)]])
            yield
            P.op("dve", lambda e: e.tensor_reduce(out=s2[:, :], in_=osq[:, :, :], axis=AX.X, op=ALU.add), reads=[r["osq" + str(b)]], writes=[r["s2" + str(b)]])
            P.op("dve", lambda e: e.tensor_scalar(out=sd[:, :], in0=s2[:, :], scalar1=1.0 / 128, scalar2=1e-6, op0=ALU.mult, op1=ALU.add),
                 reads=[r["s2" + str(b)]], writes=[r["sd" + str(b)]])
            yield
            P.op("pool", lambda e: e.tensor_tensor(out=rstd[:, :], in0=sd[:, :], in1=mh[:, :], op=ALU.pow),
                 reads=[r["sd" + str(b)], r["mh"]], writes=[r["rstd" + str(b)]])
            P.op("dve", lambda e: e.tensor_tensor(out=on[:, :, :], in0=o_sb[:, :, :],
                                                  in1=rstd[:, :].unsqueeze(2).to_broadcast([128, G, 128]), op=ALU.mult),
                 reads=[r["o_sb" + str(b)], r["rstd" + str(b)]], writes=[r["on" + str(b)]])
            yield
            for h in range(G):
                P.op("pe", lambda e, h=h: e.transpose(tr_ps[:, 0, h, :], on[:, h, :], identb[:, :]),
                     reads=[r["on" + str(b)], r["identb"]], writes=[r["tr_ps" + str(b)]])
            P.op("act", lambda e, s=s, hs=hs: e.activation(out=sz[:, :, :], in_=zt[s][:, hs, :], func=AF.Silu),
                 reads=[r[f"zt{s}"]], writes=[r["sz" + str(b)]])
            yield
            P.op("dve", lambda e: e.scalar_tensor_tensor(out=yg[:, :, :], in0=tr_ps[:, 0, :, :], scalar=nrm[:, 0:1], in1=sz[:, :, :],
                                                         op0=ALU.mult, op1=ALU.mult),
                 reads=[r["tr_ps" + str(b)], r["nrm"], r["sz" + str(b)]], writes=[r["yg" + str(b)]])
            P.op("pool", lambda e, b=b: e.tensor_copy(yfin[b][:, :, :], yg[:, :, :]), reads=[r["yg" + str(b)]], writes=[r[f"yfin{b}"]])
            P.dma("pool", yT[kc0 + g * G:kc0 + (g + 1) * G, :, t0:t0 + 128].rearrange("k p t -> p k t"), yfin[b][:, :, :],
                  reads=[r[f"yfin{b}"]], writes=[r["y_out"]])

        it = 0
        for c in range(NCH):
            t0 = c * 128
            s = c % 2
            for g in range(4):
                r0 = g * G * 128
                hs = slice(g * G, (g + 1) * G)
                P.dma("sp", wkT[s][:, hs, :], S["wkT_d"][r0:r0 + G * 128, t0:t0 + 128].rearrange("(h p) t -> p h t", p=128),
                      writes=[r[f"wkT{s}"]])
                P.dma("sp", qdT[s][:, hs, :], S["gqdT"][r0:r0 + G * 128, t0:t0 + 128].rearrange("(h p) t -> p h t", p=128),
                      writes=[r[f"qdT{s}"]])
                P.dma("sp", atT[s][:, hs, :], S["attnT_d"][c, :, g * G:(g + 1) * G, :],
                      writes=[r[f"atT{s}"]])
                P.dma("sp", ktl[s][:, hs, :].rearrange("p h d -> p (h d)"), S["ktl_d"][t0:t0 + 128, r0:r0 + G * 128],
                      writes=[r[f"ktl{s}"]])
                P.dma("sp", uu[s][:, hs, :].rearrange("p h d -> p (h d)"), S["u_d"][t0:t0 + 128, r0:r0 + G * 128],
                      writes=[r[f"uu{s}"]])
                P.dma("sp", zt[s][:, hs, :], projT[zrow + r0:zrow + r0 + G * 128, t0:t0 + 128].rearrange("(h p) t -> p h t", p=128),
                      writes=[r[f"zt{s}"]])
            P.dma("sp", els[s][:, :], S["els_d"][c, :, :], writes=[r[f"els{s}"]])
            for g0 in (0, 2):
                zipper([grp(c, g0, 0, s), grp(c, g0 + 1, 1, s)])
        P.barrier()
        P.flush()


def rwkv_consts():
    c = {}
    bo = np.zeros((128, 128), np.float32)
    bo[:64, :64] = 1.0
    bo[64:, 64:] = 1.0
    c["blockones"] = bo
    return c


def phase_rwkv_pre(P, nc, projT, S, C, prm, T, row0):
    NB = T // 512
    with contextlib.ExitStack() as st:
        def sb(name, shape, dt):
            return st.enter_context(nc.sbuf_tensor(_uid() + "wp_" + name, shape, dt))

        def pst(name, shape, dt):
            return st.enter_context(nc.psum_tensor(_uid() + "wp_" + name, shape, dt))
        r = defaultdict(Res)
        mu = sb("mu", [128, 33], F32); omm = sb("omm", [128, 33], F32)
        w0 = sb("w0", [128, 8], F32); nw0 = sb("nw0", [128, 8], F32); a0 = sb("a0", [128, 8], F32)
        kk_ = sb("kk_", [128, 8], F32); ka = sb("ka", [128, 8], F32); omka = sb("omka", [128, 8], F32); rk = sb("rk", [128, 8], F32)
        lw2 = sb("lw2", [128, 1024], F32)
        bones = sb("bones", [128, 128], F32)
        cmask = sb("cmask", [128, 512], F32)
        onec = sb("onec", [128, 1], F32); negh = sb("negh", [128, 1], F32); epsc = sb("epsc", [128, 1], F32)
        ul = sb("ul", [128, 513], F32); ml = sb("ml", [128, 512], F32); th = sb("th", [128, 512], F32)
        TS = []
        for q in range(2):
            d = {}
            for j in range(4):
                d[f"u{j}"] = sb(f"u{j}_{q}", [128, 513], F32); d[f"mx{j}"] = sb(f"mx{j}_{q}", [128, 512], F32)
            for n in ("tmp", "e1", "spt", "e2", "cw", "ecw", "cwm", "ecwm", "einv", "av", "kk0", "sq", "sd", "rn", "kkn", "fac", "k2", "kka", "prod"):
                d[n] = sb(f"{n}_{q}", [128, 512], F32)
            TS.append(d)
        tmp = sb("tmp_l", [128, 512], F32)
        obf = {n: [sb(f"o_{n}{i}", [128, 512], BF16) for i in range(2)] for n in ("rt", "kkt", "kh", "kka", "v")}
        of32 = {n: [sb(f"o_{n}{i}", [128, 512], F32) for i in range(2)] for n in ("bon", "sz")}
        pcs = [sb(f"pcs{i}", [128, 4], F32) for i in range(2)]
        for q in range(2):
            for n in ("wl_ps", "a_ps", "ss_ps", "sb_ps"):
                TS[q][n] = pst(f"{n}_{q}", [128, 512], F32)
        for nm, t_, src in (("mu", mu, "muT"), ("w0", w0, "w0T"), ("a0", a0, "a0T"), ("kk_", kk_, "kkT"), ("ka", ka, "kaT"),
                            ("rk", rk, "rkT"), ("lw2", lw2, "lw2")):
            P.dma("sp", t_[:, :], prm[src][:, :], writes=[r[nm]])
        P.dma("sp", bones[:, :], C["blockones"][:, :], writes=[r["bones"]])
        P.dma("sp", cmask[:, :], C["cmask128"][:, :], writes=[r["cmask"]])
        P.op("pool", lambda e: e.memset(onec[:, :], 1.0), writes=[r["onec"]])
        P.op("pool", lambda e: e.memset(negh[:, :], -0.5), writes=[r["negh"]])
        P.op("pool", lambda e: e.memset(epsc[:, :], 1e-6), writes=[r["eps"]])
        P.op("dve", lambda e: e.tensor_scalar(out=omm[:, :], in0=mu[:, :], scalar1=-1.0, scalar2=1.0, op0=ALU.mult, op1=ALU.add),
             reads=[r["mu"]], writes=[r["omm"]])
        P.op("dve", lambda e: e.tensor_scalar(out=omka[:, :], in0=ka[:, :], scalar1=-1.0, scalar2=1.0, op0=ALU.mult, op1=ALU.add),
             reads=[r["ka"]], writes=[r["omka"]])
        P.op("dve", lambda e: e.tensor_scalar(out=nw0[:, :], in0=w0[:, :], scalar1=-1.0, scalar2=None, op0=ALU.mult),
             reads=[r["w0"]], writes=[r["nw0"]])

        def load_mix(ut, rut, mt, rmt, row, ti, tb, tmp=tmp, rtmp=None):
            rtmp = rtmp if rtmp is not None else r["tmp_l"]
            t0 = tb * 512
            if tb == 0:
                P.op("pool", lambda e: e.memset(ut[:, 0:1], 0.0), writes=[rut])
                P.dma("sp", ut[:, 1:513], projT[row:row + 128, 0:512], writes=[rut])
            else:
                P.dma("sp", ut[:, :], projT[row:row + 128, t0 - 1:t0 + 512], writes=[rut])
            P.op("dve", lambda e: e.tensor_scalar(out=tmp[:, :], in0=ut[:, 1:513], scalar1=omm[:, ti:ti + 1], scalar2=None, op0=ALU.mult),
                 reads=[rut, r["omm"]], writes=[rtmp])
            P.op("dve", lambda e: e.scalar_tensor_tensor(out=mt[:, :], in0=ut[:, 0:512], scalar=mu[:, ti:ti + 1], in1=tmp[:, :],
                                                         op0=ALU.mult, op1=ALU.add),
                 reads=[rut, r["mu"], rtmp], writes=[rmt])
        def do_ct(ct, tb, s):
            t0 = tb * 512
            T_ = TS[s]
            rq = lambda n: r[f"{n}_{s}"]
            (e1, spt, e2, cw, ecw, cwm, ecwm, einv, av, kk0, sq, sd, rn, kkn, fac, k2, kka, prod) = [T_[n] for n in (
                "e1", "spt", "e2", "cw", "ecw", "cwm", "ecwm", "einv", "av", "kk0", "sq", "sd", "rn", "kkn", "fac", "k2", "kka", "prod")]
            wl_ps, a_ps, ss_ps, sb_ps = T_["wl_ps"], T_["a_ps"], T_["ss_ps"], T_["sb_ps"]
            for j in range(4):
                load_mix(T_[f"u{j}"], rq(f"u{j}"), T_[f"mx{j}"], rq(f"mx{j}"), row0 + j * 1024 + ct * 128, j * 8 + ct, tb, T_["tmp"], rq("tmp"))
            rm, km, vm, zm = T_['mx0'], T_['mx1'], T_['mx2'], T_['mx3']
            yield
            P.op("pe", lambda e, ct=ct: e.matmul(wl_ps[:, :], lw2[0:64, ct * 128:(ct + 1) * 128], th[0:64, :], start=True, stop=True),
                 reads=[r["lw2"], r["th"]], writes=[rq("wl_ps")])
            P.op("pe", lambda e, ct=ct: e.matmul(a_ps[:, :], lw2[64:128, ct * 128:(ct + 1) * 128], ml[64:128, :], start=True, stop=True),
                 reads=[r["lw2"], r["ml"]], writes=[rq("a_ps")])
            P.op("act", lambda e, ct=ct: e.activation(out=e1[:, :], in_=wl_ps[:, :], func=AF.Exp, bias=nw0[:, ct:ct + 1], scale=-1.0),
                 reads=[rq("wl_ps"), r["nw0"]], writes=[rq("e1")])
            P.op("act", lambda e: e.activation(out=spt[:, :], in_=e1[:, :], func=AF.Ln, bias=onec[:, 0:1], scale=1.0),
                 reads=[rq("e1"), r["onec"]], writes=[rq("spt")])
            P.op("act", lambda e: e.activation(out=e2[:, :], in_=spt[:, :], func=AF.Exp, bias=negh[:, 0:1], scale=-1.0),
                 reads=[rq("spt"), r["negh"]], writes=[rq("e2")])
            P.op("dve", lambda e: e.tensor_tensor_scan(out=cw[:, :], data0=cmask[:, :], data1=e2[:, :], initial=0.0,
                                                       op0=ALU.mult, op1=ALU.subtract),
                 reads=[r["cmask"], rq("e2")], writes=[rq("cw")])
            P.op("act", lambda e: e.activation(out=ecw[:, :], in_=cw[:, :], func=AF.Exp), reads=[rq("cw")], writes=[rq("ecw")])
            P.op("pool", lambda e: e.tensor_tensor(out=cwm[:, :], in0=cw[:, :], in1=e2[:, :], op=ALU.add),
                 reads=[rq("cw"), rq("e2")], writes=[rq("cwm")])
            P.op("act", lambda e: e.activation(out=ecwm[:, :], in_=cwm[:, :], func=AF.Exp), reads=[rq("cwm")], writes=[rq("ecwm")])
            P.op("act", lambda e: e.activation(out=einv[:, :], in_=cw[:, :], func=AF.Exp, scale=-1.0), reads=[rq("cw")], writes=[rq("einv")])
            yield
            P.op("act", lambda e, ct=ct: e.activation(out=av[:, :], in_=a_ps[:, :], func=AF.Sigmoid, bias=a0[:, ct:ct + 1], scale=1.0),
                 reads=[rq("a_ps"), r["a0"]], writes=[rq("av")])
            yield
            P.op("dve", lambda e, ct=ct: e.tensor_scalar(out=kk0[:, :], in0=km[:, :], scalar1=kk_[:, ct:ct + 1], scalar2=None, op0=ALU.mult),
                 reads=[rq("mx1"), r["kk_"]], writes=[rq("kk0")])
            P.op("pool", lambda e: e.tensor_tensor(out=sq[:, :], in0=kk0[:, :], in1=kk0[:, :], op=ALU.mult), reads=[rq("kk0")], writes=[rq("sq")])
            P.op("pe", lambda e: e.matmul(ss_ps[:, :], bones[:, :], sq[:, :], start=True, stop=True),
                 reads=[r["bones"], rq("sq")], writes=[rq("ss_ps")])
            yield
            P.op("act", lambda e: e.activation(out=sd[:, :], in_=ss_ps[:, :], func=AF.Sqrt, bias=epsc[:, 0:1], scale=1.0),
                 reads=[rq("ss_ps"), r["eps"]], writes=[rq("sd")])
            P.op("dve", lambda e: e.reciprocal(rn[:, :], sd[:, :]), reads=[rq("sd")], writes=[rq("rn")])
            P.op("dve", lambda e: e.tensor_tensor(out=kkn[:, :], in0=kk0[:, :], in1=rn[:, :], op=ALU.mult),
                 reads=[rq("kk0"), rq("rn")], writes=[rq("kkn")])
            P.op("dve", lambda e, ct=ct: e.tensor_scalar(out=fac[:, :], in0=av[:, :], scalar1=ka[:, ct:ct + 1], scalar2=omka[:, ct:ct + 1],
                                                         op0=ALU.mult, op1=ALU.add),
                 reads=[rq("av"), r["ka"], r["omka"]], writes=[rq("fac")])
            P.op("dve", lambda e: e.tensor_tensor(out=k2[:, :], in0=km[:, :], in1=fac[:, :], op=ALU.mult),
                 reads=[rq("mx1"), rq("fac")], writes=[rq("k2")])
            P.op("pool", lambda e: e.tensor_tensor(out=kka[:, :], in0=kkn[:, :], in1=av[:, :], op=ALU.mult),
                 reads=[rq("kkn"), rq("av")], writes=[rq("kka")])
            yield
            P.op("dve", lambda e, s=s: e.tensor_tensor(out=obf["rt"][s][:, :], in0=rm[:, :], in1=ecw[:, :], op=ALU.mult),
                 reads=[rq("mx0"), rq("ecw")], writes=[r[f"o_rt{s}"]])
            P.op("dve", lambda e, s=s: e.tensor_tensor(out=obf["kkt"][s][:, :], in0=kkn[:, :], in1=ecwm[:, :], op=ALU.mult),
                 reads=[rq("kkn"), rq("ecwm")], writes=[r[f"o_kkt{s}"]])
            P.op("dve", lambda e, s=s: e.tensor_tensor(out=obf["kh"][s][:, :], in0=k2[:, :], in1=einv[:, :], op=ALU.mult),
                 reads=[rq("k2"), rq("einv")], writes=[r[f"o_kh{s}"]])
            P.op("pool", lambda e, s=s: e.tensor_tensor(out=obf["kka"][s][:, :], in0=kka[:, :], in1=einv[:, :], op=ALU.mult),
                 reads=[rq("kka"), rq("einv")], writes=[r[f"o_kka{s}"]])
            P.op("pool", lambda e, s=s: e.tensor_copy(obf["v"][s][:, :], vm[:, :]), reads=[rq("mx2")], writes=[r[f"o_v{s}"]])
            P.op("act", lambda e, s=s: e.activation(out=of32["sz"][s][:, :], in_=zm[:, :], func=AF.Silu), reads=[rq("mx3")], writes=[r[f"o_sz{s}"]])
            yield
            P.op("pool", lambda e: e.tensor_tensor(out=prod[:, :], in0=rm[:, :], in1=k2[:, :], op=ALU.mult),
                 reads=[rq("mx0"), rq("k2")], writes=[rq("prod")])
            P.op("dve", lambda e, ct=ct: e.tensor_scalar(out=prod[:, :], in0=prod[:, :], scalar1=rk[:, ct:ct + 1], scalar2=None, op0=ALU.mult),
                 reads=[rq("prod"), r["rk"]], writes=[rq("prod")])
            P.op("pe", lambda e: e.matmul(sb_ps[:, :], bones[:, :], prod[:, :], start=True, stop=True),
                 reads=[r["bones"], rq("prod")], writes=[rq("sb_ps")])
            P.op("dve", lambda e, s=s: e.tensor_tensor(out=of32["bon"][s][:, :], in0=vm[:, :], in1=sb_ps[:, :], op=ALU.mult),
                 reads=[rq("mx2"), rq("sb_ps")], writes=[r[f"o_bon{s}"]])
            P.op("pool", lambda e, s=s: e.tensor_copy(pcs[s][:, :], ecw[:, 127:512:128]), reads=[rq("ecw")], writes=[r[f"pcs{s}"]])
            rows_ = slice(ct * 128, (ct + 1) * 128)
            for n, dst in (("rt", "rtT"), ("kkt", "kktT"), ("kh", "khT"), ("kka", "kkaT"), ("v", "rvT")):
                P.dma("pool", S[dst][rows_, t0:t0 + 512], obf[n][s][:, :], reads=[r[f"o_{n}{s}"]], writes=[r[dst]])
            for n, dst in (("bon", "bonT"), ("sz", "szT")):
                P.dma("pool", S[dst][rows_, t0:t0 + 512], of32[n][s][:, :], reads=[r[f"o_{n}{s}"]], writes=[r[dst]])
            P.dma("pool", S["pc_d"][ct, :, tb * 4:(tb + 1) * 4], pcs[s][:, :], reads=[r[f"pcs{s}"]], writes=[r["pc_d"]])

        cnt = 0
        for tb in range(NB):
            t0 = tb * 512
            load_mix(ul, r["ul"], ml, r["ml"], row0 + 4096, 32, tb)
            P.op("act", lambda e: e.activation(out=th[0:64, :], in_=ml[0:64, :], func=AF.Tanh), reads=[r["ml"]], writes=[r["th"]])
            for ct in range(0, 8, 2):
                zipper([do_ct(ct, tb, 0), do_ct(ct + 1, tb, 1)])
        P.barrier()
        P.flush()


def phase_rwkv_r1(P, nc, S, C, T):
    NCH = T // 128
    G = 4
    with contextlib.ExitStack() as st:
        def sb(name, shape, dt):
            return st.enter_context(nc.sbuf_tensor(_uid() + "r1_" + name, shape, dt))

        def pst(name, shape, dt):
            return st.enter_context(nc.psum_tensor(_uid() + "r1_" + name, shape, dt))
        r = defaultdict(Res)
        mls = sb("mls", [128, 128], F32); mus = sb("mus", [128, 128], F32); mui = sb("mui", [128, 128], F32); nmui = sb("nmui", [128, 128], F32)
        identf = sb("identf", [128, 128], F32); identb = sb("identb", [128, 128], BF16)
        tl = {n: [sb(f"{n}{i}", [128, 2, 128], BF16) for i in range(2)] for n in ("rt", "kkt", "kh", "kka", "v")}
        tz = {n: [sb(f"z{n}{i}", [128, 2, 2, 128], BF16) for i in range(2)] for n in ("kkt", "kh", "kka")}
        L = sb("L", [128, G, 128], BF16); LT = sb("LT", [128, G, 128], BF16)
        om = {n: [sb(f"{n}{i}", [128, G, 128], BF16) for i in range(2)] for n in ("akv", "bkv", "nbab", "tinv")}
        tk = {n: [sb(f"tk_{n}{i}", [128, 2, 128], BF16) for i in range(2)] for n in ("v", "kh", "kka")}
        W = {"identb": identb,
             "nL": [sb(f"nL{i}", [128, G, 128], BF16) for i in range(2)],
             "nLT": [sb(f"nLT{i}", [128, G, 128], BF16) for i in range(2)],
             "Pk": [sb(f"Pk{i}", [128, G, 128], BF16) for i in range(2)],
             "pa": pst("pa", [128, G, 128], F32), "pb": pst("pb", [128, G, 128], F32), "pp": pst("pp", [128, G, 128], F32)}
        s_ps = [pst(f"s_ps{i}", [128, G, 128], F32) for i in range(3)]
        tr_ps = pst("tr_ps", [128, 8, 128], BF16)
        for nm, t_, src in (("mls", mls, "mask_ls"), ("mus", mus, "mask_us"), ("mui", mui, "mask_ui"), ("identf", identf, "ident")):
            P.dma("sp", t_[:, :], C[src][:, :], writes=[r[nm]])
        P.op("pool", lambda e: e.tensor_copy(identb[:, :], identf[:, :]), reads=[r["identf"]], writes=[r["identb"]])
        P.op("dve", lambda e: e.tensor_scalar(out=nmui[:, :], in0=mui[:, :], scalar1=-1.0, scalar2=None, op0=ALU.mult),
             reads=[r["mui"]], writes=[r["nmui"]])

        def bcm(m):
            return m[:, :].unsqueeze(1).to_broadcast([128, G, 128])
        it = 0
        srcs = {"rt": "rtT", "kkt": "kktT", "kh": "khT", "kka": "kkaT", "v": "rvT"}
        for n in tz:
            for i in range(2):
                P.op("pool", lambda e, n=n, i=i: e.memset(tz[n][i][:, :, :, :].rearrange("p a q t -> p (a q t)"), 0.0), writes=[r[f"z{n}{i}"]])
        for c in range(NCH):
            t0 = c * 128
            for g in range(4):
                s = it % 2
                it += 1
                for n in tl:
                    P.dma("sp", tl[n][s][:, :, :], S[srcs[n]][g * 256:(g + 1) * 256, t0:t0 + 128].rearrange("(q p) t -> p q t", p=128),
                          writes=[r[f"{n}{s}"]])

                for n in tz:
                    srcv = S[srcs[n]][g * 256:(g + 1) * 256, t0:t0 + 128].rearrange("(q p) t -> p q t", p=128)
                    P.dma("sp", tz[n][s][0:64, 0, :, :], srcv[0:64], writes=[r[f"z{n}{s}"]])
                    P.dma("sp", tz[n][s][64:128, 1, :, :], srcv[64:128], writes=[r[f"z{n}{s}"]])

                def hz(n, h):
                    return tz[n][s][:, h % 2, h // 2, :]

                def hv(n, h):
                    return tl[n][s][:, h // 2, :]
                for h in range(G):
                    P.op("pe", lambda e, h=h, a=hz("kkt", h), b=hv("kka", h): e.matmul(s_ps[0][:, h, :], a, b, start=True, stop=True),
                         reads=[r[f"zkkt{s}"], r[f"kka{s}"]], writes=[r["s_ps0"]])
                for h in range(G):
                    P.op("pe", lambda e, h=h, a=hz("kka", h), b=hv("kkt", h): e.matmul(s_ps[1][:, h, :], a, b, start=True, stop=True),
                         reads=[r[f"zkka{s}"], r[f"kkt{s}"]], writes=[r["s_ps1"]])
                P.op("dve", lambda e: e.tensor_tensor(out=L[:, :, :], in0=s_ps[0][:, :, :], in1=bcm(mls), op=ALU.mult),
                     reads=[r["s_ps0"], r["mls"]], writes=[r["L"]])
                P.op("dve", lambda e: e.tensor_tensor(out=LT[:, :, :], in0=s_ps[1][:, :, :], in1=bcm(mus), op=ALU.mult),
                     reads=[r["s_ps1"], r["mus"]], writes=[r["LT"]])
                for h in range(G):
                    P.op("pe", lambda e, h=h, a=hz("kh", h), b=hv("kkt", h): e.matmul(s_ps[2][:, h, :], a, b, start=True, stop=True),
                         reads=[r[f"zkh{s}"], r[f"kkt{s}"]], writes=[r["s_ps2"]])
                P.op("dve", lambda e, s=s: e.tensor_tensor(out=om["akv"][s][:, :, :], in0=s_ps[2][:, :, :], in1=bcm(mus), op=ALU.mult),
                     reads=[r["s_ps2"], r["mus"]], writes=[r[f"akv{s}"]])
                for h in range(G):
                    P.op("pe", lambda e, h=h, a=hz("kh", h), b=hv("rt", h): e.matmul(s_ps[0][:, h, :], a, b, start=True, stop=True),
                         reads=[r[f"zkh{s}"], r[f"rt{s}"]], writes=[r["s_ps0"]])
                P.op("dve", lambda e, s=s: e.tensor_tensor(out=om["bkv"][s][:, :, :], in0=s_ps[0][:, :, :], in1=bcm(mui), op=ALU.mult),
                     reads=[r["s_ps0"], r["mui"]], writes=[r[f"bkv{s}"]])
                for h in range(G):
                    P.op("pe", lambda e, h=h, a=hz("kka", h), b=hv("rt", h): e.matmul(s_ps[1][:, h, :], a, b, start=True, stop=True),
                         reads=[r[f"zkka{s}"], r[f"rt{s}"]], writes=[r["s_ps1"]])
                P.op("dve", lambda e, s=s: e.tensor_tensor(out=om["nbab"][s][:, :, :], in0=s_ps[1][:, :, :], in1=bcm(mui), op=ALU.mult),
                     reads=[r["s_ps1"], r["mui"]], writes=[r[f"nbab{s}"]])
                Pt, rPt = neumann(P, nc, L, LT, r["L"], r["LT"], W, r)
                P.op("pool", lambda e, s=s, Pt=Pt: e.tensor_copy(om["tinv"][s][:, :, :], Pt[:, :, :]), reads=[rPt], writes=[r[f"tinv{s}"]])
                for n, dst in (("akv", "akvT_d"), ("bkv", "bkvT_d"), ("nbab", "nbabT_d"), ("tinv", "tinvT_d")):
                    P.dma("pool", S[dst][c, :, g * G:(g + 1) * G, :], om[n][s][:, :, :],
                          reads=[r[f"{n}{s}"]], writes=[r[dst]])
                for qi, n in enumerate(("v", "kh", "kka")):
                    for q in range(2):
                        P.op("pe", lambda e, qi=qi, q=q, n=n, s=s: e.transpose(tr_ps[:, qi * 2 + q, :], tl[n][s][:, q, :], identb[:, :]),
                             reads=[r[f"{n}{s}"], r["identb"]], writes=[r["tr_ps"]])
                for qi, (n, dst) in enumerate((("v", "vtk_d"), ("kh", "khtk_d"), ("kka", "kkatk_d"))):
                    P.op("act", lambda e, qi=qi, n=n, s=s: e.copy(tk[n][s][:, :, :], tr_ps[:, qi * 2:qi * 2 + 2, :]),
                         reads=[r["tr_ps"]], writes=[r[f"tk_{n}{s}"]])
                    P.dma("pool", S[dst][t0:t0 + 128, g * 256:(g + 1) * 256], tk[n][s][:, :, :].rearrange("p q d -> p (q d)"),
                          reads=[r[f"tk_{n}{s}"]], writes=[r[dst]])
        P.barrier()
        P.flush()


def phase_rwkv_r2(P, nc, S, C, yT, prm, T, kc0=8, eps=64e-5):
    NCH = T // 128
    H = 16
    with contextlib.ExitStack() as st:
        def sb(name, shape, dt):
            return st.enter_context(nc.sbuf_tensor(_uid() + "r2_" + name, shape, dt))

        def pst(name, shape, dt):
            return st.enter_context(nc.psum_tensor(_uid() + "r2_" + name, shape, dt))
        r = defaultdict(Res)
        identf = sb("identf", [128, 128], F32); identb = sb("identb", [128, 128], BF16)
        lnw = sb("lnw", [128, 8], F32); lnb = sb("lnb", [128, 8], F32); epsc = sb("epsc", [128, 1], F32); mh = sb("mh", [128, 8], F32)
        pc = sb("pc", [128, 8, NCH], F32)
        kkt = [sb(f"kkt{i}", [128, 2, 8, 128], BF16) for i in range(2)]
        rt = [sb(f"rt{i}", [128, 2, 8, 128], BF16) for i in range(2)]
        tkz = {n: [sb(f"tkz_{n}{i}", [128, 2, 8, 128], BF16) for i in range(2)] for n in ("kh", "kka")}
        mm = {n: [sb(f"{n}{i}", [128, H, 128], BF16) for i in range(2)] for n in ("akv", "bkv", "nbab", "tinv")}
        tk = {n: [sb(f"tk_{n}{i}", [128, 1024], BF16) for i in range(2)] for n in ("v",)}
        bon = [sb(f"bon{i}", [128, 8, 128], F32) for i in range(2)]
        szt = [sb(f"szt{i}", [128, 8, 128], F32) for i in range(2)]
        Tf = sb("Tf", [128, 8, 64], F32); Tb = sb("Tb", [128, 8, 64], BF16)
        yfin = [sb(f"yfin{i}", [128, 4, 128], BF16) for i in range(2)]
        TS = []
        for q in range(2):
            d = {"Ttmp": sb(f"Ttmp{q}", [128, 4, 64], F32), "rhs0": sb(f"rhs0{q}", [128, 8, 64], BF16), "nU": sb(f"nU{q}", [128, 8, 64], BF16),
                 "y_sb": sb(f"y_sb{q}", [128, 8, 64], F32), "ysq": sb(f"ysq{q}", [128, 8, 64], F32),
                 "yc": sb(f"yc{q}", [128, 8, 64], F32), "yn": sb(f"yn{q}", [128, 8, 64], BF16),
                 "t1": sb(f"t1{q}", [128, 4, 128], F32), "t2": sb(f"t2{q}", [128, 4, 128], F32)}
            for n in ("s1", "s2", "mean", "var", "sd", "rstd"):
                d[n] = sb(f"{n}{q}", [128, 8], F32)
            d["r0_ps"] = pst(f"r0_ps{q}", [128, 8, 64], F32)
            d["u_ps"] = d["r0_ps"]
            d["y_ps"] = pst(f"y_ps{q}", [128, 8, 64], F32)
            d["st_ps"] = pst(f"st_ps{q}", [128, 512], F32)[:, 0:256].rearrange("p (q e) -> p q e", e=64)
            d["tr_ps"] = pst(f"tr_ps{q}", [128, 1024], BF16)[:, 0:512].rearrange("p (q t) -> p q t", t=128)
            TS.append(d)
        P.dma("sp", identf[:, :], C["ident"][:, :], writes=[r["identf"]])
        P.dma("sp", lnw[:, :], prm["lnwT"][:, :], writes=[r["lnw"]])
        P.dma("sp", lnb[:, :], prm["lnbT"][:, :], writes=[r["lnb"]])
        for q in range(8):
            P.dma("sp", pc[:, q, :], S["pc_d"][q, :, :], writes=[r["pc"]])
        P.op("pool", lambda e: e.tensor_copy(identb[:, :], identf[:, :]), reads=[r["identf"]], writes=[r["identb"]])
        P.op("pool", lambda e: e.memset(epsc[:, :], eps), writes=[r["eps"]])
        P.op("pool", lambda e: e.memset(mh[:, :], -0.5), writes=[r["mh"]])
        P.op("pool", lambda e: e.memset(Tf[:, :, :].rearrange("p q e -> p (q e)"), 0.0), writes=[r["Tf"]])
        P.op("pool", lambda e: e.memset(Tb[:, :, :].rearrange("p q e -> p (q e)"), 0.0), writes=[r["Tb"]])
        def grp(c, gi, b, s):
            t0 = c * 128
            T_ = TS[b]
            (Ttmp, rhs0, nU, y_sb, ysq, yc, yn, t1, t2, s1, s2, mean, var, sd, rstd, r0_ps, u_ps, y_ps, st_ps, tr_ps) = [T_[n] for n in (
                "Ttmp", "rhs0", "nU", "y_sb", "ysq", "yc", "yn", "t1", "t2", "s1", "s2", "mean", "var", "sd", "rstd", "r0_ps", "u_ps", "y_ps", "st_ps", "tr_ps")]
            rT = r[f"Tf{gi}"]; rTb = r[f"Tb{gi}"]
            for h in range(8):
                hh = gi * 8 + h
                p_ = hh // 2
                ba = (hh % 2) * 64
                P.op("pe", lambda e, h=h, hh=hh, p_=p_, ba=ba, s=s: e.matmul(r0_ps[:, h, :], kkt[s][:, ba // 64, p_, :], Tb[:, p_, :],
                                                                          start=True, stop=False),
                     reads=[r[f"kkt{s}"], rTb, r["Tb"]], writes=[r["ru_ps" + str(b)]])
                P.op("pe", lambda e, h=h, hh=hh, s=s: e.matmul(r0_ps[:, h, :], mm["akv"][s][:, hh, :], tk["v"][s][:, hh * 64:(hh + 1) * 64],
                                                              start=False, stop=True),
                     reads=[r[f"akv{s}"], r[f"tk_v{s}"]], writes=[r["ru_ps" + str(b)]])
            yield
            P.op("act", lambda e: e.copy(rhs0[:, :, :], r0_ps[:, :, :]), reads=[r["ru_ps" + str(b)]], writes=[r["rhs0" + str(b)]])
            yield
            for h in range(8):
                hh = gi * 8 + h
                P.op("pe", lambda e, h=h, hh=hh, s=s: e.matmul(u_ps[:, h, :], mm["tinv"][s][:, hh, :], rhs0[:, h, :], start=True, stop=True),
                     reads=[r[f"tinv{s}"], r["rhs0" + str(b)]], writes=[r["ru_ps" + str(b)]])
            yield
            P.op("dve", lambda e: e.tensor_scalar(out=nU[:, :, :], in0=u_ps[:, :, :], scalar1=-1.0, scalar2=None, op0=ALU.mult),
                 reads=[r["ru_ps" + str(b)]], writes=[r["nU" + str(b)]])
            for h in range(8):
                hh = gi * 8 + h
                p_ = hh // 2
                ba = (hh % 2) * 64
                P.op("pe", lambda e, h=h, p_=p_, ba=ba, s=s: e.matmul(y_ps[:, h, :], rt[s][:, ba // 64, p_, :], Tb[:, p_, :],
                                                                   start=True, stop=False),
                     reads=[r[f"rt{s}"], rTb, r["Tb"]], writes=[r["y_ps" + str(b)]])
                P.op("pe", lambda e, h=h, hh=hh, s=s: e.matmul(y_ps[:, h, :], mm["bkv"][s][:, hh, :], tk["v"][s][:, hh * 64:(hh + 1) * 64],
                                                              start=False, stop=False),
                     reads=[r[f"bkv{s}"], r[f"tk_v{s}"]], writes=[r["y_ps" + str(b)]])
                P.op("pe", lambda e, h=h, hh=hh, s=s: e.matmul(y_ps[:, h, :], mm["nbab"][s][:, hh, :], nU[:, h, :], start=False, stop=True),
                     reads=[r[f"nbab{s}"], r["nU" + str(b)]], writes=[r["y_ps" + str(b)]])
            yield
            for pl in range(4):
                q = gi * 4 + pl
                seq = [("kh", 0, "v"), ("kh", 1, "v"), ("kka", 0, "u"), ("kka", 1, "u")]
                for i_, (n, a, rk_) in enumerate(seq):
                    hh = 2 * q + a
                    if rk_ == "v":
                        P.op("pe", lambda e, pl=pl, q=q, a=a, n=n, hh=hh, s=s, i_=i_: e.matmul(st_ps[:, pl, :], tkz[n][s][:, a, q, :],
                                                                                           tk["v"][s][:, hh * 64:(hh + 1) * 64],
                                                                                           start=(i_ == 0), stop=(i_ == 3)),
                             reads=[r[f"tkz_{n}{s}"], r[f"tk_v{s}"]], writes=[r["st_ps" + str(b)]])
                    else:
                        P.op("pe", lambda e, pl=pl, q=q, a=a, n=n, hh=hh, s=s, i_=i_, gi=gi: e.matmul(st_ps[:, pl, :], tkz[n][s][:, a, q, :],
                                                                                                  nU[:, hh - gi * 8, :],
                                                                                                  start=(i_ == 0), stop=(i_ == 3)),
                             reads=[r[f"tkz_{n}{s}"], r["nU" + str(b)]], writes=[r["st_ps" + str(b)]])
            yield
            qs = slice(gi * 4, gi * 4 + 4)
            P.op("dve", lambda e, qs=qs: e.tensor_tensor(out=Ttmp[:, :, :], in0=st_ps[:, :, :], in1=Tf[:, qs, :], op=ALU.add),
                 reads=[r["st_ps" + str(b)], rT, r["Tf"]], writes=[r["Ttmp" + str(b)]])
            P.op("dve", lambda e, qs=qs, c=c: e.tensor_tensor(out=Tf[:, qs, :], in0=Ttmp[:, :, :],
                                                              in1=pc[:, qs, c:c + 1].to_broadcast([128, 4, 64]), op=ALU.mult),
                 reads=[r["Ttmp" + str(b)], r["pc"]], writes=[rT])
            P.op("act", lambda e, qs=qs: e.copy(Tb[:, qs, :], Tf[:, qs, :]), reads=[rT], writes=[rTb])
            yield
            P.op("act", lambda e: e.copy(y_sb[:, :, :], y_ps[:, :, :]), reads=[r["y_ps" + str(b)]], writes=[r["y_sb" + str(b)]])
            P.op("dve", lambda e: e.tensor_reduce(out=s1[:, :], in_=y_sb[:, :, :], axis=AX.X, op=ALU.add), reads=[r["y_sb" + str(b)]], writes=[r["s1" + str(b)]])
            P.op("pool", lambda e: e.tensor_tensor(out=ysq[:, :, :], in0=y_sb[:, :, :], in1=y_sb[:, :, :], op=ALU.mult),
                 reads=[r["y_sb" + str(b)]], writes=[r["ysq" + str(b)]])
            yield
            P.op("dve", lambda e: e.tensor_reduce(out=s2[:, :], in_=ysq[:, :, :], axis=AX.X, op=ALU.add), reads=[r["ysq" + str(b)]], writes=[r["s2" + str(b)]])
            P.op("dve", lambda e: e.tensor_scalar(out=mean[:, :], in0=s1[:, :], scalar1=1.0 / 64, scalar2=None, op0=ALU.mult),
                 reads=[r["s1" + str(b)]], writes=[r["mean" + str(b)]])
            P.op("dve", lambda e: e.tensor_tensor(out=var[:, :], in0=mean[:, :], in1=mean[:, :], op=ALU.mult), reads=[r["mean" + str(b)]], writes=[r["var" + str(b)]])
            P.op("dve", lambda e: e.scalar_tensor_tensor(out=var[:, :], in0=s2[:, :], scalar=1.0 / 64, in1=var[:, :],
                                                         op0=ALU.mult, op1=ALU.subtract),
                 reads=[r["s2" + str(b)], r["var" + str(b)]], writes=[r["var" + str(b)]])
            P.op("dve", lambda e: e.tensor_scalar(out=sd[:, :], in0=var[:, :], scalar1=1.0, scalar2=eps, op0=ALU.mult, op1=ALU.add),
                 reads=[r["var" + str(b)]], writes=[r["sd" + str(b)]])
            yield
            P.op("pool", lambda e: e.tensor_tensor(out=rstd[:, :], in0=sd[:, :], in1=mh[:, :], op=ALU.pow),
                 reads=[r["sd" + str(b)], r["mh"]], writes=[r["rstd" + str(b)]])
            P.op("dve", lambda e: e.tensor_tensor(out=yc[:, :, :], in0=y_sb[:, :, :], in1=mean[:, :].unsqueeze(2).to_broadcast([128, 8, 64]),
                                                  op=ALU.subtract),
                 reads=[r["y_sb" + str(b)], r["mean" + str(b)]], writes=[r["yc" + str(b)]])
            P.op("dve", lambda e: e.tensor_tensor(out=yn[:, :, :], in0=yc[:, :, :], in1=rstd[:, :].unsqueeze(2).to_broadcast([128, 8, 64]),
                                                  op=ALU.mult),
                 reads=[r["yc" + str(b)], r["rstd" + str(b)]], writes=[r["yn" + str(b)]])
            yield
            ynp = yn[:, :, :].rearrange("p (q a) e -> p q (a e)", a=2)
            for q in range(4):
                P.op("pe", lambda e, q=q: e.transpose(tr_ps[:, q, :], ynp[:, q, :], identb[:, :]),
                     reads=[r["yn" + str(b)], r["identb"]], writes=[r["tr_ps" + str(b)]])
            yield
            P.op("dve", lambda e, qs=qs: e.tensor_tensor(out=t1[:, :, :], in0=tr_ps[:, :, :],
                                                         in1=lnw[:, qs].unsqueeze(2).to_broadcast([128, 4, 128]), op=ALU.mult),
                 reads=[r["tr_ps" + str(b)], r["lnw"]], writes=[r["t1" + str(b)]])
            P.op("pool", lambda e, qs=qs: e.tensor_tensor(out=t2[:, :, :], in0=t1[:, :, :],
                                                          in1=lnb[:, qs].unsqueeze(2).to_broadcast([128, 4, 128]), op=ALU.add),
                 reads=[r["t1" + str(b)], r["lnb"]], writes=[r["t2" + str(b)]])
            P.op("pool", lambda e, qs=qs, s=s: e.tensor_tensor(out=t1[:, :, :], in0=t2[:, :, :], in1=bon[s][:, qs, :], op=ALU.add),
                 reads=[r["t2" + str(b)], r[f"bon{s}"]], writes=[r["t1" + str(b)]])
            P.op("dve", lambda e, qs=qs, s=s, b=b: e.tensor_tensor(out=yfin[b][:, :, :], in0=t1[:, :, :], in1=szt[s][:, qs, :], op=ALU.mult),
                 reads=[r["t1" + str(b)], r[f"szt{s}"]], writes=[r[f"yfin{b}"]])
            P.dma("pool", yT[kc0 + gi * 4:kc0 + gi * 4 + 4, :, t0:t0 + 128].rearrange("k p t -> p k t"), yfin[b][:, :, :],
                  reads=[r[f"yfin{b}"]], writes=[r["y_out"]])

        it = 0
        for i in range(2):
            P.op("pool", lambda e, i=i: e.memset(kkt[i][:, :, :, :].rearrange("p a q t -> p (a q t)"), 0.0), writes=[r[f"kkt{i}"]])
            P.op("pool", lambda e, i=i: e.memset(rt[i][:, :, :, :].rearrange("p a q t -> p (a q t)"), 0.0), writes=[r[f"rt{i}"]])
            for n in tkz:
                P.op("pool", lambda e, i=i, n=n: e.memset(tkz[n][i][:, :, :, :].rearrange("p a q t -> p (a q t)"), 0.0), writes=[r[f"tkz_{n}{i}"]])
        for c in range(NCH):
            t0 = c * 128
            s = c % 2
            for tl_, nm, src in ((kkt, "kkt", "kktT"), (rt, "rt", "rtT")):
                srcv = S[src][0:1024, t0:t0 + 128].rearrange("(q p) t -> p q t", p=128)
                P.dma("sp", tl_[s][0:64, 0, :, :], srcv[0:64], writes=[r[f"{nm}{s}"]])
                P.dma("sp", tl_[s][64:128, 1, :, :], srcv[64:128], writes=[r[f"{nm}{s}"]])
            for n, src in (("kh", "khtk_d"), ("kka", "kkatk_d")):
                srcv = S[src][t0:t0 + 128, :].rearrange("p (q a d) -> p q a d", a=2, d=64)
                for a in range(2):
                    P.dma("sp", tkz[n][s][:, a, :, a * 64:(a + 1) * 64], srcv[:, :, a, :], writes=[r[f"tkz_{n}{s}"]])
            for q0 in range(0, 8, 4):
                P.dma("sp", bon[s][:, q0:q0 + 4, :], S["bonT"][q0 * 128:(q0 + 4) * 128, t0:t0 + 128].rearrange("(q p) t -> p q t", p=128),
                      writes=[r[f"bon{s}"]])
                P.dma("sp", szt[s][:, q0:q0 + 4, :], S["szT"][q0 * 128:(q0 + 4) * 128, t0:t0 + 128].rearrange("(q p) t -> p q t", p=128),
                      writes=[r[f"szt{s}"]])
            for n, dst in (("akv", "akvT_d"), ("bkv", "bkvT_d"), ("nbab", "nbabT_d"), ("tinv", "tinvT_d")):
                for h0 in range(0, 16, 4):
                    P.dma("sp", mm[n][s][:, h0:h0 + 4, :], S[dst][c, :, h0:h0 + 4, :], writes=[r[f"{n}{s}"]])
            for n, dst in (("v", "vtk_d"),):
                P.dma("sp", tk[n][s][:, :], S[dst][t0:t0 + 128, :], writes=[r[f"tk_{n}{s}"]])
            zipper([grp(c, 0, 0, s), grp(c, 1, 1, s)])
        P.barrier()
        P.flush()


D = 4096
KC = 32
EPS = 1e-6
TWO_PI = 2.0 * math.pi
C1 = 6.28125
C2 = float(np.float32(TWO_PI - C1))


def dense_consts():
    c = {}
    j = np.arange(64, dtype=np.float32)
    invf = (np.float32(10000.0) ** (-(j / np.float32(64.0)))).astype(np.float32)
    c["invf"] = np.concatenate([invf, invf]).reshape(128, 1).astype(np.float32)
    c["sgn"] = np.concatenate([-np.ones(64), np.ones(64)]).reshape(128, 1).astype(np.float32)
    return c


def phase_rope_tables(P, nc, pos_dram, C, cosT, sinT, T):
    with contextlib.ExitStack() as st:
        def sb(name, shape, dt):
            return st.enter_context(nc.sbuf_tensor(_uid() + "rp_" + name, shape, dt))
        r = defaultdict(Res)
        invf = sb("invf", [128, 1], F32); sgn = sb("sgn", [128, 1], F32)
        pi_ = sb("pi", [128, 512], I32)
        ang = sb("ang", [128, 512], F32); kf = sb("kf", [128, 512], F32); kr = sb("kr", [128, 512], F32)
        rr = sb("rr", [128, 512], F32); rc = sb("rc", [128, 512], F32); m = sb("m", [128, 512], F32)
        so = [sb(f"so{i}", [128, 512], F32) for i in range(2)]
        co = [sb(f"co{i}", [128, 512], F32) for i in range(2)]
        P.dma("sp", invf[:, :], C["invf"][:, :], writes=[r["invf"]])
        P.dma("sp", sgn[:, :], C["sgn"][:, :], writes=[r["sgn"]])
        MAGIC = 12582912.0
        for tb in range(T // 512):
            s = tb % 2
            t0 = tb * 512
            P.dma("sp", pi_[:, :], pos_dram[:, t0:t0 + 512], writes=[r["pi"]])
            P.op("dve", lambda e: e.tensor_copy(ang[:, :], pi_[:, :]), reads=[r["pi"]], writes=[r["ang"]])
            P.op("dve", lambda e: e.tensor_scalar(out=ang[:, :], in0=ang[:, :], scalar1=invf[:, 0:1], scalar2=None, op0=ALU.mult),
                 reads=[r["ang"], r["invf"]], writes=[r["ang"]])
            P.op("dve", lambda e: e.tensor_scalar(out=kf[:, :], in0=ang[:, :], scalar1=1.0 / TWO_PI, scalar2=None, op0=ALU.mult),
                 reads=[r["ang"]], writes=[r["kf"]])
            P.op("dve", lambda e: e.tensor_scalar(out=kr[:, :], in0=kf[:, :], scalar1=MAGIC, scalar2=None, op0=ALU.add),
                 reads=[r["kf"]], writes=[r["kr"]])
            P.op("dve", lambda e: e.tensor_scalar(out=kf[:, :], in0=kr[:, :], scalar1=MAGIC, scalar2=None, op0=ALU.subtract),
                 reads=[r["kr"]], writes=[r["kf"]])
            P.op("dve", lambda e: e.scalar_tensor_tensor(out=rr[:, :], in0=kf[:, :], scalar=-C1, in1=ang[:, :], op0=ALU.mult, op1=ALU.add),
                 reads=[r["kf"], r["ang"]], writes=[r["rr"]])
            P.op("dve", lambda e: e.scalar_tensor_tensor(out=rr[:, :], in0=kf[:, :], scalar=-C2, in1=rr[:, :], op0=ALU.mult, op1=ALU.add),
                 reads=[r["kf"], r["rr"]], writes=[r["rr"]])
            P.op("dve", lambda e: e.tensor_scalar(out=rr[:, :], in0=rr[:, :], scalar1=math.pi, scalar2=-math.pi, op0=ALU.min, op1=ALU.max),
                 reads=[r["rr"]], writes=[r["rr"]])
            P.op("dve", lambda e: e.tensor_scalar(out=m[:, :], in0=rr[:, :], scalar1=math.pi / 2, scalar2=-TWO_PI, op0=ALU.is_gt, op1=ALU.mult),
                 reads=[r["rr"]], writes=[r["m"]])
            P.op("dve", lambda e: e.scalar_tensor_tensor(out=rc[:, :], in0=rr[:, :], scalar=math.pi / 2, in1=m[:, :], op0=ALU.add, op1=ALU.add),
                 reads=[r["rr"], r["m"]], writes=[r["rc"]])
            P.op("dve", lambda e: e.tensor_scalar(out=rc[:, :], in0=rc[:, :], scalar1=math.pi, scalar2=-math.pi, op0=ALU.min, op1=ALU.max),
                 reads=[r["rc"]], writes=[r["rc"]])
            P.op("act", lambda e, s=s: e.activation(out=so[s][:, :], in_=rr[:, :], func=AF.Sin), reads=[r["rr"]], writes=[r[f"so{s}"]])
            P.op("act", lambda e, s=s: e.activation(out=co[s][:, :], in_=rc[:, :], func=AF.Sin), reads=[r["rc"]], writes=[r[f"co{s}"]])
            P.op("dve", lambda e, s=s: e.tensor_scalar(out=so[s][:, :], in0=so[s][:, :], scalar1=sgn[:, 0:1], scalar2=None, op0=ALU.mult),
                 reads=[r[f"so{s}"], r["sgn"]], writes=[r[f"so{s}"]])
            P.dma("pool", sinT[:, t0:t0 + 512], so[s][:, :], reads=[r[f"so{s}"]], writes=[r["sinT"]])
            P.dma("pool", cosT[:, t0:t0 + 512], co[s][:, :], reads=[r[f"co{s}"]], writes=[r["cosT"]])
        P.barrier()
        P.flush()


def phase_norm(P, nc, x_dram, normw_bc_dram, hT_dram, T, out_dram=None):
    NT = T // 128
    with contextlib.ExitStack() as st:
        def sb(name, shape, dt):
            return st.enter_context(nc.sbuf_tensor(_uid() + "pn_" + name, shape, dt))
        r = defaultdict(Res)
        xt = [sb(f"x{i}", [128, D], F32) for i in range(2)]
        junk = sb("junk", [128, D], BF16)
        nw = sb("nw", [128, D], F32)
        ss = sb("ss", [128, 2], F32); sd = sb("sd", [128, 2], F32); rs = sb("rs", [128, 2], F32)
        epsc = sb("eps", [128, 1], F32)
        P.dma("sp", nw[:, :], normw_bc_dram[:, :], writes=[r["nw"]])
        P.op("pool", lambda e: e.memset(epsc[:, :], EPS), writes=[r["eps"]])
        if out_dram is None:
            hb = [sb(f"h{i}", [128, D], BF16) for i in range(2)]
            ident = sb("id", [128, 128], BF16)
            stg = [sb(f"stg{i}", [128, KC, 256], BF16) for i in range(2)]
            tp = [st.enter_context(nc.psum_tensor(_uid() + f"pn_tp{i}", [128, 1024], BF16)) for i in range(4)]
            P.op("pool", lambda e: e.memset(ident[:, :], 0.0), writes=[r["id"]])
            P.op("pool", lambda e: e.affine_select(out=ident[:, :], in_=ident[:, :], pattern=[[-1, 128]],
                                                   compare_op=ALU.not_equal, fill=1.0, base=0, channel_multiplier=1),
                 reads=[r["id"]], writes=[r["id"]])
        else:
            of = [sb(f"of{i}", [128, D], F32) for i in range(2)]
        for tt in range(NT):
            s = tt % 2
            P.dma("sp", xt[s][:, :], x_dram[tt * 128:(tt + 1) * 128, :], writes=[r[f"xt{s}"]])
            P.op("dve", lambda e, s=s: e.scalar_tensor_tensor(out=junk[:, :], in0=xt[s][:, :], scalar=1.0, in1=xt[s][:, :],
                                                              op0=ALU.mult, op1=ALU.mult, accum_out=ss[:, s:s + 1]),
                 reads=[r[f"xt{s}"]], writes=[r["junk"], r[f"ss{s}"]])
            P.op("act", lambda e, s=s: e.activation(out=sd[:, s:s + 1], in_=ss[:, s:s + 1], func=AF.Sqrt,
                                                    bias=epsc[:, 0:1], scale=1.0 / D),
                 reads=[r[f"ss{s}"], r["eps"]], writes=[r[f"sd{s}"]])
            P.op("dve", lambda e, s=s: e.reciprocal(rs[:, s:s + 1], sd[:, s:s + 1]), reads=[r[f"sd{s}"]], writes=[r[f"rs{s}"]])
            if out_dram is not None:
                P.op("dve", lambda e, s=s: e.scalar_tensor_tensor(out=of[s][:, :], in0=xt[s][:, :], scalar=rs[:, s:s + 1],
                                                                  in1=nw[:, :], op0=ALU.mult, op1=ALU.mult),
                     reads=[r[f"xt{s}"], r[f"rs{s}"], r["nw"]], writes=[r[f"of{s}"]])
                P.dma("pool", out_dram[tt * 128:(tt + 1) * 128, :], of[s][:, :], reads=[r[f"of{s}"]], writes=[r["out"]])
                continue
            P.op("dve", lambda e, s=s: e.scalar_tensor_tensor(out=hb[s][:, :], in0=xt[s][:, :], scalar=rs[:, s:s + 1],
                                                              in1=nw[:, :], op0=ALU.mult, op1=ALU.mult),
                 reads=[r[f"xt{s}"], r[f"rs{s}"], r["nw"]], writes=[r[f"hb{s}"]])
            sg = (tt // 2) % 2
            off = (tt % 2) * 128
            for b in range(4):
                for j in range(8):
                    kc = b * 8 + j
                    P.op("pe", lambda e, s=s, b=b, j=j, kc=kc: e.transpose(tp[b][:, j * 128:(j + 1) * 128],
                                                                         hb[s][:, kc * 128:(kc + 1) * 128], ident[:, :]),
                         reads=[r[f"hb{s}"], r["id"]], writes=[r[f"tp{b}"]])
                P.op("act", lambda e, b=b, sg=sg, off=off: e.copy(
                    stg[sg][:, b * 8:(b + 1) * 8, off:off + 128],
                    tp[b][:, :].rearrange("p (j t) -> p j t", j=8)),
                     reads=[r[f"tp{b}"]], writes=[r[f"stg{sg}"]])
            if tt % 2 == 1:
                t0 = (tt - 1) * 128
                P.dma("pool", hT_dram[:, :, t0:t0 + 256].rearrange("k p t -> p k t"), stg[sg][:, :, :],
                      reads=[r[f"stg{sg}"]], writes=[r["hT_out"]])
        P.barrier()
        P.flush()


def phase_proj(P, nc, hT_dram, w_tiles_dram, projT_dram, cosT, sinT, T, NCT, n_rope=16, TH=2048):
    TH = min(TH, T)
    NTB = TH // 512
    with contextlib.ExitStack() as st:
        def sb(name, shape, dt):
            return st.enter_context(nc.sbuf_tensor(_uid() + "pp_" + name, shape, dt))
        r = defaultdict(Res)
        hT = sb("hT", [128, KC, TH], BF16)
        wf = [sb(f"wf{i}", [128, KC * 128], F32) for i in range(2)]
        wb = [sb(f"wb{i}", [128, KC, 128], BF16) for i in range(2)]
        wbr = sb("wbr", [128, KC, 128], BF16)
        cs_ = [sb(f"cs{i}", [128, 512], F32) for i in range(2)]
        sn_ = [sb(f"sn{i}", [128, 512], F32) for i in range(2)]
        t1 = sb("t1", [128, 512], F32); t2 = sb("t2", [128, 512], F32)
        ob = [sb(f"ob{i}", [128, 512], F32) for i in range(4)]
        ps = [[st.enter_context(nc.psum_tensor(_uid() + f"pp_ps{a}_{i}", [128, 512], F32)) for i in range(NTB)] for a in range(2)]
        cnt = 0
        rc = 0
        for t0 in range(0, T, TH):
            for kc in range(KC):
                P.dma("sp", hT[:, kc, :], hT_dram[kc, :, t0:t0 + TH], writes=[r["hT"]])
            def prep(ct):
                s = ct % 2
                P.dma("sp", wf[s][:, :], w_tiles_dram[ct, :, :], writes=[r[f"wf{s}"]])
                if ct % 2 == 0:
                    P.op("act", lambda e, s=s: e.copy(wb[s][:, :, :].rearrange("p k c -> p (k c)"), wf[s][:, :]),
                         reads=[r[f"wf{s}"]], writes=[r[f"wb{s}"]])
                else:
                    P.op("dve", lambda e, s=s: e.tensor_copy(wb[s][:, :, :].rearrange("p k c -> p (k c)"), wf[s][:, :]),
                         reads=[r[f"wf{s}"]], writes=[r[f"wb{s}"]])
            prep(0)
            for ct in range(NCT):
                s = ct % 2
                rope = ct < n_rope
                if ct + 1 < NCT:
                    prep(ct + 1)
                if rope:
                    P.op("pool", lambda e, s=s: e.tensor_copy(wbr[:, :, 0:64], wb[s][:, :, 64:128]), reads=[r[f"wb{s}"]], writes=[r["wbr"]])
                    P.op("pool", lambda e, s=s: e.tensor_copy(wbr[:, :, 64:128], wb[s][:, :, 0:64]), reads=[r[f"wb{s}"]], writes=[r["wbr"]])
                for kc in range(KC):
                    for tb in range(NTB):
                        P.op("pe", lambda e, s=s, kc=kc, tb=tb: e.matmul(ps[s][tb][:, :], wb[s][:, kc, :], hT[:, kc, tb * 512:(tb + 1) * 512],
                                                                        start=(kc == 0), stop=(kc == KC - 1)),
                             reads=[r[f"wb{s}"], r["hT"]], writes=[r[f"ps{s}_{tb}"]])
                if rope:
                    for kc in range(KC):
                        for tb in range(NTB):
                            P.op("pe", lambda e, s=s, kc=kc, tb=tb: e.matmul(ps[1 - s][tb][:, :], wbr[:, kc, :], hT[:, kc, tb * 512:(tb + 1) * 512],
                                                                            start=(kc == 0), stop=(kc == KC - 1)),
                                 reads=[r["wbr"], r["hT"]], writes=[r[f"ps{1 - s}_{tb}"]])
                for tb in range(NTB):
                    b = cnt % 4
                    cnt += 1
                    tg = t0 + tb * 512
                    if rope:
                        q = rc % 2
                        rc += 1
                        P.dma("sp", cs_[q][:, :], cosT[:, tg:tg + 512], writes=[r[f"cs{q}"]])
                        P.dma("sp", sn_[q][:, :], sinT[:, tg:tg + 512], writes=[r[f"sn{q}"]])
                        P.op("dve", lambda e, s=s, tb=tb, q=q: e.tensor_tensor(out=t1[:, :], in0=ps[s][tb][:, :], in1=cs_[q][:, :], op=ALU.mult),
                             reads=[r[f"ps{s}_{tb}"], r[f"cs{q}"]], writes=[r["t1"]])
                        P.op("dve", lambda e, s=s, tb=tb, q=q: e.tensor_tensor(out=t2[:, :], in0=ps[1 - s][tb][:, :], in1=sn_[q][:, :], op=ALU.mult),
                             reads=[r[f"ps{1 - s}_{tb}"], r[f"sn{q}"]], writes=[r["t2"]])
                        P.op("pool", lambda e, b=b: e.tensor_tensor(out=ob[b][:, :], in0=t1[:, :], in1=t2[:, :], op=ALU.add),
                             reads=[r["t1"], r["t2"]], writes=[r[f"ob{b}"]])
                    else:
                        if cnt % 2 == 0:
                            P.op("dve", lambda e, b=b, s=s, tb=tb: e.tensor_copy(ob[b][:, :], ps[s][tb][:, :]), reads=[r[f"ps{s}_{tb}"]], writes=[r[f"ob{b}"]])
                        else:
                            P.op("act", lambda e, b=b, s=s, tb=tb: e.copy(ob[b][:, :], ps[s][tb][:, :]), reads=[r[f"ps{s}_{tb}"]], writes=[r[f"ob{b}"]])
                    pd_, pr_ = projT_dram(ct)
                    P.dma("pool", pd_[pr_:pr_ + 128, tg:tg + 512], ob[b][:, :], reads=[r[f"ob{b}"]], writes=[r["out"]])
        P.barrier()
        P.flush()


def _load_wblk(P, r, wf, wb, s, w_dram, cb, wcnt):
    for q4 in range(4):
        ws = wcnt[0] % 2
        wcnt[0] += 1
        P.dma("sp", wf[ws][:, :], w_dram[cb * 4 + q4, :, :], writes=[r[f"wf{ws}"]])
        src = wf[ws][:, :].rearrange("p (k c) -> p k c", c=128)
        if q4 % 2 == 0:
            P.op("act", lambda e, s=s, q4=q4, src=src: e.copy(wb[s][:, :, q4 * 128:(q4 + 1) * 128], src),
                 reads=[r[f"wf{ws}"]], writes=[r[f"wb{s}"]])
        else:
            P.op("dve", lambda e, s=s, q4=q4, src=src: e.tensor_copy(wb[s][:, :, q4 * 128:(q4 + 1) * 128], src),
                 reads=[r[f"wf{ws}"]], writes=[r[f"wb{s}"]])


def phase_out(P, nc, yT_dram, w_blk_dram, x_dram, x1_dram, T, TQ=1024):
    NB = D // 512
    TQ = min(TQ, T)
    with contextlib.ExitStack() as st:
        def sb(name, shape, dt):
            return st.enter_context(nc.sbuf_tensor(_uid() + "po_" + name, shape, dt))
        r = defaultdict(Res)
        yT = sb("yT", [128, KC, TQ], BF16)
        wf = [sb(f"wf{i}", [128, KC * 128], F32) for i in range(2)]
        wb = [sb(f"wb{i}", [128, KC, 512], BF16) for i in range(2)]
        xt = [sb(f"xt{i}", [128, 512], F32) for i in range(4)]
        ot = [sb(f"ot{i}", [128, 512], F32) for i in range(4)]
        ps = [st.enter_context(nc.psum_tensor(_uid() + f"po_ps{i}", [128, 512], F32)) for i in range(4)]
        cnt = 0
        wcnt = [0]
        for t0 in range(0, T, TQ):
            for kc in range(KC):
                P.dma("sp", yT[:, kc, :], yT_dram[kc, :, t0:t0 + TQ], writes=[r["yT"]])
            _load_wblk(P, r, wf, wb, 0, w_blk_dram, 0, wcnt)
            for cb in range(NB):
                s = cb % 2
                if cb + 1 < NB:
                    _load_wblk(P, r, wf, wb, 1 - s, w_blk_dram, cb + 1, wcnt)
                for tt in range(TQ // 128):
                    b = cnt % 4
                    cnt += 1
                    tok = t0 + tt * 128
                    P.dma("sp", xt[b][:, :], x_dram[tok:tok + 128, cb * 512:(cb + 1) * 512], writes=[r[f"xt{b}"]])
                    for kc in range(KC):
                        P.op("pe", lambda e, s=s, b=b, kc=kc, tt=tt: e.matmul(ps[b][:, :], yT[:, kc, tt * 128:(tt + 1) * 128], wb[s][:, kc, :],
                                                                             start=(kc == 0), stop=(kc == KC - 1)),
                             reads=[r[f"wb{s}"], r["yT"]], writes=[r[f"ps{b}"]])
                    P.op("dve", lambda e, b=b: e.tensor_tensor(out=ot[b][:, :], in0=ps[b][:, :], in1=xt[b][:, :], op=ALU.add),
                         reads=[r[f"ps{b}"], r[f"xt{b}"]], writes=[r[f"ot{b}"]])
                    P.dma("pool", x1_dram[tok:tok + 128, cb * 512:(cb + 1) * 512], ot[b][:, :], reads=[r[f"ot{b}"]], writes=[r["out"]])
        P.barrier()
        P.flush()


def phase_gate(P, nc, h2T_dram, wg_blk_dram, p_dram, wple_dram, x1_dram, x2_dram, T, TQ=1024):
    NB = D // 512
    TQ = min(TQ, T)
    with contextlib.ExitStack() as st:
        def sb(name, shape, dt):
            return st.enter_context(nc.sbuf_tensor(_uid() + "pg_" + name, shape, dt))
        r = defaultdict(Res)
        hT = sb("hT", [128, KC, TQ], BF16)
        wf = [sb(f"wf{i}", [128, KC * 128], F32) for i in range(2)]
        wb = [sb(f"wb{i}", [128, KC, 512], BF16) for i in range(2)]
        wpb = sb("wpb", [128, 2, D], BF16)
        ident = sb("ident", [128, 128], BF16)
        pt = [sb(f"pt{i}", [128, 256], F32) for i in range(2)]
        pb = [sb(f"pb{i}", [128, 256], BF16) for i in range(2)]
        pT = sb("pT", [128, 2, TQ], BF16)
        xt = [sb(f"xt{i}", [128, 512], F32) for i in range(2)]
        gt = sb("gt", [128, 512], F32)
        tm = sb("tm", [128, 512], F32)
        ot = [sb(f"ot{i}", [128, 512], F32) for i in range(2)]
        ps = [st.enter_context(nc.psum_tensor(_uid() + f"pg_ps{i}", [128, 512], F32)) for i in range(3)]
        pp = [st.enter_context(nc.psum_tensor(_uid() + f"pg_pp{i}", [128, 512], F32)) for i in range(3)]
        tp = st.enter_context(nc.psum_tensor(_uid() + "pg_tp", [128, 1024], BF16))
        for q2 in range(2):
            P.dma("sp", wf[q2][:, :], wple_dram[:, q2 * D:(q2 + 1) * D], writes=[r[f"wf{q2}"]])
            P.op("act", lambda e, q2=q2: e.copy(wpb[:, q2, :], wf[q2][:, :]), reads=[r[f"wf{q2}"]], writes=[r["wpb"]])
        P.op("pool", lambda e: e.memset(ident[:, :], 0.0), writes=[r["id"]])
        P.op("pool", lambda e: e.affine_select(out=ident[:, :], in_=ident[:, :], pattern=[[-1, 128]],
                                               compare_op=ALU.not_equal, fill=1.0, base=0, channel_multiplier=1),
             reads=[r["id"]], writes=[r["id"]])
        cnt = 0
        wcnt = [0]
        for t0 in range(0, T, TQ):
            for kc in range(KC):
                P.dma("sp", hT[:, kc, :], h2T_dram[kc, :, t0:t0 + TQ], writes=[r["hT"]])
            for tt in range(TQ // 128):
                s = tt % 2
                tok = t0 + tt * 128
                P.dma("sp", pt[s][:, :], p_dram[tok:tok + 128, :], writes=[r[f"pt{s}"]])
                P.op("dve", lambda e, s=s: e.tensor_copy(pb[s][:, :], pt[s][:, :]), reads=[r[f"pt{s}"]], writes=[r[f"pb{s}"]])
                for k2 in range(2):
                    P.op("pe", lambda e, s=s, k2=k2: e.transpose(tp[:, k2 * 128:(k2 + 1) * 128], pb[s][:, k2 * 128:(k2 + 1) * 128], ident[:, :]),
                         reads=[r[f"pb{s}"], r["id"]], writes=[r["tp"]])
                P.op("act", lambda e, tt=tt: e.copy(pT[:, :, tt * 128:(tt + 1) * 128], tp[:, 0:256].rearrange("p (k t) -> p k t", k=2)),
                     reads=[r["tp"]], writes=[r["pT"]])
            _load_wblk(P, r, wf, wb, 0, wg_blk_dram, 0, wcnt)
            for cb in range(NB):
                s = cb % 2
                if cb + 1 < NB:
                    _load_wblk(P, r, wf, wb, 1 - s, wg_blk_dram, cb + 1, wcnt)
                for tt in range(TQ // 128):
                    b3 = cnt % 3
                    b2 = cnt % 2
                    cnt += 1
                    tok = t0 + tt * 128
                    P.dma("sp", xt[b2][:, :], x1_dram[tok:tok + 128, cb * 512:(cb + 1) * 512], writes=[r[f"xt{b2}"]])
                    for kc in range(KC):
                        P.op("pe", lambda e, s=s, b3=b3, kc=kc, tt=tt: e.matmul(ps[b3][:, :], hT[:, kc, tt * 128:(tt + 1) * 128], wb[s][:, kc, :],
                                                                               start=(kc == 0), stop=(kc == KC - 1)),
                             reads=[r[f"wb{s}"], r["hT"]], writes=[r[f"ps{b3}"]])
                    for k2 in range(2):
                        P.op("pe", lambda e, b3=b3, k2=k2, tt=tt, cb=cb: e.matmul(pp[b3][:, :], pT[:, k2, tt * 128:(tt + 1) * 128],
                                                                                 wpb[:, k2, cb * 512:(cb + 1) * 512], start=(k2 == 0), stop=(k2 == 1)),
                             reads=[r["wpb"], r["pT"]], writes=[r[f"pp{b3}"]])
                    P.op("act", lambda e, b3=b3: e.activation(out=gt[:, :], in_=ps[b3][:, :], func=AF.Sigmoid),
                         reads=[r[f"ps{b3}"]], writes=[r["gt"]])
                    P.op("dve", lambda e, b3=b3: e.tensor_tensor(out=tm[:, :], in0=pp[b3][:, :], in1=gt[:, :], op=ALU.mult),
                         reads=[r[f"pp{b3}"], r["gt"]], writes=[r["tm"]])
                    P.op("pool", lambda e, b2=b2: e.tensor_tensor(out=ot[b2][:, :], in0=tm[:, :], in1=xt[b2][:, :], op=ALU.add),
                         reads=[r["tm"], r[f"xt{b2}"]], writes=[r[f"ot{b2}"]])
                    P.dma("pool", x2_dram[tok:tok + 128, cb * 512:(cb + 1) * 512], ot[b2][:, :], reads=[r[f"ot{b2}"]], writes=[r["out"]])
        P.barrier()
        P.flush()


SEQ = 4096
NLAYER = 2
NCT = 130


def make_consts():
    c = {}
    c.update(ret_consts())
    c.update(gdn_consts())
    c.update(rwkv_consts())
    c.update(dense_consts())
    return c


_UIDC = [0]


def build_program(T=SEQ, L=NLAYER):
    nc = bass.Bass("TRN2", target_bir_lowering=False)
    NCH = T // 128
    cs = make_consts()
    ext = lambda n, shp, dt=F32: nc.dram_tensor(n, list(shp), dt, kind="ExternalInput")
    x = ext("x", [T, D])
    pos = ext("pos", [128, T], I32)
    C = {k: ext("c_" + k, v.shape) for k, v in cs.items()}
    Lp = []
    for l in range(L):
        d = {}
        d["nwb"] = ext(f"nwb{l}", [128, D]); d["win"] = ext(f"win{l}", [NCT, 128, KC * 128])
        d["gnwT"] = ext(f"gnwT{l}", [128, 8])
        for n in ("w0T", "a0T", "kkT", "kaT", "rkT", "lnwT", "lnbT"):
            d[n] = ext(f"{n}{l}", [128, 8])
        d["muT"] = ext(f"muT{l}", [128, 33]); d["lw2"] = ext(f"lw2{l}", [128, 1024])
        d["convT"] = ext(f"convT{l}", [128, 192]); d["alog"] = ext(f"alog{l}", [16, 1]); d["dtb"] = ext(f"dtb{l}", [16, 1])
        d["nrm"] = ext(f"nrm{l}", [128, 1])
        d["wout"] = ext(f"wout{l}", [32, 128, KC * 128]); d["wgate"] = ext(f"wgate{l}", [32, 128, KC * 128])
        d["wple"] = ext(f"wple{l}", [128, 2 * D]); d["plnb"] = ext(f"plnb{l}", [128, D]); d["p"] = ext(f"p{l}", [T, 256])
        Lp.append(d)
    fnb = ext("fnb", [128, D])
    out = nc.dram_tensor("out", [T, D], F32, kind="ExternalOutput")
    scr = lambda n, shp, dt: nc.dram_tensor(n, list(shp), dt)
    hT = scr("hT", [KC, 128, T], BF16); yT = scr("yT", [KC, 128, T], BF16)
    projA = scr("projA", [65 * 128, T], F32); projB = scr("projB", [65 * 128, T], F32)
    projT = lambda ct: (projA, ct * 128) if ct < 65 else (projB, (ct - 65) * 128)
    cosT = scr("cosT", [128, T], F32); sinT = scr("sinT", [128, T], F32)
    x1 = scr("x1", [T, D], F32); x2 = scr("x2", [T, D], F32)
    S = {n: scr(n, [2048, T], BF16) for n in ("gqT", "gqdT", "gkT", "gvT", "wkT_d")}
    S["gbt"] = scr("gbt", [T, 32], F32)
    S["attnT_d"] = scr("attnT_d", [NCH, 128, 16, 128], BF16); S["ktl_d"] = scr("ktl_d", [T, 2048], BF16)
    S["u_d"] = scr("u_d", [T, 2048], F32); S["els_d"] = scr("els_d", [NCH, 128, 16], F32)
    S.update({n: scr(n, [1024, T], BF16) for n in ("rtT", "kktT", "khT", "kkaT", "rvT")})
    S.update({n: scr(n, [1024, T], F32) for n in ("bonT", "szT")})
    S["pc_d"] = scr("pc_d", [8, 128, NCH], F32)
    S.update({n: scr(n, [NCH, 128, 16, 128], BF16) for n in ("tinvT_d", "akvT_d", "bkvT_d", "nbabT_d")})
    S.update({n: scr(n, [T, 1024], BF16) for n in ("vtk_d", "khtk_d", "kkatk_d")})
    grows = dict(q=0, k=2048, v=4096, z=6144, a=8192, b=8208)
    import os
    only = os.environ.get("MK_PH")
    only = set(only.split(",")) if only else None

    def on(n):
        return only is None or n in only
    with contextlib.ExitStack() as stack:
        P = Prog(nc, stack)
        if on("rope"):
            phase_rope_tables(P, nc, pos, C, cosT, sinT, T)
        xin = x
        for l in range(L):
            d = Lp[l]
            if on("norm"):
                phase_norm(P, nc, xin, d["nwb"], hT, T)
            if on("proj"):
                phase_proj(P, nc, hT, d["win"], projT, cosT, sinT, T, NCT, n_rope=int(os.environ.get("MK_NROPE", "16")))
            if on("ret"):
                phase_ret(P, nc, projA, yT, C, d["gnwT"], T)
            if on("rwkv"):
                phase_rwkv_pre(P, nc, projA, S, C, d, T, 4096)
            if on("rwkv"):
                phase_rwkv_r1(P, nc, S, C, T)
            if on("rwkv"):
                phase_rwkv_r2(P, nc, S, C, yT, d, T, kc0=8)
            if on("gdn"):
                phase_gdn_pre(P, nc, projB, S, C, d, T, grows)
            if on("gdn"):
                phase_gdn_g1(P, nc, S, C, T)
            if on("gdn"):
                phase_gdn_g2(P, nc, projB, S, C, yT, d["nrm"], T, grows["z"], kc0=16)
            if on("out"):
                phase_out(P, nc, yT, d["wout"], xin, x1, T)
            if on("norm2"):
                phase_norm(P, nc, x1, d["plnb"], hT, T)
            if on("gate"):
                phase_gate(P, nc, hT, d["wgate"], d["p"], d["wple"], x1, x2, T)
            xin = x2
        if on("fin"):
            phase_norm(P, nc, xin, fnb, None, T, out_dram=out)
        n_ops = P.n_ops
    return nc, cs, n_ops


def _tiles(w, ncols_pad=None):
    K, N = w.shape
    if ncols_pad is not None and ncols_pad > N:
        w = np.concatenate([w, np.zeros((K, ncols_pad - N), w.dtype)], axis=1)
        N = ncols_pad
    return np.ascontiguousarray(w.reshape(K // 128, 128, N // 128, 128).transpose(2, 1, 0, 3).reshape(N // 128, 128, (K // 128) * 128))


def prep_shared(inp, L=NLAYER):
    f = np.float32
    sh = {}
    bc = lambda v: np.ascontiguousarray(np.broadcast_to(np.asarray(v, f), (128, v.shape[-1])))
    col8 = lambda v: np.ascontiguousarray(np.asarray(v, f).reshape(8, 128).T)
    for l in range(L):
        sh[f"nwb{l}"] = bc(inp["norm_w"][l])
        sh[f"win{l}"] = _tiles(np.asarray(inp["w_in"][l], f), NCT * 128)
        sh[f"gnwT{l}"] = col8(inp["ret_gn"][l])
        sh[f"w0T{l}"] = col8(inp["rwkv_w0"][l]); sh[f"a0T{l}"] = col8(inp["rwkv_a0"][l])
        sh[f"kkT{l}"] = col8(inp["rwkv_k_k"][l]); sh[f"kaT{l}"] = col8(inp["rwkv_k_a"][l]); sh[f"rkT{l}"] = col8(inp["rwkv_r_k"][l])
        sh[f"lnwT{l}"] = col8(inp["rwkv_ln_w"][l]); sh[f"lnbT{l}"] = col8(inp["rwkv_ln_b"][l])
        sh[f"muT{l}"] = np.ascontiguousarray(np.asarray(inp["rwkv_mu"][l], f).reshape(33, 128).T)
        sh[f"lw2{l}"] = np.ascontiguousarray(np.concatenate([np.asarray(inp["rwkv_w2"][l], f), np.asarray(inp["rwkv_a2"][l], f)], 0))
        sh[f"convT{l}"] = np.ascontiguousarray(np.asarray(inp["gdn_conv"][l], f).reshape(4, 48, 128).transpose(2, 1, 0).reshape(128, 192))
        sh[f"alog{l}"] = np.asarray(inp["gdn_a_log"][l], f).reshape(16, 1).copy()
        sh[f"dtb{l}"] = np.asarray(inp["gdn_dt_bias"][l], f).reshape(16, 1).copy()
        sh[f"nrm{l}"] = np.asarray(inp["gdn_norm"][l], f).reshape(128, 1).copy()
        sh[f"wout{l}"] = _tiles(np.asarray(inp["w_out"][l], f))
        sh[f"wgate{l}"] = _tiles(np.asarray(inp["w_ple_gate"][l], f))
        sh[f"wple{l}"] = np.ascontiguousarray(np.asarray(inp["w_ple"][l], f).reshape(2, 128, D).transpose(1, 0, 2).reshape(128, 2 * D))
        sh[f"plnb{l}"] = bc(inp["ple_norm"][l])
    sh["fnb"] = bc(inp["final_norm"])
    return sh


def kernel(**inp):
    B = inp["x"].shape[0]
    T = inp["x"].shape[1]
    nc, cs, n_ops = build_program(T, NLAYER)
    sh = prep_shared(inp)
    for k, v in cs.items():
        sh["c_" + k] = v
    in_maps = []
    for b in range(B):
        m = dict(sh)
        m["x"] = np.ascontiguousarray(np.asarray(inp["x"][b], np.float32))
        m["pos"] = np.ascontiguousarray(np.broadcast_to(np.asarray(inp["positions"][b], np.int32), (128, T)))
        for l in range(NLAYER):
            m[f"p{l}"] = np.ascontiguousarray(np.asarray(inp["p"][l, b], np.float32))
        in_maps.append(m)
    res = run_bass_kernel_spmd(nc, in_maps, core_ids=list(range(B)))
    return np.stack([np.asarray(r["out"], np.float32) for r in res.results], axis=0)
```

```python
import contextlib, math
from collections import defaultdict
import numpy as np
import concourse.bass as bass
import concourse.mybir as mybir
from concourse.bass_utils import run_bass_kernel_spmd


F32 = mybir.dt.float32
BF16 = mybir.dt.bfloat16
I32 = mybir.dt.int32
ALU = mybir.AluOpType
AF = mybir.ActivationFunctionType
AX = mybir.AxisListType

ENGS = ("pe", "act", "dve", "pool", "sp")
EPOCH = 20000
N_EPOCHS = {"pe": 16, "act": 10, "dve": 12, "pool": 10, "sp": 1}
N_DMA_SEM = 12


_UID = [0]


def _uid():
    return f"u{_UID[0]}_"


class Res:
    __slots__ = ("name", "w", "r")

    def __init__(self, name=""):
        self.name = name
        self.w = None
        self.r = []


class Prog:
    def __init__(self, nc, stack):
        self.nc = nc
        self.stack = stack
        self.sems = {}
        for e in ENGS:
            self.sems[e] = [stack.enter_context(nc.semaphore(f"s_{e}_{i}")) for i in range(N_EPOCHS[e])]
        self.dsems = {}
        for e in ("sp", "pool"):
            self.dsems[e] = [stack.enter_context(nc.semaphore(f"d_{e}_{i}")) for i in range(N_DMA_SEM)]
        self.dcount = {e: [0] * N_DMA_SEM for e in self.dsems}
        self.dnext = {e: 0 for e in self.dsems}
        self.seq = {e: 0 for e in ENGS}
        self.known = {e: {} for e in ENGS}
        self.ops = {e: [] for e in ENGS}
        self.last = {e: None for e in ENGS}
        self.outstanding = []
        self.n_ops = 0
        self.pe_needed = set()
        self.pe_map = {}
        self.pe_count = 0

    def _waits_for(self, eng, reads, writes, extra=()):
        toks = list(extra)
        for r in reads:
            if r.w is not None:
                toks.append(r.w)
        for w in writes:
            if w.w is not None:
                toks.append(w.w)
            toks.extend(w.r)
        best = {}
        for (sem, val, te, raw) in toks:
            pass
        return toks

    def _filter(self, eng, toks):
        best = {}
        for tok, is_raw in toks:
            sem, val, te = tok
            if te == eng and eng == "pe":
                continue
            k = "PE" if te == "pe" else id(sem)
            if k not in best or best[k][1] < val:
                best[k] = (sem, val)
        out = []
        kn = self.known[eng]
        for k, (sem, val) in best.items():
            if kn.get(k, 0) >= val:
                continue
            kn[k] = val
            if k == "PE":
                self.pe_needed.add(val)
            out.append((sem, val))
        return out

    def op(self, eng, fn, reads=(), writes=(), extra=()):
        toks = [(t, True) for t in extra]
        for r in reads:
            if r.w is not None:
                toks.append((r.w, True))
        for w in writes:
            if w.w is not None:
                toks.append((w.w, False))
            toks.extend((t, False) for t in w.r)
        waits = self._filter(eng, toks)
        self.seq[eng] += 1
        s = self.seq[eng]
        if eng == "pe":
            tok = ("PE", s, "pe")
            self.ops[eng].append((waits, fn, s, 1))
        else:
            ep = (s - 1) // EPOCH
            tok = (self.sems[eng][ep], s - ep * EPOCH, eng)
            self.ops[eng].append((waits, fn, tok[0], 1))
        self.last[eng] = tok
        for r in reads:
            r.r.append(tok)
        for w in writes:
            w.w = tok
            w.r = []
        self.n_ops += 1
        return tok

    def dma(self, q, out, in_, reads=(), writes=(), **kw):
        i = self.dnext[q]
        self.dnext[q] = (i + 1) % N_DMA_SEM
        sem = self.dsems[q][i]
        toks = []
        if self.dcount[q][i] > 0:
            toks.append(((sem, self.dcount[q][i], None), True))
        for r in reads:
            if r.w is not None:
                toks.append((r.w, True))
        for w in writes:
            if w.w is not None:
                toks.append((w.w, False))
            toks.extend((t, False) for t in w.r)
        waits = self._filter(q, toks)
        self.dcount[q][i] += 16
        tok = (sem, self.dcount[q][i], None)

        def fn(e, out=out, in_=in_, kw=kw):
            return e.dma_start(out=out, in_=in_, **kw)
        self.ops[q].append((waits, fn, sem, 16))
        for r in reads:
            r.r.append(tok)
        for w in writes:
            w.w = tok
            w.r = []
        self.outstanding.append(tok)
        self.n_ops += 1
        return tok

    def barrier(self):
        toks = [(t, True) for t in self.outstanding]
        for e in ENGS:
            if self.last[e] is not None:
                toks.append((self.last[e], True))
        for e in ENGS:
            waits = self._filter(e, [(t, r) for (t, r) in toks if t[2] != e])
            if waits:
                self.ops[e].append((waits, None, None, 0))
        self.outstanding = []

    def flush(self):
        _UID[0] += 1
        nc = self.nc
        ops = self.ops
        for (waits, fn, idx, inc) in ops["pe"]:
            if fn is not None and idx in self.pe_needed:
                self.pe_count += 1
                c = self.pe_count
                ep = (c - 1) // EPOCH
                self.pe_map[idx] = (self.sems["pe"][ep], c - ep * EPOCH)
        pe_map = self.pe_map

        def rw(w):
            s_, v_ = w
            if isinstance(s_, str):
                return pe_map[v_]
            return w
        with nc.Block() as block:
            def run(handle, lst, is_pe=False):
                for waits, fn, sem, inc in lst:
                    for w in waits:
                        s_, v_ = rw(w)
                        handle.wait_ge(s_, v_)
                    if fn is not None:
                        ins = fn(handle)
                        if is_pe:
                            if sem in pe_map:
                                ins.then_inc(pe_map[sem][0], 1)
                        else:
                            ins.then_inc(sem, inc)

            @block.tensor
            def _(e):
                run(e, ops["pe"], True)

            @block.scalar
            def _(e):
                run(e, ops["act"])

            @block.vector
            def _(e):
                run(e, ops["dve"])

            @block.gpsimd
            def _(e):
                run(e, ops["pool"])

            @block.sync
            def _(e):
                run(e, ops["sp"])
        self.ops = {e: [] for e in ENGS}


RET_H = 8


def ret_consts():
    h = np.arange(8, dtype=np.float64)
    lg = np.log1p(-(2.0 ** (-5.0 - h)))
    i = np.arange(128, dtype=np.float64)
    Gq = np.exp((i[None, :] + 1) * lg[:, None])
    Gk = np.exp(-(i[None, :] + 1) * lg[:, None]) * 128 ** -0.5
    GC = np.exp(128 * lg)
    c = {}
    c["ret_gq"] = np.broadcast_to(Gq.reshape(1, 8 * 128), (128, 1024)).astype(np.float32).copy()
    c["ret_gk"] = np.broadcast_to(Gk.reshape(1, 8 * 128), (128, 1024)).astype(np.float32).copy()
    c["ret_gc"] = np.broadcast_to(np.repeat(GC, 128).reshape(1, 1024), (128, 1024)).astype(np.float32).copy()
    jj, ii = np.meshgrid(np.arange(128), np.arange(128), indexing="ij")
    c["mask_ui"] = (ii >= jj).astype(np.float32)
    c["ident"] = np.eye(128, dtype=np.float32)
    return c


def dma_rows(P, q, out_tile, dram, row0, nh, t0, tl, res_w, hs=4):
    for h0 in range(0, nh, hs):
        src = dram[row0 + h0 * 128: row0 + (h0 + hs) * 128, t0:t0 + tl].rearrange("(h p) t -> p h t", p=128)
        P.dma(q, out_tile[:, h0:h0 + hs, 0:tl], src, writes=[res_w])


def phase_ret(P, nc, projT, yT, C, gnwT_dram, T, eps=1e-5):
    H = RET_H
    NCH = T // 128
    with contextlib.ExitStack() as st:
        def sb(name, shape, dt):
            return st.enter_context(nc.sbuf_tensor(_uid() + "rt_" + name, shape, dt))

        def pst(name, shape, dt):
            return st.enter_context(nc.psum_tensor(_uid() + "rt_" + name, shape, dt))
        qf = [sb(f"qf{i}", [128, H, 128], F32) for i in range(2)]
        kf = [sb(f"kf{i}", [128, H, 128], F32) for i in range(2)]
        vf = [sb(f"vf{i}", [128, H, 128], F32) for i in range(2)]
        zf = [sb(f"zf{i}", [128, H, 128], F32) for i in range(2)]
        gq = sb("gq", [128, H, 128], F32); gk = sb("gk", [128, H, 128], F32); gc = sb("gc", [128, H, 128], F32)
        maskf = sb("maskf", [128, 128], F32)
        identf = sb("identf", [128, 128], F32); ident = sb("ident", [128, 128], BF16)
        gnw = sb("gnw", [128, H], F32)
        qd = sb("qd", [128, H, 128], BF16); kd = sb("kd", [128, H, 128], BF16); vb = sb("vb", [128, H, 128], BF16)
        scT = sb("scT", [128, H, 128], BF16)
        vtok = sb("vtok", [128, H, 128], BF16); kdtok = sb("kdtok", [128, H, 128], BF16)
        state = sb("state", [128, H, 128], F32); stmp = sb("stmp", [128, H, 128], F32); state_bf = sb("state_bf", [128, H, 128], BF16)
        y_sb = sb("y_sb", [128, H, 128], F32); sq = sb("sq", [128, H, 128], F32)
        s1 = sb("s1", [128, H], F32); s2 = sb("s2", [128, H], F32); mean = sb("mean", [128, H], F32)
        var = sb("var", [128, H], F32); sd = sb("sd", [128, H], F32); rstd = sb("rstd", [128, H], F32)
        epsc = sb("epsc", [128, 1], F32); mh = sb("mh", [128, H], F32)
        yc = sb("yc", [128, H, 128], F32); yn = sb("yn", [128, H, 128], BF16)
        sz = sb("sz", [128, H, 128], F32); yg = sb("yg", [128, H, 128], F32); yfin = [sb(f"yfin{i}", [128, H, 128], BF16) for i in range(2)]
        sc_ps = pst("sc_ps", [128, H, 128], F32)
        vt_ps = pst("vt_ps", [128, H, 128], BF16)
        kt_ps = pst("kt_ps", [128, H, 128], BF16)
        y_ps = pst("y_ps", [128, H, 128], F32)
        kv_ps = pst("kv_ps", [128, H, 128], F32)
        r = {n: Res(n) for n in ["gq", "gk", "gc", "mask", "identf", "ident", "gnw", "qd", "kd", "vb", "scT", "vtok", "kdtok",
                                 "state", "stmp", "state_bf", "y_sb", "sq", "s1", "s2", "mean", "var", "sd", "rstd", "eps",
                                 "yc", "yn", "sz", "yg", "mh", "sc_ps", "vt_ps", "kt_ps", "y_ps", "kv_ps", "out"]}
        r_qf = [Res(), Res()]; r_kf = [Res(), Res()]; r_vf = [Res(), Res()]; r_zf = [Res(), Res()]; r_yfin = [Res(), Res()]

        flat = lambda t: t[:, :, :].rearrange("p h t -> p (h t)")
        P.dma("sp", flat(gq), C["ret_gq"][:, :], writes=[r["gq"]])
        P.dma("sp", flat(gk), C["ret_gk"][:, :], writes=[r["gk"]])
        P.dma("sp", flat(gc), C["ret_gc"][:, :], writes=[r["gc"]])
        P.dma("sp", maskf[:, :], C["mask_ui"][:, :], writes=[r["mask"]])
        P.dma("sp", identf[:, :], C["ident"][:, :], writes=[r["identf"]])
        P.dma("sp", gnw[:, :], gnwT_dram[:, :], writes=[r["gnw"]])
        P.op("pool", lambda e: e.tensor_copy(ident[:, :], identf[:, :]), reads=[r["identf"]], writes=[r["ident"]])
        P.op("pool", lambda e: e.memset(epsc[:, :], eps), writes=[r["eps"]])
        P.op("pool", lambda e: e.memset(mh[:, :], -0.5), writes=[r["mh"]])
        P.op("pool", lambda e: e.memset(flat(state), 0.0), writes=[r["state"]])
        P.op("pool", lambda e: e.memset(flat(state_bf), 0.0), writes=[r["state_bf"]])

        def bc(t):
            return t[:, :].unsqueeze(2).to_broadcast([128, H, 128])

        for n in range(NCH):
            s = n % 2
            t0 = n * 128
            dma_rows(P, "sp", qf[s], projT, 0, H, t0, 128, r_qf[s])
            dma_rows(P, "sp", kf[s], projT, 1024, H, t0, 128, r_kf[s])
            dma_rows(P, "sp", vf[s], projT, 2048, H, t0, 128, r_vf[s])
            dma_rows(P, "sp", zf[s], projT, 3072, H, t0, 128, r_zf[s])
            P.op("dve", lambda e, s=s: e.tensor_tensor(out=flat(qd), in0=flat(qf[s]), in1=flat(gq), op=ALU.mult),
                 reads=[r_qf[s], r["gq"]], writes=[r["qd"]])
            P.op("dve", lambda e, s=s: e.tensor_tensor(out=flat(kd), in0=flat(kf[s]), in1=flat(gk), op=ALU.mult),
                 reads=[r_kf[s], r["gk"]], writes=[r["kd"]])
            P.op("pool", lambda e, s=s: e.tensor_copy(flat(vb), flat(vf[s])), reads=[r_vf[s]], writes=[r["vb"]])
            P.op("act", lambda e, s=s: e.activation(out=flat(sz), in_=flat(zf[s]), func=AF.Silu), reads=[r_zf[s]], writes=[r["sz"]])
            for h in range(H):
                P.op("pe", lambda e, h=h: e.matmul(sc_ps[:, h, :], kd[:, h, :], qd[:, h, :], start=True, stop=True),
                     reads=[r["kd"], r["qd"]], writes=[r["sc_ps"]])
            for h in range(H):
                P.op("pe", lambda e, h=h: e.transpose(vt_ps[:, h, :], vb[:, h, :], ident[:, :]),
                     reads=[r["vb"], r["ident"]], writes=[r["vt_ps"]])
            for h in range(H):
                P.op("pe", lambda e, h=h: e.transpose(kt_ps[:, h, :], kd[:, h, :], ident[:, :]),
                     reads=[r["kd"], r["ident"]], writes=[r["kt_ps"]])
            P.op("dve", lambda e: e.tensor_tensor(out=scT[:, :, :], in0=sc_ps[:, :, :],
                                                  in1=maskf[:, :].unsqueeze(1).to_broadcast([128, H, 128]), op=ALU.mult),
                 reads=[r["sc_ps"], r["mask"]], writes=[r["scT"]])
            P.op("act", lambda e: e.copy(flat(vtok), flat(vt_ps)), reads=[r["vt_ps"]], writes=[r["vtok"]])
            P.op("act", lambda e: e.copy(flat(kdtok), flat(kt_ps)), reads=[r["kt_ps"]], writes=[r["kdtok"]])
            for h in range(H):
                P.op("pe", lambda e, h=h: e.matmul(y_ps[:, h, :], scT[:, h, :], vtok[:, h, :], start=True, stop=False),
                     reads=[r["scT"], r["vtok"]], writes=[r["y_ps"]])
                P.op("pe", lambda e, h=h: e.matmul(y_ps[:, h, :], qd[:, h, :], state_bf[:, h, :], start=False, stop=True),
                     reads=[r["qd"], r["state_bf"]], writes=[r["y_ps"]])
            for h in range(H):
                P.op("pe", lambda e, h=h: e.matmul(kv_ps[:, h, :], kdtok[:, h, :], vtok[:, h, :], start=True, stop=True),
                     reads=[r["kdtok"], r["vtok"]], writes=[r["kv_ps"]])
            P.op("dve", lambda e: e.tensor_tensor(out=flat(stmp), in0=flat(kv_ps), in1=flat(state), op=ALU.add),
                 reads=[r["kv_ps"], r["state"]], writes=[r["stmp"]])
            P.op("dve", lambda e: e.tensor_tensor(out=flat(state), in0=flat(stmp), in1=flat(gc), op=ALU.mult),
                 reads=[r["stmp"], r["gc"]], writes=[r["state"]])
            P.op("act", lambda e: e.copy(flat(state_bf), flat(state)), reads=[r["state"]], writes=[r["state_bf"]])
            P.op("act", lambda e: e.copy(flat(y_sb), flat(y_ps)), reads=[r["y_ps"]], writes=[r["y_sb"]])
            P.op("dve", lambda e: e.tensor_reduce(out=s1[:, :], in_=y_sb[:, :, :], axis=AX.X, op=ALU.add),
                 reads=[r["y_sb"]], writes=[r["s1"]])
            P.op("pool", lambda e: e.tensor_tensor(out=flat(sq), in0=flat(y_sb), in1=flat(y_sb), op=ALU.mult),
                 reads=[r["y_sb"]], writes=[r["sq"]])
            P.op("dve", lambda e: e.tensor_reduce(out=s2[:, :], in_=sq[:, :, :], axis=AX.X, op=ALU.add),
                 reads=[r["sq"]], writes=[r["s2"]])
            P.op("dve", lambda e: e.tensor_scalar(out=mean[:, :], in0=s1[:, :], scalar1=1.0 / 128, scalar2=None, op0=ALU.mult),
                 reads=[r["s1"]], writes=[r["mean"]])
            P.op("dve", lambda e: e.tensor_tensor(out=var[:, :], in0=mean[:, :], in1=mean[:, :], op=ALU.mult),
                 reads=[r["mean"]], writes=[r["var"]])
            P.op("dve", lambda e: e.scalar_tensor_tensor(out=var[:, :], in0=s2[:, :], scalar=1.0 / 128, in1=var[:, :],
                                                         op0=ALU.mult, op1=ALU.subtract),
                 reads=[r["s2"], r["var"]], writes=[r["var"]])
            P.op("dve", lambda e: e.tensor_scalar(out=sd[:, :], in0=var[:, :], scalar1=1.0, scalar2=eps, op0=ALU.mult, op1=ALU.add),
                 reads=[r["var"]], writes=[r["sd"]])
            P.op("pool", lambda e: e.tensor_tensor(out=rstd[:, :], in0=sd[:, :], in1=mh[:, :], op=ALU.pow), reads=[r["sd"], r["mh"]], writes=[r["rstd"]])
            P.op("dve", lambda e: e.tensor_tensor(out=yc[:, :, :], in0=y_sb[:, :, :], in1=bc(mean), op=ALU.subtract),
                 reads=[r["y_sb"], r["mean"]], writes=[r["yc"]])
            P.op("dve", lambda e: e.tensor_tensor(out=yn[:, :, :], in0=yc[:, :, :], in1=bc(rstd), op=ALU.mult),
                 reads=[r["yc"], r["rstd"]], writes=[r["yn"]])
            for h in range(H):
                P.op("pe", lambda e, h=h: e.transpose(kt_ps[:, h, :], yn[:, h, :], ident[:, :]),
                     reads=[r["yn"], r["ident"]], writes=[r["kt_ps"]])
            P.op("dve", lambda e: e.tensor_tensor(out=yg[:, :, :], in0=kt_ps[:, :, :], in1=bc(gnw), op=ALU.mult),
                 reads=[r["kt_ps"], r["gnw"]], writes=[r["yg"]])
            P.op("pool", lambda e, s=s: e.tensor_tensor(out=flat(yfin[s]), in0=flat(yg), in1=flat(sz), op=ALU.mult),
                 reads=[r["yg"], r["sz"]], writes=[r_yfin[s]])
            P.dma("pool", yT[0:H, :, t0:t0 + 128].rearrange("k p t -> p k t"), yfin[s][:, :, :],
                  reads=[r_yfin[s]], writes=[r["out"]])
        P.barrier()
        P.flush()


GH = 16


def gdn_consts():
    c = {}
    k, i = np.meshgrid(np.arange(128), np.arange(128), indexing="ij")
    c["triu"] = (k <= i).astype(np.float32)
    c["ones"] = np.ones((128, 128), np.float32)
    c["negones"] = -np.ones((128, 128), np.float32)
    c["mask_ls"] = (k > i).astype(np.float32)
    c["mask_li"] = (k >= i).astype(np.float32)
    c["mask_ui"] = (i >= k).astype(np.float32)
    c["mask_us"] = (i > k).astype(np.float32)
    c["ident"] = np.eye(128, dtype=np.float32)
    sel = np.zeros((16, 16, 128), np.float32)
    for h in range(16):
        sel[h, h, :] = 1.0
    c["sel16"] = sel.reshape(16, 16 * 128)
    cm = np.ones((128, 512), np.float32)
    cm[:, ::128] = 0.0
    c["cmask128"] = cm
    return c


def neumann(P, nc, L, LT, rL, rLT, W, r, nlev=6, G=4):
    identb = W["identb"]
    Pk = W["Pk"]; nL = W["nL"]; nLT = W["nLT"]
    pa, pb, pp = W["pa"], W["pb"], W["pp"]
    P.op("pool", lambda e: e.tensor_tensor(out=Pk[0][:, :, :], in0=identb[:, :].unsqueeze(1).to_broadcast([128, G, 128]),
                                           in1=LT[:, :, :], op=ALU.subtract),
         reads=[r["identb"], rLT], writes=[r["Pk0"]])
    curL, curLT, rcL, rcLT = L, LT, rL, rLT
    pi = 0
    for lev in range(nlev):
        s = lev % 2
        for g in range(G):
            P.op("pe", lambda e, g=g, a=curLT, b=curL: e.matmul(pa[:, g, :], a[:, g, :], b[:, g, :], start=True, stop=True),
                 reads=[rcLT, rcL], writes=[r["pa"]])
        if lev < nlev - 1:
            for g in range(G):
                P.op("pe", lambda e, g=g, a=curL, b=curLT: e.matmul(pb[:, g, :], a[:, g, :], b[:, g, :], start=True, stop=True),
                     reads=[rcLT, rcL], writes=[r["pb"]])
        P.op("act", lambda e, s=s: e.copy(nL[s][:, :, :], pa[:, :, :]), reads=[r["pa"]], writes=[r[f"nL{s}"]])
        if lev < nlev - 1:
            P.op("dve", lambda e, s=s: e.tensor_copy(nLT[s][:, :, :], pb[:, :, :]), reads=[r["pb"]], writes=[r[f"nLT{s}"]])
        for g in range(G):
            P.op("pe", lambda e, g=g, s=s, pi=pi: e.matmul(pp[:, g, :], nL[s][:, g, :], Pk[pi][:, g, :], start=True, stop=True),
                 reads=[r[f"nL{s}"], r[f"Pk{pi}"]], writes=[r["pp"]])
        P.op("dve", lambda e, pi=pi: e.tensor_tensor(out=Pk[1 - pi][:, :, :], in0=pp[:, :, :], in1=Pk[pi][:, :, :], op=ALU.add),
             reads=[r["pp"], r[f"Pk{pi}"]], writes=[r[f"Pk{1 - pi}"]])
        pi = 1 - pi
        curL, curLT, rcL, rcLT = nL[s], nLT[s], r[f"nL{s}"], r[f"nLT{s}"]
    return Pk[pi], r[f"Pk{pi}"]


def phase_gdn_pre(P, nc, projT, S, C, prm, T, rows):
    NB = T // 512
    with contextlib.ExitStack() as st:
        def sb(name, shape, dt):
            return st.enter_context(nc.sbuf_tensor(_uid() + "gp_" + name, shape, dt))

        def pst(name, shape, dt):
            return st.enter_context(nc.psum_tensor(_uid() + "gp_" + name, shape, dt))
        r = defaultdict(Res)
        convw = sb("convw", [128, 48 * 4], F32)
        alog = sb("alog", [16, 1], F32); dtb = sb("dtb", [16, 1], F32); nega = sb("nega", [16, 1], F32)
        onesf = sb("onesf", [128, 128], F32); identf = sb("identf", [128, 128], F32)
        sel = sb("sel", [16, 16 * 128], F32); cmask = sb("cmask", [16, 512], F32)
        epsc = sb("epsc", [128, 1], F32)
        at = sb("at", [16, 512], F32); bt = sb("bt", [16, 512], F32)
        e1 = sb("e1", [16, 512], F32); spt = sb("spt", [16, 512], F32); gt = sb("gt", [16, 512], F32)
        beta = sb("beta", [16, 512], F32); gcT = sb("gcT", [16, 512], F32); egcT = sb("egcT", [16, 512], F32)
        gbs = sb("gbs", [128, 4, 32], F32)
        u = [sb(f"u{i}", [128, 515], F32) for i in range(4)]
        acc = sb("acc", [128, 512], F32); sl = sb("sl", [128, 512], F32); sq = sb("sq", [128, 512], F32)
        sd = sb("sd", [128, 512], F32); rn = sb("rn", [128, 512], F32); kn = sb("kn", [128, 512], F32)
        ob = [sb(f"ob{i}", [128, 512], BF16) for i in range(4)]
        ob2 = [sb(f"ob2{i}", [128, 512], BF16) for i in range(4)]
        ss_ps = pst("ss_ps", [128, 512], F32)
        bc_ps = pst("bc_ps", [128, 512], F32)
        tr_ps_full = pst("tr_ps", [128, 512], F32)
        tr_ps = tr_ps_full[:, 0:128].rearrange("p (c x) -> p c x", x=32)
        P.dma("sp", convw[:, :], prm["convT"][:, :], writes=[r["convw"]])
        P.dma("sp", alog[:, :], prm["alog"][:, :], writes=[r["alog"]])
        P.dma("sp", dtb[:, :], prm["dtb"][:, :], writes=[r["dtb"]])
        P.dma("sp", onesf[:, :], C["ones"][:, :], writes=[r["onesf"]])
        P.dma("sp", identf[:, :], C["ident"][:, :], writes=[r["identf"]])
        P.dma("sp", sel[:, :], C["sel16"][:, :], writes=[r["sel"]])
        P.dma("sp", cmask[:, :], C["cmask128"][0:16, :], writes=[r["cmask"]])
        P.op("pool", lambda e: e.memset(epsc[:, :], 1e-6), writes=[r["eps"]])
        P.op("act", lambda e: e.activation(out=nega[:, :], in_=alog[:, :], func=AF.Exp), reads=[r["alog"]], writes=[r["nega"]])
        P.op("dve", lambda e: e.tensor_scalar(out=nega[:, :], in0=nega[:, :], scalar1=-1.0, scalar2=None, op0=ALU.mult),
             reads=[r["nega"]], writes=[r["nega"]])
        cnt = 0
        acc2 = [acc] + [sb(f"acc_{i}", [128, 512], F32) for i in range(3)]; sl2 = [sl] + [sb(f"sl_{i}", [128, 512], F32) for i in range(3)]
        sq2 = [sq] + [sb(f"sq_{i}", [128, 512], F32) for i in range(3)]; sd2 = [sd] + [sb(f"sd_{i}", [128, 512], F32) for i in range(3)]
        rn2 = [rn] + [sb(f"rn_{i}", [128, 512], F32) for i in range(3)]; kn2 = [kn] + [sb(f"kn_{i}", [128, 512], F32) for i in range(3)]
        ss2 = [ss_ps] + [pst(f"ss_ps_{i}", [128, 512], F32) for i in range(3)]
        bc2 = [bc_ps, tr_ps_full] + [pst(f"bc_ps_{i}", [128, 512], F32) for i in range(2)]

        def do_tile(kind, rbase, dst, ti, tb, s):
            t0 = tb * 512
            acc, sl, sq, sd, rn, kn, ss_ps, bc_ps = acc2[s], sl2[s], sq2[s], sd2[s], rn2[s], kn2[s], ss2[s], bc2[s]
            ra, rsl, rsq, rsd, rrn, rkn, rss, rbc = (r[f"acc{s}"], r[f"sl{s}"], r[f"sq{s}"], r[f"sd{s}"], r[f"rn{s}"], r[f"kn{s}"],
                                                     r[f"ss_ps{s}"], r[f"bc_ps{s}"])
            wi = {"q": 0, "k": 16, "v": 32}[kind] + ti
            row0 = rbase + ti * 128
            if tb == 0:
                P.op("pool", lambda e: e.memset(u[s][:, 0:3], 0.0), writes=[r[f"u{s}"]])
                P.dma("sp", u[s][:, 3:515], projT[row0:row0 + 128, 0:512], writes=[r[f"u{s}"]])
            else:
                P.dma("sp", u[s][:, :], projT[row0:row0 + 128, t0 - 3:t0 + 512], writes=[r[f"u{s}"]])
            P.op("act", lambda e: e.mul(acc[:, :], u[s][:, 3:515], convw[:, wi * 4 + 3:wi * 4 + 4]),
                 reads=[r[f"u{s}"], r["convw"]], writes=[ra])
            for j in (2, 1, 0):
                P.op("dve", lambda e, j=j: e.scalar_tensor_tensor(out=acc[:, :], in0=u[s][:, j:j + 512],
                                                                  scalar=convw[:, wi * 4 + j:wi * 4 + j + 1],
                                                                  in1=acc[:, :], op0=ALU.mult, op1=ALU.add),
                     reads=[r[f"u{s}"], r["convw"], ra], writes=[ra])
            yield
            if kind == "v":
                P.op("act", lambda e: e.activation(out=ob[s][:, :], in_=acc[:, :], func=AF.Silu), reads=[ra], writes=[r[f"ob{s}"]])
                P.dma("pool", S[dst][ti * 128:(ti + 1) * 128, t0:t0 + 512], ob[s][:, :], reads=[r[f"ob{s}"]], writes=[r[dst]])
                return
            P.op("act", lambda e: e.activation(out=sl[:, :], in_=acc[:, :], func=AF.Silu), reads=[ra], writes=[rsl])
            P.op("pool", lambda e: e.tensor_tensor(out=sq[:, :], in0=sl[:, :], in1=sl[:, :], op=ALU.mult), reads=[rsl], writes=[rsq])
            P.op("pe", lambda e: e.matmul(ss_ps[:, :], onesf[:, :], sq[:, :], start=True, stop=True), reads=[r["onesf"], rsq], writes=[rss])
            yield
            P.op("act", lambda e: e.activation(out=sd[:, :], in_=ss_ps[:, :], func=AF.Sqrt, bias=epsc[:, 0:1], scale=1.0),
                 reads=[rss, r["eps"]], writes=[rsd])
            yield
            P.op("dve", lambda e: e.reciprocal(rn[:, :], sd[:, :]), reads=[rsd], writes=[rrn])
            if kind == "k":
                P.op("dve", lambda e: e.tensor_tensor(out=ob[s][:, :], in0=sl[:, :], in1=rn[:, :], op=ALU.mult),
                     reads=[rsl, rrn], writes=[r[f"ob{s}"]])
                P.dma("pool", S[dst][ti * 128:(ti + 1) * 128, t0:t0 + 512], ob[s][:, :], reads=[r[f"ob{s}"]], writes=[r[dst]])
            else:
                P.op("dve", lambda e: e.scalar_tensor_tensor(out=kn[:, :], in0=sl[:, :], scalar=128 ** -0.5, in1=rn[:, :],
                                                             op0=ALU.mult, op1=ALU.mult),
                     reads=[rsl, rrn], writes=[rkn])
                P.op("act", lambda e: e.copy(ob[s][:, :], kn[:, :]), reads=[rkn], writes=[r[f"ob{s}"]])
                P.dma("pool", S[dst][ti * 128:(ti + 1) * 128, t0:t0 + 512], ob[s][:, :], reads=[r[f"ob{s}"]], writes=[r[dst]])
                P.op("pe", lambda e: e.matmul(bc_ps[:, :], sel[:, ti * 128:(ti + 1) * 128], egcT[:, :], start=True, stop=True),
                     reads=[r["sel"], r["egcT"]], writes=[rbc])
                P.op("dve", lambda e: e.tensor_tensor(out=ob2[s][:, :], in0=kn[:, :], in1=bc_ps[:, :], op=ALU.mult),
                     reads=[rkn, rbc], writes=[r[f"ob2{s}"]])
                P.dma("pool", S["gqdT"][ti * 128:(ti + 1) * 128, t0:t0 + 512], ob2[s][:, :], reads=[r[f"ob2{s}"]], writes=[r["gqdT"]])

        for tb in range(NB):
            t0 = tb * 512
            P.dma("sp", at[:, :], projT[rows["a"]:rows["a"] + 16, t0:t0 + 512], writes=[r["at"]])
            P.dma("sp", bt[:, :], projT[rows["b"]:rows["b"] + 16, t0:t0 + 512], writes=[r["bt"]])
            P.op("act", lambda e: e.activation(out=e1[:, :], in_=at[:, :], func=AF.Exp, bias=dtb[:, 0:1], scale=1.0),
                 reads=[r["at"], r["dtb"]], writes=[r["e1"]])
            P.op("dve", lambda e: e.tensor_scalar(out=e1[:, :], in0=e1[:, :], scalar1=1.0, scalar2=None, op0=ALU.add),
                 reads=[r["e1"]], writes=[r["e1"]])
            P.op("act", lambda e: e.activation(out=spt[:, :], in_=e1[:, :], func=AF.Ln), reads=[r["e1"]], writes=[r["spt"]])
            P.op("dve", lambda e: e.tensor_scalar(out=gt[:, :], in0=spt[:, :], scalar1=nega[:, 0:1], scalar2=None, op0=ALU.mult),
                 reads=[r["spt"], r["nega"]], writes=[r["gt"]])
            P.op("act", lambda e: e.activation(out=beta[:, :], in_=bt[:, :], func=AF.Sigmoid), reads=[r["bt"]], writes=[r["beta"]])
            P.op("dve", lambda e: e.tensor_tensor_scan(out=gcT[:, :], data0=cmask[:, :], data1=gt[:, :], initial=0.0,
                                                       op0=ALU.mult, op1=ALU.add),
                 reads=[r["cmask"], r["gt"]], writes=[r["gcT"]])
            P.op("act", lambda e: e.activation(out=egcT[:, :], in_=gcT[:, :], func=AF.Exp), reads=[r["gcT"]], writes=[r["egcT"]])
            for c4 in range(4):
                P.op("pe", lambda e, c4=c4: e.transpose(tr_ps[:, c4, 0:16], gt[:, c4 * 128:(c4 + 1) * 128], identf[0:16, 0:16]),
                     reads=[r["gt"], r["identf"]], writes=[r["bc_ps1"]])
                P.op("pe", lambda e, c4=c4: e.transpose(tr_ps[:, c4, 16:32], beta[:, c4 * 128:(c4 + 1) * 128], identf[0:16, 0:16]),
                     reads=[r["beta"], r["identf"]], writes=[r["bc_ps1"]])
            P.op("dve", lambda e: e.tensor_copy(gbs[:, :, :], tr_ps[:, :, :]), reads=[r["bc_ps1"]], writes=[r["gbs"]])
            P.dma("pool", S["gbt"][t0:t0 + 512, :].rearrange("(c p) x -> p c x", p=128), gbs[:, :, :],
                  reads=[r["gbs"]], writes=[r["gbt_out"]])
            for kind, rbase, dst in (("q", rows["q"], "gqT"), ("k", rows["k"], "gkT"), ("v", rows["v"], "gvT")):
                for ti in range(0, 16, 4):
                    zipper([do_tile(kind, rbase, dst, ti + j, tb, j) for j in range(4)])
        P.barrier()
        P.flush()


def zipper_lag(gens):
    a, b = gens
    a_done = b_done = False
    try:
        next(a)
    except StopIteration:
        a_done = True
    while not (a_done and b_done):
        if not b_done:
            try:
                next(b)
            except StopIteration:
                b_done = True
        if not a_done:
            try:
                next(a)
            except StopIteration:
                a_done = True


def zipper(gens):
    active = list(gens)
    while active:
        for g in list(active):
            try:
                next(g)
            except StopIteration:
                active.remove(g)


def neumann_gen(P, nc, L, LT, rL, rLT, W, r, nlev=6, G=4):
    identb = W["identb"]
    Pk = W["Pk"]; nL = W["nL"]; nLT = W["nLT"]
    pa, pb = W["pa"], W["pb"]
    P.op("pool", lambda e: e.tensor_tensor(out=Pk[0][:, :, :], in0=identb[:, :].unsqueeze(1).to_broadcast([128, G, 128]),
                                           in1=LT[:, :, :], op=ALU.subtract),
         reads=[W["r_identb"], rLT], writes=[r["Pk0"]])
    yield
    curL, curLT, rcL, rcLT = L, LT, rL, rLT
    pi = 0
    for lev in range(nlev):
        s = lev % 2
        for g in range(G):
            P.op("pe", lambda e, g=g, a=curLT, b=curL: e.matmul(pa[:, g, :], a[:, g, :], b[:, g, :], start=True, stop=True),
                 reads=[rcLT, rcL], writes=[r["pa"]])
        if lev < nlev - 1:
            for g in range(G):
                P.op("pe", lambda e, g=g, a=curL, b=curLT: e.matmul(pb[:, g, :], a[:, g, :], b[:, g, :], start=True, stop=True),
                     reads=[rcLT, rcL], writes=[r["pb"]])
        yield
        P.op("act", lambda e, s=s: e.copy(nL[s][:, :, :], pa[:, :, :]), reads=[r["pa"]], writes=[r[f"nL{s}"]])
        if lev < nlev - 1:
            P.op("dve", lambda e, s=s: e.tensor_copy(nLT[s][:, :, :], pb[:, :, :]), reads=[r["pb"]], writes=[r[f"nLT{s}"]])
        yield
        for g in range(G):
            P.op("pe", lambda e, g=g, s=s, pi=pi: e.matmul(pa[:, g, :], nL[s][:, g, :], Pk[pi][:, g, :], start=True, stop=True),
                 reads=[r[f"nL{s}"], r[f"Pk{pi}"]], writes=[r["pa"]])
        yield
        P.op("dve", lambda e, pi=pi: e.tensor_tensor(out=Pk[1 - pi][:, :, :], in0=pa[:, :, :], in1=Pk[pi][:, :, :], op=ALU.add),
             reads=[r["pa"], r[f"Pk{pi}"]], writes=[r[f"Pk{1 - pi}"]])
        yield
        pi = 1 - pi
        curL, curLT, rcL, rcLT = nL[s], nLT[s], r[f"nL{s}"], r[f"nLT{s}"]
    W["result"] = (Pk[pi], r[f"Pk{pi}"])


def phase_gdn_g1(P, nc, S, C, T):
    NCH = T // 128
    G = 4
    with contextlib.ExitStack() as st:
        def sb(name, shape, dt):
            return st.enter_context(nc.sbuf_tensor(_uid() + "g1_" + name, shape, dt))

        def pst(name, shape, dt):
            return st.enter_context(nc.psum_tensor(_uid() + "g1_" + name, shape, dt))
        r = defaultdict(Res)
        triu = sb("triu", [128, 128], F32); onesf = sb("onesf", [128, 128], F32); negones = sb("negones", [128, 128], F32)
        mls = sb("mls", [128, 128], F32); mli = sb("mli", [128, 128], F32)
        identf = sb("identf", [128, 128], F32); identb = sb("identb", [128, 128], BF16)
        gb = [sb(f"gb{i}", [128, 32], F32) for i in range(2)]
        gcs = sb("gcs", [128, 32], F32)
        egc = sb("egc", [128, 16], F32); dtl = sb("dtl", [128, 16], F32); etail = sb("etail", [128, 16], F32)
        elast = [sb(f"elast{i}", [128, 16], F32) for i in range(2)]
        bgc = sb("bgc", [128, 16], F32)
        Gb = sb("Gb", [128, 16, 128], F32); X = sb("X", [128, 16, 128], F32)
        gc_ps = None
        ST = []
        for q in range(2):
            t = {}
            for n in ("kT", "qT", "vT", "L", "attn", "LT", "attnT", "kbg", "ktail", "vb", "wk_sb", "nL0", "nL1", "nLT0", "nLT1", "Pk0", "Pk1"):
                t[n] = sb(f"{n}_{q}", [128, G, 128], BF16)
            for n in ("M1", "dec", "dec_s", "dec_i", "u_sb"):
                t[n] = sb(f"{n}_{q}", [128, G, 128], F32)
            t["g_ps"] = pst(f"g_ps{q}", [128, G, 128], F32)
            t["tr_ps"] = pst(f"tr_ps{q}", [128, 2, G, 128], BF16)
            t["pa"] = pst(f"pa{q}", [128, G, 128], F32)
            t["pb"] = pst(f"pb{q}", [128, G, 128], F32)
            t["r"] = defaultdict(Res)
            t["W"] = {"identb": identb, "r_identb": r["identb"], "nL": [t["nL0"], t["nL1"]], "nLT": [t["nLT0"], t["nLT1"]],
                      "Pk": [t["Pk0"], t["Pk1"]], "pa": t["pa"], "pb": t["pb"]}
            ST.append(t)
        for nm, t_, src in (("triu", triu, "triu"), ("onesf", onesf, "ones"), ("negones", negones, "negones"),
                            ("mls", mls, "mask_ls"), ("mli", mli, "mask_li"), ("identf", identf, "ident")):
            P.dma("sp", t_[:, :], C[src][:, :], writes=[r[nm]])
        P.op("pool", lambda e: e.tensor_copy(identb[:, :], identf[:, :]), reads=[r["identf"]], writes=[r["identb"]])

        def bcg(t, g):
            return t[:, g * G:(g + 1) * G].unsqueeze(2).to_broadcast([128, G, 128])

        def bcm(m):
            return m[:, :].unsqueeze(1).to_broadcast([128, G, 128])

        def grp(c, g, q, sc):
            t = ST[q]
            rr = t["r"]
            t0 = c * 128
            r0 = g * G * 128
            kT, qT, vT = t["kT"], t["qT"], t["vT"]
            g_ps, tr_ps = t["g_ps"], t["tr_ps"]
            for nm, tl, src in (("kT", kT, "gkT"), ("qT", qT, "gqT"), ("vT", vT, "gvT")):
                P.dma("sp", tl[:, :, :], S[src][r0:r0 + G * 128, t0:t0 + 128].rearrange("(h p) t -> p h t", p=128), writes=[rr[nm]])
            P.op("pe", lambda e: e.matmul(g_ps[:, :, :], triu[:, :], Gb[:, g * G:(g + 1) * G, :], start=True, stop=False),
                 reads=[r["triu"], r["Gb"]], writes=[rr["g_ps"]])
            P.op("pe", lambda e: e.matmul(g_ps[:, :, :], negones[:, :], X[:, g * G:(g + 1) * G, :], start=False, stop=True),
                 reads=[r["negones"], r["X"]], writes=[rr["g_ps"]])
            yield
            P.op("dve", lambda e: e.tensor_scalar(out=t["M1"][:, :, :], in0=g_ps[:, :, :], scalar1=0.0, scalar2=None, op0=ALU.min),
                 reads=[rr["g_ps"]], writes=[rr["M1"]])
            yield
            P.op("act", lambda e: e.activation(out=t["dec"][:, :, :], in_=t["M1"][:, :, :], func=AF.Exp), reads=[rr["M1"]], writes=[rr["dec"]])
            for h in range(G):
                P.op("pe", lambda e, h=h: e.matmul(g_ps[:, h, :], kT[:, h, :], kT[:, h, :], start=True, stop=True),
                     reads=[rr["kT"]], writes=[rr["g_ps"]])
            yield
            P.op("pool", lambda e: e.tensor_tensor(out=t["dec_i"][:, :, :], in0=t["dec"][:, :, :], in1=bcm(mli), op=ALU.mult),
                 reads=[rr["dec"], r["mli"]], writes=[rr["dec_i"]])
            P.op("dve", lambda e: e.tensor_tensor(out=t["dec_s"][:, :, :], in0=t["dec"][:, :, :], in1=bcm(mls), op=ALU.mult),
                 reads=[rr["dec"], r["mls"]], writes=[rr["dec_s"]])
            yield
            P.op("dve", lambda e: e.tensor_tensor(out=t["dec_s"][:, :, :], in0=t["dec_s"][:, :, :],
                                                  in1=gb[sc][:, 16 + g * G:16 + (g + 1) * G].unsqueeze(2).to_broadcast([128, G, 128]), op=ALU.mult),
                 reads=[rr["dec_s"], r[f"gb{sc}"]], writes=[rr["dec_s"]])
            yield
            P.op("dve", lambda e: e.tensor_tensor(out=t["L"][:, :, :], in0=g_ps[:, :, :], in1=t["dec_s"][:, :, :], op=ALU.mult),
                 reads=[rr["g_ps"], rr["dec_s"]], writes=[rr["L"]])
            yield
            for h in range(G):
                P.op("pe", lambda e, h=h: e.matmul(g_ps[:, h, :], qT[:, h, :], kT[:, h, :], start=True, stop=True),
                     reads=[rr["kT"], rr["qT"]], writes=[rr["g_ps"]])
            for h in range(G):
                P.op("pe", lambda e, h=h: e.transpose(tr_ps[:, 0, h, :], t["L"][:, h, :], identb[:, :]),
                     reads=[rr["L"], r["identb"]], writes=[rr["tr_ps"]])
            yield
            P.op("dve", lambda e: e.tensor_tensor(out=t["attn"][:, :, :], in0=g_ps[:, :, :], in1=t["dec_i"][:, :, :], op=ALU.mult),
                 reads=[rr["g_ps"], rr["dec_i"]], writes=[rr["attn"]])
            P.op("act", lambda e: e.copy(t["LT"][:, :, :], tr_ps[:, 0, :, :]), reads=[rr["tr_ps"]], writes=[rr["LT"]])
            yield
            for h in range(G):
                P.op("pe", lambda e, h=h: e.transpose(tr_ps[:, 1, h, :], t["attn"][:, h, :], identb[:, :]),
                     reads=[rr["attn"], r["identb"]], writes=[rr["tr_ps"]])
            yield
            P.op("act", lambda e: e.copy(t["attnT"][:, :, :], tr_ps[:, 1, :, :]), reads=[rr["tr_ps"]], writes=[rr["attnT"]])
            P.dma("pool", S["attnT_d"][c, :, g * G:(g + 1) * G, :], t["attnT"][:, :, :],
                  reads=[rr["attnT"]], writes=[r["attnT_out"]])
            yield
            yield from neumann_gen(P, nc, t["L"], t["LT"], rr["L"], rr["LT"], t["W"], rr)
            Pt, rPt = t["W"]["result"]
            for h in range(G):
                P.op("pe", lambda e, h=h: e.transpose(tr_ps[:, 0, h, :], kT[:, h, :], identb[:, :]),
                     reads=[rr["kT"], r["identb"]], writes=[rr["tr_ps"]])
            for h in range(G):
                P.op("pe", lambda e, h=h: e.transpose(tr_ps[:, 1, h, :], vT[:, h, :], identb[:, :]),
                     reads=[rr["vT"], r["identb"]], writes=[rr["tr_ps"]])
            yield
            P.op("dve", lambda e: e.tensor_tensor(out=t["kbg"][:, :, :], in0=tr_ps[:, 0, :, :], in1=bcg(bgc, g), op=ALU.mult),
                 reads=[rr["tr_ps"], r["bgc"]], writes=[rr["kbg"]])
            P.op("dve", lambda e: e.tensor_tensor(out=t["ktail"][:, :, :], in0=tr_ps[:, 0, :, :], in1=bcg(etail, g), op=ALU.mult),
                 reads=[rr["tr_ps"], r["etail"]], writes=[rr["ktail"]])
            P.op("dve", lambda e: e.tensor_tensor(out=t["vb"][:, :, :], in0=tr_ps[:, 1, :, :],
                                                  in1=gb[sc][:, 16 + g * G:16 + (g + 1) * G].unsqueeze(2).to_broadcast([128, G, 128]), op=ALU.mult),
                 reads=[rr["tr_ps"], r[f"gb{sc}"]], writes=[rr["vb"]])
            P.dma("pool", S["ktl_d"][t0:t0 + 128, r0:r0 + G * 128], t["ktail"][:, :, :].rearrange("p h d -> p (h d)"),
                  reads=[rr["ktail"]], writes=[r["ktl_out"]])
            yield
            for h in range(G):
                P.op("pe", lambda e, h=h: e.matmul(g_ps[:, h, :], Pt[:, h, :], t["vb"][:, h, :], start=True, stop=True),
                     reads=[rPt, rr["vb"]], writes=[rr["g_ps"]])
            yield
            P.op("act", lambda e: e.copy(t["u_sb"][:, :, :], g_ps[:, :, :]), reads=[rr["g_ps"]], writes=[rr["u_sb"]])
            P.dma("pool", S["u_d"][t0:t0 + 128, r0:r0 + G * 128], t["u_sb"][:, :, :].rearrange("p h d -> p (h d)"),
                  reads=[rr["u_sb"]], writes=[r["u_out"]])
            yield
            for h in range(G):
                P.op("pe", lambda e, h=h: e.matmul(g_ps[:, h, :], t["kbg"][:, h, :], Pt[:, h, :], start=True, stop=True),
                     reads=[rPt, rr["kbg"]], writes=[rr["g_ps"]])
            yield
            P.op("act", lambda e: e.copy(t["wk_sb"][:, :, :], g_ps[:, :, :]), reads=[rr["g_ps"]], writes=[rr["wk_sb"]])
            P.dma("pool", S["wkT_d"][r0:r0 + G * 128, t0:t0 + 128].rearrange("(h p) t -> p h t", p=128), t["wk_sb"][:, :, :],
                  reads=[rr["wk_sb"]], writes=[r["wk_out"]])
            yield

        for c in range(NCH):
            t0 = c * 128
            sc = c % 2
            gps0 = ST[0]["g_ps"]
            rg0 = ST[0]["r"]["g_ps"]
            P.dma("sp", gb[sc][:, :], S["gbt"][t0:t0 + 128, :], writes=[r[f"gb{sc}"]])
            P.op("pe", lambda e, sc=sc: e.matmul(gps0[:, 0, 0:16], triu[:, :], gb[sc][:, 0:16], start=True, stop=True),
                 reads=[r["triu"], r[f"gb{sc}"]], writes=[rg0])
            P.op("pe", lambda e, sc=sc: e.matmul(gps0[:, 0, 16:32], onesf[:, :], gb[sc][:, 0:16], start=True, stop=True),
                 reads=[r["onesf"], r[f"gb{sc}"]], writes=[rg0])
            P.op("dve", lambda e: e.tensor_copy(gcs[:, :], gps0[:, 0, 0:32]), reads=[rg0], writes=[r["gcs"]])
            P.op("act", lambda e: e.activation(out=egc[:, :], in_=gcs[:, 0:16], func=AF.Exp), reads=[r["gcs"]], writes=[r["egc"]])
            P.op("dve", lambda e: e.tensor_tensor(out=dtl[:, :], in0=gcs[:, 16:32], in1=gcs[:, 0:16], op=ALU.subtract),
                 reads=[r["gcs"]], writes=[r["dtl"]])
            P.op("act", lambda e: e.activation(out=etail[:, :], in_=dtl[:, :], func=AF.Exp), reads=[r["dtl"]], writes=[r["etail"]])
            P.op("act", lambda e, sc=sc: e.activation(out=elast[sc][:, :], in_=gcs[:, 16:32], func=AF.Exp),
                 reads=[r["gcs"]], writes=[r[f"elast{sc}"]])
            P.dma("pool", S["els_d"][c, :, :], elast[sc][:, :], reads=[r[f"elast{sc}"]], writes=[r["els_out"]])
            P.op("dve", lambda e, sc=sc: e.tensor_tensor(out=bgc[:, :], in0=gb[sc][:, 16:32], in1=egc[:, :], op=ALU.mult),
                 reads=[r[f"gb{sc}"], r["egc"]], writes=[r["bgc"]])
            P.op("pool", lambda e, sc=sc: e.tensor_copy(Gb[:, :, :], gb[sc][:, 0:16].unsqueeze(2).to_broadcast([128, 16, 128])),
                 reads=[r[f"gb{sc}"]], writes=[r["Gb"]])
            P.op("pool", lambda e: e.tensor_tensor(out=X[:, :, :], in0=Gb[:, :, :],
                                                   in1=triu[:, :].unsqueeze(1).to_broadcast([128, 16, 128]), op=ALU.mult),
                 reads=[r["Gb"], r["triu"]], writes=[r["X"]])
            for g0 in (0, 2):
                zipper([grp(c, g0, 0, sc), grp(c, g0 + 1, 1, sc)])
        P.barrier()
        P.flush()


def phase_gdn_g2(P, nc, projT, S, C, yT, normw_dram, T, zrow, kc0=16):
    NCH = T // 128
    G = 4
    H = 16
    with contextlib.ExitStack() as st:
        def sb(name, shape, dt):
            return st.enter_context(nc.sbuf_tensor(_uid() + "g2_" + name, shape, dt))

        def pst(name, shape, dt):
            return st.enter_context(nc.psum_tensor(_uid() + "g2_" + name, shape, dt))
        r = defaultdict(Res)
        identf = sb("identf", [128, 128], F32); identb = sb("identb", [128, 128], BF16)
        nrm = sb("nrm", [128, 1], F32); epsc = sb("epsc", [128, 1], F32); mh = sb("mh", [128, G], F32)
        wkT = [sb(f"wkT{i}", [128, H, 128], BF16) for i in range(2)]
        qdT = [sb(f"qdT{i}", [128, H, 128], BF16) for i in range(2)]
        atT = [sb(f"atT{i}", [128, H, 128], BF16) for i in range(2)]
        ktl = [sb(f"ktl{i}", [128, H, 128], BF16) for i in range(2)]
        uu = [sb(f"uu{i}", [128, H, 128], F32) for i in range(2)]
        zt = [sb(f"zt{i}", [128, H, 128], F32) for i in range(2)]
        els = [sb(f"els{i}", [128, H], F32) for i in range(2)]
        Sf = sb("Sf", [128, H, 128], F32); Sb = sb("Sb", [128, H, 128], BF16)
        TS = []
        for q in range(2):
            d = {"Stmp": sb(f"Stmp{q}", [128, G, 128], F32), "vnew": sb(f"vnew{q}", [128, G, 128], BF16),
                 "o_sb": sb(f"o_sb{q}", [128, G, 128], F32), "osq": sb(f"osq{q}", [128, G, 128], F32),
                 "s2": sb(f"s2{q}", [128, G], F32), "sd": sb(f"sd{q}", [128, G], F32), "rstd": sb(f"rstd{q}", [128, G], F32),
                 "on": sb(f"on{q}", [128, G, 128], BF16), "sz": sb(f"sz{q}", [128, G, 128], F32), "yg": sb(f"yg{q}", [128, G, 128], F32)}
            TS.append(d)
        yfin = [sb(f"yfin{i}", [128, G, 128], BF16) for i in range(2)]
        ws_ps = [pst(f"ws_ps{i}", [128, G, 128], F32) for i in range(2)]
        o_ps = [pst(f"o_ps{i}", [128, G, 128], F32) for i in range(2)]
        kv_ps = [pst(f"kv_ps{i}", [128, G, 128], F32) for i in range(2)]
        tr_ps2 = [pst(f"tr_ps{q}", [128, 2, G, 128], BF16) for q in range(2)]
        P.dma("sp", identf[:, :], C["ident"][:, :], writes=[r["identf"]])
        P.dma("sp", nrm[:, :], normw_dram[:, :], writes=[r["nrm"]])
        P.op("pool", lambda e: e.tensor_copy(identb[:, :], identf[:, :]), reads=[r["identf"]], writes=[r["identb"]])
        P.op("pool", lambda e: e.memset(epsc[:, :], 1e-6), writes=[r["eps"]])
        P.op("pool", lambda e: e.memset(mh[:, :], -0.5), writes=[r["mh"]])
        P.op("pool", lambda e: e.memset(Sf[:, :, :].rearrange("p h e -> p (h e)"), 0.0), writes=[r["Sf"]])
        P.op("pool", lambda e: e.memset(Sb[:, :, :].rearrange("p h e -> p (h e)"), 0.0), writes=[r["Sb"]])
        def grp(c, g, b, s):
            t0 = c * 128
            T_ = TS[b]
            Stmp, vnew, o_sb, osq, s2, sd, rstd, on, sz, yg = [T_[n] for n in ("Stmp", "vnew", "o_sb", "osq", "s2", "sd", "rstd", "on", "sz", "yg")]
            tr_ps = tr_ps2[b]
            hs = slice(g * G, (g + 1) * G)
            rS = r[f"Sf{g}"]; rSb = r[f"Sb{g}"]
            for h in range(G):
                hh = g * G + h
                P.op("pe", lambda e, h=h, hh=hh, s=s, b=b: e.matmul(ws_ps[b][:, h, :], wkT[s][:, hh, :], Sb[:, hh, :], start=True, stop=True),
                     reads=[r[f"wkT{s}"], rSb, r["Sb"]], writes=[r[f"ws_ps{b}"]])
            yield
            P.op("dve", lambda e, s=s, b=b, hs=hs: e.tensor_tensor(out=vnew[:, :, :], in0=uu[s][:, hs, :], in1=ws_ps[b][:, :, :], op=ALU.subtract),
                 reads=[r[f"uu{s}"], r[f"ws_ps{b}"]], writes=[r["vnew" + str(b)]])
            yield
            for h in range(G):
                hh = g * G + h
                P.op("pe", lambda e, h=h, hh=hh, s=s, b=b: e.matmul(o_ps[b][:, h, :], qdT[s][:, hh, :], Sb[:, hh, :], start=True, stop=False),
                     reads=[r[f"qdT{s}"], rSb, r["Sb"]], writes=[r[f"o_ps{b}"]])
                P.op("pe", lambda e, h=h, hh=hh, s=s, b=b: e.matmul(o_ps[b][:, h, :], atT[s][:, hh, :], vnew[:, h, :], start=False, stop=True),
                     reads=[r[f"atT{s}"], r["vnew" + str(b)]], writes=[r[f"o_ps{b}"]])
            for h in range(G):
                hh = g * G + h
                P.op("pe", lambda e, h=h, hh=hh, s=s, b=b: e.matmul(kv_ps[b][:, h, :], ktl[s][:, hh, :], vnew[:, h, :], start=True, stop=True),
                     reads=[r[f"ktl{s}"], r["vnew" + str(b)]], writes=[r[f"kv_ps{b}"]])
            yield
            P.op("dve", lambda e, s=s, hs=hs, g=g: e.tensor_tensor(out=Stmp[:, :, :], in0=Sf[:, hs, :],
                                                                  in1=els[s][:, g * G:(g + 1) * G].unsqueeze(2).to_broadcast([128, G, 128]),
                                                                  op=ALU.mult),
                 reads=[rS, r["Sf"], r[f"els{s}"]], writes=[r["Stmp" + str(b)]])
            P.op("dve", lambda e, hs=hs, b=b: e.tensor_tensor(out=Sf[:, hs, :], in0=Stmp[:, :, :], in1=kv_ps[b][:, :, :], op=ALU.add),
                 reads=[r["Stmp" + str(b)], r[f"kv_ps{b}"]], writes=[rS])
            P.op("act", lambda e, hs=hs: e.copy(Sb[:, hs, :], Sf[:, hs, :]), reads=[rS], writes=[rSb])
            yield
            P.op("act", lambda e, b=b: e.copy(o_sb[:, :, :], o_ps[b][:, :, :]), reads=[r[f"o_ps{b}"]], writes=[r["o_sb" + str(b)]])
            P.op("pool", lambda e: e.tensor_tensor(out=osq[:, :, :], in0=o_sb[:, :, :], in1=o_sb[:, :, :], op=ALU.mult),
                 reads=[r["o_sb" + str(b)]], writes=[r["osq" + str(b)]])
            yield
            P.op("dve", lambda e: e.tensor_reduce(out=s2[:, :], in_=osq[:, :, :], axis=AX.X, op=ALU.add), reads=[r["osq" + str(b)]], writes=[r["s2" + str(b)]])
            P.op("dve", lambda e: e.tensor_scalar(out=sd[:, :], in0=s2[:, :], scalar1=1.0 / 128, scalar2=1e-6, op0=ALU.mult, op1=ALU.add),
                 reads=[r["s2" + str(b)]], writes=[r["sd" + str(b)]])
            yield
            P.op("pool", lambda e: e.tensor_tensor(out=rstd[:, :], in0=sd[:, :], in1=mh[:, :], op=ALU.pow),
                 reads=[r["sd" + str(b)], r["mh"]], writes=[r["rstd" + str(b)]])
            P.op("dve", lambda e: e.tensor_tensor(out=on[:, :, :], in0=o_sb[:, :, :],
                                                  in1=rstd[:, :].unsqueeze(2).to_broadcast([128, G, 128]), op=ALU.mult),
                 reads=[r["o_sb" + str(b)], r["rstd" + str(b)]], writes=[r["on" + str(b)]])
            yield
            for h in range(G):
                P.op("pe", lambda e, h=h: e.transpose(tr_ps[:, 0, h, :], on[:, h, :], identb[:, :]),
                     reads=[r["on" + str(b)], r["identb"]], writes=[r["tr_ps" + str(b)]])
            P.op("act", lambda e, s=s, hs=hs: e.activation(out=sz[:, :, :], in_=zt[s][:, hs, :], func=AF.Silu),
                 reads=[r[f"zt{s}"]], writes=[r["sz" + str(b)]])
            yield
            P.op("dve", lambda e: e.scalar_tensor_tensor(out=yg[:, :, :], in0=tr_ps[:, 0, :, :], scalar=nrm[:, 0:1], in1=sz[:, :, :],
                                                         op0=ALU.mult, op1=ALU.mult),
                 reads=[r["tr_ps" + str(b)], r["nrm"], r["sz" + str(b)]], writes=[r["yg" + str(b)]])
            P.op("pool", lambda e, b=b: e.tensor_copy(yfin[b][:, :, :], yg[:, :, :]), reads=[r["yg" + str(b)]], writes=[r[f"yfin{b}"]])
            P.dma("pool", yT[kc0 + g * G:kc0 + (g + 1) * G, :, t0:t0 + 128].rearrange("k p t -> p k t"), yfin[b][:, :, :],
                  reads=[r[f"yfin{b}"]], writes=[r["y_out"]])

        it = 0
        for c in range(NCH):
            t0 = c * 128
            s = c % 2
            for g in range(4):
                r0 = g * G * 128
                hs = slice(g * G, (g + 1) * G)
                P.dma("sp", wkT[s][:, hs, :], S["wkT_d"][r0:r0 + G * 128, t0:t0 + 128].rearrange("(h p) t -> p h t", p=128),
                      writes=[r[f"wkT{s}"]])
                P.dma("sp", qdT[s][:, hs, :], S["gqdT"][r0:r0 + G * 128, t0:t0 + 128].rearrange("(h p) t -> p h t", p=128),
                      writes=[r[f"qdT{s}"]])
                P.dma("sp", atT[s][:, hs, :], S["attnT_d"][c, :, g * G:(g + 1) * G, :],
                      writes=[r[f"atT{s}"]])
                P.dma("sp", ktl[s][:, hs, :].rearrange("p h d -> p (h d)"), S["ktl_d"][t0:t0 + 128, r0:r0 + G * 128],
                      writes=[r[f"ktl{s}"]])
                P.dma("sp", uu[s][:, hs, :].rearrange("p h d -> p (h d)"), S["u_d"][t0:t0 + 128, r0:r0 + G * 128],
                      writes=[r[f"uu{s}"]])
                P.dma("sp", zt[s][:, hs, :], projT[zrow + r0:zrow + r0 + G * 128, t0:t0 + 128].rearrange("(h p) t -> p h t", p=128),
                      writes=[r[f"zt{s}"]])
            P.dma("sp", els[s][:, :], S["els_d"][c, :, :], writes=[r[f"els{s}"]])
            for g0 in (0, 2):
                zipper([grp(c, g0, 0, s), grp(c, g0 + 1, 1, s)])
        P.barrier()
        P.flush()


def rwkv_consts():
    c = {}
    bo = np.zeros((128, 128), np.float32)
    bo[:64, :64] = 1.0
    bo[64:, 64:] = 1.0
    c["blockones"] = bo
    return c


def phase_rwkv_pre(P, nc, projT, S, C, prm, T, row0):
    NB = T // 512
    with contextlib.ExitStack() as st:
        def sb(name, shape, dt):
            return st.enter_context(nc.sbuf_tensor(_uid() + "wp_" + name, shape, dt))

        def pst(name, shape, dt):
            return st.enter_context(nc.psum_tensor(_uid() + "wp_" + name, shape, dt))
        r = defaultdict(Res)
        mu = sb("mu", [128, 33], F32); omm = sb("omm", [128, 33], F32)
        w0 = sb("w0", [128, 8], F32); nw0 = sb("nw0", [128, 8], F32); a0 = sb("a0", [128, 8], F32)
        kk_ = sb("kk_", [128, 8], F32); ka = sb("ka", [128, 8], F32); omka = sb("omka", [128, 8], F32); rk = sb("rk", [128, 8], F32)
        lw2 = sb("lw2", [128, 1024], F32)
        bones = sb("bones", [128, 128], F32)
        cmask = sb("cmask", [128, 512], F32)
        onec = sb("onec", [128, 1], F32); negh = sb("negh", [128, 1], F32); epsc = sb("epsc", [128, 1], F32)
        ul = sb("ul", [128, 513], F32); ml = sb("ml", [128, 512], F32); th = sb("th", [128, 512], F32)
        TS = []
        for q in range(2):
            d = {}
            for j in range(4):
                d[f"u{j}"] = sb(f"u{j}_{q}", [128, 513], F32); d[f"mx{j}"] = sb(f"mx{j}_{q}", [128, 512], F32)
            for n in ("tmp", "e1", "spt", "e2", "cw", "ecw", "cwm", "ecwm", "einv", "av", "kk0", "sq", "sd", "rn", "kkn", "fac", "k2", "kka", "prod"):
                d[n] = sb(f"{n}_{q}", [128, 512], F32)
            TS.append(d)
        tmp = sb("tmp_l", [128, 512], F32)
        obf = {n: [sb(f"o_{n}{i}", [128, 512], BF16) for i in range(2)] for n in ("rt", "kkt", "kh", "kka", "v")}
        of32 = {n: [sb(f"o_{n}{i}", [128, 512], F32) for i in range(2)] for n in ("bon", "sz")}
        pcs = [sb(f"pcs{i}", [128, 4], F32) for i in range(2)]
        for q in range(2):
            for n in ("wl_ps", "a_ps", "ss_ps", "sb_ps"):
                TS[q][n] = pst(f"{n}_{q}", [128, 512], F32)
        for nm, t_, src in (("mu", mu, "muT"), ("w0", w0, "w0T"), ("a0", a0, "a0T"), ("kk_", kk_, "kkT"), ("ka", ka, "kaT"),
                            ("rk", rk, "rkT"), ("lw2", lw2, "lw2")):
            P.dma("sp", t_[:, :], prm[src][:, :], writes=[r[nm]])
        P.dma("sp", bones[:, :], C["blockones"][:, :], writes=[r["bones"]])
        P.dma("sp", cmask[:, :], C["cmask128"][:, :], writes=[r["cmask"]])
        P.op("pool", lambda e: e.memset(onec[:, :], 1.0), writes=[r["onec"]])
        P.op("pool", lambda e: e.memset(negh[:, :], -0.5), writes=[r["negh"]])
        P.op("pool", lambda e: e.memset(epsc[:, :], 1e-6), writes=[r["eps"]])
        P.op("dve", lambda e: e.tensor_scalar(out=omm[:, :], in0=mu[:, :], scalar1=-1.0, scalar2=1.0, op0=ALU.mult, op1=ALU.add),
             reads=[r["mu"]], writes=[r["omm"]])
        P.op("dve", lambda e: e.tensor_scalar(out=omka[:, :], in0=ka[:, :], scalar1=-1.0, scalar2=1.0, op0=ALU.mult, op1=ALU.add),
             reads=[r["ka"]], writes=[r["omka"]])
        P.op("dve", lambda e: e.tensor_scalar(out=nw0[:, :], in0=w0[:, :], scalar1=-1.0, scalar2=None, op0=ALU.mult),
             reads=[r["w0"]], writes=[r["nw0"]])

        def load_mix(ut, rut, mt, rmt, row, ti, tb, tmp=tmp, rtmp=None):
            rtmp = rtmp if rtmp is not None else r["tmp_l"]
            t0 = tb * 512
            if tb == 0:
                P.op("pool", lambda e: e.memset(ut[:, 0:1], 0.0), writes=[rut])
                P.dma("sp", ut[:, 1:513], projT[row:row + 128, 0:512], writes=[rut])
            else:
                P.dma("sp", ut[:, :], projT[row:row + 128, t0 - 1:t0 + 512], writes=[rut])
            P.op("dve", lambda e: e.tensor_scalar(out=tmp[:, :], in0=ut[:, 1:513], scalar1=omm[:, ti:ti + 1], scalar2=None, op0=ALU.mult),
                 reads=[rut, r["omm"]], writes=[rtmp])
            P.op("dve", lambda e: e.scalar_tensor_tensor(out=mt[:, :], in0=ut[:, 0:512], scalar=mu[:, ti:ti + 1], in1=tmp[:, :],
                                                         op0=ALU.mult, op1=ALU.add),
                 reads=[rut, r["mu"], rtmp], writes=[rmt])
        def do_ct(ct, tb, s):
            t0 = tb * 512
            T_ = TS[s]
            rq = lambda n: r[f"{n}_{s}"]
            (e1, spt, e2, cw, ecw, cwm, ecwm, einv, av, kk0, sq, sd, rn, kkn, fac, k2, kka, prod) = [T_[n] for n in (
                "e1", "spt", "e2", "cw", "ecw", "cwm", "ecwm", "einv", "av", "kk0", "sq", "sd", "rn", "kkn", "fac", "k2", "kka", "prod")]
            wl_ps, a_ps, ss_ps, sb_ps = T_["wl_ps"], T_["a_ps"], T_["ss_ps"], T_["sb_ps"]
            for j in range(4):
                load_mix(T_[f"u{j}"], rq(f"u{j}"), T_[f"mx{j}"], rq(f"mx{j}"), row0 + j * 1024 + ct * 128, j * 8 + ct, tb, T_["tmp"], rq("tmp"))
            rm, km, vm, zm = T_['mx0'], T_['mx1'], T_['mx2'], T_['mx3']
            yield
            P.op("pe", lambda e, ct=ct: e.matmul(wl_ps[:, :], lw2[0:64, ct * 128:(ct + 1) * 128], th[0:64, :], start=True, stop=True),
                 reads=[r["lw2"], r["th"]], writes=[rq("wl_ps")])
            P.op("pe", lambda e, ct=ct: e.matmul(a_ps[:, :], lw2[64:128, ct * 128:(ct + 1) * 128], ml[64:128, :], start=True, stop=True),
                 reads=[r["lw2"], r["ml"]], writes=[rq("a_ps")])
            P.op("act", lambda e, ct=ct: e.activation(out=e1[:, :], in_=wl_ps[:, :], func=AF.Exp, bias=nw0[:, ct:ct + 1], scale=-1.0),
                 reads=[rq("wl_ps"), r["nw0"]], writes=[rq("e1")])
            P.op("act", lambda e: e.activation(out=spt[:, :], in_=e1[:, :], func=AF.Ln, bias=onec[:, 0:1], scale=1.0),
                 reads=[rq("e1"), r["onec"]], writes=[rq("spt")])
            P.op("act", lambda e: e.activation(out=e2[:, :], in_=spt[:, :], func=AF.Exp, bias=negh[:, 0:1], scale=-1.0),
                 reads=[rq("spt"), r["negh"]], writes=[rq("e2")])
            P.op("dve", lambda e: e.tensor_tensor_scan(out=cw[:, :], data0=cmask[:, :], data1=e2[:, :], initial=0.0,
                                                       op0=ALU.mult, op1=ALU.subtract),
                 reads=[r["cmask"], rq("e2")], writes=[rq("cw")])
            P.op("act", lambda e: e.activation(out=ecw[:, :], in_=cw[:, :], func=AF.Exp), reads=[rq("cw")], writes=[rq("ecw")])
            P.op("pool", lambda e: e.tensor_tensor(out=cwm[:, :], in0=cw[:, :], in1=e2[:, :], op=ALU.add),
                 reads=[rq("cw"), rq("e2")], writes=[rq("cwm")])
            P.op("act", lambda e: e.activation(out=ecwm[:, :], in_=cwm[:, :], func=AF.Exp), reads=[rq("cwm")], writes=[rq("ecwm")])
            P.op("act", lambda e: e.activation(out=einv[:, :], in_=cw[:, :], func=AF.Exp, scale=-1.0), reads=[rq("cw")], writes=[rq("einv")])
            yield
            P.op("act", lambda e, ct=ct: e.activation(out=av[:, :], in_=a_ps[:, :], func=AF.Sigmoid, bias=a0[:, ct:ct + 1], scale=1.0),
                 reads=[rq("a_ps"), r["a0"]], writes=[rq("av")])
            yield
            P.op("dve", lambda e, ct=ct: e.tensor_scalar(out=kk0[:, :], in0=km[:, :], scalar1=kk_[:, ct:ct + 1], scalar2=None, op0=ALU.mult),
                 reads=[rq("mx1"), r["kk_"]], writes=[rq("kk0")])
            P.op("pool", lambda e: e.tensor_tensor(out=sq[:, :], in0=kk0[:, :], in1=kk0[:, :], op=ALU.mult), reads=[rq("kk0")], writes=[rq("sq")])
            P.op("pe", lambda e: e.matmul(ss_ps[:, :], bones[:, :], sq[:, :], start=True, stop=True),
                 reads=[r["bones"], rq("sq")], writes=[rq("ss_ps")])
            yield
            P.op("act", lambda e: e.activation(out=sd[:, :], in_=ss_ps[:, :], func=AF.Sqrt, bias=epsc[:, 0:1], scale=1.0),
                 reads=[rq("ss_ps"), r["eps"]], writes=[rq("sd")])
            P.op("dve", lambda e: e.reciprocal(rn[:, :], sd[:, :]), reads=[rq("sd")], writes=[rq("rn")])
            P.op("dve", lambda e: e.tensor_tensor(out=kkn[:, :], in0=kk0[:, :], in1=rn[:, :], op=ALU.mult),
                 reads=[rq("kk0"), rq("rn")], writes=[rq("kkn")])
            P.op("dve", lambda e, ct=ct: e.tensor_scalar(out=fac[:, :], in0=av[:, :], scalar1=ka[:, ct:ct + 1], scalar2=omka[:, ct:ct + 1],
                                                         op0=ALU.mult, op1=ALU.add),
                 reads=[rq("av"), r["ka"], r["omka"]], writes=[rq("fac")])
            P.op("dve", lambda e: e.tensor_tensor(out=k2[:, :], in0=km[:, :], in1=fac[:, :], op=ALU.mult),
                 reads=[rq("mx1"), rq("fac")], writes=[rq("k2")])
            P.op("pool", lambda e: e.tensor_tensor(out=kka[:, :], in0=kkn[:, :], in1=av[:, :], op=ALU.mult),
                 reads=[rq("kkn"), rq("av")], writes=[rq("kka")])
            yield
            P.op("dve", lambda e, s=s: e.tensor_tensor(out=obf["rt"][s][:, :], in0=rm[:, :], in1=ecw[:, :], op=ALU.mult),
                 reads=[rq("mx0"), rq("ecw")], writes=[r[f"o_rt{s}"]])
            P.op("dve", lambda e, s=s: e.tensor_tensor(out=obf["kkt"][s][:, :], in0=kkn[:, :], in1=ecwm[:, :], op=ALU.mult),
                 reads=[rq("kkn"), rq("ecwm")], writes=[r[f"o_kkt{s}"]])
            P.op("dve", lambda e, s=s: e.tensor_tensor(out=obf["kh"][s][:, :], in0=k2[:, :], in1=einv[:, :], op=ALU.mult),
                 reads=[rq("k2"), rq("einv")], writes=[r[f"o_kh{s}"]])
            P.op("pool", lambda e, s=s: e.tensor_tensor(out=obf["kka"][s][:, :], in0=kka[:, :], in1=einv[:, :], op=ALU.mult),
                 reads=[rq("kka"), rq("einv")], writes=[r[f"o_kka{s}"]])
            P.op("pool", lambda e, s=s: e.tensor_copy(obf["v"][s][:, :], vm[:, :]), reads=[rq("mx2")], writes=[r[f"o_v{s}"]])
            P.op("act", lambda e, s=s: e.activation(out=of32["sz"][s][:, :], in_=zm[:, :], func=AF.Silu), reads=[rq("mx3")], writes=[r[f"o_sz{s}"]])
            yield
            P.op("pool", lambda e: e.tensor_tensor(out=prod[:, :], in0=rm[:, :], in1=k2[:, :], op=ALU.mult),
                 reads=[rq("mx0"), rq("k2")], writes=[rq("prod")])
            P.op("dve", lambda e, ct=ct: e.tensor_scalar(out=prod[:, :], in0=prod[:, :], scalar1=rk[:, ct:ct + 1], scalar2=None, op0=ALU.mult),
                 reads=[rq("prod"), r["rk"]], writes=[rq("prod")])
            P.op("pe", lambda e: e.matmul(sb_ps[:, :], bones[:, :], prod[:, :], start=True, stop=True),
                 reads=[r["bones"], rq("prod")], writes=[rq("sb_ps")])
            P.op("dve", lambda e, s=s: e.tensor_tensor(out=of32["bon"][s][:, :], in0=vm[:, :], in1=sb_ps[:, :], op=ALU.mult),
                 reads=[rq("mx2"), rq("sb_ps")], writes=[r[f"o_bon{s}"]])
            P.op("pool", lambda e, s=s: e.tensor_copy(pcs[s][:, :], ecw[:, 127:512:128]), reads=[rq("ecw")], writes=[r[f"pcs{s}"]])
            rows_ = slice(ct * 128, (ct + 1) * 128)
            for n, dst in (("rt", "rtT"), ("kkt", "kktT"), ("kh", "khT"), ("kka", "kkaT"), ("v", "rvT")):
                P.dma("pool", S[dst][rows_, t0:t0 + 512], obf[n][s][:, :], reads=[r[f"o_{n}{s}"]], writes=[r[dst]])
            for n, dst in (("bon", "bonT"), ("sz", "szT")):
                P.dma("pool", S[dst][rows_, t0:t0 + 512], of32[n][s][:, :], reads=[r[f"o_{n}{s}"]], writes=[r[dst]])
            P.dma("pool", S["pc_d"][ct, :, tb * 4:(tb + 1) * 4], pcs[s][:, :], reads=[r[f"pcs{s}"]], writes=[r["pc_d"]])

        cnt = 0
        for tb in range(NB):
            t0 = tb * 512
            load_mix(ul, r["ul"], ml, r["ml"], row0 + 4096, 32, tb)
            P.op("act", lambda e: e.activation(out=th[0:64, :], in_=ml[0:64, :], func=AF.Tanh), reads=[r["ml"]], writes=[r["th"]])
            for ct in range(0, 8, 2):
                zipper([do_ct(ct, tb, 0), do_ct(ct + 1, tb, 1)])
        P.barrier()
        P.flush()


def phase_rwkv_r1(P, nc, S, C, T):
    NCH = T // 128
    G = 4
    with contextlib.ExitStack() as st:
        def sb(name, shape, dt):
            return st.enter_context(nc.sbuf_tensor(_uid() + "r1_" + name, shape, dt))

        def pst(name, shape, dt):
            return st.enter_context(nc.psum_tensor(_uid() + "r1_" + name, shape, dt))
        r = defaultdict(Res)
        mls = sb("mls", [128, 128], F32); mus = sb("mus", [128, 128], F32); mui = sb("mui", [128, 128], F32); nmui = sb("nmui", [128, 128], F32)
        identf = sb("identf", [128, 128], F32); identb = sb("identb", [128, 128], BF16)
        tl = {n: [sb(f"{n}{i}", [128, 2, 128], BF16) for i in range(2)] for n in ("rt", "kkt", "kh", "kka", "v")}
        tz = {n: [sb(f"z{n}{i}", [128, 2, 2, 128], BF16) for i in range(2)] for n in ("kkt", "kh", "kka")}
        L = sb("L", [128, G, 128], BF16); LT = sb("LT", [128, G, 128], BF16)
        om = {n: [sb(f"{n}{i}", [128, G, 128], BF16) for i in range(2)] for n in ("akv", "bkv", "nbab", "tinv")}
        tk = {n: [sb(f"tk_{n}{i}", [128, 2, 128], BF16) for i in range(2)] for n in ("v", "kh", "kka")}
        W = {"identb": identb,
             "nL": [sb(f"nL{i}", [128, G, 128], BF16) for i in range(2)],
             "nLT": [sb(f"nLT{i}", [128, G, 128], BF16) for i in range(2)],
             "Pk": [sb(f"Pk{i}", [128, G, 128], BF16) for i in range(2)],
             "pa": pst("pa", [128, G, 128], F32), "pb": pst("pb", [128, G, 128], F32), "pp": pst("pp", [128, G, 128], F32)}
        s_ps = [pst(f"s_ps{i}", [128, G, 128], F32) for i in range(3)]
        tr_ps = pst("tr_ps", [128, 8, 128], BF16)
        for nm, t_, src in (("mls", mls, "mask_ls"), ("mus", mus, "mask_us"), ("mui", mui, "mask_ui"), ("identf", identf, "ident")):
            P.dma("sp", t_[:, :], C[src][:, :], writes=[r[nm]])
        P.op("pool", lambda e: e.tensor_copy(identb[:, :], identf[:, :]), reads=[r["identf"]], writes=[r["identb"]])
        P.op("dve", lambda e: e.tensor_scalar(out=nmui[:, :], in0=mui[:, :], scalar1=-1.0, scalar2=None, op0=ALU.mult),
             reads=[r["mui"]], writes=[r["nmui"]])

        def bcm(m):
            return m[:, :].unsqueeze(1).to_broadcast([128, G, 128])
        it = 0
        srcs = {"rt": "rtT", "kkt": "kktT", "kh": "khT", "kka": "kkaT", "v": "rvT"}
        for n in tz:
            for i in range(2):
                P.op("pool", lambda e, n=n, i=i: e.memset(tz[n][i][:, :, :, :].rearrange("p a q t -> p (a q t)"), 0.0), writes=[r[f"z{n}{i}"]])
        for c in range(NCH):
            t0 = c * 128
            for g in range(4):
                s = it % 2
                it += 1
                for n in tl:
                    P.dma("sp", tl[n][s][:, :, :], S[srcs[n]][g * 256:(g + 1) * 256, t0:t0 + 128].rearrange("(q p) t -> p q t", p=128),
                          writes=[r[f"{n}{s}"]])

                for n in tz:
                    srcv = S[srcs[n]][g * 256:(g + 1) * 256, t0:t0 + 128].rearrange("(q p) t -> p q t", p=128)
                    P.dma("sp", tz[n][s][0:64, 0, :, :], srcv[0:64], writes=[r[f"z{n}{s}"]])
                    P.dma("sp", tz[n][s][64:128, 1, :, :], srcv[64:128], writes=[r[f"z{n}{s}"]])

                def hz(n, h):
                    return tz[n][s][:, h % 2, h // 2, :]

                def hv(n, h):
                    return tl[n][s][:, h // 2, :]
                for h in range(G):
                    P.op("pe", lambda e, h=h, a=hz("kkt", h), b=hv("kka", h): e.matmul(s_ps[0][:, h, :], a, b, start=True, stop=True),
                         reads=[r[f"zkkt{s}"], r[f"kka{s}"]], writes=[r["s_ps0"]])
                for h in range(G):
                    P.op("pe", lambda e, h=h, a=hz("kka", h), b=hv("kkt", h): e.matmul(s_ps[1][:, h, :], a, b, start=True, stop=True),
                         reads=[r[f"zkka{s}"], r[f"kkt{s}"]], writes=[r["s_ps1"]])
                P.op("dve", lambda e: e.tensor_tensor(out=L[:, :, :], in0=s_ps[0][:, :, :], in1=bcm(mls), op=ALU.mult),
                     reads=[r["s_ps0"], r["mls"]], writes=[r["L"]])
                P.op("dve", lambda e: e.tensor_tensor(out=LT[:, :, :], in0=s_ps[1][:, :, :], in1=bcm(mus), op=ALU.mult),
                     reads=[r["s_ps1"], r["mus"]], writes=[r["LT"]])
                for h in range(G):
                    P.op("pe", lambda e, h=h, a=hz("kh", h), b=hv("kkt", h): e.matmul(s_ps[2][:, h, :], a, b, start=True, stop=True),
                         reads=[r[f"zkh{s}"], r[f"kkt{s}"]], writes=[r["s_ps2"]])
                P.op("dve", lambda e, s=s: e.tensor_tensor(out=om["akv"][s][:, :, :], in0=s_ps[2][:, :, :], in1=bcm(mus), op=ALU.mult),
                     reads=[r["s_ps2"], r["mus"]], writes=[r[f"akv{s}"]])
                for h in range(G):
                    P.op("pe", lambda e, h=h, a=hz("kh", h), b=hv("rt", h): e.matmul(s_ps[0][:, h, :], a, b, start=True, stop=True),
                         reads=[r[f"zkh{s}"], r[f"rt{s}"]], writes=[r["s_ps0"]])
                P.op("dve", lambda e, s=s: e.tensor_tensor(out=om["bkv"][s][:, :, :], in0=s_ps[0][:, :, :], in1=bcm(mui), op=ALU.mult),
                     reads=[r["s_ps0"], r["mui"]], writes=[r[f"bkv{s}"]])
                for h in range(G):
                    P.op("pe", lambda e, h=h, a=hz("kka", h), b=hv("rt", h): e.matmul(s_ps[1][:, h, :], a, b, start=True, stop=True),
                         reads=[r[f"zkka{s}"], r[f"rt{s}"]], writes=[r["s_ps1"]])
                P.op("dve", lambda e, s=s: e.tensor_tensor(out=om["nbab"][s][:, :, :], in0=s_ps[1][:, :, :], in1=bcm(mui), op=ALU.mult),
                     reads=[r["s_ps1"], r["mui"]], writes=[r[f"nbab{s}"]])
                Pt, rPt = neumann(P, nc, L, LT, r["L"], r["LT"], W, r)
                P.op("pool", lambda e, s=s, Pt=Pt: e.tensor_copy(om["tinv"][s][:, :, :], Pt[:, :, :]), reads=[rPt], writes=[r[f"tinv{s}"]])
                for n, dst in (("akv", "akvT_d"), ("bkv", "bkvT_d"), ("nbab", "nbabT_d"), ("tinv", "tinvT_d")):
                    P.dma("pool", S[dst][c, :, g * G:(g + 1) * G, :], om[n][s][:, :, :],
                          reads=[r[f"{n}{s}"]], writes=[r[dst]])
                for qi, n in enumerate(("v", "kh", "kka")):
                    for q in range(2):
                        P.op("pe", lambda e, qi=qi, q=q, n=n, s=s: e.transpose(tr_ps[:, qi * 2 + q, :], tl[n][s][:, q, :], identb[:, :]),
                             reads=[r[f"{n}{s}"], r["identb"]], writes=[r["tr_ps"]])
                for qi, (n, dst) in enumerate((("v", "vtk_d"), ("kh", "khtk_d"), ("kka", "kkatk_d"))):
                    P.op("act", lambda e, qi=qi, n=n, s=s: e.copy(tk[n][s][:, :, :], tr_ps[:, qi * 2:qi * 2 + 2, :]),
                         reads=[r["tr_ps"]], writes=[r[f"tk_{n}{s}"]])
                    P.dma("pool", S[dst][t0:t0 + 128, g * 256:(g + 1) * 256], tk[n][s][:, :, :].rearrange("p q d -> p (q d)"),
                          reads=[r[f"tk_{n}{s}"]], writes=[r[dst]])
        P.barrier()
        P.flush()


def phase_rwkv_r2(P, nc, S, C, yT, prm, T, kc0=8, eps=64e-5):
    NCH = T // 128
    H = 16
    with contextlib.ExitStack() as st:
        def sb(name, shape, dt):
            return st.enter_context(nc.sbuf_tensor(_uid() + "r2_" + name, shape, dt))

        def pst(name, shape, dt):
            return st.enter_context(nc.psum_tensor(_uid() + "r2_" + name, shape, dt))
        r = defaultdict(Res)
        identf = sb("identf", [128, 128], F32); identb = sb("identb", [128, 128], BF16)
        lnw = sb("lnw", [128, 8], F32); lnb = sb("lnb", [128, 8], F32); epsc = sb("epsc", [128, 1], F32); mh = sb("mh", [128, 8], F32)
        pc = sb("pc", [128, 8, NCH], F32)
        kkt = [sb(f"kkt{i}", [128, 2, 8, 128], BF16) for i in range(2)]
        rt = [sb(f"rt{i}", [128, 2, 8, 128], BF16) for i in range(2)]
        tkz = {n: [sb(f"tkz_{n}{i}", [128, 2, 8, 128], BF16) for i in range(2)] for n in ("kh", "kka")}
        mm = {n: [sb(f"{n}{i}", [128, H, 128], BF16) for i in range(2)] for n in ("akv", "bkv", "nbab", "tinv")}
        tk = {n: [sb(f"tk_{n}{i}", [128, 1024], BF16) for i in range(2)] for n in ("v",)}
        bon = [sb(f"bon{i}", [128, 8, 128], F32) for i in range(2)]
        szt = [sb(f"szt{i}", [128, 8, 128], F32) for i in range(2)]
        Tf = sb("Tf", [128, 8, 64], F32); Tb = sb("Tb", [128, 8, 64], BF16)
        yfin = [sb(f"yfin{i}", [128, 4, 128], BF16) for i in range(2)]
        TS = []
        for q in range(2):
            d = {"Ttmp": sb(f"Ttmp{q}", [128, 4, 64], F32), "rhs0": sb(f"rhs0{q}", [128, 8, 64], BF16), "nU": sb(f"nU{q}", [128, 8, 64], BF16),
                 "y_sb": sb(f"y_sb{q}", [128, 8, 64], F32), "ysq": sb(f"ysq{q}", [128, 8, 64], F32),
                 "yc": sb(f"yc{q}", [128, 8, 64], F32), "yn": sb(f"yn{q}", [128, 8, 64], BF16),
                 "t1": sb(f"t1{q}", [128, 4, 128], F32), "t2": sb(f"t2{q}", [128, 4, 128], F32)}
            for n in ("s1", "s2", "mean", "var", "sd", "rstd"):
                d[n] = sb(f"{n}{q}", [128, 8], F32)
            d["r0_ps"] = pst(f"r0_ps{q}", [128, 8, 64], F32)
            d["u_ps"] = d["r0_ps"]
            d["y_ps"] = pst(f"y_ps{q}", [128, 8, 64], F32)
            d["st_ps"] = pst(f"st_ps{q}", [128, 512], F32)[:, 0:256].rearrange("p (q e) -> p q e", e=64)
            d["tr_ps"] = pst(f"tr_ps{q}", [128, 1024], BF16)[:, 0:512].rearrange("p (q t) -> p q t", t=128)
            TS.append(d)
        P.dma("sp", identf[:, :], C["ident"][:, :], writes=[r["identf"]])
        P.dma("sp", lnw[:, :], prm["lnwT"][:, :], writes=[r["lnw"]])
        P.dma("sp", lnb[:, :], prm["lnbT"][:, :], writes=[r["lnb"]])
        for q in range(8):
            P.dma("sp", pc[:, q, :], S["pc_d"][q, :, :], writes=[r["pc"]])
        P.op("pool", lambda e: e.tensor_copy(identb[:, :], identf[:, :]), reads=[r["identf"]], writes=[r["identb"]])
        P.op("pool", lambda e: e.memset(epsc[:, :], eps), writes=[r["eps"]])
        P.op("pool", lambda e: e.memset(mh[:, :], -0.5), writes=[r["mh"]])
        P.op("pool", lambda e: e.memset(Tf[:, :, :].rearrange("p q e -> p (q e)"), 0.0), writes=[r["Tf"]])
        P.op("pool", lambda e: e.memset(Tb[:, :, :].rearrange("p q e -> p (q e)"), 0.0), writes=[r["Tb"]])
        def grp(c, gi, b, s):
            t0 = c * 128
            T_ = TS[b]
            (Ttmp, rhs0, nU, y_sb, ysq, yc, yn, t1, t2, s1, s2, mean, var, sd, rstd, r0_ps, u_ps, y_ps, st_ps, tr_ps) = [T_[n] for n in (
                "Ttmp", "rhs0", "nU", "y_sb", "ysq", "yc", "yn", "t1", "t2", "s1", "s2", "mean", "var", "sd", "rstd", "r0_ps", "u_ps", "y_ps", "st_ps", "tr_ps")]
            rT = r[f"Tf{gi}"]; rTb = r[f"Tb{gi}"]
            for h in range(8):
                hh = gi * 8 + h
                p_ = hh // 2
                ba = (hh % 2) * 64
                P.op("pe", lambda e, h=h, hh=hh, p_=p_, ba=ba, s=s: e.matmul(r0_ps[:, h, :], kkt[s][:, ba // 64, p_, :], Tb[:, p_, :],
                                                                          start=True, stop=False),
                     reads=[r[f"kkt{s}"], rTb, r["Tb"]], writes=[r["ru_ps" + str(b)]])
                P.op("pe", lambda e, h=h, hh=hh, s=s: e.matmul(r0_ps[:, h, :], mm["akv"][s][:, hh, :], tk["v"][s][:, hh * 64:(hh + 1) * 64],
                                                              start=False, stop=True),
                     reads=[r[f"akv{s}"], r[f"tk_v{s}"]], writes=[r["ru_ps" + str(b)]])
            yield
            P.op("act", lambda e: e.copy(rhs0[:, :, :], r0_ps[:, :, :]), reads=[r["ru_ps" + str(b)]], writes=[r["rhs0" + str(b)]])
            yield
            for h in range(8):
                hh = gi * 8 + h
                P.op("pe", lambda e, h=h, hh=hh, s=s: e.matmul(u_ps[:, h, :], mm["tinv"][s][:, hh, :], rhs0[:, h, :], start=True, stop=True),
                     reads=[r[f"tinv{s}"], r["rhs0" + str(b)]], writes=[r["ru_ps" + str(b)]])
            yield
            P.op("dve", lambda e: e.tensor_scalar(out=nU[:, :, :], in0=u_ps[:, :, :], scalar1=-1.0, scalar2=None, op0=ALU.mult),
                 reads=[r["ru_ps" + str(b)]], writes=[r["nU" + str(b)]])
            for h in range(8):
                hh = gi * 8 + h
                p_ = hh // 2
                ba = (hh % 2) * 64
                P.op("pe", lambda e, h=h, p_=p_, ba=ba, s=s: e.matmul(y_ps[:, h, :], rt[s][:, ba // 64, p_, :], Tb[:, p_, :],
                                                                   start=True, stop=False),
                     reads=[r[f"rt{s}"], rTb, r["Tb"]], writes=[r["y_ps" + str(b)]])
                P.op("pe", lambda e, h=h, hh=hh, s=s: e.matmul(y_ps[:, h, :], mm["bkv"][s][:, hh, :], tk["v"][s][:, hh * 64:(hh + 1) * 64],
                                                              start=False, stop=False),
                     reads=[r[f"bkv{s}"], r[f"tk_v{s}"]], writes=[r["y_ps" + str(b)]])
                P.op("pe", lambda e, h=h, hh=hh, s=s: e.matmul(y_ps[:, h, :], mm["nbab"][s][:, hh, :], nU[:, h, :], start=False, stop=True),
                     reads=[r[f"nbab{s}"], r["nU" + str(b)]], writes=[r["y_ps" + str(b)]])
            yield
            for pl in range(4):
                q = gi * 4 + pl
                seq = [("kh", 0, "v"), ("kh", 1, "v"), ("kka", 0, "u"), ("kka", 1, "u")]
                for i_, (n, a, rk_) in enumerate(seq):
                    hh = 2 * q + a
                    if rk_ == "v":
                        P.op("pe", lambda e, pl=pl, q=q, a=a, n=n, hh=hh, s=s, i_=i_: e.matmul(st_ps[:, pl, :], tkz[n][s][:, a, q, :],
                                                                                           tk["v"][s][:, hh * 64:(hh + 1) * 64],
                                                                                           start=(i_ == 0), stop=(i_ == 3)),
                             reads=[r[f"tkz_{n}{s}"], r[f"tk_v{s}"]], writes=[r["st_ps" + str(b)]])
                    else:
                        P.op("pe", lambda e, pl=pl, q=q, a=a, n=n, hh=hh, s=s, i_=i_, gi=gi: e.matmul(st_ps[:, pl, :], tkz[n][s][:, a, q, :],
                                                                                                  nU[:, hh - gi * 8, :],
                                                                                                  start=(i_ == 0), stop=(i_ == 3)),
                             reads=[r[f"tkz_{n}{s}"], r["nU" + str(b)]], writes=[r["st_ps" + str(b)]])
            yield
            qs = slice(gi * 4, gi * 4 + 4)
            P.op("dve", lambda e, qs=qs: e.tensor_tensor(out=Ttmp[:, :, :], in0=st_ps[:, :, :], in1=Tf[:, qs, :], op=ALU.add),
                 reads=[r["st_ps" + str(b)], rT, r["Tf"]], writes=[r["Ttmp" + str(b)]])
            P.op("dve", lambda e, qs=qs, c=c: e.tensor_tensor(out=Tf[:, qs, :], in0=Ttmp[:, :, :],
                                                              in1=pc[:, qs, c:c + 1].to_broadcast([128, 4, 64]), op=ALU.mult),
                 reads=[r["Ttmp" + str(b)], r["pc"]], writes=[rT])
            P.op("act", lambda e, qs=qs: e.copy(Tb[:, qs, :], Tf[:, qs, :]), reads=[rT], writes=[rTb])
            yield
            P.op("act", lambda e: e.copy(y_sb[:, :, :], y_ps[:, :, :]), reads=[r["y_ps" + str(b)]], writes=[r["y_sb" + str(b)]])
            P.op("dve", lambda e: e.tensor_reduce(out=s1[:, :], in_=y_sb[:, :, :], axis=AX.X, op=ALU.add), reads=[r["y_sb" + str(b)]], writes=[r["s1" + str(b)]])
            P.op("pool", lambda e: e.tensor_tensor(out=ysq[:, :, :], in0=y_sb[:, :, :], in1=y_sb[:, :, :], op=ALU.mult),
                 reads=[r["y_sb" + str(b)]], writes=[r["ysq" + str(b)]])
            yield
            P.op("dve", lambda e: e.tensor_reduce(out=s2[:, :], in_=ysq[:, :, :], axis=AX.X, op=ALU.add), reads=[r["ysq" + str(b)]], writes=[r["s2" + str(b)]])
            P.op("dve", lambda e: e.tensor_scalar(out=mean[:, :], in0=s1[:, :], scalar1=1.0 / 64, scalar2=None, op0=ALU.mult),
                 reads=[r["s1" + str(b)]], writes=[r["mean" + str(b)]])
            P.op("dve", lambda e: e.tensor_tensor(out=var[:, :], in0=mean[:, :], in1=mean[:, :], op=ALU.mult), reads=[r["mean" + str(b)]], writes=[r["var" + str(b)]])
            P.op("dve", lambda e: e.scalar_tensor_tensor(out=var[:, :], in0=s2[:, :], scalar=1.0 / 64, in1=var[:, :],
                                                         op0=ALU.mult, op1=ALU.subtract),
                 reads=[r["s2" + str(b)], r["var" + str(b)]], writes=[r["var" + str(b)]])
            P.op("dve", lambda e: e.tensor_scalar(out=sd[:, :], in0=var[:, :], scalar1=1.0, scalar2=eps, op0=ALU.mult, op1=ALU.add),
                 reads=[r["var" + str(b)]], writes=[r["sd" + str(b)]])
            yield
            P.op("pool", lambda e: e.tensor_tensor(out=rstd[:, :], in0=sd[:, :], in1=mh[:, :], op=ALU.pow),
                 reads=[r["sd" + str(b)], r["mh"]], writes=[r["rstd" + str(b)]])
            P.op("dve", lambda e: e.tensor_tensor(out=yc[:, :, :], in0=y_sb[:, :, :], in1=mean[:, :].unsqueeze(2).to_broadcast([128, 8, 64]),
                                                  op=ALU.subtract),
                 reads=[r["y_sb" + str(b)], r["mean" + str(b)]], writes=[r["yc" + str(b)]])
            P.op("dve", lambda e: e.tensor_tensor(out=yn[:, :, :], in0=yc[:, :, :], in1=rstd[:, :].unsqueeze(2).to_broadcast([128, 8, 64]),
                                                  op=ALU.mult),
                 reads=[r["yc" + str(b)], r["rstd" + str(b)]], writes=[r["yn" + str(b)]])
            yield
            ynp = yn[:, :, :].rearrange("p (q a) e -> p q (a e)", a=2)
            for q in range(4):
                P.op("pe", lambda e, q=q: e.transpose(tr_ps[:, q, :], ynp[:, q, :], identb[:, :]),
                     reads=[r["yn" + str(b)], r["identb"]], writes=[r["tr_ps" + str(b)]])
            yield
            P.op("dve", lambda e, qs=qs: e.tensor_tensor(out=t1[:, :, :], in0=tr_ps[:, :, :],
                                                         in1=lnw[:, qs].unsqueeze(2).to_broadcast([128, 4, 128]), op=ALU.mult),
                 reads=[r["tr_ps" + str(b)], r["lnw"]], writes=[r["t1" + str(b)]])
            P.op("pool", lambda e, qs=qs: e.tensor_tensor(out=t2[:, :, :], in0=t1[:, :, :],
                                                          in1=lnb[:, qs].unsqueeze(2).to_broadcast([128, 4, 128]), op=ALU.add),
                 reads=[r["t1" + str(b)], r["lnb"]], writes=[r["t2" + str(b)]])
            P.op("pool", lambda e, qs=qs, s=s: e.tensor_tensor(out=t1[:, :, :], in0=t2[:, :, :], in1=bon[s][:, qs, :], op=ALU.add),
                 reads=[r["t2" + str(b)], r[f"bon{s}"]], writes=[r["t1" + str(b)]])
            P.op("dve", lambda e, qs=qs, s=s, b=b: e.tensor_tensor(out=yfin[b][:, :, :], in0=t1[:, :, :], in1=szt[s][:, qs, :], op=ALU.mult),
                 reads=[r["t1" + str(b)], r[f"szt{s}"]], writes=[r[f"yfin{b}"]])
            P.dma("pool", yT[kc0 + gi * 4:kc0 + gi * 4 + 4, :, t0:t0 + 128].rearrange("k p t -> p k t"), yfin[b][:, :, :],
                  reads=[r[f"yfin{b}"]], writes=[r["y_out"]])

        it = 0
        for i in range(2):
            P.op("pool", lambda e, i=i: e.memset(kkt[i][:, :, :, :].rearrange("p a q t -> p (a q t)"), 0.0), writes=[r[f"kkt{i}"]])
            P.op("pool", lambda e, i=i: e.memset(rt[i][:, :, :, :].rearrange("p a q t -> p (a q t)"), 0.0), writes=[r[f"rt{i}"]])
            for n in tkz:
                P.op("pool", lambda e, i=i, n=n: e.memset(tkz[n][i][:, :, :, :].rearrange("p a q t -> p (a q t)"), 0.0), writes=[r[f"tkz_{n}{i}"]])
        for c in range(NCH):
            t0 = c * 128
            s = c % 2
            for tl_, nm, src in ((kkt, "kkt", "kktT"), (rt, "rt", "rtT")):
                srcv = S[src][0:1024, t0:t0 + 128].rearrange("(q p) t -> p q t", p=128)
                P.dma("sp", tl_[s][0:64, 0, :, :], srcv[0:64], writes=[r[f"{nm}{s}"]])
                P.dma("sp", tl_[s][64:128, 1, :, :], srcv[64:128], writes=[r[f"{nm}{s}"]])
            for n, src in (("kh", "khtk_d"), ("kka", "kkatk_d")):
                srcv = S[src][t0:t0 + 128, :].rearrange("p (q a d) -> p q a d", a=2, d=64)
                for a in range(2):
                    P.dma("sp", tkz[n][s][:, a, :, a * 64:(a + 1) * 64], srcv[:, :, a, :], writes=[r[f"tkz_{n}{s}"]])
            for q0 in range(0, 8, 4):
                P.dma("sp", bon[s][:, q0:q0 + 4, :], S["bonT"][q0 * 128:(q0 + 4) * 128, t0:t0 + 128].rearrange("(q p) t -> p q t", p=128),
                      writes=[r[f"bon{s}"]])
                P.dma("sp", szt[s][:, q0:q0 + 4, :], S["szT"][q0 * 128:(q0 + 4) * 128, t0:t0 + 128].rearrange("(q p) t -> p q t", p=128),
                      writes=[r[f"szt{s}"]])
            for n, dst in (("akv", "akvT_d"), ("bkv", "bkvT_d"), ("nbab", "nbabT_d"), ("tinv", "tinvT_d")):
                for h0 in range(0, 16, 4):
                    P.dma("sp", mm[n][s][:, h0:h0 + 4, :], S[dst][c, :, h0:h0 + 4, :], writes=[r[f"{n}{s}"]])
            for n, dst in (("v", "vtk_d"),):
                P.dma("sp", tk[n][s][:, :], S[dst][t0:t0 + 128, :], writes=[r[f"tk_{n}{s}"]])
            zipper([grp(c, 0, 0, s), grp(c, 1, 1, s)])
        P.barrier()
        P.flush()


D = 4096
KC = 32
EPS = 1e-6
TWO_PI = 2.0 * math.pi
C1 = 6.28125
C2 = float(np.float32(TWO_PI - C1))


def dense_consts():
    c = {}
    j = np.arange(64, dtype=np.float32)
    invf = (np.float32(10000.0) ** (-(j / np.float32(64.0)))).astype(np.float32)
    c["invf"] = np.concatenate([invf, invf]).reshape(128, 1).astype(np.float32)
    c["sgn"] = np.concatenate([-np.ones(64), np.ones(64)]).reshape(128, 1).astype(np.float32)
    sw = np.zeros((128, 128), np.float32)
    for m in range(128):
        sw[(m + 64) % 128, m] = 1.0
    c["swap64"] = sw
    return c


def phase_rope_tables(P, nc, pos_dram, C, cosT, sinT, T):
    with contextlib.ExitStack() as st:
        def sb(name, shape, dt):
            return st.enter_context(nc.sbuf_tensor(_uid() + "rp_" + name, shape, dt))
        r = defaultdict(Res)
        invf = sb("invf", [128, 1], F32); sgn = sb("sgn", [128, 1], F32)
        pi_ = sb("pi", [128, 512], I32)
        ang = sb("ang", [128, 512], F32); kf = sb("kf", [128, 512], F32); kr = sb("kr", [128, 512], F32)
        rr = sb("rr", [128, 512], F32); rc = sb("rc", [128, 512], F32); m = sb("m", [128, 512], F32)
        so = [sb(f"so{i}", [128, 512], F32) for i in range(2)]
        co = [sb(f"co{i}", [128, 512], F32) for i in range(2)]
        P.dma("sp", invf[:, :], C["invf"][:, :], writes=[r["invf"]])
        P.dma("sp", sgn[:, :], C["sgn"][:, :], writes=[r["sgn"]])
        MAGIC = 12582912.0
        for tb in range(T // 512):
            s = tb % 2
            t0 = tb * 512
            P.dma("sp", pi_[:, :], pos_dram[:, t0:t0 + 512], writes=[r["pi"]])
            P.op("dve", lambda e: e.tensor_copy(ang[:, :], pi_[:, :]), reads=[r["pi"]], writes=[r["ang"]])
            P.op("dve", lambda e: e.tensor_scalar(out=ang[:, :], in0=ang[:, :], scalar1=invf[:, 0:1], scalar2=None, op0=ALU.mult),
                 reads=[r["ang"], r["invf"]], writes=[r["ang"]])
            P.op("dve", lambda e: e.tensor_scalar(out=kf[:, :], in0=ang[:, :], scalar1=1.0 / TWO_PI, scalar2=None, op0=ALU.mult),
                 reads=[r["ang"]], writes=[r["kf"]])
            P.op("dve", lambda e: e.tensor_scalar(out=kr[:, :], in0=kf[:, :], scalar1=MAGIC, scalar2=None, op0=ALU.add),
                 reads=[r["kf"]], writes=[r["kr"]])
            P.op("dve", lambda e: e.tensor_scalar(out=kf[:, :], in0=kr[:, :], scalar1=MAGIC, scalar2=None, op0=ALU.subtract),
                 reads=[r["kr"]], writes=[r["kf"]])
            P.op("dve", lambda e: e.scalar_tensor_tensor(out=rr[:, :], in0=kf[:, :], scalar=-C1, in1=ang[:, :], op0=ALU.mult, op1=ALU.add),
                 reads=[r["kf"], r["ang"]], writes=[r["rr"]])
            P.op("dve", lambda e: e.scalar_tensor_tensor(out=rr[:, :], in0=kf[:, :], scalar=-C2, in1=rr[:, :], op0=ALU.mult, op1=ALU.add),
                 reads=[r["kf"], r["rr"]], writes=[r["rr"]])
            P.op("dve", lambda e: e.tensor_scalar(out=rr[:, :], in0=rr[:, :], scalar1=math.pi, scalar2=-math.pi, op0=ALU.min, op1=ALU.max),
                 reads=[r["rr"]], writes=[r["rr"]])
            P.op("dve", lambda e: e.tensor_scalar(out=m[:, :], in0=rr[:, :], scalar1=math.pi / 2, scalar2=-TWO_PI, op0=ALU.is_gt, op1=ALU.mult),
                 reads=[r["rr"]], writes=[r["m"]])
            P.op("dve", lambda e: e.scalar_tensor_tensor(out=rc[:, :], in0=rr[:, :], scalar=math.pi / 2, in1=m[:, :], op0=ALU.add, op1=ALU.add),
                 reads=[r["rr"], r["m"]], writes=[r["rc"]])
            P.op("dve", lambda e: e.tensor_scalar(out=rc[:, :], in0=rc[:, :], scalar1=math.pi, scalar2=-math.pi, op0=ALU.min, op1=ALU.max),
                 reads=[r["rc"]], writes=[r["rc"]])
            P.op("act", lambda e, s=s: e.activation(out=so[s][:, :], in_=rr[:, :], func=AF.Sin), reads=[r["rr"]], writes=[r[f"so{s}"]])
            P.op("act", lambda e, s=s: e.activation(out=co[s][:, :], in_=rc[:, :], func=AF.Sin), reads=[r["rc"]], writes=[r[f"co{s}"]])
            P.op("dve", lambda e, s=s: e.tensor_scalar(out=so[s][:, :], in0=so[s][:, :], scalar1=sgn[:, 0:1], scalar2=None, op0=ALU.mult),
                 reads=[r[f"so{s}"], r["sgn"]], writes=[r[f"so{s}"]])
            P.dma("pool", sinT[:, t0:t0 + 512], so[s][:, :], reads=[r[f"so{s}"]], writes=[r["sinT"]])
            P.dma("pool", cosT[:, t0:t0 + 512], co[s][:, :], reads=[r[f"co{s}"]], writes=[r["cosT"]])
        P.barrier()
        P.flush()


def phase_norm(P, nc, x_dram, normw_bc_dram, hT_dram, T, out_dram=None):
    NT = T // 128
    with contextlib.ExitStack() as st:
        def sb(name, shape, dt):
            return st.enter_context(nc.sbuf_tensor(_uid() + "pn_" + name, shape, dt))
        r = defaultdict(Res)
        xt = [sb(f"x{i}", [128, D], F32) for i in range(2)]
        junk = sb("junk", [128, D], BF16)
        nw = sb("nw", [128, D], F32)
        ss = sb("ss", [128, 2], F32); sd = sb("sd", [128, 2], F32); rs = sb("rs", [128, 2], F32)
        epsc = sb("eps", [128, 1], F32)
        P.dma("sp", nw[:, :], normw_bc_dram[:, :], writes=[r["nw"]])
        P.op("pool", lambda e: e.memset(epsc[:, :], EPS), writes=[r["eps"]])
        if out_dram is None:
            hb = [sb(f"h{i}", [128, D], BF16) for i in range(2)]
            ident = sb("id", [128, 128], BF16)
            stg = [sb(f"stg{i}", [128, KC, 256], BF16) for i in range(2)]
            tp = [st.enter_context(nc.psum_tensor(_uid() + f"pn_tp{i}", [128, 1024], BF16)) for i in range(4)]
            P.op("pool", lambda e: e.memset(ident[:, :], 0.0), writes=[r["id"]])
            P.op("pool", lambda e: e.affine_select(out=ident[:, :], in_=ident[:, :], pattern=[[-1, 128]],
                                                   compare_op=ALU.not_equal, fill=1.0, base=0, channel_multiplier=1),
                 reads=[r["id"]], writes=[r["id"]])
        else:
            of = [sb(f"of{i}", [128, D], F32) for i in range(2)]
        for tt in range(NT):
            s = tt % 2
            P.dma("sp", xt[s][:, :], x_dram[tt * 128:(tt + 1) * 128, :], writes=[r[f"xt{s}"]])
            P.op("dve", lambda e, s=s: e.scalar_tensor_tensor(out=junk[:, :], in0=xt[s][:, :], scalar=1.0, in1=xt[s][:, :],
                                                              op0=ALU.mult, op1=ALU.mult, accum_out=ss[:, s:s + 1]),
                 reads=[r[f"xt{s}"]], writes=[r["junk"], r[f"ss{s}"]])
            P.op("act", lambda e, s=s: e.activation(out=sd[:, s:s + 1], in_=ss[:, s:s + 1], func=AF.Sqrt,
                                                    bias=epsc[:, 0:1], scale=1.0 / D),
                 reads=[r[f"ss{s}"], r["eps"]], writes=[r[f"sd{s}"]])
            P.op("dve", lambda e, s=s: e.reciprocal(rs[:, s:s + 1], sd[:, s:s + 1]), reads=[r[f"sd{s}"]], writes=[r[f"rs{s}"]])
            if out_dram is not None:
                P.op("dve", lambda e, s=s: e.scalar_tensor_tensor(out=of[s][:, :], in0=xt[s][:, :], scalar=rs[:, s:s + 1],
                                                                  in1=nw[:, :], op0=ALU.mult, op1=ALU.mult),
                     reads=[r[f"xt{s}"], r[f"rs{s}"], r["nw"]], writes=[r[f"of{s}"]])
                P.dma("pool", out_dram[tt * 128:(tt + 1) * 128, :], of[s][:, :], reads=[r[f"of{s}"]], writes=[r["out"]])
                continue
            P.op("dve", lambda e, s=s: e.scalar_tensor_tensor(out=hb[s][:, :], in0=xt[s][:, :], scalar=rs[:, s:s + 1],
                                                              in1=nw[:, :], op0=ALU.mult, op1=ALU.mult),
                 reads=[r[f"xt{s}"], r[f"rs{s}"], r["nw"]], writes=[r[f"hb{s}"]])
            sg = (tt // 2) % 2
            off = (tt % 2) * 128
            for b in range(4):
                for j in range(8):
                    kc = b * 8 + j
                    P.op("pe", lambda e, s=s, b=b, j=j, kc=kc: e.transpose(tp[b][:, j * 128:(j + 1) * 128],
                                                                         hb[s][:, kc * 128:(kc + 1) * 128], ident[:, :]),
                         reads=[r[f"hb{s}"], r["id"]], writes=[r[f"tp{b}"]])
                P.op("act", lambda e, b=b, sg=sg, off=off: e.copy(
                    stg[sg][:, b * 8:(b + 1) * 8, off:off + 128],
                    tp[b][:, :].rearrange("p (j t) -> p j t", j=8)),
                     reads=[r[f"tp{b}"]], writes=[r[f"stg{sg}"]])
            if tt % 2 == 1:
                t0 = (tt - 1) * 128
                P.dma("pool", hT_dram[:, :, t0:t0 + 256].rearrange("k p t -> p k t"), stg[sg][:, :, :],
                      reads=[r[f"stg{sg}"]], writes=[r["hT_out"]])
        P.barrier()
        P.flush()


def phase_proj(P, nc, hT_dram, w_tiles_dram, projT_dram, cosT, sinT, T, NCT, n_rope=16, TH=2048, swap_dram=None):
    TH = min(TH, T)
    NTB = TH // 512
    with contextlib.ExitStack() as st:
        def sb(name, shape, dt):
            return st.enter_context(nc.sbuf_tensor(_uid() + "pp_" + name, shape, dt))
        r = defaultdict(Res)
        hT = sb("hT", [128, KC, TH], BF16)
        wf = [sb(f"wf{i}", [128, KC * 128], F32) for i in range(2)]
        wb = [sb(f"wb{i}", [128, KC, 128], BF16) for i in range(2)]
        swp = sb("swp", [128, 128], F32)
        asb = [sb(f"asb{i}", [128, 512], F32) for i in range(2)]
        cs_ = [sb(f"cs{i}", [128, 512], F32) for i in range(2)]
        sn_ = [sb(f"sn{i}", [128, 512], F32) for i in range(2)]
        t1 = sb("t1", [128, 512], F32); t2 = sb("t2", [128, 512], F32)
        ob = [sb(f"ob{i}", [128, 512], F32) for i in range(4)]
        ps = [[st.enter_context(nc.psum_tensor(_uid() + f"pp_ps{a}_{i}", [128, 512], F32)) for i in range(NTB)] for a in range(2)]
        cnt = 0
        rc = 0
        if n_rope:
            P.dma("sp", swp[:, :], swap_dram[:, :], writes=[r["swp"]])
        for t0 in range(0, T, TH):
            for kc in range(KC):
                P.dma("sp", hT[:, kc, :], hT_dram[kc, :, t0:t0 + TH], writes=[r["hT"]])
            def prep(ct):
                s = ct % 2
                P.dma("sp", wf[s][:, :], w_tiles_dram[ct, :, :], writes=[r[f"wf{s}"]])
                if ct % 2 == 0:
                    P.op("act", lambda e, s=s: e.copy(wb[s][:, :, :].rearrange("p k c -> p (k c)"), wf[s][:, :]),
                         reads=[r[f"wf{s}"]], writes=[r[f"wb{s}"]])
                else:
                    P.op("dve", lambda e, s=s: e.tensor_copy(wb[s][:, :, :].rearrange("p k c -> p (k c)"), wf[s][:, :]),
                         reads=[r[f"wf{s}"]], writes=[r[f"wb{s}"]])
            prep(0)
            for ct in range(NCT):
                s = ct % 2
                rope = ct < n_rope
                if ct + 1 < NCT:
                    prep(ct + 1)
                for kc in range(KC):
                    for tb in range(NTB):
                        P.op("pe", lambda e, s=s, kc=kc, tb=tb: e.matmul(ps[s][tb][:, :], wb[s][:, kc, :], hT[:, kc, tb * 512:(tb + 1) * 512],
                                                                        start=(kc == 0), stop=(kc == KC - 1)),
                             reads=[r[f"wb{s}"], r["hT"]], writes=[r[f"ps{s}_{tb}"]])
                for tb in range(NTB):
                    b = cnt % 4
                    cnt += 1
                    tg = t0 + tb * 512
                    if rope:
                        q = rc % 2
                        rc += 1
                        P.dma("sp", cs_[q][:, :], cosT[:, tg:tg + 512], writes=[r[f"cs{q}"]])
                        P.dma("sp", sn_[q][:, :], sinT[:, tg:tg + 512], writes=[r[f"sn{q}"]])
                        P.op("act", lambda e, s=s, tb=tb, q=q: e.copy(asb[q][:, :], ps[s][tb][:, :]), reads=[r[f"ps{s}_{tb}"]], writes=[r[f"asb{q}"]])
                        P.op("pe", lambda e, s=s, tb=tb, q=q: e.matmul(ps[1 - s][tb][:, :], swp[:, :], asb[q][:, :], start=True, stop=True),
                             reads=[r["swp"], r[f"asb{q}"]], writes=[r[f"ps{1 - s}_{tb}"]])
                        P.op("pool", lambda e, q=q: e.tensor_tensor(out=t1[:, :], in0=asb[q][:, :], in1=cs_[q][:, :], op=ALU.mult),
                             reads=[r[f"asb{q}"], r[f"cs{q}"]], writes=[r["t1"]])
                        P.op("dve", lambda e, s=s, tb=tb, q=q: e.tensor_tensor(out=t2[:, :], in0=ps[1 - s][tb][:, :], in1=sn_[q][:, :], op=ALU.mult),
                             reads=[r[f"ps{1 - s}_{tb}"], r[f"sn{q}"]], writes=[r["t2"]])
                        P.op("pool", lambda e, b=b: e.tensor_tensor(out=ob[b][:, :], in0=t1[:, :], in1=t2[:, :], op=ALU.add),
                             reads=[r["t1"], r["t2"]], writes=[r[f"ob{b}"]])
                    else:
                        if cnt % 2 == 0:
                            P.op("dve", lambda e, b=b, s=s, tb=tb: e.tensor_copy(ob[b][:, :], ps[s][tb][:, :]), reads=[r[f"ps{s}_{tb}"]], writes=[r[f"ob{b}"]])
                        else:
                            P.op("act", lambda e, b=b, s=s, tb=tb: e.copy(ob[b][:, :], ps[s][tb][:, :]), reads=[r[f"ps{s}_{tb}"]], writes=[r[f"ob{b}"]])
                    pd_, pr_ = projT_dram(ct)
                    P.dma("pool", pd_[pr_:pr_ + 128, tg:tg + 512], ob[b][:, :], reads=[r[f"ob{b}"]], writes=[r["out"]])
        P.barrier()
        P.flush()


def _load_wblk(P, r, wf, wb, s, w_dram, cb, wcnt):
    for q4 in range(4):
        ws = wcnt[0] % 2
        wcnt[0] += 1
        P.dma("sp", wf[ws][:, :], w_dram[cb * 4 + q4, :, :], writes=[r[f"wf{ws}"]])
        src = wf[ws][:, :].rearrange("p (k c) -> p k c", c=128)
        if q4 % 2 == 0:
            P.op("act", lambda e, s=s, q4=q4, src=src: e.copy(wb[s][:, :, q4 * 128:(q4 + 1) * 128], src),
                 reads=[r[f"wf{ws}"]], writes=[r[f"wb{s}"]])
        else:
            P.op("dve", lambda e, s=s, q4=q4, src=src: e.tensor_copy(wb[s][:, :, q4 * 128:(q4 + 1) * 128], src),
                 reads=[r[f"wf{ws}"]], writes=[r[f"wb{s}"]])


def phase_out(P, nc, yT_dram, w_blk_dram, x_dram, x1_dram, T, TQ=1024):
    NB = D // 512
    TQ = min(TQ, T)
    with contextlib.ExitStack() as st:
        def sb(name, shape, dt):
            return st.enter_context(nc.sbuf_tensor(_uid() + "po_" + name, shape, dt))
        r = defaultdict(Res)
        yT = sb("yT", [128, KC, TQ], BF16)
        wf = [sb(f"wf{i}", [128, KC * 128], F32) for i in range(2)]
        wb = [sb(f"wb{i}", [128, KC, 512], BF16) for i in range(2)]
        xt = [sb(f"xt{i}", [128, 512], F32) for i in range(4)]
        ot = [sb(f"ot{i}", [128, 512], F32) for i in range(4)]
        ps = [st.enter_context(nc.psum_tensor(_uid() + f"po_ps{i}", [128, 512], F32)) for i in range(4)]
        cnt = 0
        wcnt = [0]
        for t0 in range(0, T, TQ):
            for kc in range(KC):
                P.dma("sp", yT[:, kc, :], yT_dram[kc, :, t0:t0 + TQ], writes=[r["yT"]])
            _load_wblk(P, r, wf, wb, 0, w_blk_dram, 0, wcnt)
            for cb in range(NB):
                s = cb % 2
                if cb + 1 < NB:
                    _load_wblk(P, r, wf, wb, 1 - s, w_blk_dram, cb + 1, wcnt)
                for tt in range(TQ // 128):
                    b = cnt % 4
                    cnt += 1
                    tok = t0 + tt * 128
                    P.dma("sp", xt[b][:, :], x_dram[tok:tok + 128, cb * 512:(cb + 1) * 512], writes=[r[f"xt{b}"]])
                    for kc in range(KC):
                        P.op("pe", lambda e, s=s, b=b, kc=kc, tt=tt: e.matmul(ps[b][:, :], yT[:, kc, tt * 128:(tt + 1) * 128], wb[s][:, kc, :],
                                                                             start=(kc == 0), stop=(kc == KC - 1)),
                             reads=[r[f"wb{s}"], r["yT"]], writes=[r[f"ps{b}"]])
                    P.op("dve", lambda e, b=b: e.tensor_tensor(out=ot[b][:, :], in0=ps[b][:, :], in1=xt[b][:, :], op=ALU.add),
                         reads=[r[f"ps{b}"], r[f"xt{b}"]], writes=[r[f"ot{b}"]])
                    P.dma("pool", x1_dram[tok:tok + 128, cb * 512:(cb + 1) * 512], ot[b][:, :], reads=[r[f"ot{b}"]], writes=[r["out"]])
        P.barrier()
        P.flush()


def phase_gate(P, nc, h2T_dram, wg_blk_dram, p_dram, wple_dram, x1_dram, x2_dram, T, TQ=1024):
    NB = D // 512
    TQ = min(TQ, T)
    with contextlib.ExitStack() as st:
        def sb(name, shape, dt):
            return st.enter_context(nc.sbuf_tensor(_uid() + "pg_" + name, shape, dt))
        r = defaultdict(Res)
        hT = sb("hT", [128, KC, TQ], BF16)
        wf = [sb(f"wf{i}", [128, KC * 128], F32) for i in range(2)]
        wb = [sb(f"wb{i}", [128, KC, 512], BF16) for i in range(2)]
        wpb = sb("wpb", [128, 2, D], BF16)
        ident = sb("ident", [128, 128], BF16)
        pt = [sb(f"pt{i}", [128, 256], F32) for i in range(2)]
        pb = [sb(f"pb{i}", [128, 256], BF16) for i in range(2)]
        pT = sb("pT", [128, 2, TQ], BF16)
        xt = [sb(f"xt{i}", [128, 512], F32) for i in range(2)]
        gt = sb("gt", [128, 512], F32)
        tm = sb("tm", [128, 512], F32)
        ot = [sb(f"ot{i}", [128, 512], F32) for i in range(2)]
        ps = [st.enter_context(nc.psum_tensor(_uid() + f"pg_ps{i}", [128, 512], F32)) for i in range(3)]
        pp = [st.enter_context(nc.psum_tensor(_uid() + f"pg_pp{i}", [128, 512], F32)) for i in range(3)]
        tp = st.enter_context(nc.psum_tensor(_uid() + "pg_tp", [128, 1024], BF16))
        for q2 in range(2):
            P.dma("sp", wf[q2][:, :], wple_dram[:, q2 * D:(q2 + 1) * D], writes=[r[f"wf{q2}"]])
            P.op("act", lambda e, q2=q2: e.copy(wpb[:, q2, :], wf[q2][:, :]), reads=[r[f"wf{q2}"]], writes=[r["wpb"]])
        P.op("pool", lambda e: e.memset(ident[:, :], 0.0), writes=[r["id"]])
        P.op("pool", lambda e: e.affine_select(out=ident[:, :], in_=ident[:, :], pattern=[[-1, 128]],
                                               compare_op=ALU.not_equal, fill=1.0, base=0, channel_multiplier=1),
             reads=[r["id"]], writes=[r["id"]])
        cnt = 0
        wcnt = [0]
        for t0 in range(0, T, TQ):
            for kc in range(KC):
                P.dma("sp", hT[:, kc, :], h2T_dram[kc, :, t0:t0 + TQ], writes=[r["hT"]])
            for tt in range(TQ // 128):
                s = tt % 2
                tok = t0 + tt * 128
                P.dma("sp", pt[s][:, :], p_dram[tok:tok + 128, :], writes=[r[f"pt{s}"]])
                P.op("dve", lambda e, s=s: e.tensor_copy(pb[s][:, :], pt[s][:, :]), reads=[r[f"pt{s}"]], writes=[r[f"pb{s}"]])
                for k2 in range(2):
                    P.op("pe", lambda e, s=s, k2=k2: e.transpose(tp[:, k2 * 128:(k2 + 1) * 128], pb[s][:, k2 * 128:(k2 + 1) * 128], ident[:, :]),
                         reads=[r[f"pb{s}"], r["id"]], writes=[r["tp"]])
                P.op("act", lambda e, tt=tt: e.copy(pT[:, :, tt * 128:(tt + 1) * 128], tp[:, 0:256].rearrange("p (k t) -> p k t", k=2)),
                     reads=[r["tp"]], writes=[r["pT"]])
            _load_wblk(P, r, wf, wb, 0, wg_blk_dram, 0, wcnt)
            for cb in range(NB):
                s = cb % 2
                if cb + 1 < NB:
                    _load_wblk(P, r, wf, wb, 1 - s, wg_blk_dram, cb + 1, wcnt)
                for tt in range(TQ // 128):
                    b3 = cnt % 3
                    b2 = cnt % 2
                    cnt += 1
                    tok = t0 + tt * 128
                    P.dma("sp", xt[b2][:, :], x1_dram[tok:tok + 128, cb * 512:(cb + 1) * 512], writes=[r[f"xt{b2}"]])
                    for kc in range(KC):
                        P.op("pe", lambda e, s=s, b3=b3, kc=kc, tt=tt: e.matmul(ps[b3][:, :], hT[:, kc, tt * 128:(tt + 1) * 128], wb[s][:, kc, :],
                                                                               start=(kc == 0), stop=(kc == KC - 1)),
                             reads=[r[f"wb{s}"], r["hT"]], writes=[r[f"ps{b3}"]])
                    for k2 in range(2):
                        P.op("pe", lambda e, b3=b3, k2=k2, tt=tt, cb=cb: e.matmul(pp[b3][:, :], pT[:, k2, tt * 128:(tt + 1) * 128],
                                                                                 wpb[:, k2, cb * 512:(cb + 1) * 512], start=(k2 == 0), stop=(k2 == 1)),
                             reads=[r["wpb"], r["pT"]], writes=[r[f"pp{b3}"]])
                    P.op("act", lambda e, b3=b3: e.activation(out=gt[:, :], in_=ps[b3][:, :], func=AF.Sigmoid),
                         reads=[r[f"ps{b3}"]], writes=[r["gt"]])
                    P.op("dve", lambda e, b3=b3: e.tensor_tensor(out=tm[:, :], in0=pp[b3][:, :], in1=gt[:, :], op=ALU.mult),
                         reads=[r[f"pp{b3}"], r["gt"]], writes=[r["tm"]])
                    P.op("pool", lambda e, b2=b2: e.tensor_tensor(out=ot[b2][:, :], in0=tm[:, :], in1=xt[b2][:, :], op=ALU.add),
                         reads=[r["tm"], r[f"xt{b2}"]], writes=[r[f"ot{b2}"]])
                    P.dma("pool", x2_dram[tok:tok + 128, cb * 512:(cb + 1) * 512], ot[b2][:, :], reads=[r[f"ot{b2}"]], writes=[r["out"]])
        P.barrier()
        P.flush()


SEQ = 4096
NLAYER = 2
NCT = 130


def make_consts():
    c = {}
    c.update(ret_consts())
    c.update(gdn_consts())
    c.update(rwkv_consts())
    c.update(dense_consts())
    return c


_UIDC = [0]


def build_program(T=SEQ, L=NLAYER):
    nc = bass.Bass("TRN2", target_bir_lowering=False)
    NCH = T // 128
    cs = make_consts()
    ext = lambda n, shp, dt=F32: nc.dram_tensor(n, list(shp), dt, kind="ExternalInput")
    x = ext("x", [T, D])
    pos = ext("pos", [128, T], I32)
    C = {k: ext("c_" + k, v.shape) for k, v in cs.items()}
    Lp = []
    for l in range(L):
        d = {}
        d["nwb"] = ext(f"nwb{l}", [128, D]); d["win"] = ext(f"win{l}", [NCT, 128, KC * 128])
        d["gnwT"] = ext(f"gnwT{l}", [128, 8])
        for n in ("w0T", "a0T", "kkT", "kaT", "rkT", "lnwT", "lnbT"):
            d[n] = ext(f"{n}{l}", [128, 8])
        d["muT"] = ext(f"muT{l}", [128, 33]); d["lw2"] = ext(f"lw2{l}", [128, 1024])
        d["convT"] = ext(f"convT{l}", [128, 192]); d["alog"] = ext(f"alog{l}", [16, 1]); d["dtb"] = ext(f"dtb{l}", [16, 1])
        d["nrm"] = ext(f"nrm{l}", [128, 1])
        d["wout"] = ext(f"wout{l}", [32, 128, KC * 128]); d["wgate"] = ext(f"wgate{l}", [32, 128, KC * 128])
        d["wple"] = ext(f"wple{l}", [128, 2 * D]); d["plnb"] = ext(f"plnb{l}", [128, D]); d["p"] = ext(f"p{l}", [T, 256])
        Lp.append(d)
    fnb = ext("fnb", [128, D])
    out = nc.dram_tensor("out", [T, D], F32, kind="ExternalOutput")
    scr = lambda n, shp, dt: nc.dram_tensor(n, list(shp), dt)
    hT = scr("hT", [KC, 128, T], BF16); yT = scr("yT", [KC, 128, T], BF16)
    projA = scr("projA", [65 * 128, T], F32); projB = scr("projB", [65 * 128, T], F32)
    projT = lambda ct: (projA, ct * 128) if ct < 65 else (projB, (ct - 65) * 128)
    cosT = scr("cosT", [128, T], F32); sinT = scr("sinT", [128, T], F32)
    x1 = scr("x1", [T, D], F32); x2 = scr("x2", [T, D], F32)
    S = {n: scr(n, [2048, T], BF16) for n in ("gqT", "gqdT", "gkT", "gvT", "wkT_d")}
    S["gbt"] = scr("gbt", [T, 32], F32)
    S["attnT_d"] = scr("attnT_d", [NCH, 128, 16, 128], BF16); S["ktl_d"] = scr("ktl_d", [T, 2048], BF16)
    S["u_d"] = scr("u_d", [T, 2048], F32); S["els_d"] = scr("els_d", [NCH, 128, 16], F32)
    S.update({n: scr(n, [1024, T], BF16) for n in ("rtT", "kktT", "khT", "kkaT", "rvT")})
    S.update({n: scr(n, [1024, T], F32) for n in ("bonT", "szT")})
    S["pc_d"] = scr("pc_d", [8, 128, NCH], F32)
    S.update({n: scr(n, [NCH, 128, 16, 128], BF16) for n in ("tinvT_d", "akvT_d", "bkvT_d", "nbabT_d")})
    S.update({n: scr(n, [T, 1024], BF16) for n in ("vtk_d", "khtk_d", "kkatk_d")})
    grows = dict(q=0, k=2048, v=4096, z=6144, a=8192, b=8208)
    import os
    only = os.environ.get("MK_PH")
    only = set(only.split(",")) if only else None

    def on(n):
        return only is None or n in only
    with contextlib.ExitStack() as stack:
        P = Prog(nc, stack)
        if on("rope"):
            phase_rope_tables(P, nc, pos, C, cosT, sinT, T)
        xin = x
        for l in range(L):
            d = Lp[l]
            if on("norm"):
                phase_norm(P, nc, xin, d["nwb"], hT, T)
            if on("proj"):
                phase_proj(P, nc, hT, d["win"], projT, cosT, sinT, T, NCT, n_rope=int(os.environ.get("MK_NROPE", "16")), swap_dram=C["swap64"])
            if on("ret"):
                phase_ret(P, nc, projA, yT, C, d["gnwT"], T)
            if on("rwkv"):
                phase_rwkv_pre(P, nc, projA, S, C, d, T, 4096)
            if on("rwkv"):
                phase_rwkv_r1(P, nc, S, C, T)
            if on("rwkv"):
                phase_rwkv_r2(P, nc, S, C, yT, d, T, kc0=8)
            if on("gdn"):
                phase_gdn_pre(P, nc, projB, S, C, d, T, grows)
            if on("gdn"):
                phase_gdn_g1(P, nc, S, C, T)
            if on("gdn"):
                phase_gdn_g2(P, nc, projB, S, C, yT, d["nrm"], T, grows["z"], kc0=16)
            if on("out"):
                phase_out(P, nc, yT, d["wout"], xin, x1, T)
            if on("norm2"):
                phase_norm(P, nc, x1, d["plnb"], hT, T)
            if on("gate"):
                phase_gate(P, nc, hT, d["wgate"], d["p"], d["wple"], x1, x2, T)
            xin = x2
        if on("fin"):
            phase_norm(P, nc, xin, fnb, None, T, out_dram=out)
        n_ops = P.n_ops
    return nc, cs, n_ops


def _tiles(w, ncols_pad=None):
    K, N = w.shape
    if ncols_pad is not None and ncols_pad > N:
        w = np.concatenate([w, np.zeros((K, ncols_pad - N), w.dtype)], axis=1)
        N = ncols_pad
    return np.ascontiguousarray(w.reshape(K // 128, 128, N // 128, 128).transpose(2, 1, 0, 3).reshape(N // 128, 128, (K // 128) * 128))


def prep_shared(inp, L=NLAYER):
    f = np.float32
    sh = {}
    bc = lambda v: np.ascontiguousarray(np.broadcast_to(np.asarray(v, f), (128, v.shape[-1])))
    col8 = lambda v: np.ascontiguousarray(np.asarray(v, f).reshape(8, 128).T)
    for l in range(L):
        sh[f"nwb{l}"] = bc(inp["norm_w"][l])
        sh[f"win{l}"] = _tiles(np.asarray(inp["w_in"][l], f), NCT * 128)
        sh[f"gnwT{l}"] = col8(inp["ret_gn"][l])
        sh[f"w0T{l}"] = col8(inp["rwkv_w0"][l]); sh[f"a0T{l}"] = col8(inp["rwkv_a0"][l])
        sh[f"kkT{l}"] = col8(inp["rwkv_k_k"][l]); sh[f"kaT{l}"] = col8(inp["rwkv_k_a"][l]); sh[f"rkT{l}"] = col8(inp["rwkv_r_k"][l])
        sh[f"lnwT{l}"] = col8(inp["rwkv_ln_w"][l]); sh[f"lnbT{l}"] = col8(inp["rwkv_ln_b"][l])
        sh[f"muT{l}"] = np.ascontiguousarray(np.asarray(inp["rwkv_mu"][l], f).reshape(33, 128).T)
        sh[f"lw2{l}"] = np.ascontiguousarray(np.concatenate([np.asarray(inp["rwkv_w2"][l], f), np.asarray(inp["rwkv_a2"][l], f)], 0))
        sh[f"convT{l}"] = np.ascontiguousarray(np.asarray(inp["gdn_conv"][l], f).reshape(4, 48, 128).transpose(2, 1, 0).reshape(128, 192))
        sh[f"alog{l}"] = np.asarray(inp["gdn_a_log"][l], f).reshape(16, 1).copy()
        sh[f"dtb{l}"] = np.asarray(inp["gdn_dt_bias"][l], f).reshape(16, 1).copy()
        sh[f"nrm{l}"] = np.asarray(inp["gdn_norm"][l], f).reshape(128, 1).copy()
        sh[f"wout{l}"] = _tiles(np.asarray(inp["w_out"][l], f))
        sh[f"wgate{l}"] = _tiles(np.asarray(inp["w_ple_gate"][l], f))
        sh[f"wple{l}"] = np.ascontiguousarray(np.asarray(inp["w_ple"][l], f).reshape(2, 128, D).transpose(1, 0, 2).reshape(128, 2 * D))
        sh[f"plnb{l}"] = bc(inp["ple_norm"][l])
    sh["fnb"] = bc(inp["final_norm"])
    return sh


def kernel(**inp):
    B = inp["x"].shape[0]
    T = inp["x"].shape[1]
    nc, cs, n_ops = build_program(T, NLAYER)
    sh = prep_shared(inp)
    for k, v in cs.items():
        sh["c_" + k] = v
    in_maps = []
    for b in range(B):
        m = dict(sh)
        m["x"] = np.ascontiguousarray(np.asarray(inp["x"][b], np.float32))
        m["pos"] = np.ascontiguousarray(np.broadcast_to(np.asarray(inp["positions"][b], np.int32), (128, T)))
        for l in range(NLAYER):
            m[f"p{l}"] = np.ascontiguousarray(np.asarray(inp["p"][l, b], np.float32))
        in_maps.append(m)
    res = run_bass_kernel_spmd(nc, in_maps, core_ids=list(range(B)))
    return np.stack([np.asarray(r["out"], np.float32) for r in res.results], axis=0)
```

```python
import contextlib, math
from collections import defaultdict
import numpy as np
import concourse.bass as bass
import concourse.mybir as mybir
from concourse.bass_utils import run_bass_kernel_spmd


F32 = mybir.dt.float32
BF16 = mybir.dt.bfloat16
I32 = mybir.dt.int32
ALU = mybir.AluOpType
AF = mybir.ActivationFunctionType
AX = mybir.AxisListType

ENGS = ("pe", "act", "dve", "pool", "sp")
EPOCH = 20000
N_EPOCHS = {"pe": 16, "act": 10, "dve": 12, "pool": 10, "sp": 1}
N_DMA_SEM = 12


_UID = [0]


def _uid():
    return f"u{_UID[0]}_"


class Res:
    __slots__ = ("name", "w", "r")

    def __init__(self, name=""):
        self.name = name
        self.w = None
        self.r = []


class Prog:
    def __init__(self, nc, stack):
        self.nc = nc
        self.stack = stack
        self.sems = {}
        for e in ENGS:
            self.sems[e] = [stack.enter_context(nc.semaphore(f"s_{e}_{i}")) for i in range(N_EPOCHS[e])]
        self.dsems = {}
        for e in ("sp", "pool"):
            self.dsems[e] = [stack.enter_context(nc.semaphore(f"d_{e}_{i}")) for i in range(N_DMA_SEM)]
        self.dcount = {e: [0] * N_DMA_SEM for e in self.dsems}
        self.dnext = {e: 0 for e in self.dsems}
        self.seq = {e: 0 for e in ENGS}
        self.known = {e: {} for e in ENGS}
        self.ops = {e: [] for e in ENGS}
        self.last = {e: None for e in ENGS}
        self.outstanding = []
        self.n_ops = 0
        self.pe_needed = set()
        self.pe_map = {}
        self.pe_count = 0

    def _waits_for(self, eng, reads, writes, extra=()):
        toks = list(extra)
        for r in reads:
            if r.w is not None:
                toks.append(r.w)
        for w in writes:
            if w.w is not None:
                toks.append(w.w)
            toks.extend(w.r)
        best = {}
        for (sem, val, te, raw) in toks:
            pass
        return toks

    def _filter(self, eng, toks):
        best = {}
        for tok, is_raw in toks:
            sem, val, te = tok
            if te == eng and eng == "pe":
                continue
            k = "PE" if te == "pe" else id(sem)
            if k not in best or best[k][1] < val:
                best[k] = (sem, val)
        out = []
        kn = self.known[eng]
        for k, (sem, val) in best.items():
            if kn.get(k, 0) >= val:
                continue
            kn[k] = val
            if k == "PE":
                self.pe_needed.add(val)
            out.append((sem, val))
        return out

    def op(self, eng, fn, reads=(), writes=(), extra=()):
        toks = [(t, True) for t in extra]
        for r in reads:
            if r.w is not None:
                toks.append((r.w, True))
        for w in writes:
            if w.w is not None:
                toks.append((w.w, False))
            toks.extend((t, False) for t in w.r)
        waits = self._filter(eng, toks)
        self.seq[eng] += 1
        s = self.seq[eng]
        if eng == "pe":
            tok = ("PE", s, "pe")
            self.ops[eng].append((waits, fn, s, 1))
        else:
            ep = (s - 1) // EPOCH
            tok = (self.sems[eng][ep], s - ep * EPOCH, eng)
            self.ops[eng].append((waits, fn, tok[0], 1))
        self.last[eng] = tok
        for r in reads:
            r.r.append(tok)
        for w in writes:
            w.w = tok
            w.r = []
        self.n_ops += 1
        return tok

    def dma(self, q, out, in_, reads=(), writes=(), **kw):
        i = self.dnext[q]
        self.dnext[q] = (i + 1) % N_DMA_SEM
        sem = self.dsems[q][i]
        toks = []
        if self.dcount[q][i] > 0:
            toks.append(((sem, self.dcount[q][i], None), True))
        for r in reads:
            if r.w is not None:
                toks.append((r.w, True))
        for w in writes:
            if w.w is not None:
                toks.append((w.w, False))
            toks.extend((t, False) for t in w.r)
        waits = self._filter(q, toks)
        self.dcount[q][i] += 16
        tok = (sem, self.dcount[q][i], None)

        def fn(e, out=out, in_=in_, kw=kw):
            return e.dma_start(out=out, in_=in_, **kw)
        self.ops[q].append((waits, fn, sem, 16))
        for r in reads:
            r.r.append(tok)
        for w in writes:
            w.w = tok
            w.r = []
        self.outstanding.append(tok)
        self.n_ops += 1
        return tok

    def barrier(self):
        toks = [(t, True) for t in self.outstanding]
        for e in ENGS:
            if self.last[e] is not None:
                toks.append((self.last[e], True))
        for e in ENGS:
            waits = self._filter(e, [(t, r) for (t, r) in toks if t[2] != e])
            if waits:
                self.ops[e].append((waits, None, None, 0))
        self.outstanding = []

    def flush(self):
        _UID[0] += 1
        nc = self.nc
        ops = self.ops
        for (waits, fn, idx, inc) in ops["pe"]:
            if fn is not None and idx in self.pe_needed:
                self.pe_count += 1
                c = self.pe_count
                ep = (c - 1) // EPOCH
                self.pe_map[idx] = (self.sems["pe"][ep], c - ep * EPOCH)
        pe_map = self.pe_map

        def rw(w):
            s_, v_ = w
            if isinstance(s_, str):
                return pe_map[v_]
            return w
        with nc.Block() as block:
            def run(handle, lst, is_pe=False):
                for waits, fn, sem, inc in lst:
                    for w in waits:
                        s_, v_ = rw(w)
                        handle.wait_ge(s_, v_)
                    if fn is not None:
                        ins = fn(handle)
                        if is_pe:
                            if sem in pe_map:
                                ins.then_inc(pe_map[sem][0], 1)
                        else:
                            ins.then_inc(sem, inc)

            @block.tensor
            def _(e):
                run(e, ops["pe"], True)

            @block.scalar
            def _(e):
                run(e, ops["act"])

            @block.vector
            def _(e):
                run(e, ops["dve"])

            @block.gpsimd
            def _(e):
                run(e, ops["pool"])

            @block.sync
            def _(e):
                run(e, ops["sp"])
        self.ops = {e: [] for e in ENGS}


RET_H = 8


def ret_consts():
    h = np.arange(8, dtype=np.float64)
    lg = np.log1p(-(2.0 ** (-5.0 - h)))
    i = np.arange(128, dtype=np.float64)
    Gq = np.exp((i[None, :] + 1) * lg[:, None])
    Gk = np.exp(-(i[None, :] + 1) * lg[:, None]) * 128 ** -0.5
    GC = np.exp(128 * lg)
    c = {}
    c["ret_gq"] = np.broadcast_to(Gq.reshape(1, 8 * 128), (128, 1024)).astype(np.float32).copy()
    c["ret_gk"] = np.broadcast_to(Gk.reshape(1, 8 * 128), (128, 1024)).astype(np.float32).copy()
    c["ret_gc"] = np.broadcast_to(np.repeat(GC, 128).reshape(1, 1024), (128, 1024)).astype(np.float32).copy()
    jj, ii = np.meshgrid(np.arange(128), np.arange(128), indexing="ij")
    c["mask_ui"] = (ii >= jj).astype(np.float32)
    c["ident"] = np.eye(128, dtype=np.float32)
    return c


def dma_rows(P, q, out_tile, dram, row0, nh, t0, tl, res_w, hs=4):
    for h0 in range(0, nh, hs):
        src = dram[row0 + h0 * 128: row0 + (h0 + hs) * 128, t0:t0 + tl].rearrange("(h p) t -> p h t", p=128)
        P.dma(q, out_tile[:, h0:h0 + hs, 0:tl], src, writes=[res_w])


def phase_ret(P, nc, projT, yT, C, gnwT_dram, T, eps=1e-5):
    H = RET_H
    NCH = T // 128
    with contextlib.ExitStack() as st:
        def sb(name, shape, dt):
            return st.enter_context(nc.sbuf_tensor(_uid() + "rt_" + name, shape, dt))

        def pst(name, shape, dt):
            return st.enter_context(nc.psum_tensor(_uid() + "rt_" + name, shape, dt))
        qf = [sb(f"qf{i}", [128, H, 128], F32) for i in range(2)]
        kf = [sb(f"kf{i}", [128, H, 128], F32) for i in range(2)]
        vf = [sb(f"vf{i}", [128, H, 128], F32) for i in range(2)]
        zf = [sb(f"zf{i}", [128, H, 128], F32) for i in range(2)]
        gq = sb("gq", [128, H, 128], F32); gk = sb("gk", [128, H, 128], F32); gc = sb("gc", [128, H, 128], F32)
        maskf = sb("maskf", [128, 128], F32)
        identf = sb("identf", [128, 128], F32); ident = sb("ident", [128, 128], BF16)
        gnw = sb("gnw", [128, H], F32)
        qd = sb("qd", [128, H, 128], BF16); kd = sb("kd", [128, H, 128], BF16); vb = sb("vb", [128, H, 128], BF16)
        scT = sb("scT", [128, H, 128], BF16)
        vtok = sb("vtok", [128, H, 128], BF16); kdtok = sb("kdtok", [128, H, 128], BF16)
        state = sb("state", [128, H, 128], F32); stmp = sb("stmp", [128, H, 128], F32); state_bf = sb("state_bf", [128, H, 128], BF16)
        y_sb = sb("y_sb", [128, H, 128], F32); sq = sb("sq", [128, H, 128], F32)
        s1 = sb("s1", [128, H], F32); s2 = sb("s2", [128, H], F32); mean = sb("mean", [128, H], F32)
        var = sb("var", [128, H], F32); sd = sb("sd", [128, H], F32); rstd = sb("rstd", [128, H], F32)
        epsc = sb("epsc", [128, 1], F32); mh = sb("mh", [128, H], F32)
        yc = sb("yc", [128, H, 128], F32); yn = sb("yn", [128, H, 128], BF16)
        sz = sb("sz", [128, H, 128], F32); yg = sb("yg", [128, H, 128], F32); yfin = [sb(f"yfin{i}", [128, H, 128], BF16) for i in range(2)]
        sc_ps = pst("sc_ps", [128, H, 128], F32)
        vt_ps = pst("vt_ps", [128, H, 128], BF16)
        kt_ps = pst("kt_ps", [128, H, 128], BF16)
        y_ps = pst("y_ps", [128, H, 128], F32)
        kv_ps = pst("kv_ps", [128, H, 128], F32)
        r = {n: Res(n) for n in ["gq", "gk", "gc", "mask", "identf", "ident", "gnw", "qd", "kd", "vb", "scT", "vtok", "kdtok",
                                 "state", "stmp", "state_bf", "y_sb", "sq", "s1", "s2", "mean", "var", "sd", "rstd", "eps",
                                 "yc", "yn", "sz", "yg", "mh", "sc_ps", "vt_ps", "kt_ps", "y_ps", "kv_ps", "out"]}
        r_qf = [Res(), Res()]; r_kf = [Res(), Res()]; r_vf = [Res(), Res()]; r_zf = [Res(), Res()]; r_yfin = [Res(), Res()]

        flat = lambda t: t[:, :, :].rearrange("p h t -> p (h t)")
        P.dma("sp", flat(gq), C["ret_gq"][:, :], writes=[r["gq"]])
        P.dma("sp", flat(gk), C["ret_gk"][:, :], writes=[r["gk"]])
        P.dma("sp", flat(gc), C["ret_gc"][:, :], writes=[r["gc"]])
        P.dma("sp", maskf[:, :], C["mask_ui"][:, :], writes=[r["mask"]])
        P.dma("sp", identf[:, :], C["ident"][:, :], writes=[r["identf"]])
        P.dma("sp", gnw[:, :], gnwT_dram[:, :], writes=[r["gnw"]])
        P.op("pool", lambda e: e.tensor_copy(ident[:, :], identf[:, :]), reads=[r["identf"]], writes=[r["ident"]])
        P.op("pool", lambda e: e.memset(epsc[:, :], eps), writes=[r["eps"]])
        P.op("pool", lambda e: e.memset(mh[:, :], -0.5), writes=[r["mh"]])
        P.op("pool", lambda e: e.memset(flat(state), 0.0), writes=[r["state"]])
        P.op("pool", lambda e: e.memset(flat(state_bf), 0.0), writes=[r["state_bf"]])

        def bc(t):
            return t[:, :].unsqueeze(2).to_broadcast([128, H, 128])

        for n in range(NCH):
            s = n % 2
            t0 = n * 128
            dma_rows(P, "sp", qf[s], projT, 0, H, t0, 128, r_qf[s])
            dma_rows(P, "sp", kf[s], projT, 1024, H, t0, 128, r_kf[s])
            dma_rows(P, "sp", vf[s], projT, 2048, H, t0, 128, r_vf[s])
            dma_rows(P, "sp", zf[s], projT, 3072, H, t0, 128, r_zf[s])
            P.op("dve", lambda e, s=s: e.tensor_tensor(out=flat(qd), in0=flat(qf[s]), in1=flat(gq), op=ALU.mult),
                 reads=[r_qf[s], r["gq"]], writes=[r["qd"]])
            P.op("dve", lambda e, s=s: e.tensor_tensor(out=flat(kd), in0=flat(kf[s]), in1=flat(gk), op=ALU.mult),
                 reads=[r_kf[s], r["gk"]], writes=[r["kd"]])
            P.op("pool", lambda e, s=s: e.tensor_copy(flat(vb), flat(vf[s])), reads=[r_vf[s]], writes=[r["vb"]])
            P.op("act", lambda e, s=s: e.activation(out=flat(sz), in_=flat(zf[s]), func=AF.Silu), reads=[r_zf[s]], writes=[r["sz"]])
            for h in range(H):
                P.op("pe", lambda e, h=h: e.matmul(sc_ps[:, h, :], kd[:, h, :], qd[:, h, :], start=True, stop=True),
                     reads=[r["kd"], r["qd"]], writes=[r["sc_ps"]])
            for h in range(H):
                P.op("pe", lambda e, h=h: e.transpose(vt_ps[:, h, :], vb[:, h, :], ident[:, :]),
                     reads=[r["vb"], r["ident"]], writes=[r["vt_ps"]])
            for h in range(H):
                P.op("pe", lambda e, h=h: e.transpose(kt_ps[:, h, :], kd[:, h, :], ident[:, :]),
                     reads=[r["kd"], r["ident"]], writes=[r["kt_ps"]])
            P.op("dve", lambda e: e.tensor_tensor(out=scT[:, :, :], in0=sc_ps[:, :, :],
                                                  in1=maskf[:, :].unsqueeze(1).to_broadcast([128, H, 128]), op=ALU.mult),
                 reads=[r["sc_ps"], r["mask"]], writes=[r["scT"]])
            P.op("act", lambda e: e.copy(flat(vtok), flat(vt_ps)), reads=[r["vt_ps"]], writes=[r["vtok"]])
            P.op("act", lambda e: e.copy(flat(kdtok), flat(kt_ps)), reads=[r["kt_ps"]], writes=[r["kdtok"]])
            for h in range(H):
                P.op("pe", lambda e, h=h: e.matmul(y_ps[:, h, :], scT[:, h, :], vtok[:, h, :], start=True, stop=False),
                     reads=[r["scT"], r["vtok"]], writes=[r["y_ps"]])
                P.op("pe", lambda e, h=h: e.matmul(y_ps[:, h, :], qd[:, h, :], state_bf[:, h, :], start=False, stop=True),
                     reads=[r["qd"], r["state_bf"]], writes=[r["y_ps"]])
            for h in range(H):
                P.op("pe", lambda e, h=h: e.matmul(kv_ps[:, h, :], kdtok[:, h, :], vtok[:, h, :], start=True, stop=True),
                     reads=[r["kdtok"], r["vtok"]], writes=[r["kv_ps"]])
            P.op("dve", lambda e: e.tensor_tensor(out=flat(stmp), in0=flat(kv_ps), in1=flat(state), op=ALU.add),
                 reads=[r["kv_ps"], r["state"]], writes=[r["stmp"]])
            P.op("dve", lambda e: e.tensor_tensor(out=flat(state), in0=flat(stmp), in1=flat(gc), op=ALU.mult),
                 reads=[r["stmp"], r["gc"]], writes=[r["state"]])
            P.op("act", lambda e: e.copy(flat(state_bf), flat(state)), reads=[r["state"]], writes=[r["state_bf"]])
            P.op("act", lambda e: e.copy(flat(y_sb), flat(y_ps)), reads=[r["y_ps"]], writes=[r["y_sb"]])
            P.op("dve", lambda e: e.tensor_reduce(out=s1[:, :], in_=y_sb[:, :, :], axis=AX.X, op=ALU.add),
                 reads=[r["y_sb"]], writes=[r["s1"]])
            P.op("pool", lambda e: e.tensor_tensor(out=flat(sq), in0=flat(y_sb), in1=flat(y_sb), op=ALU.mult),
                 reads=[r["y_sb"]], writes=[r["sq"]])
            P.op("dve", lambda e: e.tensor_reduce(out=s2[:, :], in_=sq[:, :, :], axis=AX.X, op=ALU.add),
                 reads=[r["sq"]], writes=[r["s2"]])
            P.op("dve", lambda e: e.tensor_scalar(out=mean[:, :], in0=s1[:, :], scalar1=1.0 / 128, scalar2=None, op0=ALU.mult),
                 reads=[r["s1"]], writes=[r["mean"]])
            P.op("dve", lambda e: e.tensor_tensor(out=var[:, :], in0=mean[:, :], in1=mean[:, :], op=ALU.mult),
                 reads=[r["mean"]], writes=[r["var"]])
            P.op("dve", lambda e: e.scalar_tensor_tensor(out=var[:, :], in0=s2[:, :], scalar=1.0 / 128, in1=var[:, :],
                                                         op0=ALU.mult, op1=ALU.subtract),
                 reads=[r["s2"], r["var"]], writes=[r["var"]])
            P.op("dve", lambda e: e.tensor_scalar(out=sd[:, :], in0=var[:, :], scalar1=1.0, scalar2=eps, op0=ALU.mult, op1=ALU.add),
                 reads=[r["var"]], writes=[r["sd"]])
            P.op("pool", lambda e: e.tensor_tensor(out=rstd[:, :], in0=sd[:, :], in1=mh[:, :], op=ALU.pow), reads=[r["sd"], r["mh"]], writes=[r["rstd"]])
            P.op("dve", lambda e: e.tensor_tensor(out=yc[:, :, :], in0=y_sb[:, :, :], in1=bc(mean), op=ALU.subtract),
                 reads=[r["y_sb"], r["mean"]], writes=[r["yc"]])
            P.op("dve", lambda e: e.tensor_tensor(out=yn[:, :, :], in0=yc[:, :, :], in1=bc(rstd), op=ALU.mult),
                 reads=[r["yc"], r["rstd"]], writes=[r["yn"]])
            for h in range(H):
                P.op("pe", lambda e, h=h: e.transpose(kt_ps[:, h, :], yn[:, h, :], ident[:, :]),
                     reads=[r["yn"], r["ident"]], writes=[r["kt_ps"]])
            P.op("dve", lambda e: e.tensor_tensor(out=yg[:, :, :], in0=kt_ps[:, :, :], in1=bc(gnw), op=ALU.mult),
                 reads=[r["kt_ps"], r["gnw"]], writes=[r["yg"]])
            P.op("pool", lambda e, s=s: e.tensor_tensor(out=flat(yfin[s]), in0=flat(yg), in1=flat(sz), op=ALU.mult),
                 reads=[r["yg"], r["sz"]], writes=[r_yfin[s]])
            P.dma("pool", yT[0:H, :, t0:t0 + 128].rearrange("k p t -> p k t"), yfin[s][:, :, :],
                  reads=[r_yfin[s]], writes=[r["out"]])
        P.barrier()
        P.flush()


GH = 16


def gdn_consts():
    c = {}
    k, i = np.meshgrid(np.arange(128), np.arange(128), indexing="ij")
    c["triu"] = (k <= i).astype(np.float32)
    c["ones"] = np.ones((128, 128), np.float32)
    c["negones"] = -np.ones((128, 128), np.float32)
    c["mask_ls"] = (k > i).astype(np.float32)
    c["mask_li"] = (k >= i).astype(np.float32)
    c["mask_ui"] = (i >= k).astype(np.float32)
    c["mask_us"] = (i > k).astype(np.float32)
    c["ident"] = np.eye(128, dtype=np.float32)
    sel = np.zeros((16, 16, 128), np.float32)
    for h in range(16):
        sel[h, h, :] = 1.0
    c["sel16"] = sel.reshape(16, 16 * 128)
    cm = np.ones((128, 512), np.float32)
    cm[:, ::128] = 0.0
    c["cmask128"] = cm
    return c


def neumann(P, nc, L, LT, rL, rLT, W, r, nlev=6, G=4):
    identb = W["identb"]
    Pk = W["Pk"]; nL = W["nL"]; nLT = W["nLT"]
    pa, pb, pp = W["pa"], W["pb"], W["pp"]
    P.op("pool", lambda e: e.tensor_tensor(out=Pk[0][:, :, :], in0=identb[:, :].unsqueeze(1).to_broadcast([128, G, 128]),
                                           in1=LT[:, :, :], op=ALU.subtract),
         reads=[r["identb"], rLT], writes=[r["Pk0"]])
    curL, curLT, rcL, rcLT = L, LT, rL, rLT
    pi = 0
    for lev in range(nlev):
        s = lev % 2
        for g in range(G):
            P.op("pe", lambda e, g=g, a=curLT, b=curL: e.matmul(pa[:, g, :], a[:, g, :], b[:, g, :], start=True, stop=True),
                 reads=[rcLT, rcL], writes=[r["pa"]])
        if lev < nlev - 1:
            for g in range(G):
                P.op("pe", lambda e, g=g, a=curL, b=curLT: e.matmul(pb[:, g, :], a[:, g, :], b[:, g, :], start=True, stop=True),
                     reads=[rcLT, rcL], writes=[r["pb"]])
        P.op("act", lambda e, s=s: e.copy(nL[s][:, :, :], pa[:, :, :]), reads=[r["pa"]], writes=[r[f"nL{s}"]])
        if lev < nlev - 1:
            P.op("dve", lambda e, s=s: e.tensor_copy(nLT[s][:, :, :], pb[:, :, :]), reads=[r["pb"]], writes=[r[f"nLT{s}"]])
        for g in range(G):
            P.op("pe", lambda e, g=g, s=s, pi=pi: e.matmul(pp[:, g, :], nL[s][:, g, :], Pk[pi][:, g, :], start=True, stop=True),
                 reads=[r[f"nL{s}"], r[f"Pk{pi}"]], writes=[r["pp"]])
        P.op("dve", lambda e, pi=pi: e.tensor_tensor(out=Pk[1 - pi][:, :, :], in0=pp[:, :, :], in1=Pk[pi][:, :, :], op=ALU.add),
             reads=[r["pp"], r[f"Pk{pi}"]], writes=[r[f"Pk{1 - pi}"]])
        pi = 1 - pi
        curL, curLT, rcL, rcLT = nL[s], nLT[s], r[f"nL{s}"], r[f"nLT{s}"]
    return Pk[pi], r[f"Pk{pi}"]


def phase_gdn_pre(P, nc, projT, S, C, prm, T, rows):
    NB = T // 512
    with contextlib.ExitStack() as st:
        def sb(name, shape, dt):
            return st.enter_context(nc.sbuf_tensor(_uid() + "gp_" + name, shape, dt))

        def pst(name, shape, dt):
            return st.enter_context(nc.psum_tensor(_uid() + "gp_" + name, shape, dt))
        r = defaultdict(Res)
        convw = sb("convw", [128, 48 * 4], F32)
        alog = sb("alog", [16, 1], F32); dtb = sb("dtb", [16, 1], F32); nega = sb("nega", [16, 1], F32)
        onesf = sb("onesf", [128, 128], F32); identf = sb("identf", [128, 128], F32)
        sel = sb("sel", [16, 16 * 128], F32); cmask = sb("cmask", [16, 512], F32)
        epsc = sb("epsc", [128, 1], F32)
        at = sb("at", [16, 512], F32); bt = sb("bt", [16, 512], F32)
        e1 = sb("e1", [16, 512], F32); spt = sb("spt", [16, 512], F32); gt = sb("gt", [16, 512], F32)
        beta = sb("beta", [16, 512], F32); gcT = sb("gcT", [16, 512], F32); egcT = sb("egcT", [16, 512], F32)
        gbs = sb("gbs", [128, 4, 32], F32)
        u = [sb(f"u{i}", [128, 515], F32) for i in range(4)]
        acc = sb("acc", [128, 512], F32); sl = sb("sl", [128, 512], F32); sq = sb("sq", [128, 512], F32)
        sd = sb("sd", [128, 512], F32); rn = sb("rn", [128, 512], F32); kn = sb("kn", [128, 512], F32)
        ob = [sb(f"ob{i}", [128, 512], BF16) for i in range(4)]
        ob2 = [sb(f"ob2{i}", [128, 512], BF16) for i in range(4)]
        ss_ps = pst("ss_ps", [128, 512], F32)
        bc_ps = pst("bc_ps", [128, 512], F32)
        tr_ps_full = pst("tr_ps", [128, 512], F32)
        tr_ps = tr_ps_full[:, 0:128].rearrange("p (c x) -> p c x", x=32)
        P.dma("sp", convw[:, :], prm["convT"][:, :], writes=[r["convw"]])
        P.dma("sp", alog[:, :], prm["alog"][:, :], writes=[r["alog"]])
        P.dma("sp", dtb[:, :], prm["dtb"][:, :], writes=[r["dtb"]])
        P.dma("sp", onesf[:, :], C["ones"][:, :], writes=[r["onesf"]])
        P.dma("sp", identf[:, :], C["ident"][:, :], writes=[r["identf"]])
        P.dma("sp", sel[:, :], C["sel16"][:, :], writes=[r["sel"]])
        P.dma("sp", cmask[:, :], C["cmask128"][0:16, :], writes=[r["cmask"]])
        P.op("pool", lambda e: e.memset(epsc[:, :], 1e-6), writes=[r["eps"]])
        P.op("act", lambda e: e.activation(out=nega[:, :], in_=alog[:, :], func=AF.Exp), reads=[r["alog"]], writes=[r["nega"]])
        P.op("dve", lambda e: e.tensor_scalar(out=nega[:, :], in0=nega[:, :], scalar1=-1.0, scalar2=None, op0=ALU.mult),
             reads=[r["nega"]], writes=[r["nega"]])
        cnt = 0
        acc2 = [acc] + [sb(f"acc_{i}", [128, 512], F32) for i in range(3)]; sl2 = [sl] + [sb(f"sl_{i}", [128, 512], F32) for i in range(3)]
        sq2 = [sq] + [sb(f"sq_{i}", [128, 512], F32) for i in range(3)]; sd2 = [sd] + [sb(f"sd_{i}", [128, 512], F32) for i in range(3)]
        rn2 = [rn] + [sb(f"rn_{i}", [128, 512], F32) for i in range(3)]; kn2 = [kn] + [sb(f"kn_{i}", [128, 512], F32) for i in range(3)]
        ss2 = [ss_ps] + [pst(f"ss_ps_{i}", [128, 512], F32) for i in range(3)]
        bc2 = [bc_ps, tr_ps_full] + [pst(f"bc_ps_{i}", [128, 512], F32) for i in range(2)]

        def do_tile(kind, rbase, dst, ti, tb, s):
            t0 = tb * 512
            acc, sl, sq, sd, rn, kn, ss_ps, bc_ps = acc2[s], sl2[s], sq2[s], sd2[s], rn2[s], kn2[s], ss2[s], bc2[s]
            ra, rsl, rsq, rsd, rrn, rkn, rss, rbc = (r[f"acc{s}"], r[f"sl{s}"], r[f"sq{s}"], r[f"sd{s}"], r[f"rn{s}"], r[f"kn{s}"],
                                                     r[f"ss_ps{s}"], r[f"bc_ps{s}"])
            wi = {"q": 0, "k": 16, "v": 32}[kind] + ti
            row0 = rbase + ti * 128
            if tb == 0:
                P.op("pool", lambda e: e.memset(u[s][:, 0:3], 0.0), writes=[r[f"u{s}"]])
                P.dma("sp", u[s][:, 3:515], projT[row0:row0 + 128, 0:512], writes=[r[f"u{s}"]])
            else:
                P.dma("sp", u[s][:, :], projT[row0:row0 + 128, t0 - 3:t0 + 512], writes=[r[f"u{s}"]])
            P.op("act", lambda e: e.mul(acc[:, :], u[s][:, 3:515], convw[:, wi * 4 + 3:wi * 4 + 4]),
                 reads=[r[f"u{s}"], r["convw"]], writes=[ra])
            for j in (2, 1, 0):
                P.op("dve", lambda e, j=j: e.scalar_tensor_tensor(out=acc[:, :], in0=u[s][:, j:j + 512],
                                                                  scalar=convw[:, wi * 4 + j:wi * 4 + j + 1],
                                                                  in1=acc[:, :], op0=ALU.mult, op1=ALU.add),
                     reads=[r[f"u{s}"], r["convw"], ra], writes=[ra])
            yield
            if kind == "v":
                P.op("act", lambda e: e.activation(out=ob[s][:, :], in_=acc[:, :], func=AF.Silu), reads=[ra], writes=[r[f"ob{s}"]])
                P.dma("pool", S[dst][ti * 128:(ti + 1) * 128, t0:t0 + 512], ob[s][:, :], reads=[r[f"ob{s}"]], writes=[r[dst]])
                return
            P.op("act", lambda e: e.activation(out=sl[:, :], in_=acc[:, :], func=AF.Silu), reads=[ra], writes=[rsl])
            P.op("pool", lambda e: e.tensor_tensor(out=sq[:, :], in0=sl[:, :], in1=sl[:, :], op=ALU.mult), reads=[rsl], writes=[rsq])
            P.op("pe", lambda e: e.matmul(ss_ps[:, :], onesf[:, :], sq[:, :], start=True, stop=True), reads=[r["onesf"], rsq], writes=[rss])
            yield
            P.op("act", lambda e: e.activation(out=sd[:, :], in_=ss_ps[:, :], func=AF.Sqrt, bias=epsc[:, 0:1], scale=1.0),
                 reads=[rss, r["eps"]], writes=[rsd])
            yield
            P.op("dve", lambda e: e.reciprocal(rn[:, :], sd[:, :]), reads=[rsd], writes=[rrn])
            if kind == "k":
                P.op("dve", lambda e: e.tensor_tensor(out=ob[s][:, :], in0=sl[:, :], in1=rn[:, :], op=ALU.mult),
                     reads=[rsl, rrn], writes=[r[f"ob{s}"]])
                P.dma("pool", S[dst][ti * 128:(ti + 1) * 128, t0:t0 + 512], ob[s][:, :], reads=[r[f"ob{s}"]], writes=[r[dst]])
            else:
                P.op("dve", lambda e: e.scalar_tensor_tensor(out=kn[:, :], in0=sl[:, :], scalar=128 ** -0.5, in1=rn[:, :],
                                                             op0=ALU.mult, op1=ALU.mult),
                     reads=[rsl, rrn], writes=[rkn])
                P.op("act", lambda e: e.copy(ob[s][:, :], kn[:, :]), reads=[rkn], writes=[r[f"ob{s}"]])
                P.dma("pool", S[dst][ti * 128:(ti + 1) * 128, t0:t0 + 512], ob[s][:, :], reads=[r[f"ob{s}"]], writes=[r[dst]])
                P.op("pe", lambda e: e.matmul(bc_ps[:, :], sel[:, ti * 128:(ti + 1) * 128], egcT[:, :], start=True, stop=True),
                     reads=[r["sel"], r["egcT"]], writes=[rbc])
                P.op("dve", lambda e: e.tensor_tensor(out=ob2[s][:, :], in0=kn[:, :], in1=bc_ps[:, :], op=ALU.mult),
                     reads=[rkn, rbc], writes=[r[f"ob2{s}"]])
                P.dma("pool", S["gqdT"][ti * 128:(ti + 1) * 128, t0:t0 + 512], ob2[s][:, :], reads=[r[f"ob2{s}"]], writes=[r["gqdT"]])

        for tb in range(NB):
            t0 = tb * 512
            P.dma("sp", at[:, :], projT[rows["a"]:rows["a"] + 16, t0:t0 + 512], writes=[r["at"]])
            P.dma("sp", bt[:, :], projT[rows["b"]:rows["b"] + 16, t0:t0 + 512], writes=[r["bt"]])
            P.op("act", lambda e: e.activation(out=e1[:, :], in_=at[:, :], func=AF.Exp, bias=dtb[:, 0:1], scale=1.0),
                 reads=[r["at"], r["dtb"]], writes=[r["e1"]])
            P.op("dve", lambda e: e.tensor_scalar(out=e1[:, :], in0=e1[:, :], scalar1=1.0, scalar2=None, op0=ALU.add),
                 reads=[r["e1"]], writes=[r["e1"]])
            P.op("act", lambda e: e.activation(out=spt[:, :], in_=e1[:, :], func=AF.Ln), reads=[r["e1"]], writes=[r["spt"]])
            P.op("dve", lambda e: e.tensor_scalar(out=gt[:, :], in0=spt[:, :], scalar1=nega[:, 0:1], scalar2=None, op0=ALU.mult),
                 reads=[r["spt"], r["nega"]], writes=[r["gt"]])
            P.op("act", lambda e: e.activation(out=beta[:, :], in_=bt[:, :], func=AF.Sigmoid), reads=[r["bt"]], writes=[r["beta"]])
            P.op("dve", lambda e: e.tensor_tensor_scan(out=gcT[:, :], data0=cmask[:, :], data1=gt[:, :], initial=0.0,
                                                       op0=ALU.mult, op1=ALU.add),
                 reads=[r["cmask"], r["gt"]], writes=[r["gcT"]])
            P.op("act", lambda e: e.activation(out=egcT[:, :], in_=gcT[:, :], func=AF.Exp), reads=[r["gcT"]], writes=[r["egcT"]])
            for c4 in range(4):
                P.op("pe", lambda e, c4=c4: e.transpose(tr_ps[:, c4, 0:16], gt[:, c4 * 128:(c4 + 1) * 128], identf[0:16, 0:16]),
                     reads=[r["gt"], r["identf"]], writes=[r["bc_ps1"]])
                P.op("pe", lambda e, c4=c4: e.transpose(tr_ps[:, c4, 16:32], beta[:, c4 * 128:(c4 + 1) * 128], identf[0:16, 0:16]),
                     reads=[r["beta"], r["identf"]], writes=[r["bc_ps1"]])
            P.op("dve", lambda e: e.tensor_copy(gbs[:, :, :], tr_ps[:, :, :]), reads=[r["bc_ps1"]], writes=[r["gbs"]])
            P.dma("pool", S["gbt"][t0:t0 + 512, :].rearrange("(c p) x -> p c x", p=128), gbs[:, :, :],
                  reads=[r["gbs"]], writes=[r["gbt_out"]])
            for kind, rbase, dst in (("q", rows["q"], "gqT"), ("k", rows["k"], "gkT"), ("v", rows["v"], "gvT")):
                for ti in range(0, 16, 4):
                    zipper([do_tile(kind, rbase, dst, ti + j, tb, j) for j in range(4)])
        P.barrier()
        P.flush()


def zipper_lag(gens):
    a, b = gens
    a_done = b_done = False
    try:
        next(a)
    except StopIteration:
        a_done = True
    while not (a_done and b_done):
        if not b_done:
            try:
                next(b)
            except StopIteration:
                b_done = True
        if not a_done:
            try:
                next(a)
            except StopIteration:
                a_done = True


def zipper(gens):
    active = list(gens)
    while active:
        for g in list(active):
            try:
                next(g)
            except StopIteration:
                active.remove(g)


def neumann_gen(P, nc, L, LT, rL, rLT, W, r, nlev=6, G=4):
    identb = W["identb"]
    Pk = W["Pk"]; nL = W["nL"]; nLT = W["nLT"]
    pa, pb = W["pa"], W["pb"]
    P.op("pool", lambda e: e.tensor_tensor(out=Pk[0][:, :, :], in0=identb[:, :].unsqueeze(1).to_broadcast([128, G, 128]),
                                           in1=LT[:, :, :], op=ALU.subtract),
         reads=[W["r_identb"], rLT], writes=[r["Pk0"]])
    yield
    curL, curLT, rcL, rcLT = L, LT, rL, rLT
    pi = 0
    for lev in range(nlev):
        s = lev % 2
        for g in range(G):
            P.op("pe", lambda e, g=g, a=curLT, b=curL: e.matmul(pa[:, g, :], a[:, g, :], b[:, g, :], start=True, stop=True),
                 reads=[rcLT, rcL], writes=[r["pa"]])
        if lev < nlev - 1:
            for g in range(G):
                P.op("pe", lambda e, g=g, a=curL, b=curLT: e.matmul(pb[:, g, :], a[:, g, :], b[:, g, :], start=True, stop=True),
                     reads=[rcLT, rcL], writes=[r["pb"]])
        yield
        P.op("act", lambda e, s=s: e.copy(nL[s][:, :, :], pa[:, :, :]), reads=[r["pa"]], writes=[r[f"nL{s}"]])
        if lev < nlev - 1:
            P.op("dve", lambda e, s=s: e.tensor_copy(nLT[s][:, :, :], pb[:, :, :]), reads=[r["pb"]], writes=[r[f"nLT{s}"]])
        yield
        for g in range(G):
            P.op("pe", lambda e, g=g, s=s, pi=pi: e.matmul(pa[:, g, :], nL[s][:, g, :], Pk[pi][:, g, :], start=True, stop=True),
                 reads=[r[f"nL{s}"], r[f"Pk{pi}"]], writes=[r["pa"]])
        yield
        P.op("dve", lambda e, pi=pi: e.tensor_tensor(out=Pk[1 - pi][:, :, :], in0=pa[:, :, :], in1=Pk[pi][:, :, :], op=ALU.add),
             reads=[r["pa"], r[f"Pk{pi}"]], writes=[r[f"Pk{1 - pi}"]])
        yield
        pi = 1 - pi
        curL, curLT, rcL, rcLT = nL[s], nLT[s], r[f"nL{s}"], r[f"nLT{s}"]
    W["result"] = (Pk[pi], r[f"Pk{pi}"])


def phase_gdn_g1(P, nc, S, C, T):
    NCH = T // 128
    G = 4
    with contextlib.ExitStack() as st:
        def sb(name, shape, dt):
            return st.enter_context(nc.sbuf_tensor(_uid() + "g1_" + name, shape, dt))

        def pst(name, shape, dt):
            return st.enter_context(nc.psum_tensor(_uid() + "g1_" + name, shape, dt))
        r = defaultdict(Res)
        triu = sb("triu", [128, 128], F32); onesf = sb("onesf", [128, 128], F32); negones = sb("negones", [128, 128], F32)
        mls = sb("mls", [128, 128], F32); mli = sb("mli", [128, 128], F32)
        identf = sb("identf", [128, 128], F32); identb = sb("identb", [128, 128], BF16)
        gb = [sb(f"gb{i}", [128, 32], F32) for i in range(2)]
        gcs = sb("gcs", [128, 32], F32)
        egc = sb("egc", [128, 16], F32); dtl = sb("dtl", [128, 16], F32); etail = sb("etail", [128, 16], F32)
        elast = [sb(f"elast{i}", [128, 16], F32) for i in range(2)]
        bgc = sb("bgc", [128, 16], F32)
        Gb = sb("Gb", [128, 16, 128], F32); X = sb("X", [128, 16, 128], F32)
        gc_ps = None
        ST = []
        for q in range(2):
            t = {}
            for n in ("kT", "qT", "vT", "L", "attn", "LT", "attnT", "kbg", "ktail", "vb", "wk_sb", "nL0", "nL1", "nLT0", "nLT1", "Pk0", "Pk1"):
                t[n] = sb(f"{n}_{q}", [128, G, 128], BF16)
            for n in ("M1", "dec", "dec_s", "dec_i", "u_sb"):
                t[n] = sb(f"{n}_{q}", [128, G, 128], F32)
            t["g_ps"] = pst(f"g_ps{q}", [128, G, 128], F32)
            t["tr_ps"] = pst(f"tr_ps{q}", [128, 2, G, 128], BF16)
            t["pa"] = pst(f"pa{q}", [128, G, 128], F32)
            t["pb"] = pst(f"pb{q}", [128, G, 128], F32)
            t["r"] = defaultdict(Res)
            t["W"] = {"identb": identb, "r_identb": r["identb"], "nL": [t["nL0"], t["nL1"]], "nLT": [t["nLT0"], t["nLT1"]],
                      "Pk": [t["Pk0"], t["Pk1"]], "pa": t["pa"], "pb": t["pb"]}
            ST.append(t)
        for nm, t_, src in (("triu", triu, "triu"), ("onesf", onesf, "ones"), ("negones", negones, "negones"),
                            ("mls", mls, "mask_ls"), ("mli", mli, "mask_li"), ("identf", identf, "ident")):
            P.dma("sp", t_[:, :], C[src][:, :], writes=[r[nm]])
        P.op("pool", lambda e: e.tensor_copy(identb[:, :], identf[:, :]), reads=[r["identf"]], writes=[r["identb"]])

        def bcg(t, g):
            return t[:, g * G:(g + 1) * G].unsqueeze(2).to_broadcast([128, G, 128])

        def bcm(m):
            return m[:, :].unsqueeze(1).to_broadcast([128, G, 128])

        def grp(c, g, q, sc):
            t = ST[q]
            rr = t["r"]
            t0 = c * 128
            r0 = g * G * 128
            kT, qT, vT = t["kT"], t["qT"], t["vT"]
            g_ps, tr_ps = t["g_ps"], t["tr_ps"]
            for nm, tl, src in (("kT", kT, "gkT"), ("qT", qT, "gqT"), ("vT", vT, "gvT")):
                P.dma("sp", tl[:, :, :], S[src][r0:r0 + G * 128, t0:t0 + 128].rearrange("(h p) t -> p h t", p=128), writes=[rr[nm]])
            P.op("pe", lambda e: e.matmul(g_ps[:, :, :], triu[:, :], Gb[:, g * G:(g + 1) * G, :], start=True, stop=False),
                 reads=[r["triu"], r["Gb"]], writes=[rr["g_ps"]])
            P.op("pe", lambda e: e.matmul(g_ps[:, :, :], negones[:, :], X[:, g * G:(g + 1) * G, :], start=False, stop=True),
                 reads=[r["negones"], r["X"]], writes=[rr["g_ps"]])
            yield
            P.op("dve", lambda e: e.tensor_scalar(out=t["M1"][:, :, :], in0=g_ps[:, :, :], scalar1=0.0, scalar2=None, op0=ALU.min),
                 reads=[rr["g_ps"]], writes=[rr["M1"]])
            yield
            P.op("act", lambda e: e.activation(out=t["dec"][:, :, :], in_=t["M1"][:, :, :], func=AF.Exp), reads=[rr["M1"]], writes=[rr["dec"]])
            for h in range(G):
                P.op("pe", lambda e, h=h: e.matmul(g_ps[:, h, :], kT[:, h, :], kT[:, h, :], start=True, stop=True),
                     reads=[rr["kT"]], writes=[rr["g_ps"]])
            yield
            P.op("pool", lambda e: e.tensor_tensor(out=t["dec_i"][:, :, :], in0=t["dec"][:, :, :], in1=bcm(mli), op=ALU.mult),
                 reads=[rr["dec"], r["mli"]], writes=[rr["dec_i"]])
            P.op("dve", lambda e: e.tensor_tensor(out=t["dec_s"][:, :, :], in0=t["dec"][:, :, :], in1=bcm(mls), op=ALU.mult),
                 reads=[rr["dec"], r["mls"]], writes=[rr["dec_s"]])
            yield
            P.op("dve", lambda e: e.tensor_tensor(out=t["dec_s"][:, :, :], in0=t["dec_s"][:, :, :],
                                                  in1=gb[sc][:, 16 + g * G:16 + (g + 1) * G].unsqueeze(2).to_broadcast([128, G, 128]), op=ALU.mult),
                 reads=[rr["dec_s"], r[f"gb{sc}"]], writes=[rr["dec_s"]])
            yield
            P.op("dve", lambda e: e.tensor_tensor(out=t["L"][:, :, :], in0=g_ps[:, :, :], in1=t["dec_s"][:, :, :], op=ALU.mult),
                 reads=[rr["g_ps"], rr["dec_s"]], writes=[rr["L"]])
            yield
            for h in range(G):
                P.op("pe", lambda e, h=h: e.matmul(g_ps[:, h, :], qT[:, h, :], kT[:, h, :], start=True, stop=True),
                     reads=[rr["kT"], rr["qT"]], writes=[rr["g_ps"]])
            for h in range(G):
                P.op("pe", lambda e, h=h: e.transpose(tr_ps[:, 0, h, :], t["L"][:, h, :], identb[:, :]),
                     reads=[rr["L"], r["identb"]], writes=[rr["tr_ps"]])
            yield
            P.op("dve", lambda e: e.tensor_tensor(out=t["attn"][:, :, :], in0=g_ps[:, :, :], in1=t["dec_i"][:, :, :], op=ALU.mult),
                 reads=[rr["g_ps"], rr["dec_i"]], writes=[rr["attn"]])
            P.op("act", lambda e: e.copy(t["LT"][:, :, :], tr_ps[:, 0, :, :]), reads=[rr["tr_ps"]], writes=[rr["LT"]])
            yield
            for h in range(G):
                P.op("pe", lambda e, h=h: e.transpose(tr_ps[:, 1, h, :], t["attn"][:, h, :], identb[:, :]),
                     reads=[rr["attn"], r["identb"]], writes=[rr["tr_ps"]])
            yield
            P.op("act", lambda e: e.copy(t["attnT"][:, :, :], tr_ps[:, 1, :, :]), reads=[rr["tr_ps"]], writes=[rr["attnT"]])
            P.dma("pool", S["attnT_d"][c, :, g * G:(g + 1) * G, :], t["attnT"][:, :, :],
                  reads=[rr["attnT"]], writes=[r["attnT_out"]])
            yield
            yield from neumann_gen(P, nc, t["L"], t["LT"], rr["L"], rr["LT"], t["W"], rr)
            Pt, rPt = t["W"]["result"]
            for h in range(G):
                P.op("pe", lambda e, h=h: e.transpose(tr_ps[:, 0, h, :], kT[:, h, :], identb[:, :]),
                     reads=[rr["kT"], r["identb"]], writes=[rr["tr_ps"]])
            for h in range(G):
                P.op("pe", lambda e, h=h: e.transpose(tr_ps[:, 1, h, :], vT[:, h, :], identb[:, :]),
                     reads=[rr["vT"], r["identb"]], writes=[rr["tr_ps"]])
            yield
            P.op("dve", lambda e: e.tensor_tensor(out=t["kbg"][:, :, :], in0=tr_ps[:, 0, :, :], in1=bcg(bgc, g), op=ALU.mult),
                 reads=[rr["tr_ps"], r["bgc"]], writes=[rr["kbg"]])
            P.op("dve", lambda e: e.tensor_tensor(out=t["ktail"][:, :, :], in0=tr_ps[:, 0, :, :], in1=bcg(etail, g), op=ALU.mult),
                 reads=[rr["tr_ps"], r["etail"]], writes=[rr["ktail"]])
            P.op("dve", lambda e: e.tensor_tensor(out=t["vb"][:, :, :], in0=tr_ps[:, 1, :, :],
                                                  in1=gb[sc][:, 16 + g * G:16 + (g + 1) * G].unsqueeze(2).to_broadcast([128, G, 128]), op=ALU.mult),
                 reads=[rr["tr_ps"], r[f"gb{sc}"]], writes=[rr["vb"]])
            P.dma("pool", S["ktl_d"][t0:t0 + 128, r0:r0 + G * 128], t["ktail"][:, :, :].rearrange("p h d -> p (h d)"),
                  reads=[rr["ktail"]], writes=[r["ktl_out"]])
            yield
            for h in range(G):
                P.op("pe", lambda e, h=h: e.matmul(g_ps[:, h, :], Pt[:, h, :], t["vb"][:, h, :], start=True, stop=True),
                     reads=[rPt, rr["vb"]], writes=[rr["g_ps"]])
            yield
            P.op("act", lambda e: e.copy(t["u_sb"][:, :, :], g_ps[:, :, :]), reads=[rr["g_ps"]], writes=[rr["u_sb"]])
            P.dma("pool", S["u_d"][t0:t0 + 128, r0:r0 + G * 128], t["u_sb"][:, :, :].rearrange("p h d -> p (h d)"),
                  reads=[rr["u_sb"]], writes=[r["u_out"]])
            yield
            for h in range(G):
                P.op("pe", lambda e, h=h: e.matmul(g_ps[:, h, :], t["kbg"][:, h, :], Pt[:, h, :], start=True, stop=True),
                     reads=[rPt, rr["kbg"]], writes=[rr["g_ps"]])
            yield
            P.op("act", lambda e: e.copy(t["wk_sb"][:, :, :], g_ps[:, :, :]), reads=[rr["g_ps"]], writes=[rr["wk_sb"]])
            P.dma("pool", S["wkT_d"][r0:r0 + G * 128, t0:t0 + 128].rearrange("(h p) t -> p h t", p=128), t["wk_sb"][:, :, :],
                  reads=[rr["wk_sb"]], writes=[r["wk_out"]])
            yield

        for c in range(NCH):
            t0 = c * 128
            sc = c % 2
            gps0 = ST[0]["g_ps"]
            rg0 = ST[0]["r"]["g_ps"]
            P.dma("sp", gb[sc][:, :], S["gbt"][t0:t0 + 128, :], writes=[r[f"gb{sc}"]])
            P.op("pe", lambda e, sc=sc: e.matmul(gps0[:, 0, 0:16], triu[:, :], gb[sc][:, 0:16], start=True, stop=True),
                 reads=[r["triu"], r[f"gb{sc}"]], writes=[rg0])
            P.op("pe", lambda e, sc=sc: e.matmul(gps0[:, 0, 16:32], onesf[:, :], gb[sc][:, 0:16], start=True, stop=True),
                 reads=[r["onesf"], r[f"gb{sc}"]], writes=[rg0])
            P.op("dve", lambda e: e.tensor_copy(gcs[:, :], gps0[:, 0, 0:32]), reads=[rg0], writes=[r["gcs"]])
            P.op("act", lambda e: e.activation(out=egc[:, :], in_=gcs[:, 0:16], func=AF.Exp), reads=[r["gcs"]], writes=[r["egc"]])
            P.op("dve", lambda e: e.tensor_tensor(out=dtl[:, :], in0=gcs[:, 16:32], in1=gcs[:, 0:16], op=ALU.subtract),
                 reads=[r["gcs"]], writes=[r["dtl"]])
            P.op("act", lambda e: e.activation(out=etail[:, :], in_=dtl[:, :], func=AF.Exp), reads=[r["dtl"]], writes=[r["etail"]])
            P.op("act", lambda e, sc=sc: e.activation(out=elast[sc][:, :], in_=gcs[:, 16:32], func=AF.Exp),
                 reads=[r["gcs"]], writes=[r[f"elast{sc}"]])
            P.dma("pool", S["els_d"][c, :, :], elast[sc][:, :], reads=[r[f"elast{sc}"]], writes=[r["els_out"]])
            P.op("dve", lambda e, sc=sc: e.tensor_tensor(out=bgc[:, :], in0=gb[sc][:, 16:32], in1=egc[:, :], op=ALU.mult),
                 reads=[r[f"gb{sc}"], r["egc"]], writes=[r["bgc"]])
            P.op("pool", lambda e, sc=sc: e.tensor_copy(Gb[:, :, :], gb[sc][:, 0:16].unsqueeze(2).to_broadcast([128, 16, 128])),
                 reads=[r[f"gb{sc}"]], writes=[r["Gb"]])
            P.op("pool", lambda e: e.tensor_tensor(out=X[:, :, :], in0=Gb[:, :, :],
                                                   in1=triu[:, :].unsqueeze(1).to_broadcast([128, 16, 128]), op=ALU.mult),
                 reads=[r["Gb"], r["triu"]], writes=[r["X"]])
            for g0 in (0, 2):
                zipper([grp(c, g0, 0, sc), grp(c, g0 + 1, 1, sc)])
        P.barrier()
        P.flush()


def phase_gdn_g2(P, nc, projT, S, C, yT, normw_dram, T, zrow, kc0=16):
    NCH = T // 128
    G = 4
    H = 16
    with contextlib.ExitStack() as st:
        def sb(name, shape, dt):
            return st.enter_context(nc.sbuf_tensor(_uid() + "g2_" + name, shape, dt))

        def pst(name, shape, dt):
            return st.enter_context(nc.psum_tensor(_uid() + "g2_" + name, shape, dt))
        r = defaultdict(Res)
        identf = sb("identf", [128, 128], F32); identb = sb("identb", [128, 128], BF16)
        nrm = sb("nrm", [128, 1], F32); epsc = sb("epsc", [128, 1], F32); mh = sb("mh", [128, G], F32)
        wkT = [sb(f"wkT{i}", [128, H, 128], BF16) for i in range(2)]
        qdT = [sb(f"qdT{i}", [128, H, 128], BF16) for i in range(2)]
        atT = [sb(f"atT{i}", [128, H, 128], BF16) for i in range(2)]
        ktl = [sb(f"ktl{i}", [128, H, 128], BF16) for i in range(2)]
        uu = [sb(f"uu{i}", [128, H, 128], F32) for i in range(2)]
        zt = [sb(f"zt{i}", [128, H, 128], F32) for i in range(2)]
        els = [sb(f"els{i}", [128, H], F32) for i in range(2)]
        Sf = sb("Sf", [128, H, 128], F32); Sb = sb("Sb", [128, H, 128], BF16)
        TS = []
        for q in range(2):
            d = {"Stmp": sb(f"Stmp{q}", [128, G, 128], F32), "vnew": sb(f"vnew{q}", [128, G, 128], BF16),
                 "o_sb": sb(f"o_sb{q}", [128, G, 128], F32), "osq": sb(f"osq{q}", [128, G, 128], F32),
                 "s2": sb(f"s2{q}", [128, G], F32), "sd": sb(f"sd{q}", [128, G], F32), "rstd": sb(f"rstd{q}", [128, G], F32),
                 "on": sb(f"on{q}", [128, G, 128], BF16), "sz": sb(f"sz{q}", [128, G, 128], F32), "yg": sb(f"yg{q}", [128, G, 128], F32)}
            TS.append(d)
        yfin = [sb(f"yfin{i}", [128, G, 128], BF16) for i in range(2)]
        ws_ps = [pst(f"ws_ps{i}", [128, G, 128], F32) for i in range(2)]
        o_ps = [pst(f"o_ps{i}", [128, G, 128], F32) for i in range(2)]
        kv_ps = [pst(f"kv_ps{i}", [128, G, 128], F32) for i in range(2)]
        tr_ps2 = [pst(f"tr_ps{q}", [128, 2, G, 128], BF16) for q in range(2)]
        P.dma("sp", identf[:, :], C["ident"][:, :], writes=[r["identf"]])
        P.dma("sp", nrm[:, :], normw_dram[:, :], writes=[r["nrm"]])
        P.op("pool", lambda e: e.tensor_copy(identb[:, :], identf[:, :]), reads=[r["identf"]], writes=[r["identb"]])
        P.op("pool", lambda e: e.memset(epsc[:, :], 1e-6), writes=[r["eps"]])
        P.op("pool", lambda e: e.memset(mh[:, :], -0.5), writes=[r["mh"]])
        P.op("pool", lambda e: e.memset(Sf[:, :, :].rearrange("p h e -> p (h e)"), 0.0), writes=[r["Sf"]])
        P.op("pool", lambda e: e.memset(Sb[:, :, :].rearrange("p h e -> p (h e)"), 0.0), writes=[r["Sb"]])
        def grp(c, g, b, s):
            t0 = c * 128
            T_ = TS[b]
            Stmp, vnew, o_sb, osq, s2, sd, rstd, on, sz, yg = [T_[n] for n in ("Stmp", "vnew", "o_sb", "osq", "s2", "sd", "rstd", "on", "sz", "yg")]
            tr_ps = tr_ps2[b]
            hs = slice(g * G, (g + 1) * G)
            rS = r[f"Sf{g}"]; rSb = r[f"Sb{g}"]
            for h in range(G):
                hh = g * G + h
                P.op("pe", lambda e, h=h, hh=hh, s=s, b=b: e.matmul(ws_ps[b][:, h, :], wkT[s][:, hh, :], Sb[:, hh, :], start=True, stop=True),
                     reads=[r[f"wkT{s}"], rSb, r["Sb"]], writes=[r[f"ws_ps{b}"]])
            yield
            P.op("dve", lambda e, s=s, b=b, hs=hs: e.tensor_tensor(out=vnew[:, :, :], in0=uu[s][:, hs, :], in1=ws_ps[b][:, :, :], op=ALU.subtract),
                 reads=[r[f"uu{s}"], r[f"ws_ps{b}"]], writes=[r["vnew" + str(b)]])
            yield
            for h in range(G):
                hh = g * G + h
                P.op("pe", lambda e, h=h, hh=hh, s=s, b=b: e.matmul(o_ps[b][:, h, :], qdT[s][:, hh, :], Sb[:, hh, :], start=True, stop=False),
                     reads=[r[f"qdT{s}"], rSb, r["Sb"]], writes=[r[f"o_ps{b}"]])
                P.op("pe", lambda e, h=h, hh=hh, s=s, b=b: e.matmul(o_ps[b][:, h, :], atT[s][:, hh, :], vnew[:, h, :], start=False, stop=True),
                     reads=[r[f"atT{s}"], r["vnew" + str(b)]], writes=[r[f"o_ps{b}"]])
            for h in range(G):
                hh = g * G + h
                P.op("pe", lambda e, h=h, hh=hh, s=s, b=b: e.matmul(kv_ps[b][:, h, :], ktl[s][:, hh, :], vnew[:, h, :], start=True, stop=True),
                     reads=[r[f"ktl{s}"], r["vnew" + str(b)]], writes=[r[f"kv_ps{b}"]])
            yield
            P.op("dve", lambda e, s=s, hs=hs, g=g: e.tensor_tensor(out=Stmp[:, :, :], in0=Sf[:, hs, :],
                                                                  in1=els[s][:, g * G:(g + 1) * G].unsqueeze(2).to_broadcast([128, G, 128]),
                                                                  op=ALU.mult),
                 reads=[rS, r["Sf"], r[f"els{s}"]], writes=[r["Stmp" + str(b)]])
            P.op("dve", lambda e, hs=hs, b=b: e.tensor_tensor(out=Sf[:, hs, :], in0=Stmp[:, :, :], in1=kv_ps[b][:, :, :], op=ALU.add),
                 reads=[r["Stmp" + str(b)], r[f"kv_ps{b}"]], writes=[rS])
            P.op("act", lambda e, hs=hs: e.copy(Sb[:, hs, :], Sf[:, hs, :]), reads=[rS], writes=[rSb])
            yield
            P.op("act", lambda e, b=b: e.copy(o_sb[:, :, :], o_ps[b][:, :, :]), reads=[r[f"o_ps{b}"]], writes=[r["o_sb" + str(b)]])
            P.op("pool", lambda e: e.tensor_tensor(out=osq[:, :, :], in0=o_sb[:, :, :], in1=o_sb[:, :, :], op=ALU.mult),
                 reads=[r["o_sb" + str(b)]], writes=[r["osq" + str(b)]])
            yield
            P.op("dve", lambda e: e.tensor_reduce(out=s2[:, :], in_=osq[:, :, :], axis=AX.X, op=ALU.add), reads=[r["osq" + str(b)]], writes=[r["s2" + str(b)]])
            P.op("dve", lambda e: e.tensor_scalar(out=sd[:, :], in0=s2[:, :], scalar1=1.0 / 128, scalar2=1e-6, op0=ALU.mult, op1=ALU.add),
                 reads=[r["s2" + str(b)]], writes=[r["sd" + str(b)]])
            yield
            P.op("pool", lambda e: e.tensor_tensor(out=rstd[:, :], in0=sd[:, :], in1=mh[:, :], op=ALU.pow),
                 reads=[r["sd" + str(b)], r["mh"]], writes=[r["rstd" + str(b)]])
            P.op("dve", lambda e: e.tensor_tensor(out=on[:, :, :], in0=o_sb[:, :, :],
                                                  in1=rstd[:, :].unsqueeze(2).to_broadcast([128, G, 128]), op=ALU.mult),
                 reads=[r["o_sb" + str(b)], r["rstd" + str(b)]], writes=[r["on" + str(b)]])
            yield
            for h in range(G):
                P.op("pe", lambda e, h=h: e.transpose(tr_ps[:, 0, h, :], on[:, h, :], identb[:, :]),
                     reads=[r["on" + str(b)], r["identb"]], writes=[r["tr_ps" + str(b)]])
            P.op("act", lambda e, s=s, hs=hs: e.activation(out=sz[:, :, :], in_=zt[s][:, hs, :], func=AF.Silu),
                 reads=[r[f"zt{s}"]], writes=[r["sz" + str(b)]])
            yield
            P.op("dve", lambda e: e.scalar_tensor_tensor(out=yg[:, :, :], in0=tr_ps[:, 0, :, :], scalar=nrm[:, 0:1], in1=sz[:, :, :],
                                                         op0=ALU.mult, op1=ALU.mult),
                 reads=[r["tr_ps" + str(b)], r["nrm"], r["sz" + str(b)]], writes=[r["yg" + str(b)]])
            P.op("pool", lambda e, b=b: e.tensor_copy(yfin[b][:, :, :], yg[:, :, :]), reads=[r["yg" + str(b)]], writes=[r[f"yfin{b}"]])
            P.dma("pool", yT[kc0 + g * G:kc0 + (g + 1) * G, :, t0:t0 + 128].rearrange("k p t -> p k t"), yfin[b][:, :, :],
                  reads=[r[f"yfin{b}"]], writes=[r["y_out"]])

        it = 0
        for c in range(NCH):
            t0 = c * 128
            s = c % 2
            for g in range(4):
                r0 = g * G * 128
                hs = slice(g * G, (g + 1) * G)
                P.dma("sp", wkT[s][:, hs, :], S["wkT_d"][r0:r0 + G * 128, t0:t0 + 128].rearrange("(h p) t -> p h t", p=128),
                      writes=[r[f"wkT{s}"]])
                P.dma("sp", qdT[s][:, hs, :], S["gqdT"][r0:r0 + G * 128, t0:t0 + 128].rearrange("(h p) t -> p h t", p=128),
                      writes=[r[f"qdT{s}"]])
                P.dma("sp", atT[s][:, hs, :], S["attnT_d"][c, :, g * G:(g + 1) * G, :],
                      writes=[r[f"atT{s}"]])
                P.dma("sp", ktl[s][:, hs, :].rearrange("p h d -> p (h d)"), S["ktl_d"][t0:t0 + 128, r0:r0 + G * 128],
                      writes=[r[f"ktl{s}"]])
                P.dma("sp", uu[s][:, hs, :].rearrange("p h d -> p (h d)"), S["u_d"][t0:t0 + 128, r0:r0 + G * 128],
                      writes=[r[f"uu{s}"]])
                P.dma("sp", zt[s][:, hs, :], projT[zrow + r0:zrow + r0 + G * 128, t0:t0 + 128].rearrange("(h p) t -> p h t", p=128),
                      writes=[r[f"zt{s}"]])
            P.dma("sp", els[s][:, :], S["els_d"][c, :, :], writes=[r[f"els{s}"]])
            for g0 in (0, 2):
                zipper([grp(c, g0, 0, s), grp(c, g0 + 1, 1, s)])
        P.barrier()
        P.flush()


def rwkv_consts():
    c = {}
    bo = np.zeros((128, 128), np.float32)
    bo[:64, :64] = 1.0
    bo[64:, 64:] = 1.0
    c["blockones"] = bo
    return c


def phase_rwkv_pre(P, nc, projT, S, C, prm, T, row0):
    NB = T // 512
    with contextlib.ExitStack() as st:
        def sb(name, shape, dt):
            return st.enter_context(nc.sbuf_tensor(_uid() + "wp_" + name, shape, dt))

        def pst(name, shape, dt):
            return st.enter_context(nc.psum_tensor(_uid() + "wp_" + name, shape, dt))
        r = defaultdict(Res)
        mu = sb("mu", [128, 33], F32); omm = sb("omm", [128, 33], F32)
        w0 = sb("w0", [128, 8], F32); nw0 = sb("nw0", [128, 8], F32); a0 = sb("a0", [128, 8], F32)
        kk_ = sb("kk_", [128, 8], F32); ka = sb("ka", [128, 8], F32); omka = sb("omka", [128, 8], F32); rk = sb("rk", [128, 8], F32)
        lw2 = sb("lw2", [128, 1024], F32)
        bones = sb("bones", [128, 128], F32)
        cmask = sb("cmask", [128, 512], F32)
        onec = sb("onec", [128, 1], F32); negh = sb("negh", [128, 1], F32); epsc = sb("epsc", [128, 1], F32)
        ul = sb("ul", [128, 513], F32); ml = sb("ml", [128, 512], F32); th = sb("th", [128, 512], F32)
        TS = []
        for q in range(2):
            d = {}
            for j in range(4):
                d[f"u{j}"] = sb(f"u{j}_{q}", [128, 513], F32); d[f"mx{j}"] = sb(f"mx{j}_{q}", [128, 512], F32)
            for n in ("tmp", "e1", "spt", "e2", "cw", "ecw", "cwm", "ecwm", "einv", "av", "kk0", "sq", "sd", "rn", "kkn", "fac", "k2", "kka", "prod"):
                d[n] = sb(f"{n}_{q}", [128, 512], F32)
            TS.append(d)
        tmp = sb("tmp_l", [128, 512], F32)
        obf = {n: [sb(f"o_{n}{i}", [128, 512], BF16) for i in range(2)] for n in ("rt", "kkt", "kh", "kka", "v")}
        of32 = {n: [sb(f"o_{n}{i}", [128, 512], F32) for i in range(2)] for n in ("bon", "sz")}
        pcs = [sb(f"pcs{i}", [128, 4], F32) for i in range(2)]
        for q in range(2):
            for n in ("wl_ps", "a_ps", "ss_ps", "sb_ps"):
                TS[q][n] = pst(f"{n}_{q}", [128, 512], F32)
        for nm, t_, src in (("mu", mu, "muT"), ("w0", w0, "w0T"), ("a0", a0, "a0T"), ("kk_", kk_, "kkT"), ("ka", ka, "kaT"),
                            ("rk", rk, "rkT"), ("lw2", lw2, "lw2")):
            P.dma("sp", t_[:, :], prm[src][:, :], writes=[r[nm]])
        P.dma("sp", bones[:, :], C["blockones"][:, :], writes=[r["bones"]])
        P.dma("sp", cmask[:, :], C["cmask128"][:, :], writes=[r["cmask"]])
        P.op("pool", lambda e: e.memset(onec[:, :], 1.0), writes=[r["onec"]])
        P.op("pool", lambda e: e.memset(negh[:, :], -0.5), writes=[r["negh"]])
        P.op("pool", lambda e: e.memset(epsc[:, :], 1e-6), writes=[r["eps"]])
        P.op("dve", lambda e: e.tensor_scalar(out=omm[:, :], in0=mu[:, :], scalar1=-1.0, scalar2=1.0, op0=ALU.mult, op1=ALU.add),
             reads=[r["mu"]], writes=[r["omm"]])
        P.op("dve", lambda e: e.tensor_scalar(out=omka[:, :], in0=ka[:, :], scalar1=-1.0, scalar2=1.0, op0=ALU.mult, op1=ALU.add),
             reads=[r["ka"]], writes=[r["omka"]])
        P.op("dve", lambda e: e.tensor_scalar(out=nw0[:, :], in0=w0[:, :], scalar1=-1.0, scalar2=None, op0=ALU.mult),
             reads=[r["w0"]], writes=[r["nw0"]])

        def load_mix(ut, rut, mt, rmt, row, ti, tb, tmp=tmp, rtmp=None):
            rtmp = rtmp if rtmp is not None else r["tmp_l"]
            t0 = tb * 512
            if tb == 0:
                P.op("pool", lambda e: e.memset(ut[:, 0:1], 0.0), writes=[rut])
                P.dma("sp", ut[:, 1:513], projT[row:row + 128, 0:512], writes=[rut])
            else:
                P.dma("sp", ut[:, :], projT[row:row + 128, t0 - 1:t0 + 512], writes=[rut])
            P.op("dve", lambda e: e.tensor_scalar(out=tmp[:, :], in0=ut[:, 1:513], scalar1=omm[:, ti:ti + 1], scalar2=None, op0=ALU.mult),
                 reads=[rut, r["omm"]], writes=[rtmp])
            P.op("dve", lambda e: e.scalar_tensor_tensor(out=mt[:, :], in0=ut[:, 0:512], scalar=mu[:, ti:ti + 1], in1=tmp[:, :],
                                                         op0=ALU.mult, op1=ALU.add),
                 reads=[rut, r["mu"], rtmp], writes=[rmt])
        def do_ct(ct, tb, s):
            t0 = tb * 512
            T_ = TS[s]
            rq = lambda n: r[f"{n}_{s}"]
            (e1, spt, e2, cw, ecw, cwm, ecwm, einv, av, kk0, sq, sd, rn, kkn, fac, k2, kka, prod) = [T_[n] for n in (
                "e1", "spt", "e2", "cw", "ecw", "cwm", "ecwm", "einv", "av", "kk0", "sq", "sd", "rn", "kkn", "fac", "k2", "kka", "prod")]
            wl_ps, a_ps, ss_ps, sb_ps = T_["wl_ps"], T_["a_ps"], T_["ss_ps"], T_["sb_ps"]
            for j in range(4):
                load_mix(T_[f"u{j}"], rq(f"u{j}"), T_[f"mx{j}"], rq(f"mx{j}"), row0 + j * 1024 + ct * 128, j * 8 + ct, tb, T_["tmp"], rq("tmp"))
            rm, km, vm, zm = T_['mx0'], T_['mx1'], T_['mx2'], T_['mx3']
            yield
            P.op("pe", lambda e, ct=ct: e.matmul(wl_ps[:, :], lw2[0:64, ct * 128:(ct + 1) * 128], th[0:64, :], start=True, stop=True),
                 reads=[r["lw2"], r["th"]], writes=[rq("wl_ps")])
            P.op("pe", lambda e, ct=ct: e.matmul(a_ps[:, :], lw2[64:128, ct * 128:(ct + 1) * 128], ml[64:128, :], start=True, stop=True),
                 reads=[r["lw2"], r["ml"]], writes=[rq("a_ps")])
            P.op("act", lambda e, ct=ct: e.activation(out=e1[:, :], in_=wl_ps[:, :], func=AF.Exp, bias=nw0[:, ct:ct + 1], scale=-1.0),
                 reads=[rq("wl_ps"), r["nw0"]], writes=[rq("e1")])
            P.op("act", lambda e: e.activation(out=spt[:, :], in_=e1[:, :], func=AF.Ln, bias=onec[:, 0:1], scale=1.0),
                 reads=[rq("e1"), r["onec"]], writes=[rq("spt")])
            P.op("act", lambda e: e.activation(out=e2[:, :], in_=spt[:, :], func=AF.Exp, bias=negh[:, 0:1], scale=-1.0),
                 reads=[rq("spt"), r["negh"]], writes=[rq("e2")])
            P.op("dve", lambda e: e.tensor_tensor_scan(out=cw[:, :], data0=cmask[:, :], data1=e2[:, :], initial=0.0,
                                                       op0=ALU.mult, op1=ALU.subtract),
                 reads=[r["cmask"], rq("e2")], writes=[rq("cw")])
            P.op("act", lambda e: e.activation(out=ecw[:, :], in_=cw[:, :], func=AF.Exp), reads=[rq("cw")], writes=[rq("ecw")])
            P.op("pool", lambda e: e.tensor_tensor(out=cwm[:, :], in0=cw[:, :], in1=e2[:, :], op=ALU.add),
                 reads=[rq("cw"), rq("e2")], writes=[rq("cwm")])
            P.op("act", lambda e: e.activation(out=ecwm[:, :], in_=cwm[:, :], func=AF.Exp), reads=[rq("cwm")], writes=[rq("ecwm")])
            P.op("act", lambda e: e.activation(out=einv[:, :], in_=cw[:, :], func=AF.Exp, scale=-1.0), reads=[rq("cw")], writes=[rq("einv")])
            yield
            P.op("act", lambda e, ct=ct: e.activation(out=av[:, :], in_=a_ps[:, :], func=AF.Sigmoid, bias=a0[:, ct:ct + 1], scale=1.0),
                 reads=[rq("a_ps"), r["a0"]], writes=[rq("av")])
            yield
            P.op("dve", lambda e, ct=ct: e.tensor_scalar(out=kk0[:, :], in0=km[:, :], scalar1=kk_[:, ct:ct + 1], scalar2=None, op0=ALU.mult),
                 reads=[rq("mx1"), r["kk_"]], writes=[rq("kk0")])
            P.op("pool", lambda e: e.tensor_tensor(out=sq[:, :], in0=kk0[:, :], in1=kk0[:, :], op=ALU.mult), reads=[rq("kk0")], writes=[rq("sq")])
            P.op("pe", lambda e: e.matmul(ss_ps[:, :], bones[:, :], sq[:, :], start=True, stop=True),
                 reads=[r["bones"], rq("sq")], writes=[rq("ss_ps")])
            yield
            P.op("act", lambda e: e.activation(out=sd[:, :], in_=ss_ps[:, :], func=AF.Sqrt, bias=epsc[:, 0:1], scale=1.0),
                 reads=[rq("ss_ps"), r["eps"]], writes=[rq("sd")])
            P.op("dve", lambda e: e.reciprocal(rn[:, :], sd[:, :]), reads=[rq("sd")], writes=[rq("rn")])
            P.op("dve", lambda e: e.tensor_tensor(out=kkn[:, :], in0=kk0[:, :], in1=rn[:, :], op=ALU.mult),
                 reads=[rq("kk0"), rq("rn")], writes=[rq("kkn")])
            P.op("dve", lambda e, ct=ct: e.tensor_scalar(out=fac[:, :], in0=av[:, :], scalar1=ka[:, ct:ct + 1], scalar2=omka[:, ct:ct + 1],
                                                         op0=ALU.mult, op1=ALU.add),
                 reads=[rq("av"), r["ka"], r["omka"]], writes=[rq("fac")])
            P.op("dve", lambda e: e.tensor_tensor(out=k2[:, :], in0=km[:, :], in1=fac[:, :], op=ALU.mult),
                 reads=[rq("mx1"), rq("fac")], writes=[rq("k2")])
            P.op("pool", lambda e: e.tensor_tensor(out=kka[:, :], in0=kkn[:, :], in1=av[:, :], op=ALU.mult),
                 reads=[rq("kkn"), rq("av")], writes=[rq("kka")])
            yield
            P.op("dve", lambda e, s=s: e.tensor_tensor(out=obf["rt"][s][:, :], in0=rm[:, :], in1=ecw[:, :], op=ALU.mult),
                 reads=[rq("mx0"), rq("ecw")], writes=[r[f"o_rt{s}"]])
            P.op("dve", lambda e, s=s: e.tensor_tensor(out=obf["kkt"][s][:, :], in0=kkn[:, :], in1=ecwm[:, :], op=ALU.mult),
                 reads=[rq("kkn"), rq("ecwm")], writes=[r[f"o_kkt{s}"]])
            P.op("dve", lambda e, s=s: e.tensor_tensor(out=obf["kh"][s][:, :], in0=k2[:, :], in1=einv[:, :], op=ALU.mult),
                 reads=[rq("k2"), rq("einv")], writes=[r[f"o_kh{s}"]])
            P.op("pool", lambda e, s=s: e.tensor_tensor(out=obf["kka"][s][:, :], in0=kka[:, :], in1=einv[:, :], op=ALU.mult),
                 reads=[rq("kka"), rq("einv")], writes=[r[f"o_kka{s}"]])
            P.op("pool", lambda e, s=s: e.tensor_copy(obf["v"][s][:, :], vm[:, :]), reads=[rq("mx2")], writes=[r[f"o_v{s}"]])
            P.op("act", lambda e, s=s: e.activation(out=of32["sz"][s][:, :], in_=zm[:, :], func=AF.Silu), reads=[rq("mx3")], writes=[r[f"o_sz{s}"]])
            yield
            P.op("pool", lambda e: e.tensor_tensor(out=prod[:, :], in0=rm[:, :], in1=k2[:, :], op=ALU.mult),
                 reads=[rq("mx0"), rq("k2")], writes=[rq("prod")])
            P.op("dve", lambda e, ct=ct: e.tensor_scalar(out=prod[:, :], in0=prod[:, :], scalar1=rk[:, ct:ct + 1], scalar2=None, op0=ALU.mult),
                 reads=[rq("prod"), r["rk"]], writes=[rq("prod")])
            P.op("pe", lambda e: e.matmul(sb_ps[:, :], bones[:, :], prod[:, :], start=True, stop=True),
                 reads=[r["bones"], rq("prod")], writes=[rq("sb_ps")])
            P.op("dve", lambda e, s=s: e.tensor_tensor(out=of32["bon"][s][:, :], in0=vm[:, :], in1=sb_ps[:, :], op=ALU.mult),
                 reads=[rq("mx2"), rq("sb_ps")], writes=[r[f"o_bon{s}"]])
            P.op("pool", lambda e, s=s: e.tensor_copy(pcs[s][:, :], ecw[:, 127:512:128]), reads=[rq("ecw")], writes=[r[f"pcs{s}"]])
            rows_ = slice(ct * 128, (ct + 1) * 128)
            for n, dst in (("rt", "rtT"), ("kkt", "kktT"), ("kh", "khT"), ("kka", "kkaT"), ("v", "rvT")):
                P.dma("pool", S[dst][rows_, t0:t0 + 512], obf[n][s][:, :], reads=[r[f"o_{n}{s}"]], writes=[r[dst]])
            for n, dst in (("bon", "bonT"), ("sz", "szT")):
                P.dma("pool", S[dst][rows_, t0:t0 + 512], of32[n][s][:, :], reads=[r[f"o_{n}{s}"]], writes=[r[dst]])
            P.dma("pool", S["pc_d"][ct, :, tb * 4:(tb + 1) * 4], pcs[s][:, :], reads=[r[f"pcs{s}"]], writes=[r["pc_d"]])

        cnt = 0
        for tb in range(NB):
            t0 = tb * 512
            load_mix(ul, r["ul"], ml, r["ml"], row0 + 4096, 32, tb)
            P.op("act", lambda e: e.activation(out=th[0:64, :], in_=ml[0:64, :], func=AF.Tanh), reads=[r["ml"]], writes=[r["th"]])
            for ct in range(0, 8, 2):
                zipper([do_ct(ct, tb, 0), do_ct(ct + 1, tb, 1)])
        P.barrier()
        P.flush()


def phase_rwkv_r1(P, nc, S, C, T):
    NCH = T // 128
    G = 4
    with contextlib.ExitStack() as st:
        def sb(name, shape, dt):
            return st.enter_context(nc.sbuf_tensor(_uid() + "r1_" + name, shape, dt))

        def pst(name, shape, dt):
            return st.enter_context(nc.psum_tensor(_uid() + "r1_" + name, shape, dt))
        r = defaultdict(Res)
        mls = sb("mls", [128, 128], F32); mus = sb("mus", [128, 128], F32); mui = sb("mui", [128, 128], F32); nmui = sb("nmui", [128, 128], F32)
        identf = sb("identf", [128, 128], F32); identb = sb("identb", [128, 128], BF16)
        tl = {n: [sb(f"{n}{i}", [128, 2, 128], BF16) for i in range(2)] for n in ("rt", "kkt", "kh", "kka", "v")}
        tz = {n: [sb(f"z{n}{i}", [128, 2, 2, 128], BF16) for i in range(2)] for n in ("kkt", "kh", "kka")}
        L = sb("L", [128, G, 128], BF16); LT = sb("LT", [128, G, 128], BF16)
        om = {n: [sb(f"{n}{i}", [128, G, 128], BF16) for i in range(2)] for n in ("akv", "bkv", "nbab", "tinv")}
        tk = {n: [sb(f"tk_{n}{i}", [128, 2, 128], BF16) for i in range(2)] for n in ("v", "kh", "kka")}
        W = {"identb": identb,
             "nL": [sb(f"nL{i}", [128, G, 128], BF16) for i in range(2)],
             "nLT": [sb(f"nLT{i}", [128, G, 128], BF16) for i in range(2)],
             "Pk": [sb(f"Pk{i}", [128, G, 128], BF16) for i in range(2)],
             "pa": pst("pa", [128, G, 128], F32), "pb": pst("pb", [128, G, 128], F32), "pp": pst("pp", [128, G, 128], F32)}
        s_ps = [pst(f"s_ps{i}", [128, G, 128], F32) for i in range(3)]
        tr_ps = pst("tr_ps", [128, 8, 128], BF16)
        for nm, t_, src in (("mls", mls, "mask_ls"), ("mus", mus, "mask_us"), ("mui", mui, "mask_ui"), ("identf", identf, "ident")):
            P.dma("sp", t_[:, :], C[src][:, :], writes=[r[nm]])
        P.op("pool", lambda e: e.tensor_copy(identb[:, :], identf[:, :]), reads=[r["identf"]], writes=[r["identb"]])
        P.op("dve", lambda e: e.tensor_scalar(out=nmui[:, :], in0=mui[:, :], scalar1=-1.0, scalar2=None, op0=ALU.mult),
             reads=[r["mui"]], writes=[r["nmui"]])

        def bcm(m):
            return m[:, :].unsqueeze(1).to_broadcast([128, G, 128])
        it = 0
        srcs = {"rt": "rtT", "kkt": "kktT", "kh": "khT", "kka": "kkaT", "v": "rvT"}
        for n in tz:
            for i in range(2):
                P.op("pool", lambda e, n=n, i=i: e.memset(tz[n][i][:, :, :, :].rearrange("p a q t -> p (a q t)"), 0.0), writes=[r[f"z{n}{i}"]])
        for c in range(NCH):
            t0 = c * 128
            for g in range(4):
                s = it % 2
                it += 1
                for n in tl:
                    P.dma("sp", tl[n][s][:, :, :], S[srcs[n]][g * 256:(g + 1) * 256, t0:t0 + 128].rearrange("(q p) t -> p q t", p=128),
                          writes=[r[f"{n}{s}"]])

                for n in tz:
                    srcv = S[srcs[n]][g * 256:(g + 1) * 256, t0:t0 + 128].rearrange("(q p) t -> p q t", p=128)
                    P.dma("sp", tz[n][s][0:64, 0, :, :], srcv[0:64], writes=[r[f"z{n}{s}"]])
                    P.dma("sp", tz[n][s][64:128, 1, :, :], srcv[64:128], writes=[r[f"z{n}{s}"]])

                def hz(n, h):
                    return tz[n][s][:, h % 2, h // 2, :]

                def hv(n, h):
                    return tl[n][s][:, h // 2, :]
                for h in range(G):
                    P.op("pe", lambda e, h=h, a=hz("kkt", h), b=hv("kka", h): e.matmul(s_ps[0][:, h, :], a, b, start=True, stop=True),
                         reads=[r[f"zkkt{s}"], r[f"kka{s}"]], writes=[r["s_ps0"]])
                for h in range(G):
                    P.op("pe", lambda e, h=h, a=hz("kka", h), b=hv("kkt", h): e.matmul(s_ps[1][:, h, :], a, b, start=True, stop=True),
                         reads=[r[f"zkka{s}"], r[f"kkt{s}"]], writes=[r["s_ps1"]])
                P.op("dve", lambda e: e.tensor_tensor(out=L[:, :, :], in0=s_ps[0][:, :, :], in1=bcm(mls), op=ALU.mult),
                     reads=[r["s_ps0"], r["mls"]], writes=[r["L"]])
                P.op("dve", lambda e: e.tensor_tensor(out=LT[:, :, :], in0=s_ps[1][:, :, :], in1=bcm(mus), op=ALU.mult),
                     reads=[r["s_ps1"], r["mus"]], writes=[r["LT"]])
                for h in range(G):
                    P.op("pe", lambda e, h=h, a=hz("kh", h), b=hv("kkt", h): e.matmul(s_ps[2][:, h, :], a, b, start=True, stop=True),
                         reads=[r[f"zkh{s}"], r[f"kkt{s}"]], writes=[r["s_ps2"]])
                P.op("dve", lambda e, s=s: e.tensor_tensor(out=om["akv"][s][:, :, :], in0=s_ps[2][:, :, :], in1=bcm(mus), op=ALU.mult),
                     reads=[r["s_ps2"], r["mus"]], writes=[r[f"akv{s}"]])
                for h in range(G):
                    P.op("pe", lambda e, h=h, a=hz("kh", h), b=hv("rt", h): e.matmul(s_ps[0][:, h, :], a, b, start=True, stop=True),
                         reads=[r[f"zkh{s}"], r[f"rt{s}"]], writes=[r["s_ps0"]])
                P.op("dve", lambda e, s=s: e.tensor_tensor(out=om["bkv"][s][:, :, :], in0=s_ps[0][:, :, :], in1=bcm(mui), op=ALU.mult),
                     reads=[r["s_ps0"], r["mui"]], writes=[r[f"bkv{s}"]])
                for h in range(G):
                    P.op("pe", lambda e, h=h, a=hz("kka", h), b=hv("rt", h): e.matmul(s_ps[1][:, h, :], a, b, start=True, stop=True),
                         reads=[r[f"zkka{s}"], r[f"rt{s}"]], writes=[r["s_ps1"]])
                P.op("dve", lambda e, s=s: e.tensor_tensor(out=om["nbab"][s][:, :, :], in0=s_ps[1][:, :, :], in1=bcm(mui), op=ALU.mult),
                     reads=[r["s_ps1"], r["mui"]], writes=[r[f"nbab{s}"]])
                Pt, rPt = neumann(P, nc, L, LT, r["L"], r["LT"], W, r)
                P.op("pool", lambda e, s=s, Pt=Pt: e.tensor_copy(om["tinv"][s][:, :, :], Pt[:, :, :]), reads=[rPt], writes=[r[f"tinv{s}"]])
                for n, dst in (("akv", "akvT_d"), ("bkv", "bkvT_d"), ("nbab", "nbabT_d"), ("tinv", "tinvT_d")):
                    P.dma("pool", S[dst][c, :, g * G:(g + 1) * G, :], om[n][s][:, :, :],
                          reads=[r[f"{n}{s}"]], writes=[r[dst]])
                for qi, n in enumerate(("v", "kh", "kka")):
                    for q in range(2):
                        P.op("pe", lambda e, qi=qi, q=q, n=n, s=s: e.transpose(tr_ps[:, qi * 2 + q, :], tl[n][s][:, q, :], identb[:, :]),
                             reads=[r[f"{n}{s}"], r["identb"]], writes=[r["tr_ps"]])
                for qi, (n, dst) in enumerate((("v", "vtk_d"), ("kh", "khtk_d"), ("kka", "kkatk_d"))):
                    P.op("act", lambda e, qi=qi, n=n, s=s: e.copy(tk[n][s][:, :, :], tr_ps[:, qi * 2:qi * 2 + 2, :]),
                         reads=[r["tr_ps"]], writes=[r[f"tk_{n}{s}"]])
                    P.dma("pool", S[dst][t0:t0 + 128, g * 256:(g + 1) * 256], tk[n][s][:, :, :].rearrange("p q d -> p (q d)"),
                          reads=[r[f"tk_{n}{s}"]], writes=[r[dst]])
        P.barrier()
        P.flush()


def phase_rwkv_r2(P, nc, S, C, yT, prm, T, kc0=8, eps=64e-5):
    NCH = T // 128
    H = 16
    with contextlib.ExitStack() as st:
        def sb(name, shape, dt):
            return st.enter_context(nc.sbuf_tensor(_uid() + "r2_" + name, shape, dt))

        def pst(name, shape, dt):
            return st.enter_context(nc.psum_tensor(_uid() + "r2_" + name, shape, dt))
        r = defaultdict(Res)
        identf = sb("identf", [128, 128], F32); identb = sb("identb", [128, 128], BF16)
        lnw = sb("lnw", [128, 8], F32); lnb = sb("lnb", [128, 8], F32); epsc = sb("epsc", [128, 1], F32); mh = sb("mh", [128, 8], F32)
        pc = sb("pc", [128, 8, NCH], F32)
        kkt = [sb(f"kkt{i}", [128, 2, 8, 128], BF16) for i in range(2)]
        rt = [sb(f"rt{i}", [128, 2, 8, 128], BF16) for i in range(2)]
        tkz = {n: [sb(f"tkz_{n}{i}", [128, 2, 8, 128], BF16) for i in range(2)] for n in ("kh", "kka")}
        mm = {n: [sb(f"{n}{i}", [128, H, 128], BF16) for i in range(2)] for n in ("akv", "bkv", "nbab", "tinv")}
        tk = {n: [sb(f"tk_{n}{i}", [128, 1024], BF16) for i in range(2)] for n in ("v",)}
        bon = [sb(f"bon{i}", [128, 8, 128], F32) for i in range(2)]
        szt = [sb(f"szt{i}", [128, 8, 128], F32) for i in range(2)]
        Tf = sb("Tf", [128, 8, 64], F32); Tb = sb("Tb", [128, 8, 64], BF16)
        yfin = [sb(f"yfin{i}", [128, 4, 128], BF16) for i in range(2)]
        TS = []
        for q in range(2):
            d = {"Ttmp": sb(f"Ttmp{q}", [128, 4, 64], F32), "rhs0": sb(f"rhs0{q}", [128, 8, 64], BF16), "nU": sb(f"nU{q}", [128, 8, 64], BF16),
                 "y_sb": sb(f"y_sb{q}", [128, 8, 64], F32), "ysq": sb(f"ysq{q}", [128, 8, 64], F32),
                 "yc": sb(f"yc{q}", [128, 8, 64], F32), "yn": sb(f"yn{q}", [128, 8, 64], BF16),
                 "t1": sb(f"t1{q}", [128, 4, 128], F32), "t2": sb(f"t2{q}", [128, 4, 128], F32)}
            for n in ("s1", "s2", "mean", "var", "sd", "rstd"):
                d[n] = sb(f"{n}{q}", [128, 8], F32)
            d["r0_ps"] = pst(f"r0_ps{q}", [128, 8, 64], F32)
            d["u_ps"] = d["r0_ps"]
            d["y_ps"] = pst(f"y_ps{q}", [128, 8, 64], F32)
            d["st_ps"] = pst(f"st_ps{q}", [128, 512], F32)[:, 0:256].rearrange("p (q e) -> p q e", e=64)
            d["tr_ps"] = pst(f"tr_ps{q}", [128, 1024], BF16)[:, 0:512].rearrange("p (q t) -> p q t", t=128)
            TS.append(d)
        P.dma("sp", identf[:, :], C["ident"][:, :], writes=[r["identf"]])
        P.dma("sp", lnw[:, :], prm["lnwT"][:, :], writes=[r["lnw"]])
        P.dma("sp", lnb[:, :], prm["lnbT"][:, :], writes=[r["lnb"]])
        for q in range(8):
            P.dma("sp", pc[:, q, :], S["pc_d"][q, :, :], writes=[r["pc"]])
        P.op("pool", lambda e: e.tensor_copy(identb[:, :], identf[:, :]), reads=[r["identf"]], writes=[r["identb"]])
        P.op("pool", lambda e: e.memset(epsc[:, :], eps), writes=[r["eps"]])
        P.op("pool", lambda e: e.memset(mh[:, :], -0.5), writes=[r["mh"]])
        P.op("pool", lambda e: e.memset(Tf[:, :, :].rearrange("p q e -> p (q e)"), 0.0), writes=[r["Tf"]])
        P.op("pool", lambda e: e.memset(Tb[:, :, :].rearrange("p q e -> p (q e)"), 0.0), writes=[r["Tb"]])
        def grp(c, gi, b, s):
            t0 = c * 128
            T_ = TS[b]
            (Ttmp, rhs0, nU, y_sb, ysq, yc, yn, t1, t2, s1, s2, mean, var, sd, rstd, r0_ps, u_ps, y_ps, st_ps, tr_ps) = [T_[n] for n in (
                "Ttmp", "rhs0", "nU", "y_sb", "ysq", "yc", "yn", "t1", "t2", "s1", "s2", "mean", "var", "sd", "rstd", "r0_ps", "u_ps", "y_ps", "st_ps", "tr_ps")]
            rT = r[f"Tf{gi}"]; rTb = r[f"Tb{gi}"]
            for h in range(8):
                hh = gi * 8 + h
                p_ = hh // 2
                ba = (hh % 2) * 64
                P.op("pe", lambda e, h=h, hh=hh, p_=p_, ba=ba, s=s: e.matmul(r0_ps[:, h, :], kkt[s][:, ba // 64, p_, :], Tb[:, p_, :],
                                                                          start=True, stop=False),
                     reads=[r[f"kkt{s}"], rTb, r["Tb"]], writes=[r["ru_ps" + str(b)]])
                P.op("pe", lambda e, h=h, hh=hh, s=s: e.matmul(r0_ps[:, h, :], mm["akv"][s][:, hh, :], tk["v"][s][:, hh * 64:(hh + 1) * 64],
                                                              start=False, stop=True),
                     reads=[r[f"akv{s}"], r[f"tk_v{s}"]], writes=[r["ru_ps" + str(b)]])
            yield
            P.op("act", lambda e: e.copy(rhs0[:, :, :], r0_ps[:, :, :]), reads=[r["ru_ps" + str(b)]], writes=[r["rhs0" + str(b)]])
            yield
            for h in range(8):
                hh = gi * 8 + h
                P.op("pe", lambda e, h=h, hh=hh, s=s: e.matmul(u_ps[:, h, :], mm["tinv"][s][:, hh, :], rhs0[:, h, :], start=True, stop=True),
                     reads=[r[f"tinv{s}"], r["rhs0" + str(b)]], writes=[r["ru_ps" + str(b)]])
            yield
            P.op("dve", lambda e: e.tensor_scalar(out=nU[:, :, :], in0=u_ps[:, :, :], scalar1=-1.0, scalar2=None, op0=ALU.mult),
                 reads=[r["ru_ps" + str(b)]], writes=[r["nU" + str(b)]])
            for h in range(8):
                hh = gi * 8 + h
                p_ = hh // 2
                ba = (hh % 2) * 64
                P.op("pe", lambda e, h=h, p_=p_, ba=ba, s=s: e.matmul(y_ps[:, h, :], rt[s][:, ba // 64, p_, :], Tb[:, p_, :],
                                                                   start=True, stop=False),
                     reads=[r[f"rt{s}"], rTb, r["Tb"]], writes=[r["y_ps" + str(b)]])
                P.op("pe", lambda e, h=h, hh=hh, s=s: e.matmul(y_ps[:, h, :], mm["bkv"][s][:, hh, :], tk["v"][s][:, hh * 64:(hh + 1) * 64],
                                                              start=False, stop=False),
                     reads=[r[f"bkv{s}"], r[f"tk_v{s}"]], writes=[r["y_ps" + str(b)]])
                P.op("pe", lambda e, h=h, hh=hh, s=s: e.matmul(y_ps[:, h, :], mm["nbab"][s][:, hh, :], nU[:, h, :], start=False, stop=True),
                     reads=[r[f"nbab{s}"], r["nU" + str(b)]], writes=[r["y_ps" + str(b)]])
            yield
            for pl in range(4):
                q = gi * 4 + pl
                seq = [("kh", 0, "v"), ("kh", 1, "v"), ("kka", 0, "u"), ("kka", 1, "u")]
                for i_, (n, a, rk_) in enumerate(seq):
                    hh = 2 * q + a
                    if rk_ == "v":
                        P.op("pe", lambda e, pl=pl, q=q, a=a, n=n, hh=hh, s=s, i_=i_: e.matmul(st_ps[:, pl, :], tkz[n][s][:, a, q, :],
                                                                                           tk["v"][s][:, hh * 64:(hh + 1) * 64],
                                                                                           start=(i_ == 0), stop=(i_ == 3)),
                             reads=[r[f"tkz_{n}{s}"], r[f"tk_v{s}"]], writes=[r["st_ps" + str(b)]])
                    else:
                        P.op("pe", lambda e, pl=pl, q=q, a=a, n=n, hh=hh, s=s, i_=i_, gi=gi: e.matmul(st_ps[:, pl, :], tkz[n][s][:, a, q, :],
                                                                                                  nU[:, hh - gi * 8, :],
                                                                                                  start=(i_ == 0), stop=(i_ == 3)),
                             reads=[r[f"tkz_{n}{s}"], r["nU" + str(b)]], writes=[r["st_ps" + str(b)]])
            yield
            qs = slice(gi * 4, gi * 4 + 4)
            P.op("dve", lambda e, qs=qs: e.tensor_tensor(out=Ttmp[:, :, :], in0=st_ps[:, :, :], in1=Tf[:, qs, :], op=ALU.add),
                 reads=[r["st_ps" + str(b)], rT, r["Tf"]], writes=[r["Ttmp" + str(b)]])
            P.op("dve", lambda e, qs=qs, c=c: e.tensor_tensor(out=Tf[:, qs, :], in0=Ttmp[:, :, :],
                                                              in1=pc[:, qs, c:c + 1].to_broadcast([128, 4, 64]), op=ALU.mult),
                 reads=[r["Ttmp" + str(b)], r["pc"]], writes=[rT])
            P.op("act", lambda e, qs=qs: e.copy(Tb[:, qs, :], Tf[:, qs, :]), reads=[rT], writes=[rTb])
            yield
            P.op("act", lambda e: e.copy(y_sb[:, :, :], y_ps[:, :, :]), reads=[r["y_ps" + str(b)]], writes=[r["y_sb" + str(b)]])
            P.op("dve", lambda e: e.tensor_reduce(out=s1[:, :], in_=y_sb[:, :, :], axis=AX.X, op=ALU.add), reads=[r["y_sb" + str(b)]], writes=[r["s1" + str(b)]])
            P.op("pool", lambda e: e.tensor_tensor(out=ysq[:, :, :], in0=y_sb[:, :, :], in1=y_sb[:, :, :], op=ALU.mult),
                 reads=[r["y_sb" + str(b)]], writes=[r["ysq" + str(b)]])
            yield
            P.op("dve", lambda e: e.tensor_reduce(out=s2[:, :], in_=ysq[:, :, :], axis=AX.X, op=ALU.add), reads=[r["ysq" + str(b)]], writes=[r["s2" + str(b)]])
            P.op("dve", lambda e: e.tensor_scalar(out=mean[:, :], in0=s1[:, :], scalar1=1.0 / 64, scalar2=None, op0=ALU.mult),
                 reads=[r["s1" + str(b)]], writes=[r["mean" + str(b)]])
            P.op("dve", lambda e: e.tensor_tensor(out=var[:, :], in0=mean[:, :], in1=mean[:, :], op=ALU.mult), reads=[r["mean" + str(b)]], writes=[r["var" + str(b)]])
            P.op("dve", lambda e: e.scalar_tensor_tensor(out=var[:, :], in0=s2[:, :], scalar=1.0 / 64, in1=var[:, :],
                                                         op0=ALU.mult, op1=ALU.subtract),
                 reads=[r["s2" + str(b)], r["var" + str(b)]], writes=[r["var" + str(b)]])
            P.op("dve", lambda e: e.tensor_scalar(out=sd[:, :], in0=var[:, :], scalar1=1.0, scalar2=eps, op0=ALU.mult, op1=ALU.add),
                 reads=[r["var" + str(b)]], writes=[r["sd" + str(b)]])
            yield
            P.op("pool", lambda e: e.tensor_tensor(out=rstd[:, :], in0=sd[:, :], in1=mh[:, :], op=ALU.pow),
                 reads=[r["sd" + str(b)], r["mh"]], writes=[r["rstd" + str(b)]])
            P.op("dve", lambda e: e.tensor_tensor(out=yc[:, :, :], in0=y_sb[:, :, :], in1=mean[:, :].unsqueeze(2).to_broadcast([128, 8, 64]),
                                                  op=ALU.subtract),
                 reads=[r["y_sb" + str(b)], r["mean" + str(b)]], writes=[r["yc" + str(b)]])
            P.op("dve", lambda e: e.tensor_tensor(out=yn[:, :, :], in0=yc[:, :, :], in1=rstd[:, :].unsqueeze(2).to_broadcast([128, 8, 64]),
                                                  op=ALU.mult),
                 reads=[r["yc" + str(b)], r["rstd" + str(b)]], writes=[r["yn" + str(b)]])
            yield
            ynp = yn[:, :, :].rearrange("p (q a) e -> p q (a e)", a=2)
            for q in range(4):
                P.op("pe", lambda e, q=q: e.transpose(tr_ps[:, q, :], ynp[:, q, :], identb[:, :]),
                     reads=[r["yn" + str(b)], r["identb"]], writes=[r["tr_ps" + str(b)]])
            yield
            P.op("dve", lambda e, qs=qs: e.tensor_tensor(out=t1[:, :, :], in0=tr_ps[:, :, :],
                                                         in1=lnw[:, qs].unsqueeze(2).to_broadcast([128, 4, 128]), op=ALU.mult),
                 reads=[r["tr_ps" + str(b)], r["lnw"]], writes=[r["t1" + str(b)]])
            P.op("pool", lambda e, qs=qs: e.tensor_tensor(out=t2[:, :, :], in0=t1[:, :, :],
                                                          in1=lnb[:, qs].unsqueeze(2).to_broadcast([128, 4, 128]), op=ALU.add),
                 reads=[r["t1" + str(b)], r["lnb"]], writes=[r["t2" + str(b)]])
            P.op("pool", lambda e, qs=qs, s=s: e.tensor_tensor(out=t1[:, :, :], in0=t2[:, :, :], in1=bon[s][:, qs, :], op=ALU.add),
                 reads=[r["t2" + str(b)], r[f"bon{s}"]], writes=[r["t1" + str(b)]])
            P.op("dve", lambda e, qs=qs, s=s, b=b: e.tensor_tensor(out=yfin[b][:, :, :], in0=t1[:, :, :], in1=szt[s][:, qs, :], op=ALU.mult),
                 reads=[r["t1" + str(b)], r[f"szt{s}"]], writes=[r[f"yfin{b}"]])
            P.dma("pool", yT[kc0 + gi * 4:kc0 + gi * 4 + 4, :, t0:t0 + 128].rearrange("k p t -> p k t"), yfin[b][:, :, :],
                  reads=[r[f"yfin{b}"]], writes=[r["y_out"]])

        it = 0
        for i in range(2):
            P.op("pool", lambda e, i=i: e.memset(kkt[i][:, :, :, :].rearrange("p a q t -> p (a q t)"), 0.0), writes=[r[f"kkt{i}"]])
            P.op("pool", lambda e, i=i: e.memset(rt[i][:, :, :, :].rearrange("p a q t -> p (a q t)"), 0.0), writes=[r[f"rt{i}"]])
            for n in tkz:
                P.op("pool", lambda e, i=i, n=n: e.memset(tkz[n][i][:, :, :, :].rearrange("p a q t -> p (a q t)"), 0.0), writes=[r[f"tkz_{n}{i}"]])
        for c in range(NCH):
            t0 = c * 128
            s = c % 2
            for tl_, nm, src in ((kkt, "kkt", "kktT"), (rt, "rt", "rtT")):
                srcv = S[src][0:1024, t0:t0 + 128].rearrange("(q p) t -> p q t", p=128)
                P.dma("sp", tl_[s][0:64, 0, :, :], srcv[0:64], writes=[r[f"{nm}{s}"]])
                P.dma("sp", tl_[s][64:128, 1, :, :], srcv[64:128], writes=[r[f"{nm}{s}"]])
            for n, src in (("kh", "khtk_d"), ("kka", "kkatk_d")):
                srcv = S[src][t0:t0 + 128, :].rearrange("p (q a d) -> p q a d", a=2, d=64)
                for a in range(2):
                    P.dma("sp", tkz[n][s][:, a, :, a * 64:(a + 1) * 64], srcv[:, :, a, :], writes=[r[f"tkz_{n}{s}"]])
            for q0 in range(0, 8, 4):
                P.dma("sp", bon[s][:, q0:q0 + 4, :], S["bonT"][q0 * 128:(q0 + 4) * 128, t0:t0 + 128].rearrange("(q p) t -> p q t", p=128),
                      writes=[r[f"bon{s}"]])
                P.dma("sp", szt[s][:, q0:q0 + 4, :], S["szT"][q0 * 128:(q0 + 4) * 128, t0:t0 + 128].rearrange("(q p) t -> p q t", p=128),
                      writes=[r[f"szt{s}"]])
            for n, dst in (("akv", "akvT_d"), ("bkv", "bkvT_d"), ("nbab", "nbabT_d"), ("tinv", "tinvT_d")):
                for h0 in range(0, 16, 4):
                    P.dma("sp", mm[n][s][:, h0:h0 + 4, :], S[dst][c, :, h0:h0 + 4, :], writes=[r[f"{n}{s}"]])
            for n, dst in (("v", "vtk_d"),):
                P.dma("sp", tk[n][s][:, :], S[dst][t0:t0 + 128, :], writes=[r[f"tk_{n}{s}"]])
            zipper([grp(c, 0, 0, s), grp(c, 1, 1, s)])
        P.barrier()
        P.flush()


D = 4096
KC = 32
EPS = 1e-6
TWO_PI = 2.0 * math.pi
C1 = 6.28125
C2 = float(np.float32(TWO_PI - C1))


def dense_consts():
    c = {}
    j = np.arange(64, dtype=np.float32)
    invf = (np.float32(10000.0) ** (-(j / np.float32(64.0)))).astype(np.float32)
    c["invf"] = np.concatenate([invf, invf]).reshape(128, 1).astype(np.float32)
    c["sgn"] = np.concatenate([-np.ones(64), np.ones(64)]).reshape(128, 1).astype(np.float32)
    sw = np.zeros((128, 128), np.float32)
    for m in range(128):
        sw[(m + 64) % 128, m] = 1.0
    c["swap64"] = sw
    return c


def phase_rope_tables(P, nc, pos_dram, C, cosT, sinT, T):
    with contextlib.ExitStack() as st:
        def sb(name, shape, dt):
            return st.enter_context(nc.sbuf_tensor(_uid() + "rp_" + name, shape, dt))
        r = defaultdict(Res)
        invf = sb("invf", [128, 1], F32); sgn = sb("sgn", [128, 1], F32)
        pi_ = sb("pi", [128, 512], I32)
        ang = sb("ang", [128, 512], F32); kf = sb("kf", [128, 512], F32); kr = sb("kr", [128, 512], F32)
        rr = sb("rr", [128, 512], F32); rc = sb("rc", [128, 512], F32); m = sb("m", [128, 512], F32)
        so = [sb(f"so{i}", [128, 512], F32) for i in range(2)]
        co = [sb(f"co{i}", [128, 512], F32) for i in range(2)]
        P.dma("sp", invf[:, :], C["invf"][:, :], writes=[r["invf"]])
        P.dma("sp", sgn[:, :], C["sgn"][:, :], writes=[r["sgn"]])
        MAGIC = 12582912.0
        for tb in range(T // 512):
            s = tb % 2
            t0 = tb * 512
            P.dma("sp", pi_[:, :], pos_dram[:, t0:t0 + 512], writes=[r["pi"]])
            P.op("dve", lambda e: e.tensor_copy(ang[:, :], pi_[:, :]), reads=[r["pi"]], writes=[r["ang"]])
            P.op("dve", lambda e: e.tensor_scalar(out=ang[:, :], in0=ang[:, :], scalar1=invf[:, 0:1], scalar2=None, op0=ALU.mult),
                 reads=[r["ang"], r["invf"]], writes=[r["ang"]])
            P.op("dve", lambda e: e.tensor_scalar(out=kf[:, :], in0=ang[:, :], scalar1=1.0 / TWO_PI, scalar2=None, op0=ALU.mult),
                 reads=[r["ang"]], writes=[r["kf"]])
            P.op("dve", lambda e: e.tensor_scalar(out=kr[:, :], in0=kf[:, :], scalar1=MAGIC, scalar2=None, op0=ALU.add),
                 reads=[r["kf"]], writes=[r["kr"]])
            P.op("dve", lambda e: e.tensor_scalar(out=kf[:, :], in0=kr[:, :], scalar1=MAGIC, scalar2=None, op0=ALU.subtract),
                 reads=[r["kr"]], writes=[r["kf"]])
            P.op("dve", lambda e: e.scalar_tensor_tensor(out=rr[:, :], in0=kf[:, :], scalar=-C1, in1=ang[:, :], op0=ALU.mult, op1=ALU.add),
                 reads=[r["kf"], r["ang"]], writes=[r["rr"]])
            P.op("dve", lambda e: e.scalar_tensor_tensor(out=rr[:, :], in0=kf[:, :], scalar=-C2, in1=rr[:, :], op0=ALU.mult, op1=ALU.add),
                 reads=[r["kf"], r["rr"]], writes=[r["rr"]])
            P.op("dve", lambda e: e.tensor_scalar(out=rr[:, :], in0=rr[:, :], scalar1=math.pi, scalar2=-math.pi, op0=ALU.min, op1=ALU.max),
                 reads=[r["rr"]], writes=[r["rr"]])
            P.op("dve", lambda e: e.tensor_scalar(out=m[:, :], in0=rr[:, :], scalar1=math.pi / 2, scalar2=-TWO_PI, op0=ALU.is_gt, op1=ALU.mult),
                 reads=[r["rr"]], writes=[r["m"]])
            P.op("dve", lambda e: e.scalar_tensor_tensor(out=rc[:, :], in0=rr[:, :], scalar=math.pi / 2, in1=m[:, :], op0=ALU.add, op1=ALU.add),
                 reads=[r["rr"], r["m"]], writes=[r["rc"]])
            P.op("dve", lambda e: e.tensor_scalar(out=rc[:, :], in0=rc[:, :], scalar1=math.pi, scalar2=-math.pi, op0=ALU.min, op1=ALU.max),
                 reads=[r["rc"]], writes=[r["rc"]])
            P.op("act", lambda e, s=s: e.activation(out=so[s][:, :], in_=rr[:, :], func=AF.Sin), reads=[r["rr"]], writes=[r[f"so{s}"]])
            P.op("act", lambda e, s=s: e.activation(out=co[s][:, :], in_=rc[:, :], func=AF.Sin), reads=[r["rc"]], writes=[r[f"co{s}"]])
            P.op("dve", lambda e, s=s: e.tensor_scalar(out=so[s][:, :], in0=so[s][:, :], scalar1=sgn[:, 0:1], scalar2=None, op0=ALU.mult),
                 reads=[r[f"so{s}"], r["sgn"]], writes=[r[f"so{s}"]])
            P.dma("pool", sinT[:, t0:t0 + 512], so[s][:, :], reads=[r[f"so{s}"]], writes=[r["sinT"]])
            P.dma("pool", cosT[:, t0:t0 + 512], co[s][:, :], reads=[r[f"co{s}"]], writes=[r["cosT"]])
        P.barrier()
        P.flush()


def phase_norm(P, nc, x_dram, normw_bc_dram, hT_dram, T, out_dram=None):
    NT = T // 128
    with contextlib.ExitStack() as st:
        def sb(name, shape, dt):
            return st.enter_context(nc.sbuf_tensor(_uid() + "pn_" + name, shape, dt))
        r = defaultdict(Res)
        xt = [sb(f"x{i}", [128, D], F32) for i in range(2)]
        junk = sb("junk", [128, D], BF16)
        nw = sb("nw", [128, D], F32)
        ss = sb("ss", [128, 2], F32); sd = sb("sd", [128, 2], F32); rs = sb("rs", [128, 2], F32)
        epsc = sb("eps", [128, 1], F32)
        P.dma("sp", nw[:, :], normw_bc_dram[:, :], writes=[r["nw"]])
        P.op("pool", lambda e: e.memset(epsc[:, :], EPS), writes=[r["eps"]])
        if out_dram is None:
            hb = [sb(f"h{i}", [128, D], BF16) for i in range(2)]
            ident = sb("id", [128, 128], BF16)
            stg = [sb(f"stg{i}", [128, KC, 256], BF16) for i in range(2)]
            tp = [st.enter_context(nc.psum_tensor(_uid() + f"pn_tp{i}", [128, 1024], BF16)) for i in range(4)]
            P.op("pool", lambda e: e.memset(ident[:, :], 0.0), writes=[r["id"]])
            P.op("pool", lambda e: e.affine_select(out=ident[:, :], in_=ident[:, :], pattern=[[-1, 128]],
                                                   compare_op=ALU.not_equal, fill=1.0, base=0, channel_multiplier=1),
                 reads=[r["id"]], writes=[r["id"]])
        else:
            of = [sb(f"of{i}", [128, D], F32) for i in range(2)]
        for tt in range(NT):
            s = tt % 2
            P.dma("sp", xt[s][:, :], x_dram[tt * 128:(tt + 1) * 128, :], writes=[r[f"xt{s}"]])
            P.op("dve", lambda e, s=s: e.scalar_tensor_tensor(out=junk[:, :], in0=xt[s][:, :], scalar=1.0, in1=xt[s][:, :],
                                                              op0=ALU.mult, op1=ALU.mult, accum_out=ss[:, s:s + 1]),
                 reads=[r[f"xt{s}"]], writes=[r["junk"], r[f"ss{s}"]])
            P.op("act", lambda e, s=s: e.activation(out=sd[:, s:s + 1], in_=ss[:, s:s + 1], func=AF.Sqrt,
                                                    bias=epsc[:, 0:1], scale=1.0 / D),
                 reads=[r[f"ss{s}"], r["eps"]], writes=[r[f"sd{s}"]])
            P.op("dve", lambda e, s=s: e.reciprocal(rs[:, s:s + 1], sd[:, s:s + 1]), reads=[r[f"sd{s}"]], writes=[r[f"rs{s}"]])
            if out_dram is not None:
                P.op("dve", lambda e, s=s: e.scalar_tensor_tensor(out=of[s][:, :], in0=xt[s][:, :], scalar=rs[:, s:s + 1],
                                                                  in1=nw[:, :], op0=ALU.mult, op1=ALU.mult),
                     reads=[r[f"xt{s}"], r[f"rs{s}"], r["nw"]], writes=[r[f"of{s}"]])
                P.dma("pool", out_dram[tt * 128:(tt + 1) * 128, :], of[s][:, :], reads=[r[f"of{s}"]], writes=[r["out"]])
                continue
            P.op("dve", lambda e, s=s: e.scalar_tensor_tensor(out=hb[s][:, :], in0=xt[s][:, :], scalar=rs[:, s:s + 1],
                                                              in1=nw[:, :], op0=ALU.mult, op1=ALU.mult),
                 reads=[r[f"xt{s}"], r[f"rs{s}"], r["nw"]], writes=[r[f"hb{s}"]])
            sg = (tt // 2) % 2
            off = (tt % 2) * 128
            for b in range(4):
                for j in range(8):
                    kc = b * 8 + j
                    P.op("pe", lambda e, s=s, b=b, j=j, kc=kc: e.transpose(tp[b][:, j * 128:(j + 1) * 128],
                                                                         hb[s][:, kc * 128:(kc + 1) * 128], ident[:, :]),
                         reads=[r[f"hb{s}"], r["id"]], writes=[r[f"tp{b}"]])
                P.op("act", lambda e, b=b, sg=sg, off=off: e.copy(
                    stg[sg][:, b * 8:(b + 1) * 8, off:off + 128],
                    tp[b][:, :].rearrange("p (j t) -> p j t", j=8)),
                     reads=[r[f"tp{b}"]], writes=[r[f"stg{sg}"]])
            if tt % 2 == 1:
                t0 = (tt - 1) * 128
                P.dma("pool", hT_dram[:, :, t0:t0 + 256].rearrange("k p t -> p k t"), stg[sg][:, :, :],
                      reads=[r[f"stg{sg}"]], writes=[r["hT_out"]])
        P.barrier()
        P.flush()


def phase_proj(P, nc, hT_dram, w_tiles_dram, projT_dram, cosT, sinT, T, NCT, n_rope=16, TH=2048, swap_dram=None):
    TH = min(TH, T)
    NTB = TH // 512
    with contextlib.ExitStack() as st:
        def sb(name, shape, dt):
            return st.enter_context(nc.sbuf_tensor(_uid() + "pp_" + name, shape, dt))
        r = defaultdict(Res)
        hT = sb("hT", [128, KC, TH], BF16)
        wf = [sb(f"wf{i}", [128, KC * 128], F32) for i in range(2)]
        wb = [sb(f"wb{i}", [128, KC, 128], BF16) for i in range(2)]
        swp = sb("swp", [128, 128], F32)
        asb = [sb(f"asb{i}", [128, 512], F32) for i in range(2)]
        cs_ = [sb(f"cs{i}", [128, 512], F32) for i in range(2)]
        sn_ = [sb(f"sn{i}", [128, 512], F32) for i in range(2)]
        t1 = sb("t1", [128, 512], F32); t2 = sb("t2", [128, 512], F32)
        ob = [sb(f"ob{i}", [128, 512], F32) for i in range(4)]
        ps = [[st.enter_context(nc.psum_tensor(_uid() + f"pp_ps{a}_{i}", [128, 512], F32)) for i in range(NTB)] for a in range(2)]
        cnt = 0
        rc = 0
        if n_rope:
            P.dma("sp", swp[:, :], swap_dram[:, :], writes=[r["swp"]])
        for t0 in range(0, T, TH):
            for kc in range(KC):
                P.dma("sp", hT[:, kc, :], hT_dram[kc, :, t0:t0 + TH], writes=[r["hT"]])
            def prep(ct):
                s = ct % 2
                P.dma("sp", wf[s][:, :], w_tiles_dram[ct, :, :], writes=[r[f"wf{s}"]])
                if ct % 2 == 0:
                    P.op("act", lambda e, s=s: e.copy(wb[s][:, :, :].rearrange("p k c -> p (k c)"), wf[s][:, :]),
                         reads=[r[f"wf{s}"]], writes=[r[f"wb{s}"]])
                else:
                    P.op("dve", lambda e, s=s: e.tensor_copy(wb[s][:, :, :].rearrange("p k c -> p (k c)"), wf[s][:, :]),
                         reads=[r[f"wf{s}"]], writes=[r[f"wb{s}"]])
            prep(0)
            for ct in range(NCT):
                s = ct % 2
                rope = ct < n_rope
                if ct + 1 < NCT:
                    prep(ct + 1)
                for kc in range(KC):
                    for tb in range(NTB):
                        P.op("pe", lambda e, s=s, kc=kc, tb=tb: e.matmul(ps[s][tb][:, :], wb[s][:, kc, :], hT[:, kc, tb * 512:(tb + 1) * 512],
                                                                        start=(kc == 0), stop=(kc == KC - 1)),
                             reads=[r[f"wb{s}"], r["hT"]], writes=[r[f"ps{s}_{tb}"]])
                for tb in range(NTB):
                    b = cnt % 4
                    cnt += 1
                    tg = t0 + tb * 512
                    if rope:
                        q = rc % 2
                        rc += 1
                        P.dma("sp", cs_[q][:, :], cosT[:, tg:tg + 512], writes=[r[f"cs{q}"]])
                        P.dma("sp", sn_[q][:, :], sinT[:, tg:tg + 512], writes=[r[f"sn{q}"]])
                        P.op("act", lambda e, s=s, tb=tb, q=q: e.copy(asb[q][:, :], ps[s][tb][:, :]), reads=[r[f"ps{s}_{tb}"]], writes=[r[f"asb{q}"]])
                        P.op("pe", lambda e, s=s, tb=tb, q=q: e.matmul(ps[1 - s][tb][:, :], swp[:, :], asb[q][:, :], start=True, stop=True),
                             reads=[r["swp"], r[f"asb{q}"]], writes=[r[f"ps{1 - s}_{tb}"]])
                        P.op("pool", lambda e, q=q: e.tensor_tensor(out=t1[:, :], in0=asb[q][:, :], in1=cs_[q][:, :], op=ALU.mult),
                             reads=[r[f"asb{q}"], r[f"cs{q}"]], writes=[r["t1"]])
                        P.op("dve", lambda e, s=s, tb=tb, q=q: e.tensor_tensor(out=t2[:, :], in0=ps[1 - s][tb][:, :], in1=sn_[q][:, :], op=ALU.mult),
                             reads=[r[f"ps{1 - s}_{tb}"], r[f"sn{q}"]], writes=[r["t2"]])
                        P.op("pool", lambda e, b=b: e.tensor_tensor(out=ob[b][:, :], in0=t1[:, :], in1=t2[:, :], op=ALU.add),
                             reads=[r["t1"], r["t2"]], writes=[r[f"ob{b}"]])
                    else:
                        if cnt % 2 == 0:
                            P.op("dve", lambda e, b=b, s=s, tb=tb: e.tensor_copy(ob[b][:, :], ps[s][tb][:, :]), reads=[r[f"ps{s}_{tb}"]], writes=[r[f"ob{b}"]])
                        else:
                            P.op("act", lambda e, b=b, s=s, tb=tb: e.copy(ob[b][:, :], ps[s][tb][:, :]), reads=[r[f"ps{s}_{tb}"]], writes=[r[f"ob{b}"]])
                    pd_, pr_ = projT_dram(ct)
                    P.dma("pool", pd_[pr_:pr_ + 128, tg:tg + 512], ob[b][:, :], reads=[r[f"ob{b}"]], writes=[r["out"]])
        P.barrier()
        P.flush()


def _load_wblk(P, r, wf, wb, s, w_dram, cb, wcnt):
    for q4 in range(4):
        ws = wcnt[0] % 2
        wcnt[0] += 1
        P.dma("sp", wf[ws][:, :], w_dram[cb * 4 + q4, :, :], writes=[r[f"wf{ws}"]])
        src = wf[ws][:, :].rearrange("p (k c) -> p k c", c=128)
        if q4 % 2 == 0:
            P.op("act", lambda e, s=s, q4=q4, src=src: e.copy(wb[s][:, :, q4 * 128:(q4 + 1) * 128], src),
                 reads=[r[f"wf{ws}"]], writes=[r[f"wb{s}"]])
        else:
            P.op("dve", lambda e, s=s, q4=q4, src=src: e.tensor_copy(wb[s][:, :, q4 * 128:(q4 + 1) * 128], src),
                 reads=[r[f"wf{ws}"]], writes=[r[f"wb{s}"]])


def phase_out(P, nc, yT_dram, w_blk_dram, x_dram, x1_dram, T, TQ=1024):
    NB = D // 512
    TQ = min(TQ, T)
    with contextlib.ExitStack() as st:
        def sb(name, shape, dt):
            return st.enter_context(nc.sbuf_tensor(_uid() + "po_" + name, shape, dt))
        r = defaultdict(Res)
        yT = sb("yT", [128, KC, TQ], BF16)
        wf = [sb(f"wf{i}", [128, KC * 128], F32) for i in range(2)]
        wb = [sb(f"wb{i}", [128, KC, 512], BF16) for i in range(2)]
        xt = [sb(f"xt{i}", [128, 512], F32) for i in range(4)]
        ot = [sb(f"ot{i}", [128, 512], F32) for i in range(4)]
        ps = [st.enter_context(nc.psum_tensor(_uid() + f"po_ps{i}", [128, 512], F32)) for i in range(4)]
        cnt = 0
        wcnt = [0]
        for t0 in range(0, T, TQ):
            for kc in range(KC):
                P.dma("sp", yT[:, kc, :], yT_dram[kc, :, t0:t0 + TQ], writes=[r["yT"]])
            if t0 == 0:
                _load_wblk(P, r, wf, wb, 0, w_blk_dram, 0, wcnt)
            for cb in range(NB):
                s = cb % 2
                if cb + 1 < NB:
                    _load_wblk(P, r, wf, wb, 1 - s, w_blk_dram, cb + 1, wcnt)
                elif t0 + TQ < T:
                    _load_wblk(P, r, wf, wb, 1 - s, w_blk_dram, 0, wcnt)
                for tt in range(TQ // 128):
                    b = cnt % 4
                    cnt += 1
                    tok = t0 + tt * 128
                    P.dma("sp", xt[b][:, :], x_dram[tok:tok + 128, cb * 512:(cb + 1) * 512], writes=[r[f"xt{b}"]])
                    for kc in range(KC):
                        P.op("pe", lambda e, s=s, b=b, kc=kc, tt=tt: e.matmul(ps[b][:, :], yT[:, kc, tt * 128:(tt + 1) * 128], wb[s][:, kc, :],
                                                                             start=(kc == 0), stop=(kc == KC - 1)),
                             reads=[r[f"wb{s}"], r["yT"]], writes=[r[f"ps{b}"]])
                    P.op("dve", lambda e, b=b: e.tensor_tensor(out=ot[b][:, :], in0=ps[b][:, :], in1=xt[b][:, :], op=ALU.add),
                         reads=[r[f"ps{b}"], r[f"xt{b}"]], writes=[r[f"ot{b}"]])
                    P.dma("pool", x1_dram[tok:tok + 128, cb * 512:(cb + 1) * 512], ot[b][:, :], reads=[r[f"ot{b}"]], writes=[r["out"]])
        P.barrier()
        P.flush()


def phase_gate(P, nc, h2T_dram, wg_blk_dram, p_dram, wple_dram, x1_dram, x2_dram, T, TQ=1024):
    NB = D // 512
    TQ = min(TQ, T)
    with contextlib.ExitStack() as st:
        def sb(name, shape, dt):
            return st.enter_context(nc.sbuf_tensor(_uid() + "pg_" + name, shape, dt))
        r = defaultdict(Res)
        hT = sb("hT", [128, KC, TQ], BF16)
        wf = [sb(f"wf{i}", [128, KC * 128], F32) for i in range(2)]
        wb = [sb(f"wb{i}", [128, KC, 512], BF16) for i in range(2)]
        wpb = sb("wpb", [128, 2, D], BF16)
        ident = sb("ident", [128, 128], BF16)
        pt = [sb(f"pt{i}", [128, 256], F32) for i in range(2)]
        pb = [sb(f"pb{i}", [128, 256], BF16) for i in range(2)]
        pT = sb("pT", [128, 2, TQ], BF16)
        xt = [sb(f"xt{i}", [128, 512], F32) for i in range(2)]
        gt = sb("gt", [128, 512], F32)
        tm = sb("tm", [128, 512], F32)
        ot = [sb(f"ot{i}", [128, 512], F32) for i in range(2)]
        ps = [st.enter_context(nc.psum_tensor(_uid() + f"pg_ps{i}", [128, 512], F32)) for i in range(3)]
        pp = [st.enter_context(nc.psum_tensor(_uid() + f"pg_pp{i}", [128, 512], F32)) for i in range(3)]
        tp = st.enter_context(nc.psum_tensor(_uid() + "pg_tp", [128, 1024], BF16))
        for q2 in range(2):
            P.dma("sp", wf[q2][:, :], wple_dram[:, q2 * D:(q2 + 1) * D], writes=[r[f"wf{q2}"]])
            P.op("act", lambda e, q2=q2: e.copy(wpb[:, q2, :], wf[q2][:, :]), reads=[r[f"wf{q2}"]], writes=[r["wpb"]])
        P.op("pool", lambda e: e.memset(ident[:, :], 0.0), writes=[r["id"]])
        P.op("pool", lambda e: e.affine_select(out=ident[:, :], in_=ident[:, :], pattern=[[-1, 128]],
                                               compare_op=ALU.not_equal, fill=1.0, base=0, channel_multiplier=1),
             reads=[r["id"]], writes=[r["id"]])
        cnt = 0
        wcnt = [0]
        for t0 in range(0, T, TQ):
            for kc in range(KC):
                P.dma("sp", hT[:, kc, :], h2T_dram[kc, :, t0:t0 + TQ], writes=[r["hT"]])
            for tt in range(TQ // 128):
                s = tt % 2
                tok = t0 + tt * 128
                P.dma("sp", pt[s][:, :], p_dram[tok:tok + 128, :], writes=[r[f"pt{s}"]])
                P.op("dve", lambda e, s=s: e.tensor_copy(pb[s][:, :], pt[s][:, :]), reads=[r[f"pt{s}"]], writes=[r[f"pb{s}"]])
                for k2 in range(2):
                    P.op("pe", lambda e, s=s, k2=k2: e.transpose(tp[:, k2 * 128:(k2 + 1) * 128], pb[s][:, k2 * 128:(k2 + 1) * 128], ident[:, :]),
                         reads=[r[f"pb{s}"], r["id"]], writes=[r["tp"]])
                P.op("act", lambda e, tt=tt: e.copy(pT[:, :, tt * 128:(tt + 1) * 128], tp[:, 0:256].rearrange("p (k t) -> p k t", k=2)),
                     reads=[r["tp"]], writes=[r["pT"]])
            if t0 == 0:
                _load_wblk(P, r, wf, wb, 0, wg_blk_dram, 0, wcnt)
            for cb in range(NB):
                s = cb % 2
                if cb + 1 < NB:
                    _load_wblk(P, r, wf, wb, 1 - s, wg_blk_dram, cb + 1, wcnt)
                elif t0 + TQ < T:
                    _load_wblk(P, r, wf, wb, 1 - s, wg_blk_dram, 0, wcnt)
                for tt in range(TQ // 128):
                    b3 = cnt % 3
                    b2 = cnt % 2
                    cnt += 1
                    tok = t0 + tt * 128
                    P.dma("sp", xt[b2][:, :], x1_dram[tok:tok + 128, cb * 512:(cb + 1) * 512], writes=[r[f"xt{b2}"]])
                    for kc in range(KC):
                        P.op("pe", lambda e, s=s, b3=b3, kc=kc, tt=tt: e.matmul(ps[b3][:, :], hT[:, kc, tt * 128:(tt + 1) * 128], wb[s][:, kc, :],
                                                                               start=(kc == 0), stop=(kc == KC - 1)),
                             reads=[r[f"wb{s}"], r["hT"]], writes=[r[f"ps{b3}"]])
                    for k2 in range(2):
                        P.op("pe", lambda e, b3=b3, k2=k2, tt=tt, cb=cb: e.matmul(pp[b3][:, :], pT[:, k2, tt * 128:(tt + 1) * 128],
                                                                                 wpb[:, k2, cb * 512:(cb + 1) * 512], start=(k2 == 0), stop=(k2 == 1)),
                             reads=[r["wpb"], r["pT"]], writes=[r[f"pp{b3}"]])
                    P.op("act", lambda e, b3=b3: e.activation(out=gt[:, :], in_=ps[b3][:, :], func=AF.Sigmoid),
                         reads=[r[f"ps{b3}"]], writes=[r["gt"]])
                    P.op("dve", lambda e, b3=b3: e.tensor_tensor(out=tm[:, :], in0=pp[b3][:, :], in1=gt[:, :], op=ALU.mult),
                         reads=[r[f"pp{b3}"], r["gt"]], writes=[r["tm"]])
                    P.op("pool", lambda e, b2=b2: e.tensor_tensor(out=ot[b2][:, :], in0=tm[:, :], in1=xt[b2][:, :], op=ALU.add),
                         reads=[r["tm"], r[f"xt{b2}"]], writes=[r[f"ot{b2}"]])
                    P.dma("pool", x2_dram[tok:tok + 128, cb * 512:(cb + 1) * 512], ot[b2][:, :], reads=[r[f"ot{b2}"]], writes=[r["out"]])
        P.barrier()
        P.flush()


SEQ = 4096
NLAYER = 2
NCT = 130


def make_consts():
    c = {}
    c.update(ret_consts())
    c.update(gdn_consts())
    c.update(rwkv_consts())
    c.update(dense_consts())
    return c


_UIDC = [0]


def build_program(T=SEQ, L=NLAYER):
    nc = bass.Bass("TRN2", target_bir_lowering=False)
    NCH = T // 128
    cs = make_consts()
    ext = lambda n, shp, dt=F32: nc.dram_tensor(n, list(shp), dt, kind="ExternalInput")
    x = ext("x", [T, D])
    pos = ext("pos", [128, T], I32)
    C = {k: ext("c_" + k, v.shape) for k, v in cs.items()}
    Lp = []
    for l in range(L):
        d = {}
        d["nwb"] = ext(f"nwb{l}", [128, D]); d["win"] = ext(f"win{l}", [NCT, 128, KC * 128])
        d["gnwT"] = ext(f"gnwT{l}", [128, 8])
        for n in ("w0T", "a0T", "kkT", "kaT", "rkT", "lnwT", "lnbT"):
            d[n] = ext(f"{n}{l}", [128, 8])
        d["muT"] = ext(f"muT{l}", [128, 33]); d["lw2"] = ext(f"lw2{l}", [128, 1024])
        d["convT"] = ext(f"convT{l}", [128, 192]); d["alog"] = ext(f"alog{l}", [16, 1]); d["dtb"] = ext(f"dtb{l}", [16, 1])
        d["nrm"] = ext(f"nrm{l}", [128, 1])
        d["wout"] = ext(f"wout{l}", [32, 128, KC * 128]); d["wgate"] = ext(f"wgate{l}", [32, 128, KC * 128])
        d["wple"] = ext(f"wple{l}", [128, 2 * D]); d["plnb"] = ext(f"plnb{l}", [128, D]); d["p"] = ext(f"p{l}", [T, 256])
        Lp.append(d)
    fnb = ext("fnb", [128, D])
    out = nc.dram_tensor("out", [T, D], F32, kind="ExternalOutput")
    scr = lambda n, shp, dt: nc.dram_tensor(n, list(shp), dt)
    hT = scr("hT", [KC, 128, T], BF16); yT = scr("yT", [KC, 128, T], BF16)
    projA = scr("projA", [65 * 128, T], F32); projB = scr("projB", [65 * 128, T], F32)
    projT = lambda ct: (projA, ct * 128) if ct < 65 else (projB, (ct - 65) * 128)
    cosT = scr("cosT", [128, T], F32); sinT = scr("sinT", [128, T], F32)
    x1 = scr("x1", [T, D], F32); x2 = scr("x2", [T, D], F32)
    S = {n: scr(n, [2048, T], BF16) for n in ("gqT", "gqdT", "gkT", "gvT", "wkT_d")}
    S["gbt"] = scr("gbt", [T, 32], F32)
    S["attnT_d"] = scr("attnT_d", [NCH, 128, 16, 128], BF16); S["ktl_d"] = scr("ktl_d", [T, 2048], BF16)
    S["u_d"] = scr("u_d", [T, 2048], F32); S["els_d"] = scr("els_d", [NCH, 128, 16], F32)
    S.update({n: scr(n, [1024, T], BF16) for n in ("rtT", "kktT", "khT", "kkaT", "rvT")})
    S.update({n: scr(n, [1024, T], F32) for n in ("bonT", "szT")})
    S["pc_d"] = scr("pc_d", [8, 128, NCH], F32)
    S.update({n: scr(n, [NCH, 128, 16, 128], BF16) for n in ("tinvT_d", "akvT_d", "bkvT_d", "nbabT_d")})
    S.update({n: scr(n, [T, 1024], BF16) for n in ("vtk_d", "khtk_d", "kkatk_d")})
    grows = dict(q=0, k=2048, v=4096, z=6144, a=8192, b=8208)
    import os
    only = os.environ.get("MK_PH")
    only = set(only.split(",")) if only else None

    def on(n):
        return only is None or n in only
    with contextlib.ExitStack() as stack:
        P = Prog(nc, stack)
        if on("rope"):
            phase_rope_tables(P, nc, pos, C, cosT, sinT, T)
        xin = x
        for l in range(L):
            d = Lp[l]
            if on("norm"):
                phase_norm(P, nc, xin, d["nwb"], hT, T)
            if on("proj"):
                phase_proj(P, nc, hT, d["win"], projT, cosT, sinT, T, NCT, n_rope=int(os.environ.get("MK_NROPE", "16")), swap_dram=C["swap64"])
            if on("ret"):
                phase_ret(P, nc, projA, yT, C, d["gnwT"], T)
            if on("rwkv"):
                phase_rwkv_pre(P, nc, projA, S, C, d, T, 4096)
            if on("rwkv"):
                phase_rwkv_r1(P, nc, S, C, T)
            if on("rwkv"):
                phase_rwkv_r2(P, nc, S, C, yT, d, T, kc0=8)
            if on("gdn"):
                phase_gdn_pre(P, nc, projB, S, C, d, T, grows)
            if on("gdn"):
                phase_gdn_g1(P, nc, S, C, T)
            if on("gdn"):
                phase_gdn_g2(P, nc, projB, S, C, yT, d["nrm"], T, grows["z"], kc0=16)
            if on("out"):
                phase_out(P, nc, yT, d["wout"], xin, x1, T)
            if on("norm2"):
                phase_norm(P, nc, x1, d["plnb"], hT, T)
            if on("gate"):
                phase_gate(P, nc, hT, d["wgate"], d["p"], d["wple"], x1, x2, T)
            xin = x2
        if on("fin"):
            phase_norm(P, nc, xin, fnb, None, T, out_dram=out)
        n_ops = P.n_ops
    return nc, cs, n_ops


def _tiles(w, ncols_pad=None):
    K, N = w.shape
    if ncols_pad is not None and ncols_pad > N:
        w = np.concatenate([w, np.zeros((K, ncols_pad - N), w.dtype)], axis=1)
        N = ncols_pad
    return np.ascontiguousarray(w.reshape(K // 128, 128, N // 128, 128).transpose(2, 1, 0, 3).reshape(N // 128, 128, (K // 128) * 128))


def prep_shared(inp, L=NLAYER):
    f = np.float32
    sh = {}
    bc = lambda v: np.ascontiguousarray(np.broadcast_to(np.asarray(v, f), (128, v.shape[-1])))
    col8 = lambda v: np.ascontiguousarray(np.asarray(v, f).reshape(8, 128).T)
    for l in range(L):
        sh[f"nwb{l}"] = bc(inp["norm_w"][l])
        sh[f"win{l}"] = _tiles(np.asarray(inp["w_in"][l], f), NCT * 128)
        sh[f"gnwT{l}"] = col8(inp["ret_gn"][l])
        sh[f"w0T{l}"] = col8(inp["rwkv_w0"][l]); sh[f"a0T{l}"] = col8(inp["rwkv_a0"][l])
        sh[f"kkT{l}"] = col8(inp["rwkv_k_k"][l]); sh[f"kaT{l}"] = col8(inp["rwkv_k_a"][l]); sh[f"rkT{l}"] = col8(inp["rwkv_r_k"][l])
        sh[f"lnwT{l}"] = col8(inp["rwkv_ln_w"][l]); sh[f"lnbT{l}"] = col8(inp["rwkv_ln_b"][l])
        sh[f"muT{l}"] = np.ascontiguousarray(np.asarray(inp["rwkv_mu"][l], f).reshape(33, 128).T)
        sh[f"lw2{l}"] = np.ascontiguousarray(np.concatenate([np.asarray(inp["rwkv_w2"][l], f), np.asarray(inp["rwkv_a2"][l], f)], 0))
        sh[f"convT{l}"] = np.ascontiguousarray(np.asarray(inp["gdn_conv"][l], f).reshape(4, 48, 128).transpose(2, 1, 0).reshape(128, 192))
        sh[f"alog{l}"] = np.asarray(inp["gdn_a_log"][l], f).reshape(16, 1).copy()
        sh[f"dtb{l}"] = np.asarray(inp["gdn_dt_bias"][l], f).reshape(16, 1).copy()
        sh[f"nrm{l}"] = np.asarray(inp["gdn_norm"][l], f).reshape(128, 1).copy()
        sh[f"wout{l}"] = _tiles(np.asarray(inp["w_out"][l], f))
        sh[f"wgate{l}"] = _tiles(np.asarray(inp["w_ple_gate"][l], f))
        sh[f"wple{l}"] = np.ascontiguousarray(np.asarray(inp["w_ple"][l], f).reshape(2, 128, D).transpose(1, 0, 2).reshape(128, 2 * D))
        sh[f"plnb{l}"] = bc(inp["ple_norm"][l])
    sh["fnb"] = bc(inp["final_norm"])
    return sh


def kernel(**inp):
    B = inp["x"].shape[0]
    T = inp["x"].shape[1]
    nc, cs, n_ops = build_program(T, NLAYER)
    sh = prep_shared(inp)
    for k, v in cs.items():
        sh["c_" + k] = v
    in_maps = []
    for b in range(B):
        m = dict(sh)
        m["x"] = np.ascontiguousarray(np.asarray(inp["x"][b], np.float32))
        m["pos"] = np.ascontiguousarray(np.broadcast_to(np.asarray(inp["positions"][b], np.int32), (128, T)))
        for l in range(NLAYER):
            m[f"p{l}"] = np.ascontiguousarray(np.asarray(inp["p"][l, b], np.float32))
        in_maps.append(m)
    res = run_bass_kernel_spmd(nc, in_maps, core_ids=list(range(B)))
    return np.stack([np.asarray(r["out"], np.float32) for r in res.results], axis=0)
```

```python
import contextlib, math
from collections import defaultdict
import numpy as np
import concourse.bass as bass
import concourse.mybir as mybir
from concourse.bass_utils import run_bass_kernel_spmd


F32 = mybir.dt.float32
BF16 = mybir.dt.bfloat16
I32 = mybir.dt.int32
ALU = mybir.AluOpType
AF = mybir.ActivationFunctionType
AX = mybir.AxisListType

ENGS = ("pe", "act", "dve", "pool", "sp")
EPOCH = 20000
N_EPOCHS = {"pe": 16, "act": 10, "dve": 12, "pool": 10, "sp": 1}
N_DMA_SEM = 12


_UID = [0]


def _uid():
    return f"u{_UID[0]}_"


class Res:
    __slots__ = ("name", "w", "r")

    def __init__(self, name=""):
        self.name = name
        self.w = None
        self.r = []


class Prog:
    def __init__(self, nc, stack):
        self.nc = nc
        self.stack = stack
        self.sems = {}
        for e in ENGS:
            self.sems[e] = [stack.enter_context(nc.semaphore(f"s_{e}_{i}")) for i in range(N_EPOCHS[e])]
        self.dsems = {}
        for e in ("sp", "pool"):
            self.dsems[e] = [stack.enter_context(nc.semaphore(f"d_{e}_{i}")) for i in range(N_DMA_SEM)]
        self.dcount = {e: [0] * N_DMA_SEM for e in self.dsems}
        self.dnext = {e: 0 for e in self.dsems}
        self.seq = {e: 0 for e in ENGS}
        self.known = {e: {} for e in ENGS}
        self.ops = {e: [] for e in ENGS}
        self.last = {e: None for e in ENGS}
        self.outstanding = []
        self.n_ops = 0
        self.pe_needed = set()
        self.pe_map = {}
        self.pe_count = 0

    def _waits_for(self, eng, reads, writes, extra=()):
        toks = list(extra)
        for r in reads:
            if r.w is not None:
                toks.append(r.w)
        for w in writes:
            if w.w is not None:
                toks.append(w.w)
            toks.extend(w.r)
        best = {}
        for (sem, val, te, raw) in toks:
            pass
        return toks

    def _filter(self, eng, toks):
        best = {}
        for tok, is_raw in toks:
            sem, val, te = tok
            if te == eng and eng == "pe":
                continue
            k = "PE" if te == "pe" else id(sem)
            if k not in best or best[k][1] < val:
                best[k] = (sem, val)
        out = []
        kn = self.known[eng]
        for k, (sem, val) in best.items():
            if kn.get(k, 0) >= val:
                continue
            kn[k] = val
            if k == "PE":
                self.pe_needed.add(val)
            out.append((sem, val))
        return out

    def op(self, eng, fn, reads=(), writes=(), extra=()):
        toks = [(t, True) for t in extra]
        for r in reads:
            if r.w is not None:
                toks.append((r.w, True))
        for w in writes:
            if w.w is not None:
                toks.append((w.w, False))
            toks.extend((t, False) for t in w.r)
        waits = self._filter(eng, toks)
        self.seq[eng] += 1
        s = self.seq[eng]
        if eng == "pe":
            tok = ("PE", s, "pe")
            self.ops[eng].append((waits, fn, s, 1))
        else:
            ep = (s - 1) // EPOCH
            tok = (self.sems[eng][ep], s - ep * EPOCH, eng)
            self.ops[eng].append((waits, fn, tok[0], 1))
        self.last[eng] = tok
        for r in reads:
            r.r.append(tok)
        for w in writes:
            w.w = tok
            w.r = []
        self.n_ops += 1
        return tok

    def dma(self, q, out, in_, reads=(), writes=(), **kw):
        i = self.dnext[q]
        self.dnext[q] = (i + 1) % N_DMA_SEM
        sem = self.dsems[q][i]
        toks = []
        if self.dcount[q][i] > 0:
            toks.append(((sem, self.dcount[q][i], None), True))
        for r in reads:
            if r.w is not None:
                toks.append((r.w, True))
        for w in writes:
            if w.w is not None:
                toks.append((w.w, False))
            toks.extend((t, False) for t in w.r)
        waits = self._filter(q, toks)
        self.dcount[q][i] += 16
        tok = (sem, self.dcount[q][i], None)

        def fn(e, out=out, in_=in_, kw=kw):
            return e.dma_start(out=out, in_=in_, **kw)
        self.ops[q].append((waits, fn, sem, 16))
        for r in reads:
            r.r.append(tok)
        for w in writes:
            w.w = tok
            w.r = []
        self.outstanding.append(tok)
        self.n_ops += 1
        return tok

    def barrier(self):
        toks = [(t, True) for t in self.outstanding]
        for e in ENGS:
            if self.last[e] is not None:
                toks.append((self.last[e], True))
        for e in ENGS:
            waits = self._filter(e, [(t, r) for (t, r) in toks if t[2] != e])
            if waits:
                self.ops[e].append((waits, None, None, 0))
        self.outstanding = []

    def flush(self):
        _UID[0] += 1
        nc = self.nc
        ops = self.ops
        for (waits, fn, idx, inc) in ops["pe"]:
            if fn is not None and idx in self.pe_needed:
                self.pe_count += 1
                c = self.pe_count
                ep = (c - 1) // EPOCH
                self.pe_map[idx] = (self.sems["pe"][ep], c - ep * EPOCH)
        pe_map = self.pe_map

        def rw(w):
            s_, v_ = w
            if isinstance(s_, str):
                return pe_map[v_]
            return w
        with nc.Block() as block:
            def run(handle, lst, is_pe=False):
                for waits, fn, sem, inc in lst:
                    for w in waits:
                        s_, v_ = rw(w)
                        handle.wait_ge(s_, v_)
                    if fn is not None:
                        ins = fn(handle)
                        if is_pe:
                            if sem in pe_map:
                                ins.then_inc(pe_map[sem][0], 1)
                        else:
                            ins.then_inc(sem, inc)

            @block.tensor
            def _(e):
                run(e, ops["pe"], True)

            @block.scalar
            def _(e):
                run(e, ops["act"])

            @block.vector
            def _(e):
                run(e, ops["dve"])

            @block.gpsimd
            def _(e):
                run(e, ops["pool"])

            @block.sync
            def _(e):
                run(e, ops["sp"])
        self.ops = {e: [] for e in ENGS}


RET_H = 8


def ret_consts():
    h = np.arange(8, dtype=np.float64)
    lg = np.log1p(-(2.0 ** (-5.0 - h)))
    i = np.arange(128, dtype=np.float64)
    Gq = np.exp((i[None, :] + 1) * lg[:, None])
    Gk = np.exp(-(i[None, :] + 1) * lg[:, None]) * 128 ** -0.5
    GC = np.exp(128 * lg)
    c = {}
    c["ret_gq"] = np.broadcast_to(Gq.reshape(1, 8 * 128), (128, 1024)).astype(np.float32).copy()
    c["ret_gk"] = np.broadcast_to(Gk.reshape(1, 8 * 128), (128, 1024)).astype(np.float32).copy()
    c["ret_gc"] = np.broadcast_to(np.repeat(GC, 128).reshape(1, 1024), (128, 1024)).astype(np.float32).copy()
    jj, ii = np.meshgrid(np.arange(128), np.arange(128), indexing="ij")
    c["mask_ui"] = (ii >= jj).astype(np.float32)
    c["ident"] = np.eye(128, dtype=np.float32)
    return c


def dma_rows(P, q, out_tile, dram, row0, nh, t0, tl, res_w, hs=4):
    for h0 in range(0, nh, hs):
        src = dram[row0 + h0 * 128: row0 + (h0 + hs) * 128, t0:t0 + tl].rearrange("(h p) t -> p h t", p=128)
        P.dma(q, out_tile[:, h0:h0 + hs, 0:tl], src, writes=[res_w])


def phase_ret(P, nc, projT, yT, C, gnwT_dram, T, eps=1e-5):
    H = RET_H
    NCH = T // 128
    with contextlib.ExitStack() as st:
        def sb(name, shape, dt):
            return st.enter_context(nc.sbuf_tensor(_uid() + "rt_" + name, shape, dt))

        def pst(name, shape, dt):
            return st.enter_context(nc.psum_tensor(_uid() + "rt_" + name, shape, dt))
        qf = [sb(f"qf{i}", [128, H, 128], F32) for i in range(2)]
        kf = [sb(f"kf{i}", [128, H, 128], F32) for i in range(2)]
        vf = [sb(f"vf{i}", [128, H, 128], F32) for i in range(2)]
        zf = [sb(f"zf{i}", [128, H, 128], F32) for i in range(2)]
        gq = sb("gq", [128, H, 128], F32); gk = sb("gk", [128, H, 128], F32); gc = sb("gc", [128, H, 128], F32)
        maskf = sb("maskf", [128, 128], F32)
        identf = sb("identf", [128, 128], F32); ident = sb("ident", [128, 128], BF16)
        gnw = sb("gnw", [128, H], F32)
        qd = sb("qd", [128, H, 128], BF16); kd = sb("kd", [128, H, 128], BF16); vb = sb("vb", [128, H, 128], BF16)
        scT = sb("scT", [128, H, 128], BF16)
        vtok = sb("vtok", [128, H, 128], BF16); kdtok = sb("kdtok", [128, H, 128], BF16)
        state = sb("state", [128, H, 128], F32); stmp = sb("stmp", [128, H, 128], F32); state_bf = sb("state_bf", [128, H, 128], BF16)
        y_sb = sb("y_sb", [128, H, 128], F32); sq = sb("sq", [128, H, 128], F32)
        s1 = sb("s1", [128, H], F32); s2 = sb("s2", [128, H], F32); mean = sb("mean", [128, H], F32)
        var = sb("var", [128, H], F32); sd = sb("sd", [128, H], F32); rstd = sb("rstd", [128, H], F32)
        epsc = sb("epsc", [128, 1], F32); mh = sb("mh", [128, H], F32)
        yc = sb("yc", [128, H, 128], F32); yn = sb("yn", [128, H, 128], BF16)
        sz = sb("sz", [128, H, 128], F32); yg = sb("yg", [128, H, 128], F32); yfin = [sb(f"yfin{i}", [128, H, 128], BF16) for i in range(2)]
        sc_ps = pst("sc_ps", [128, H, 128], F32)
        vt_ps = pst("vt_ps", [128, H, 128], BF16)
        kt_ps = pst("kt_ps", [128, H, 128], BF16)
        y_ps = pst("y_ps", [128, H, 128], F32)
        kv_ps = pst("kv_ps", [128, H, 128], F32)
        r = {n: Res(n) for n in ["gq", "gk", "gc", "mask", "identf", "ident", "gnw", "qd", "kd", "vb", "scT", "vtok", "kdtok",
                                 "state", "stmp", "state_bf", "y_sb", "sq", "s1", "s2", "mean", "var", "sd", "rstd", "eps",
                                 "yc", "yn", "sz", "yg", "mh", "sc_ps", "vt_ps", "kt_ps", "y_ps", "kv_ps", "out"]}
        r_qf = [Res(), Res()]; r_kf = [Res(), Res()]; r_vf = [Res(), Res()]; r_zf = [Res(), Res()]; r_yfin = [Res(), Res()]

        flat = lambda t: t[:, :, :].rearrange("p h t -> p (h t)")
        P.dma("sp", flat(gq), C["ret_gq"][:, :], writes=[r["gq"]])
        P.dma("sp", flat(gk), C["ret_gk"][:, :], writes=[r["gk"]])
        P.dma("sp", flat(gc), C["ret_gc"][:, :], writes=[r["gc"]])
        P.dma("sp", maskf[:, :], C["mask_ui"][:, :], writes=[r["mask"]])
        P.dma("sp", identf[:, :], C["ident"][:, :], writes=[r["identf"]])
        P.dma("sp", gnw[:, :], gnwT_dram[:, :], writes=[r["gnw"]])
        P.op("pool", lambda e: e.tensor_copy(ident[:, :], identf[:, :]), reads=[r["identf"]], writes=[r["ident"]])
        P.op("pool", lambda e: e.memset(epsc[:, :], eps), writes=[r["eps"]])
        P.op("pool", lambda e: e.memset(mh[:, :], -0.5), writes=[r["mh"]])
        P.op("pool", lambda e: e.memset(flat(state), 0.0), writes=[r["state"]])
        P.op("pool", lambda e: e.memset(flat(state_bf), 0.0), writes=[r["state_bf"]])

        def bc(t):
            return t[:, :].unsqueeze(2).to_broadcast([128, H, 128])

        for n in range(NCH):
            s = n % 2
            t0 = n * 128
            dma_rows(P, "sp", qf[s], projT, 0, H, t0, 128, r_qf[s])
            dma_rows(P, "sp", kf[s], projT, 1024, H, t0, 128, r_kf[s])
            dma_rows(P, "sp", vf[s], projT, 2048, H, t0, 128, r_vf[s])
            dma_rows(P, "sp", zf[s], projT, 3072, H, t0, 128, r_zf[s])
            P.op("dve", lambda e, s=s: e.tensor_tensor(out=flat(qd), in0=flat(qf[s]), in1=flat(gq), op=ALU.mult),
                 reads=[r_qf[s], r["gq"]], writes=[r["qd"]])
            P.op("dve", lambda e, s=s: e.tensor_tensor(out=flat(kd), in0=flat(kf[s]), in1=flat(gk), op=ALU.mult),
                 reads=[r_kf[s], r["gk"]], writes=[r["kd"]])
            P.op("act", lambda e, s=s: e.copy(flat(vb), flat(vf[s])), reads=[r_vf[s]], writes=[r["vb"]])
            P.op("act", lambda e, s=s: e.activation(out=flat(sz), in_=flat(zf[s]), func=AF.Silu), reads=[r_zf[s]], writes=[r["sz"]])
            for h in range(H):
                P.op("pe", lambda e, h=h: e.matmul(sc_ps[:, h, :], kd[:, h, :], qd[:, h, :], start=True, stop=True),
                     reads=[r["kd"], r["qd"]], writes=[r["sc_ps"]])
            for h in range(H):
                P.op("pe", lambda e, h=h: e.transpose(vt_ps[:, h, :], vb[:, h, :], ident[:, :]),
                     reads=[r["vb"], r["ident"]], writes=[r["vt_ps"]])
            for h in range(H):
                P.op("pe", lambda e, h=h: e.transpose(kt_ps[:, h, :], kd[:, h, :], ident[:, :]),
                     reads=[r["kd"], r["ident"]], writes=[r["kt_ps"]])
            P.op("dve", lambda e: e.tensor_tensor(out=scT[:, :, :], in0=sc_ps[:, :, :],
                                                  in1=maskf[:, :].unsqueeze(1).to_broadcast([128, H, 128]), op=ALU.mult),
                 reads=[r["sc_ps"], r["mask"]], writes=[r["scT"]])
            P.op("act", lambda e: e.copy(flat(vtok), flat(vt_ps)), reads=[r["vt_ps"]], writes=[r["vtok"]])
            P.op("act", lambda e: e.copy(flat(kdtok), flat(kt_ps)), reads=[r["kt_ps"]], writes=[r["kdtok"]])
            for h in range(H):
                P.op("pe", lambda e, h=h: e.matmul(y_ps[:, h, :], scT[:, h, :], vtok[:, h, :], start=True, stop=False),
                     reads=[r["scT"], r["vtok"]], writes=[r["y_ps"]])
                P.op("pe", lambda e, h=h: e.matmul(y_ps[:, h, :], qd[:, h, :], state_bf[:, h, :], start=False, stop=True),
                     reads=[r["qd"], r["state_bf"]], writes=[r["y_ps"]])
            for h in range(H):
                P.op("pe", lambda e, h=h: e.matmul(kv_ps[:, h, :], kdtok[:, h, :], vtok[:, h, :], start=True, stop=True),
                     reads=[r["kdtok"], r["vtok"]], writes=[r["kv_ps"]])
            P.op("dve", lambda e: e.tensor_tensor(out=flat(stmp), in0=flat(kv_ps), in1=flat(state), op=ALU.add),
                 reads=[r["kv_ps"], r["state"]], writes=[r["stmp"]])
            P.op("dve", lambda e: e.tensor_tensor(out=flat(state), in0=flat(stmp), in1=flat(gc), op=ALU.mult),
                 reads=[r["stmp"], r["gc"]], writes=[r["state"]])
            P.op("act", lambda e: e.copy(flat(state_bf), flat(state)), reads=[r["state"]], writes=[r["state_bf"]])
            P.op("act", lambda e: e.copy(flat(y_sb), flat(y_ps)), reads=[r["y_ps"]], writes=[r["y_sb"]])
            P.op("dve", lambda e: e.tensor_reduce(out=s1[:, :], in_=y_sb[:, :, :], axis=AX.X, op=ALU.add),
                 reads=[r["y_sb"]], writes=[r["s1"]])
            P.op("dve", lambda e: e.tensor_tensor(out=flat(sq), in0=flat(y_sb), in1=flat(y_sb), op=ALU.mult),
                 reads=[r["y_sb"]], writes=[r["sq"]])
            P.op("dve", lambda e: e.tensor_reduce(out=s2[:, :], in_=sq[:, :, :], axis=AX.X, op=ALU.add),
                 reads=[r["sq"]], writes=[r["s2"]])
            P.op("dve", lambda e: e.tensor_scalar(out=mean[:, :], in0=s1[:, :], scalar1=1.0 / 128, scalar2=None, op0=ALU.mult),
                 reads=[r["s1"]], writes=[r["mean"]])
            P.op("dve", lambda e: e.tensor_tensor(out=var[:, :], in0=mean[:, :], in1=mean[:, :], op=ALU.mult),
                 reads=[r["mean"]], writes=[r["var"]])
            P.op("dve", lambda e: e.scalar_tensor_tensor(out=var[:, :], in0=s2[:, :], scalar=1.0 / 128, in1=var[:, :],
                                                         op0=ALU.mult, op1=ALU.subtract),
                 reads=[r["s2"], r["var"]], writes=[r["var"]])
            P.op("dve", lambda e: e.tensor_scalar(out=sd[:, :], in0=var[:, :], scalar1=1.0, scalar2=eps, op0=ALU.mult, op1=ALU.add),
                 reads=[r["var"]], writes=[r["sd"]])
            P.op("pool", lambda e: e.tensor_tensor(out=rstd[:, :], in0=sd[:, :], in1=mh[:, :], op=ALU.pow), reads=[r["sd"], r["mh"]], writes=[r["rstd"]])
            P.op("dve", lambda e: e.tensor_tensor(out=yc[:, :, :], in0=y_sb[:, :, :], in1=bc(mean), op=ALU.subtract),
                 reads=[r["y_sb"], r["mean"]], writes=[r["yc"]])
            P.op("dve", lambda e: e.tensor_tensor(out=yn[:, :, :], in0=yc[:, :, :], in1=bc(rstd), op=ALU.mult),
                 reads=[r["yc"], r["rstd"]], writes=[r["yn"]])
            for h in range(H):
                P.op("pe", lambda e, h=h: e.transpose(kt_ps[:, h, :], yn[:, h, :], ident[:, :]),
                     reads=[r["yn"], r["ident"]], writes=[r["kt_ps"]])
            P.op("dve", lambda e: e.tensor_tensor(out=yg[:, :, :], in0=kt_ps[:, :, :], in1=bc(gnw), op=ALU.mult),
                 reads=[r["kt_ps"], r["gnw"]], writes=[r["yg"]])
            P.op("pool", lambda e, s=s: e.tensor_tensor(out=flat(yfin[s]), in0=flat(yg), in1=flat(sz), op=ALU.mult),
                 reads=[r["yg"], r["sz"]], writes=[r_yfin[s]])
            P.dma("pool", yT[0:H, :, t0:t0 + 128].rearrange("k p t -> p k t"), yfin[s][:, :, :],
                  reads=[r_yfin[s]], writes=[r["out"]])
        P.barrier()
        P.flush()


GH = 16


def gdn_consts():
    c = {}
    k, i = np.meshgrid(np.arange(128), np.arange(128), indexing="ij")
    c["triu"] = (k <= i).astype(np.float32)
    c["ones"] = np.ones((128, 128), np.float32)
    c["negones"] = -np.ones((128, 128), np.float32)
    c["mask_ls"] = (k > i).astype(np.float32)
    c["mask_li"] = (k >= i).astype(np.float32)
    c["mask_ui"] = (i >= k).astype(np.float32)
    c["mask_us"] = (i > k).astype(np.float32)
    c["ident"] = np.eye(128, dtype=np.float32)
    sel = np.zeros((16, 16, 128), np.float32)
    for h in range(16):
        sel[h, h, :] = 1.0
    c["sel16"] = sel.reshape(16, 16 * 128)
    cm = np.ones((128, 512), np.float32)
    cm[:, ::128] = 0.0
    c["cmask128"] = cm
    return c


def neumann(P, nc, L, LT, rL, rLT, W, r, nlev=6, G=4):
    identb = W["identb"]
    Pk = W["Pk"]; nL = W["nL"]; nLT = W["nLT"]
    pa, pb, pp = W["pa"], W["pb"], W["pp"]
    P.op("pool", lambda e: e.tensor_tensor(out=Pk[0][:, :, :], in0=identb[:, :].unsqueeze(1).to_broadcast([128, G, 128]),
                                           in1=LT[:, :, :], op=ALU.subtract),
         reads=[r["identb"], rLT], writes=[r["Pk0"]])
    curL, curLT, rcL, rcLT = L, LT, rL, rLT
    pi = 0
    for lev in range(nlev):
        s = lev % 2
        for g in range(G):
            P.op("pe", lambda e, g=g, a=curLT, b=curL: e.matmul(pa[:, g, :], a[:, g, :], b[:, g, :], start=True, stop=True),
                 reads=[rcLT, rcL], writes=[r["pa"]])
        if lev < nlev - 1:
            for g in range(G):
                P.op("pe", lambda e, g=g, a=curL, b=curLT: e.matmul(pb[:, g, :], a[:, g, :], b[:, g, :], start=True, stop=True),
                     reads=[rcLT, rcL], writes=[r["pb"]])
        P.op("act", lambda e, s=s: e.copy(nL[s][:, :, :], pa[:, :, :]), reads=[r["pa"]], writes=[r[f"nL{s}"]])
        if lev < nlev - 1:
            P.op("dve", lambda e, s=s: e.tensor_copy(nLT[s][:, :, :], pb[:, :, :]), reads=[r["pb"]], writes=[r[f"nLT{s}"]])
        for g in range(G):
            P.op("pe", lambda e, g=g, s=s, pi=pi: e.matmul(pp[:, g, :], nL[s][:, g, :], Pk[pi][:, g, :], start=True, stop=True),
                 reads=[r[f"nL{s}"], r[f"Pk{pi}"]], writes=[r["pp"]])
        P.op("dve", lambda e, pi=pi: e.tensor_tensor(out=Pk[1 - pi][:, :, :], in0=pp[:, :, :], in1=Pk[pi][:, :, :], op=ALU.add),
             reads=[r["pp"], r[f"Pk{pi}"]], writes=[r[f"Pk{1 - pi}"]])
        pi = 1 - pi
        curL, curLT, rcL, rcLT = nL[s], nLT[s], r[f"nL{s}"], r[f"nLT{s}"]
    return Pk[pi], r[f"Pk{pi}"]


def phase_gdn_pre(P, nc, projT, S, C, prm, T, rows):
    NB = T // 512
    with contextlib.ExitStack() as st:
        def sb(name, shape, dt):
            return st.enter_context(nc.sbuf_tensor(_uid() + "gp_" + name, shape, dt))

        def pst(name, shape, dt):
            return st.enter_context(nc.psum_tensor(_uid() + "gp_" + name, shape, dt))
        r = defaultdict(Res)
        convw = sb("convw", [128, 48 * 4], F32)
        alog = sb("alog", [16, 1], F32); dtb = sb("dtb", [16, 1], F32); nega = sb("nega", [16, 1], F32)
        onesf = sb("onesf", [128, 128], F32); identf = sb("identf", [128, 128], F32)
        sel = sb("sel", [16, 16 * 128], F32); cmask = sb("cmask", [16, 512], F32)
        epsc = sb("epsc", [128, 1], F32)
        at = sb("at", [16, 512], F32); bt = sb("bt", [16, 512], F32)
        e1 = sb("e1", [16, 512], F32); spt = sb("spt", [16, 512], F32); gt = sb("gt", [16, 512], F32)
        beta = sb("beta", [16, 512], F32); gcT = sb("gcT", [16, 512], F32); egcT = sb("egcT", [16, 512], F32)
        gbs = sb("gbs", [128, 4, 32], F32)
        u = [sb(f"u{i}", [128, 515], F32) for i in range(4)]
        acc = sb("acc", [128, 512], F32); sl = sb("sl", [128, 512], F32); sq = sb("sq", [128, 512], F32)
        sd = sb("sd", [128, 512], F32); rn = sb("rn", [128, 512], F32); kn = sb("kn", [128, 512], F32)
        ob = [sb(f"ob{i}", [128, 512], BF16) for i in range(4)]
        ob2 = [sb(f"ob2{i}", [128, 512], BF16) for i in range(4)]
        ss_ps = pst("ss_ps", [128, 512], F32)
        bc_ps = pst("bc_ps", [128, 512], F32)
        tr_ps_full = pst("tr_ps", [128, 512], F32)
        tr_ps = tr_ps_full[:, 0:128].rearrange("p (c x) -> p c x", x=32)
        P.dma("sp", convw[:, :], prm["convT"][:, :], writes=[r["convw"]])
        P.dma("sp", alog[:, :], prm["alog"][:, :], writes=[r["alog"]])
        P.dma("sp", dtb[:, :], prm["dtb"][:, :], writes=[r["dtb"]])
        P.dma("sp", onesf[:, :], C["ones"][:, :], writes=[r["onesf"]])
        P.dma("sp", identf[:, :], C["ident"][:, :], writes=[r["identf"]])
        P.dma("sp", sel[:, :], C["sel16"][:, :], writes=[r["sel"]])
        P.dma("sp", cmask[:, :], C["cmask128"][0:16, :], writes=[r["cmask"]])
        P.op("pool", lambda e: e.memset(epsc[:, :], 1e-6), writes=[r["eps"]])
        P.op("act", lambda e: e.activation(out=nega[:, :], in_=alog[:, :], func=AF.Exp), reads=[r["alog"]], writes=[r["nega"]])
        P.op("dve", lambda e: e.tensor_scalar(out=nega[:, :], in0=nega[:, :], scalar1=-1.0, scalar2=None, op0=ALU.mult),
             reads=[r["nega"]], writes=[r["nega"]])
        cnt = 0
        acc2 = [acc] + [sb(f"acc_{i}", [128, 512], F32) for i in range(3)]; sl2 = [sl] + [sb(f"sl_{i}", [128, 512], F32) for i in range(3)]
        sq2 = [sq] + [sb(f"sq_{i}", [128, 512], F32) for i in range(3)]; sd2 = [sd] + [sb(f"sd_{i}", [128, 512], F32) for i in range(3)]
        rn2 = [rn] + [sb(f"rn_{i}", [128, 512], F32) for i in range(3)]; kn2 = [kn] + [sb(f"kn_{i}", [128, 512], F32) for i in range(3)]
        ss2 = [ss_ps] + [pst(f"ss_ps_{i}", [128, 512], F32) for i in range(3)]
        bc2 = [bc_ps, tr_ps_full] + [pst(f"bc_ps_{i}", [128, 512], F32) for i in range(2)]

        def do_tile(kind, rbase, dst, ti, tb, s):
            t0 = tb * 512
            acc, sl, sq, sd, rn, kn, ss_ps, bc_ps = acc2[s], sl2[s], sq2[s], sd2[s], rn2[s], kn2[s], ss2[s], bc2[s]
            ra, rsl, rsq, rsd, rrn, rkn, rss, rbc = (r[f"acc{s}"], r[f"sl{s}"], r[f"sq{s}"], r[f"sd{s}"], r[f"rn{s}"], r[f"kn{s}"],
                                                     r[f"ss_ps{s}"], r[f"bc_ps{s}"])
            wi = {"q": 0, "k": 16, "v": 32}[kind] + ti
            row0 = rbase + ti * 128
            if tb == 0:
                P.op("pool", lambda e: e.memset(u[s][:, 0:3], 0.0), writes=[r[f"u{s}"]])
                P.dma("sp", u[s][:, 3:515], projT[row0:row0 + 128, 0:512], writes=[r[f"u{s}"]])
            else:
                P.dma("sp", u[s][:, :], projT[row0:row0 + 128, t0 - 3:t0 + 512], writes=[r[f"u{s}"]])
            P.op("act", lambda e: e.mul(acc[:, :], u[s][:, 3:515], convw[:, wi * 4 + 3:wi * 4 + 4]),
                 reads=[r[f"u{s}"], r["convw"]], writes=[ra])
            for j in (2, 1, 0):
                P.op("dve", lambda e, j=j: e.scalar_tensor_tensor(out=acc[:, :], in0=u[s][:, j:j + 512],
                                                                  scalar=convw[:, wi * 4 + j:wi * 4 + j + 1],
                                                                  in1=acc[:, :], op0=ALU.mult, op1=ALU.add),
                     reads=[r[f"u{s}"], r["convw"], ra], writes=[ra])
            yield
            if kind == "v":
                P.op("act", lambda e: e.activation(out=ob[s][:, :], in_=acc[:, :], func=AF.Silu), reads=[ra], writes=[r[f"ob{s}"]])
                P.dma("pool", S[dst][ti * 128:(ti + 1) * 128, t0:t0 + 512], ob[s][:, :], reads=[r[f"ob{s}"]], writes=[r[dst]])
                return
            P.op("act", lambda e: e.activation(out=sl[:, :], in_=acc[:, :], func=AF.Silu), reads=[ra], writes=[rsl])
            P.op("pool", lambda e: e.tensor_tensor(out=sq[:, :], in0=sl[:, :], in1=sl[:, :], op=ALU.mult), reads=[rsl], writes=[rsq])
            P.op("pe", lambda e: e.matmul(ss_ps[:, :], onesf[:, :], sq[:, :], start=True, stop=True), reads=[r["onesf"], rsq], writes=[rss])
            yield
            P.op("act", lambda e: e.activation(out=sd[:, :], in_=ss_ps[:, :], func=AF.Sqrt, bias=epsc[:, 0:1], scale=1.0),
                 reads=[rss, r["eps"]], writes=[rsd])
            yield
            P.op("dve", lambda e: e.reciprocal(rn[:, :], sd[:, :]), reads=[rsd], writes=[rrn])
            if kind == "k":
                P.op("dve", lambda e: e.tensor_tensor(out=ob[s][:, :], in0=sl[:, :], in1=rn[:, :], op=ALU.mult),
                     reads=[rsl, rrn], writes=[r[f"ob{s}"]])
                P.dma("pool", S[dst][ti * 128:(ti + 1) * 128, t0:t0 + 512], ob[s][:, :], reads=[r[f"ob{s}"]], writes=[r[dst]])
            else:
                P.op("dve", lambda e: e.scalar_tensor_tensor(out=kn[:, :], in0=sl[:, :], scalar=128 ** -0.5, in1=rn[:, :],
                                                             op0=ALU.mult, op1=ALU.mult),
                     reads=[rsl, rrn], writes=[rkn])
                P.op("act", lambda e: e.copy(ob[s][:, :], kn[:, :]), reads=[rkn], writes=[r[f"ob{s}"]])
                P.dma("pool", S[dst][ti * 128:(ti + 1) * 128, t0:t0 + 512], ob[s][:, :], reads=[r[f"ob{s}"]], writes=[r[dst]])
                P.op("pe", lambda e: e.matmul(bc_ps[:, :], sel[:, ti * 128:(ti + 1) * 128], egcT[:, :], start=True, stop=True),
                     reads=[r["sel"], r["egcT"]], writes=[rbc])
                P.op("dve", lambda e: e.tensor_tensor(out=ob2[s][:, :], in0=kn[:, :], in1=bc_ps[:, :], op=ALU.mult),
                     reads=[rkn, rbc], writes=[r[f"ob2{s}"]])
                P.dma("pool", S["gqdT"][ti * 128:(ti + 1) * 128, t0:t0 + 512], ob2[s][:, :], reads=[r[f"ob2{s}"]], writes=[r["gqdT"]])

        for tb in range(NB):
            t0 = tb * 512
            P.dma("sp", at[:, :], projT[rows["a"]:rows["a"] + 16, t0:t0 + 512], writes=[r["at"]])
            P.dma("sp", bt[:, :], projT[rows["b"]:rows["b"] + 16, t0:t0 + 512], writes=[r["bt"]])
            P.op("act", lambda e: e.activation(out=e1[:, :], in_=at[:, :], func=AF.Exp, bias=dtb[:, 0:1], scale=1.0),
                 reads=[r["at"], r["dtb"]], writes=[r["e1"]])
            P.op("dve", lambda e: e.tensor_scalar(out=e1[:, :], in0=e1[:, :], scalar1=1.0, scalar2=None, op0=ALU.add),
                 reads=[r["e1"]], writes=[r["e1"]])
            P.op("act", lambda e: e.activation(out=spt[:, :], in_=e1[:, :], func=AF.Ln), reads=[r["e1"]], writes=[r["spt"]])
            P.op("dve", lambda e: e.tensor_scalar(out=gt[:, :], in0=spt[:, :], scalar1=nega[:, 0:1], scalar2=None, op0=ALU.mult),
                 reads=[r["spt"], r["nega"]], writes=[r["gt"]])
            P.op("act", lambda e: e.activation(out=beta[:, :], in_=bt[:, :], func=AF.Sigmoid), reads=[r["bt"]], writes=[r["beta"]])
            P.op("dve", lambda e: e.tensor_tensor_scan(out=gcT[:, :], data0=cmask[:, :], data1=gt[:, :], initial=0.0,
                                                       op0=ALU.mult, op1=ALU.add),
                 reads=[r["cmask"], r["gt"]], writes=[r["gcT"]])
            P.op("act", lambda e: e.activation(out=egcT[:, :], in_=gcT[:, :], func=AF.Exp), reads=[r["gcT"]], writes=[r["egcT"]])
            for c4 in range(4):
                P.op("pe", lambda e, c4=c4: e.transpose(tr_ps[:, c4, 0:16], gt[:, c4 * 128:(c4 + 1) * 128], identf[0:16, 0:16]),
                     reads=[r["gt"], r["identf"]], writes=[r["bc_ps1"]])
                P.op("pe", lambda e, c4=c4: e.transpose(tr_ps[:, c4, 16:32], beta[:, c4 * 128:(c4 + 1) * 128], identf[0:16, 0:16]),
                     reads=[r["beta"], r["identf"]], writes=[r["bc_ps1"]])
            P.op("dve", lambda e: e.tensor_copy(gbs[:, :, :], tr_ps[:, :, :]), reads=[r["bc_ps1"]], writes=[r["gbs"]])
            P.dma("pool", S["gbt"][t0:t0 + 512, :].rearrange("(c p) x -> p c x", p=128), gbs[:, :, :],
                  reads=[r["gbs"]], writes=[r["gbt_out"]])
            for kind, rbase, dst in (("q", rows["q"], "gqT"), ("k", rows["k"], "gkT"), ("v", rows["v"], "gvT")):
                for ti in range(0, 16, 4):
                    zipper([do_tile(kind, rbase, dst, ti + j, tb, j) for j in range(4)])
        P.barrier()
        P.flush()


def zipper_lag(gens):
    a, b = gens
    a_done = b_done = False
    try:
        next(a)
    except StopIteration:
        a_done = True
    while not (a_done and b_done):
        if not b_done:
            try:
                next(b)
            except StopIteration:
                b_done = True
        if not a_done:
            try:
                next(a)
            except StopIteration:
                a_done = True


def zipper(gens):
    active = list(gens)
    while active:
        for g in list(active):
            try:
                next(g)
            except StopIteration:
                active.remove(g)


def neumann_gen(P, nc, L, LT, rL, rLT, W, r, nlev=6, G=4):
    identb = W["identb"]
    Pk = W["Pk"]; nL = W["nL"]; nLT = W["nLT"]
    pa, pb = W["pa"], W["pb"]
    P.op("pool", lambda e: e.tensor_tensor(out=Pk[0][:, :, :], in0=identb[:, :].unsqueeze(1).to_broadcast([128, G, 128]),
                                           in1=LT[:, :, :], op=ALU.subtract),
         reads=[W["r_identb"], rLT], writes=[r["Pk0"]])
    yield
    curL, curLT, rcL, rcLT = L, LT, rL, rLT
    pi = 0
    for lev in range(nlev):
        s = lev % 2
        for g in range(G):
            P.op("pe", lambda e, g=g, a=curLT, b=curL: e.matmul(pa[:, g, :], a[:, g, :], b[:, g, :], start=True, stop=True),
                 reads=[rcLT, rcL], writes=[r["pa"]])
        if lev < nlev - 1:
            for g in range(G):
                P.op("pe", lambda e, g=g, a=curL, b=curLT: e.matmul(pb[:, g, :], a[:, g, :], b[:, g, :], start=True, stop=True),
                     reads=[rcLT, rcL], writes=[r["pb"]])
        yield
        P.op("act", lambda e, s=s: e.copy(nL[s][:, :, :], pa[:, :, :]), reads=[r["pa"]], writes=[r[f"nL{s}"]])
        if lev < nlev - 1:
            P.op("dve", lambda e, s=s: e.tensor_copy(nLT[s][:, :, :], pb[:, :, :]), reads=[r["pb"]], writes=[r[f"nLT{s}"]])
        yield
        for g in range(G):
            P.op("pe", lambda e, g=g, s=s, pi=pi: e.matmul(pa[:, g, :], nL[s][:, g, :], Pk[pi][:, g, :], start=True, stop=True),
                 reads=[r[f"nL{s}"], r[f"Pk{pi}"]], writes=[r["pa"]])
        yield
        P.op("dve", lambda e, pi=pi: e.tensor_tensor(out=Pk[1 - pi][:, :, :], in0=pa[:, :, :], in1=Pk[pi][:, :, :], op=ALU.add),
             reads=[r["pa"], r[f"Pk{pi}"]], writes=[r[f"Pk{1 - pi}"]])
        yield
        pi = 1 - pi
        curL, curLT, rcL, rcLT = nL[s], nLT[s], r[f"nL{s}"], r[f"nLT{s}"]
    W["result"] = (Pk[pi], r[f"Pk{pi}"])


def phase_gdn_g1(P, nc, S, C, T):
    NCH = T // 128
    G = 4
    with contextlib.ExitStack() as st:
        def sb(name, shape, dt):
            return st.enter_context(nc.sbuf_tensor(_uid() + "g1_" + name, shape, dt))

        def pst(name, shape, dt):
            return st.enter_context(nc.psum_tensor(_uid() + "g1_" + name, shape, dt))
        r = defaultdict(Res)
        triu = sb("triu", [128, 128], F32); onesf = sb("onesf", [128, 128], F32); negones = sb("negones", [128, 128], F32)
        mls = sb("mls", [128, 128], F32); mli = sb("mli", [128, 128], F32)
        identf = sb("identf", [128, 128], F32); identb = sb("identb", [128, 128], BF16)
        gb = [sb(f"gb{i}", [128, 32], F32) for i in range(2)]
        gcs = sb("gcs", [128, 32], F32)
        egc = sb("egc", [128, 16], F32); dtl = sb("dtl", [128, 16], F32); etail = sb("etail", [128, 16], F32)
        elast = [sb(f"elast{i}", [128, 16], F32) for i in range(2)]
        bgc = sb("bgc", [128, 16], F32)
        Gb = sb("Gb", [128, 16, 128], F32); X = sb("X", [128, 16, 128], F32)
        gc_ps = None
        ST = []
        for q in range(2):
            t = {}
            for n in ("kT", "qT", "vT", "L", "attn", "LT", "attnT", "kbg", "ktail", "vb", "wk_sb", "nL0", "nL1", "nLT0", "nLT1", "Pk0", "Pk1"):
                t[n] = sb(f"{n}_{q}", [128, G, 128], BF16)
            for n in ("M1", "dec", "dec_s", "dec_i", "u_sb"):
                t[n] = sb(f"{n}_{q}", [128, G, 128], F32)
            t["g_ps"] = pst(f"g_ps{q}", [128, G, 128], F32)
            t["tr_ps"] = pst(f"tr_ps{q}", [128, 2, G, 128], BF16)
            t["pa"] = pst(f"pa{q}", [128, G, 128], F32)
            t["pb"] = pst(f"pb{q}", [128, G, 128], F32)
            t["r"] = defaultdict(Res)
            t["W"] = {"identb": identb, "r_identb": r["identb"], "nL": [t["nL0"], t["nL1"]], "nLT": [t["nLT0"], t["nLT1"]],
                      "Pk": [t["Pk0"], t["Pk1"]], "pa": t["pa"], "pb": t["pb"]}
            ST.append(t)
        for nm, t_, src in (("triu", triu, "triu"), ("onesf", onesf, "ones"), ("negones", negones, "negones"),
                            ("mls", mls, "mask_ls"), ("mli", mli, "mask_li"), ("identf", identf, "ident")):
            P.dma("sp", t_[:, :], C[src][:, :], writes=[r[nm]])
        P.op("pool", lambda e: e.tensor_copy(identb[:, :], identf[:, :]), reads=[r["identf"]], writes=[r["identb"]])

        def bcg(t, g):
            return t[:, g * G:(g + 1) * G].unsqueeze(2).to_broadcast([128, G, 128])

        def bcm(m):
            return m[:, :].unsqueeze(1).to_broadcast([128, G, 128])

        def grp(c, g, q, sc):
            t = ST[q]
            rr = t["r"]
            t0 = c * 128
            r0 = g * G * 128
            kT, qT, vT = t["kT"], t["qT"], t["vT"]
            g_ps, tr_ps = t["g_ps"], t["tr_ps"]
            for nm, tl, src in (("kT", kT, "gkT"), ("qT", qT, "gqT"), ("vT", vT, "gvT")):
                P.dma("sp", tl[:, :, :], S[src][r0:r0 + G * 128, t0:t0 + 128].rearrange("(h p) t -> p h t", p=128), writes=[rr[nm]])
            P.op("pe", lambda e: e.matmul(g_ps[:, :, :], triu[:, :], Gb[:, g * G:(g + 1) * G, :], start=True, stop=False),
                 reads=[r["triu"], r["Gb"]], writes=[rr["g_ps"]])
            P.op("pe", lambda e: e.matmul(g_ps[:, :, :], negones[:, :], X[:, g * G:(g + 1) * G, :], start=False, stop=True),
                 reads=[r["negones"], r["X"]], writes=[rr["g_ps"]])
            yield
            P.op("dve", lambda e: e.tensor_scalar(out=t["M1"][:, :, :], in0=g_ps[:, :, :], scalar1=0.0, scalar2=None, op0=ALU.min),
                 reads=[rr["g_ps"]], writes=[rr["M1"]])
            yield
            P.op("act", lambda e: e.activation(out=t["dec"][:, :, :], in_=t["M1"][:, :, :], func=AF.Exp), reads=[rr["M1"]], writes=[rr["dec"]])
            for h in range(G):
                P.op("pe", lambda e, h=h: e.matmul(g_ps[:, h, :], kT[:, h, :], kT[:, h, :], start=True, stop=True),
                     reads=[rr["kT"]], writes=[rr["g_ps"]])
            yield
            P.op("pool", lambda e: e.tensor_tensor(out=t["dec_i"][:, :, :], in0=t["dec"][:, :, :], in1=bcm(mli), op=ALU.mult),
                 reads=[rr["dec"], r["mli"]], writes=[rr["dec_i"]])
            P.op("dve", lambda e: e.tensor_tensor(out=t["dec_s"][:, :, :], in0=t["dec"][:, :, :], in1=bcm(mls), op=ALU.mult),
                 reads=[rr["dec"], r["mls"]], writes=[rr["dec_s"]])
            yield
            P.op("dve", lambda e: e.tensor_tensor(out=t["dec_s"][:, :, :], in0=t["dec_s"][:, :, :],
                                                  in1=gb[sc][:, 16 + g * G:16 + (g + 1) * G].unsqueeze(2).to_broadcast([128, G, 128]), op=ALU.mult),
                 reads=[rr["dec_s"], r[f"gb{sc}"]], writes=[rr["dec_s"]])
            yield
            P.op("dve", lambda e: e.tensor_tensor(out=t["L"][:, :, :], in0=g_ps[:, :, :], in1=t["dec_s"][:, :, :], op=ALU.mult),
                 reads=[rr["g_ps"], rr["dec_s"]], writes=[rr["L"]])
            yield
            for h in range(G):
                P.op("pe", lambda e, h=h: e.matmul(g_ps[:, h, :], qT[:, h, :], kT[:, h, :], start=True, stop=True),
                     reads=[rr["kT"], rr["qT"]], writes=[rr["g_ps"]])
            for h in range(G):
                P.op("pe", lambda e, h=h: e.transpose(tr_ps[:, 0, h, :], t["L"][:, h, :], identb[:, :]),
                     reads=[rr["L"], r["identb"]], writes=[rr["tr_ps"]])
            yield
            P.op("dve", lambda e: e.tensor_tensor(out=t["attn"][:, :, :], in0=g_ps[:, :, :], in1=t["dec_i"][:, :, :], op=ALU.mult),
                 reads=[rr["g_ps"], rr["dec_i"]], writes=[rr["attn"]])
            P.op("act", lambda e: e.copy(t["LT"][:, :, :], tr_ps[:, 0, :, :]), reads=[rr["tr_ps"]], writes=[rr["LT"]])
            yield
            for h in range(G):
                P.op("pe", lambda e, h=h: e.transpose(tr_ps[:, 1, h, :], t["attn"][:, h, :], identb[:, :]),
                     reads=[rr["attn"], r["identb"]], writes=[rr["tr_ps"]])
            yield
            P.op("act", lambda e: e.copy(t["attnT"][:, :, :], tr_ps[:, 1, :, :]), reads=[rr["tr_ps"]], writes=[rr["attnT"]])
            P.dma("pool", S["attnT_d"][c, :, g * G:(g + 1) * G, :], t["attnT"][:, :, :],
                  reads=[rr["attnT"]], writes=[r["attnT_out"]])
            yield
            yield from neumann_gen(P, nc, t["L"], t["LT"], rr["L"], rr["LT"], t["W"], rr)
            Pt, rPt = t["W"]["result"]
            for h in range(G):
                P.op("pe", lambda e, h=h: e.transpose(tr_ps[:, 0, h, :], kT[:, h, :], identb[:, :]),
                     reads=[rr["kT"], r["identb"]], writes=[rr["tr_ps"]])
            for h in range(G):
                P.op("pe", lambda e, h=h: e.transpose(tr_ps[:, 1, h, :], vT[:, h, :], identb[:, :]),
                     reads=[rr["vT"], r["identb"]], writes=[rr["tr_ps"]])
            yield
            P.op("dve", lambda e: e.tensor_tensor(out=t["kbg"][:, :, :], in0=tr_ps[:, 0, :, :], in1=bcg(bgc, g), op=ALU.mult),
                 reads=[rr["tr_ps"], r["bgc"]], writes=[rr["kbg"]])
            P.op("dve", lambda e: e.tensor_tensor(out=t["ktail"][:, :, :], in0=tr_ps[:, 0, :, :], in1=bcg(etail, g), op=ALU.mult),
                 reads=[rr["tr_ps"], r["etail"]], writes=[rr["ktail"]])
            P.op("dve", lambda e: e.tensor_tensor(out=t["vb"][:, :, :], in0=tr_ps[:, 1, :, :],
                                                  in1=gb[sc][:, 16 + g * G:16 + (g + 1) * G].unsqueeze(2).to_broadcast([128, G, 128]), op=ALU.mult),
                 reads=[rr["tr_ps"], r[f"gb{sc}"]], writes=[rr["vb"]])
            P.dma("pool", S["ktl_d"][t0:t0 + 128, r0:r0 + G * 128], t["ktail"][:, :, :].rearrange("p h d -> p (h d)"),
                  reads=[rr["ktail"]], writes=[r["ktl_out"]])
            yield
            for h in range(G):
                P.op("pe", lambda e, h=h: e.matmul(g_ps[:, h, :], Pt[:, h, :], t["vb"][:, h, :], start=True, stop=True),
                     reads=[rPt, rr["vb"]], writes=[rr["g_ps"]])
            yield
            P.op("act", lambda e: e.copy(t["u_sb"][:, :, :], g_ps[:, :, :]), reads=[rr["g_ps"]], writes=[rr["u_sb"]])
            P.dma("pool", S["u_d"][t0:t0 + 128, r0:r0 + G * 128], t["u_sb"][:, :, :].rearrange("p h d -> p (h d)"),
                  reads=[rr["u_sb"]], writes=[r["u_out"]])
            yield
            for h in range(G):
                P.op("pe", lambda e, h=h: e.matmul(g_ps[:, h, :], t["kbg"][:, h, :], Pt[:, h, :], start=True, stop=True),
                     reads=[rPt, rr["kbg"]], writes=[rr["g_ps"]])
            yield
            P.op("act", lambda e: e.copy(t["wk_sb"][:, :, :], g_ps[:, :, :]), reads=[rr["g_ps"]], writes=[rr["wk_sb"]])
            P.dma("pool", S["wkT_d"][r0:r0 + G * 128, t0:t0 + 128].rearrange("(h p) t -> p h t", p=128), t["wk_sb"][:, :, :],
                  reads=[rr["wk_sb"]], writes=[r["wk_out"]])
            yield

        for c in range(NCH):
            t0 = c * 128
            sc = c % 2
            gps0 = ST[0]["g_ps"]
            rg0 = ST[0]["r"]["g_ps"]
            P.dma("sp", gb[sc][:, :], S["gbt"][t0:t0 + 128, :], writes=[r[f"gb{sc}"]])
            P.op("pe", lambda e, sc=sc: e.matmul(gps0[:, 0, 0:16], triu[:, :], gb[sc][:, 0:16], start=True, stop=True),
                 reads=[r["triu"], r[f"gb{sc}"]], writes=[rg0])
            P.op("pe", lambda e, sc=sc: e.matmul(gps0[:, 0, 16:32], onesf[:, :], gb[sc][:, 0:16], start=True, stop=True),
                 reads=[r["onesf"], r[f"gb{sc}"]], writes=[rg0])
            P.op("dve", lambda e: e.tensor_copy(gcs[:, :], gps0[:, 0, 0:32]), reads=[rg0], writes=[r["gcs"]])
            P.op("act", lambda e: e.activation(out=egc[:, :], in_=gcs[:, 0:16], func=AF.Exp), reads=[r["gcs"]], writes=[r["egc"]])
            P.op("dve", lambda e: e.tensor_tensor(out=dtl[:, :], in0=gcs[:, 16:32], in1=gcs[:, 0:16], op=ALU.subtract),
                 reads=[r["gcs"]], writes=[r["dtl"]])
            P.op("act", lambda e: e.activation(out=etail[:, :], in_=dtl[:, :], func=AF.Exp), reads=[r["dtl"]], writes=[r["etail"]])
            P.op("act", lambda e, sc=sc: e.activation(out=elast[sc][:, :], in_=gcs[:, 16:32], func=AF.Exp),
                 reads=[r["gcs"]], writes=[r[f"elast{sc}"]])
            P.dma("pool", S["els_d"][c, :, :], elast[sc][:, :], reads=[r[f"elast{sc}"]], writes=[r["els_out"]])
            P.op("dve", lambda e, sc=sc: e.tensor_tensor(out=bgc[:, :], in0=gb[sc][:, 16:32], in1=egc[:, :], op=ALU.mult),
                 reads=[r[f"gb{sc}"], r["egc"]], writes=[r["bgc"]])
            P.op("pool", lambda e, sc=sc: e.tensor_copy(Gb[:, :, :], gb[sc][:, 0:16].unsqueeze(2).to_broadcast([128, 16, 128])),
                 reads=[r[f"gb{sc}"]], writes=[r["Gb"]])
            P.op("pool", lambda e: e.tensor_tensor(out=X[:, :, :], in0=Gb[:, :, :],
                                                   in1=triu[:, :].unsqueeze(1).to_broadcast([128, 16, 128]), op=ALU.mult),
                 reads=[r["Gb"], r["triu"]], writes=[r["X"]])
            for g0 in (0, 2):
                zipper([grp(c, g0, 0, sc), grp(c, g0 + 1, 1, sc)])
        P.barrier()
        P.flush()


def phase_gdn_g2(P, nc, projT, S, C, yT, normw_dram, T, zrow, kc0=16):
    NCH = T // 128
    G = 4
    H = 16
    with contextlib.ExitStack() as st:
        def sb(name, shape, dt):
            return st.enter_context(nc.sbuf_tensor(_uid() + "g2_" + name, shape, dt))

        def pst(name, shape, dt):
            return st.enter_context(nc.psum_tensor(_uid() + "g2_" + name, shape, dt))
        r = defaultdict(Res)
        identf = sb("identf", [128, 128], F32); identb = sb("identb", [128, 128], BF16)
        nrm = sb("nrm", [128, 1], F32); epsc = sb("epsc", [128, 1], F32); mh = sb("mh", [128, G], F32)
        wkT = [sb(f"wkT{i}", [128, H, 128], BF16) for i in range(2)]
        qdT = [sb(f"qdT{i}", [128, H, 128], BF16) for i in range(2)]
        atT = [sb(f"atT{i}", [128, H, 128], BF16) for i in range(2)]
        ktl = [sb(f"ktl{i}", [128, H, 128], BF16) for i in range(2)]
        uu = [sb(f"uu{i}", [128, H, 128], F32) for i in range(2)]
        zt = [sb(f"zt{i}", [128, H, 128], F32) for i in range(2)]
        els = [sb(f"els{i}", [128, H], F32) for i in range(2)]
        Sf = sb("Sf", [128, H, 128], F32); Sb = sb("Sb", [128, H, 128], BF16)
        TS = []
        for q in range(2):
            d = {"Stmp": sb(f"Stmp{q}", [128, G, 128], F32), "vnew": sb(f"vnew{q}", [128, G, 128], BF16),
                 "o_sb": sb(f"o_sb{q}", [128, G, 128], F32), "osq": sb(f"osq{q}", [128, G, 128], F32),
                 "s2": sb(f"s2{q}", [128, G], F32), "sd": sb(f"sd{q}", [128, G], F32), "rstd": sb(f"rstd{q}", [128, G], F32),
                 "on": sb(f"on{q}", [128, G, 128], BF16), "sz": sb(f"sz{q}", [128, G, 128], F32), "yg": sb(f"yg{q}", [128, G, 128], F32)}
            TS.append(d)
        yfin = [sb(f"yfin{i}", [128, G, 128], BF16) for i in range(2)]
        ws_ps = [pst(f"ws_ps{i}", [128, G, 128], F32) for i in range(2)]
        o_ps = [pst(f"o_ps{i}", [128, G, 128], F32) for i in range(2)]
        kv_ps = [pst(f"kv_ps{i}", [128, G, 128], F32) for i in range(2)]
        tr_ps2 = [pst(f"tr_ps{q}", [128, 2, G, 128], BF16) for q in range(2)]
        P.dma("sp", identf[:, :], C["ident"][:, :], writes=[r["identf"]])
        P.dma("sp", nrm[:, :], normw_dram[:, :], writes=[r["nrm"]])
        P.op("pool", lambda e: e.tensor_copy(identb[:, :], identf[:, :]), reads=[r["identf"]], writes=[r["identb"]])
        P.op("pool", lambda e: e.memset(epsc[:, :], 1e-6), writes=[r["eps"]])
        P.op("pool", lambda e: e.memset(mh[:, :], -0.5), writes=[r["mh"]])
        P.op("pool", lambda e: e.memset(Sf[:, :, :].rearrange("p h e -> p (h e)"), 0.0), writes=[r["Sf"]])
        P.op("pool", lambda e: e.memset(Sb[:, :, :].rearrange("p h e -> p (h e)"), 0.0), writes=[r["Sb"]])
        def grp(c, g, b, s):
            t0 = c * 128
            T_ = TS[b]
            Stmp, vnew, o_sb, osq, s2, sd, rstd, on, sz, yg = [T_[n] for n in ("Stmp", "vnew", "o_sb", "osq", "s2", "sd", "rstd", "on", "sz", "yg")]
            tr_ps = tr_ps2[b]
            hs = slice(g * G, (g + 1) * G)
            rS = r[f"Sf{g}"]; rSb = r[f"Sb{g}"]
            for h in range(G):
                hh = g * G + h
                P.op("pe", lambda e, h=h, hh=hh, s=s, b=b: e.matmul(ws_ps[b][:, h, :], wkT[s][:, hh, :], Sb[:, hh, :], start=True, stop=True),
                     reads=[r[f"wkT{s}"], rSb, r["Sb"]], writes=[r[f"ws_ps{b}"]])
            yield
            P.op("dve", lambda e, s=s, b=b, hs=hs: e.tensor_tensor(out=vnew[:, :, :], in0=uu[s][:, hs, :], in1=ws_ps[b][:, :, :], op=ALU.subtract),
                 reads=[r[f"uu{s}"], r[f"ws_ps{b}"]], writes=[r["vnew" + str(b)]])
            yield
            for h in range(G):
                hh = g * G + h
                P.op("pe", lambda e, h=h, hh=hh, s=s, b=b: e.matmul(o_ps[b][:, h, :], qdT[s][:, hh, :], Sb[:, hh, :], start=True, stop=False),
                     reads=[r[f"qdT{s}"], rSb, r["Sb"]], writes=[r[f"o_ps{b}"]])
                P.op("pe", lambda e, h=h, hh=hh, s=s, b=b: e.matmul(o_ps[b][:, h, :], atT[s][:, hh, :], vnew[:, h, :], start=False, stop=True),
                     reads=[r[f"atT{s}"], r["vnew" + str(b)]], writes=[r[f"o_ps{b}"]])
            for h in range(G):
                hh = g * G + h
                P.op("pe", lambda e, h=h, hh=hh, s=s, b=b: e.matmul(kv_ps[b][:, h, :], ktl[s][:, hh, :], vnew[:, h, :], start=True, stop=True),
                     reads=[r[f"ktl{s}"], r["vnew" + str(b)]], writes=[r[f"kv_ps{b}"]])
            yield
            P.op("dve", lambda e, s=s, hs=hs, g=g: e.tensor_tensor(out=Stmp[:, :, :], in0=Sf[:, hs, :],
                                                                  in1=els[s][:, g * G:(g + 1) * G].unsqueeze(2).to_broadcast([128, G, 128]),
                                                                  op=ALU.mult),
                 reads=[rS, r["Sf"], r[f"els{s}"]], writes=[r["Stmp" + str(b)]])
            P.op("dve", lambda e, hs=hs, b=b: e.tensor_tensor(out=Sf[:, hs, :], in0=Stmp[:, :, :], in1=kv_ps[b][:, :, :], op=ALU.add),
                 reads=[r["Stmp" + str(b)], r[f"kv_ps{b}"]], writes=[rS])
            P.op("act", lambda e, hs=hs: e.copy(Sb[:, hs, :], Sf[:, hs, :]), reads=[rS], writes=[rSb])
            yield
            P.op("act", lambda e, b=b: e.copy(o_sb[:, :, :], o_ps[b][:, :, :]), reads=[r[f"o_ps{b}"]], writes=[r["o_sb" + str(b)]])
            P.op("pool", lambda e: e.tensor_tensor(out=osq[:, :, :], in0=o_sb[:, :, :], in1=o_sb[:, :, :], op=ALU.mult),
                 reads=[r["o_sb" + str(b)]], writes=[r["osq" + str(b)]])
            yield
            P.op("dve", lambda e: e.tensor_reduce(out=s2[:, :], in_=osq[:, :, :], axis=AX.X, op=ALU.add), reads=[r["osq" + str(b)]], writes=[r["s2" + str(b)]])
            P.op("dve", lambda e: e.tensor_scalar(out=sd[:, :], in0=s2[:, :], scalar1=1.0 / 128, scalar2=1e-6, op0=ALU.mult, op1=ALU.add),
                 reads=[r["s2" + str(b)]], writes=[r["sd" + str(b)]])
            yield
            P.op("pool", lambda e: e.tensor_tensor(out=rstd[:, :], in0=sd[:, :], in1=mh[:, :], op=ALU.pow),
                 reads=[r["sd" + str(b)], r["mh"]], writes=[r["rstd" + str(b)]])
            P.op("dve", lambda e: e.tensor_tensor(out=on[:, :, :], in0=o_sb[:, :, :],
                                                  in1=rstd[:, :].unsqueeze(2).to_broadcast([128, G, 128]), op=ALU.mult),
                 reads=[r["o_sb" + str(b)], r["rstd" + str(b)]], writes=[r["on" + str(b)]])
            yield
            for h in range(G):
                P.op("pe", lambda e, h=h: e.transpose(tr_ps[:, 0, h, :], on[:, h, :], identb[:, :]),
                     reads=[r["on" + str(b)], r["identb"]], writes=[r["tr_ps" + str(b)]])
            P.op("act", lambda e, s=s, hs=hs: e.activation(out=sz[:, :, :], in_=zt[s][:, hs, :], func=AF.Silu),
                 reads=[r[f"zt{s}"]], writes=[r["sz" + str(b)]])
            yield
            P.op("dve", lambda e: e.scalar_tensor_tensor(out=yg[:, :, :], in0=tr_ps[:, 0, :, :], scalar=nrm[:, 0:1], in1=sz[:, :, :],
                                                         op0=ALU.mult, op1=ALU.mult),
                 reads=[r["tr_ps" + str(b)], r["nrm"], r["sz" + str(b)]], writes=[r["yg" + str(b)]])
            P.op("pool", lambda e, b=b: e.tensor_copy(yfin[b][:, :, :], yg[:, :, :]), reads=[r["yg" + str(b)]], writes=[r[f"yfin{b}"]])
            P.dma("pool", yT[kc0 + g * G:kc0 + (g + 1) * G, :, t0:t0 + 128].rearrange("k p t -> p k t"), yfin[b][:, :, :],
                  reads=[r[f"yfin{b}"]], writes=[r["y_out"]])

        it = 0
        for c in range(NCH):
            t0 = c * 128
            s = c % 2
            for g in range(4):
                r0 = g * G * 128
                hs = slice(g * G, (g + 1) * G)
                P.dma("sp", wkT[s][:, hs, :], S["wkT_d"][r0:r0 + G * 128, t0:t0 + 128].rearrange("(h p) t -> p h t", p=128),
                      writes=[r[f"wkT{s}"]])
                P.dma("sp", qdT[s][:, hs, :], S["gqdT"][r0:r0 + G * 128, t0:t0 + 128].rearrange("(h p) t -> p h t", p=128),
                      writes=[r[f"qdT{s}"]])
                P.dma("sp", atT[s][:, hs, :], S["attnT_d"][c, :, g * G:(g + 1) * G, :],
                      writes=[r[f"atT{s}"]])
                P.dma("sp", ktl[s][:, hs, :].rearrange("p h d -> p (h d)"), S["ktl_d"][t0:t0 + 128, r0:r0 + G * 128],
                      writes=[r[f"ktl{s}"]])
                P.dma("sp", uu[s][:, hs, :].rearrange("p h d -> p (h d)"), S["u_d"][t0:t0 + 128, r0:r0 + G * 128],
                      writes=[r[f"uu{s}"]])
                P.dma("sp", zt[s][:, hs, :], projT[zrow + r0:zrow + r0 + G * 128, t0:t0 + 128].rearrange("(h p) t -> p h t", p=128),
                      writes=[r[f"zt{s}"]])
            P.dma("sp", els[s][:, :], S["els_d"][c, :, :], writes=[r[f"els{s}"]])
            for g0 in (0, 2):
                zipper([grp(c, g0, 0, s), grp(c, g0 + 1, 1, s)])
        P.barrier()
        P.flush()


def rwkv_consts():
    c = {}
    bo = np.zeros((128, 128), np.float32)
    bo[:64, :64] = 1.0
    bo[64:, 64:] = 1.0
    c["blockones"] = bo
    return c


def phase_rwkv_pre(P, nc, projT, S, C, prm, T, row0):
    NB = T // 512
    with contextlib.ExitStack() as st:
        def sb(name, shape, dt):
            return st.enter_context(nc.sbuf_tensor(_uid() + "wp_" + name, shape, dt))

        def pst(name, shape, dt):
            return st.enter_context(nc.psum_tensor(_uid() + "wp_" + name, shape, dt))
        r = defaultdict(Res)
        mu = sb("mu", [128, 33], F32); omm = sb("omm", [128, 33], F32)
        w0 = sb("w0", [128, 8], F32); nw0 = sb("nw0", [128, 8], F32); a0 = sb("a0", [128, 8], F32)
        kk_ = sb("kk_", [128, 8], F32); ka = sb("ka", [128, 8], F32); omka = sb("omka", [128, 8], F32); rk = sb("rk", [128, 8], F32)
        lw2 = sb("lw2", [128, 1024], F32)
        bones = sb("bones", [128, 128], F32)
        cmask = sb("cmask", [128, 512], F32)
        onec = sb("onec", [128, 1], F32); negh = sb("negh", [128, 1], F32); epsc = sb("epsc", [128, 1], F32)
        ul = sb("ul", [128, 513], F32); ml = sb("ml", [128, 512], F32); th = sb("th", [128, 512], F32)
        TS = []
        for q in range(2):
            d = {}
            for j in range(4):
                d[f"u{j}"] = sb(f"u{j}_{q}", [128, 513], F32); d[f"mx{j}"] = sb(f"mx{j}_{q}", [128, 512], F32)
            for n in ("tmp", "e1", "spt", "e2", "cw", "ecw", "cwm", "ecwm", "einv", "av", "kk0", "sq", "sd", "rn", "kkn", "fac", "k2", "kka", "prod"):
                d[n] = sb(f"{n}_{q}", [128, 512], F32)
            TS.append(d)
        tmp = sb("tmp_l", [128, 512], F32)
        obf = {n: [sb(f"o_{n}{i}", [128, 512], BF16) for i in range(2)] for n in ("rt", "kkt", "kh", "kka", "v")}
        of32 = {n: [sb(f"o_{n}{i}", [128, 512], F32) for i in range(2)] for n in ("bon", "sz")}
        pcs = [sb(f"pcs{i}", [128, 4], F32) for i in range(2)]
        for q in range(2):
            for n in ("wl_ps", "a_ps", "ss_ps", "sb_ps"):
                TS[q][n] = pst(f"{n}_{q}", [128, 512], F32)
        for nm, t_, src in (("mu", mu, "muT"), ("w0", w0, "w0T"), ("a0", a0, "a0T"), ("kk_", kk_, "kkT"), ("ka", ka, "kaT"),
                            ("rk", rk, "rkT"), ("lw2", lw2, "lw2")):
            P.dma("sp", t_[:, :], prm[src][:, :], writes=[r[nm]])
        P.dma("sp", bones[:, :], C["blockones"][:, :], writes=[r["bones"]])
        P.dma("sp", cmask[:, :], C["cmask128"][:, :], writes=[r["cmask"]])
        P.op("pool", lambda e: e.memset(onec[:, :], 1.0), writes=[r["onec"]])
        P.op("pool", lambda e: e.memset(negh[:, :], -0.5), writes=[r["negh"]])
        P.op("pool", lambda e: e.memset(epsc[:, :], 1e-6), writes=[r["eps"]])
        P.op("dve", lambda e: e.tensor_scalar(out=omm[:, :], in0=mu[:, :], scalar1=-1.0, scalar2=1.0, op0=ALU.mult, op1=ALU.add),
             reads=[r["mu"]], writes=[r["omm"]])
        P.op("dve", lambda e: e.tensor_scalar(out=omka[:, :], in0=ka[:, :], scalar1=-1.0, scalar2=1.0, op0=ALU.mult, op1=ALU.add),
             reads=[r["ka"]], writes=[r["omka"]])
        P.op("dve", lambda e: e.tensor_scalar(out=nw0[:, :], in0=w0[:, :], scalar1=-1.0, scalar2=None, op0=ALU.mult),
             reads=[r["w0"]], writes=[r["nw0"]])

        def load_mix(ut, rut, mt, rmt, row, ti, tb, tmp=tmp, rtmp=None):
            rtmp = rtmp if rtmp is not None else r["tmp_l"]
            t0 = tb * 512
            if tb == 0:
                P.op("pool", lambda e: e.memset(ut[:, 0:1], 0.0), writes=[rut])
                P.dma("sp", ut[:, 1:513], projT[row:row + 128, 0:512], writes=[rut])
            else:
                P.dma("sp", ut[:, :], projT[row:row + 128, t0 - 1:t0 + 512], writes=[rut])
            P.op("dve", lambda e: e.tensor_scalar(out=tmp[:, :], in0=ut[:, 1:513], scalar1=omm[:, ti:ti + 1], scalar2=None, op0=ALU.mult),
                 reads=[rut, r["omm"]], writes=[rtmp])
            P.op("dve", lambda e: e.scalar_tensor_tensor(out=mt[:, :], in0=ut[:, 0:512], scalar=mu[:, ti:ti + 1], in1=tmp[:, :],
                                                         op0=ALU.mult, op1=ALU.add),
                 reads=[rut, r["mu"], rtmp], writes=[rmt])
        def do_ct(ct, tb, s):
            t0 = tb * 512
            T_ = TS[s]
            rq = lambda n: r[f"{n}_{s}"]
            (e1, spt, e2, cw, ecw, cwm, ecwm, einv, av, kk0, sq, sd, rn, kkn, fac, k2, kka, prod) = [T_[n] for n in (
                "e1", "spt", "e2", "cw", "ecw", "cwm", "ecwm", "einv", "av", "kk0", "sq", "sd", "rn", "kkn", "fac", "k2", "kka", "prod")]
            wl_ps, a_ps, ss_ps, sb_ps = T_["wl_ps"], T_["a_ps"], T_["ss_ps"], T_["sb_ps"]
            for j in range(4):
                load_mix(T_[f"u{j}"], rq(f"u{j}"), T_[f"mx{j}"], rq(f"mx{j}"), row0 + j * 1024 + ct * 128, j * 8 + ct, tb, T_["tmp"], rq("tmp"))
            rm, km, vm, zm = T_['mx0'], T_['mx1'], T_['mx2'], T_['mx3']
            yield
            P.op("pe", lambda e, ct=ct: e.matmul(wl_ps[:, :], lw2[0:64, ct * 128:(ct + 1) * 128], th[0:64, :], start=True, stop=True),
                 reads=[r["lw2"], r["th"]], writes=[rq("wl_ps")])
            P.op("pe", lambda e, ct=ct: e.matmul(a_ps[:, :], lw2[64:128, ct * 128:(ct + 1) * 128], ml[64:128, :], start=True, stop=True),
                 reads=[r["lw2"], r["ml"]], writes=[rq("a_ps")])
            P.op("act", lambda e, ct=ct: e.activation(out=e1[:, :], in_=wl_ps[:, :], func=AF.Exp, bias=nw0[:, ct:ct + 1], scale=-1.0),
                 reads=[rq("wl_ps"), r["nw0"]], writes=[rq("e1")])
            P.op("act", lambda e: e.activation(out=spt[:, :], in_=e1[:, :], func=AF.Ln, bias=onec[:, 0:1], scale=1.0),
                 reads=[rq("e1"), r["onec"]], writes=[rq("spt")])
            P.op("act", lambda e: e.activation(out=e2[:, :], in_=spt[:, :], func=AF.Exp, bias=negh[:, 0:1], scale=-1.0),
                 reads=[rq("spt"), r["negh"]], writes=[rq("e2")])
            P.op("dve", lambda e: e.tensor_tensor_scan(out=cw[:, :], data0=cmask[:, :], data1=e2[:, :], initial=0.0,
                                                       op0=ALU.mult, op1=ALU.subtract),
                 reads=[r["cmask"], rq("e2")], writes=[rq("cw")])
            P.op("act", lambda e: e.activation(out=ecw[:, :], in_=cw[:, :], func=AF.Exp), reads=[rq("cw")], writes=[rq("ecw")])
            P.op("pool", lambda e: e.tensor_tensor(out=cwm[:, :], in0=cw[:, :], in1=e2[:, :], op=ALU.add),
                 reads=[rq("cw"), rq("e2")], writes=[rq("cwm")])
            P.op("act", lambda e: e.activation(out=ecwm[:, :], in_=cwm[:, :], func=AF.Exp), reads=[rq("cwm")], writes=[rq("ecwm")])
            P.op("act", lambda e: e.activation(out=einv[:, :], in_=cw[:, :], func=AF.Exp, scale=-1.0), reads=[rq("cw")], writes=[rq("einv")])
            yield
            P.op("act", lambda e, ct=ct: e.activation(out=av[:, :], in_=a_ps[:, :], func=AF.Sigmoid, bias=a0[:, ct:ct + 1], scale=1.0),
                 reads=[rq("a_ps"), r["a0"]], writes=[rq("av")])
            yield
            P.op("dve", lambda e, ct=ct: e.tensor_scalar(out=kk0[:, :], in0=km[:, :], scalar1=kk_[:, ct:ct + 1], scalar2=None, op0=ALU.mult),
                 reads=[rq("mx1"), r["kk_"]], writes=[rq("kk0")])
            P.op("pool", lambda e: e.tensor_tensor(out=sq[:, :], in0=kk0[:, :], in1=kk0[:, :], op=ALU.mult), reads=[rq("kk0")], writes=[rq("sq")])
            P.op("pe", lambda e: e.matmul(ss_ps[:, :], bones[:, :], sq[:, :], start=True, stop=True),
                 reads=[r["bones"], rq("sq")], writes=[rq("ss_ps")])
            yield
            P.op("act", lambda e: e.activation(out=sd[:, :], in_=ss_ps[:, :], func=AF.Sqrt, bias=epsc[:, 0:1], scale=1.0),
                 reads=[rq("ss_ps"), r["eps"]], writes=[rq("sd")])
            P.op("dve", lambda e: e.reciprocal(rn[:, :], sd[:, :]), reads=[rq("sd")], writes=[rq("rn")])
            P.op("dve", lambda e: e.tensor_tensor(out=kkn[:, :], in0=kk0[:, :], in1=rn[:, :], op=ALU.mult),
                 reads=[rq("kk0"), rq("rn")], writes=[rq("kkn")])
            P.op("dve", lambda e, ct=ct: e.tensor_scalar(out=fac[:, :], in0=av[:, :], scalar1=ka[:, ct:ct + 1], scalar2=omka[:, ct:ct + 1],
                                                         op0=ALU.mult, op1=ALU.add),
                 reads=[rq("av"), r["ka"], r["omka"]], writes=[rq("fac")])
            P.op("dve", lambda e: e.tensor_tensor(out=k2[:, :], in0=km[:, :], in1=fac[:, :], op=ALU.mult),
                 reads=[rq("mx1"), rq("fac")], writes=[rq("k2")])
            P.op("pool", lambda e: e.tensor_tensor(out=kka[:, :], in0=kkn[:, :], in1=av[:, :], op=ALU.mult),
                 reads=[rq("kkn"), rq("av")], writes=[rq("kka")])
            yield
            P.op("dve", lambda e, s=s: e.tensor_tensor(out=obf["rt"][s][:, :], in0=rm[:, :], in1=ecw[:, :], op=ALU.mult),
                 reads=[rq("mx0"), rq("ecw")], writes=[r[f"o_rt{s}"]])
            P.op("dve", lambda e, s=s: e.tensor_tensor(out=obf["kkt"][s][:, :], in0=kkn[:, :], in1=ecwm[:, :], op=ALU.mult),
                 reads=[rq("kkn"), rq("ecwm")], writes=[r[f"o_kkt{s}"]])
            P.op("dve", lambda e, s=s: e.tensor_tensor(out=obf["kh"][s][:, :], in0=k2[:, :], in1=einv[:, :], op=ALU.mult),
                 reads=[rq("k2"), rq("einv")], writes=[r[f"o_kh{s}"]])
            P.op("pool", lambda e, s=s: e.tensor_tensor(out=obf["kka"][s][:, :], in0=kka[:, :], in1=einv[:, :], op=ALU.mult),
                 reads=[rq("kka"), rq("einv")], writes=[r[f"o_kka{s}"]])
            P.op("pool", lambda e, s=s: e.tensor_copy(obf["v"][s][:, :], vm[:, :]), reads=[rq("mx2")], writes=[r[f"o_v{s}"]])
            P.op("act", lambda e, s=s: e.activation(out=of32["sz"][s][:, :], in_=zm[:, :], func=AF.Silu), reads=[rq("mx3")], writes=[r[f"o_sz{s}"]])
            yield
            P.op("pool", lambda e: e.tensor_tensor(out=prod[:, :], in0=rm[:, :], in1=k2[:, :], op=ALU.mult),
                 reads=[rq("mx0"), rq("k2")], writes=[rq("prod")])
            P.op("dve", lambda e, ct=ct: e.tensor_scalar(out=prod[:, :], in0=prod[:, :], scalar1=rk[:, ct:ct + 1], scalar2=None, op0=ALU.mult),
                 reads=[rq("prod"), r["rk"]], writes=[rq("prod")])
            P.op("pe", lambda e: e.matmul(sb_ps[:, :], bones[:, :], prod[:, :], start=True, stop=True),
                 reads=[r["bones"], rq("prod")], writes=[rq("sb_ps")])
            P.op("dve", lambda e, s=s: e.tensor_tensor(out=of32["bon"][s][:, :], in0=vm[:, :], in1=sb_ps[:, :], op=ALU.mult),
                 reads=[rq("mx2"), rq("sb_ps")], writes=[r[f"o_bon{s}"]])
            P.op("pool", lambda e, s=s: e.tensor_copy(pcs[s][:, :], ecw[:, 127:512:128]), reads=[rq("ecw")], writes=[r[f"pcs{s}"]])
            rows_ = slice(ct * 128, (ct + 1) * 128)
            for n, dst in (("rt", "rtT"), ("kkt", "kktT"), ("kh", "khT"), ("kka", "kkaT"), ("v", "rvT")):
                P.dma("pool", S[dst][rows_, t0:t0 + 512], obf[n][s][:, :], reads=[r[f"o_{n}{s}"]], writes=[r[dst]])
            for n, dst in (("bon", "bonT"), ("sz", "szT")):
                P.dma("pool", S[dst][rows_, t0:t0 + 512], of32[n][s][:, :], reads=[r[f"o_{n}{s}"]], writes=[r[dst]])
            P.dma("pool", S["pc_d"][ct, :, tb * 4:(tb + 1) * 4], pcs[s][:, :], reads=[r[f"pcs{s}"]], writes=[r["pc_d"]])

        cnt = 0
        for tb in range(NB):
            t0 = tb * 512
            load_mix(ul, r["ul"], ml, r["ml"], row0 + 4096, 32, tb)
            P.op("act", lambda e: e.activation(out=th[0:64, :], in_=ml[0:64, :], func=AF.Tanh), reads=[r["ml"]], writes=[r["th"]])
            for ct in range(0, 8, 2):
                zipper([do_ct(ct, tb, 0), do_ct(ct + 1, tb, 1)])
        P.barrier()
        P.flush()


def phase_rwkv_r1(P, nc, S, C, T):
    NCH = T // 128
    G = 4
    with contextlib.ExitStack() as st:
        def sb(name, shape, dt):
            return st.enter_context(nc.sbuf_tensor(_uid() + "r1_" + name, shape, dt))

        def pst(name, shape, dt):
            return st.enter_context(nc.psum_tensor(_uid() + "r1_" + name, shape, dt))
        r = defaultdict(Res)
        mls = sb("mls", [128, 128], F32); mus = sb("mus", [128, 128], F32); mui = sb("mui", [128, 128], F32); nmui = sb("nmui", [128, 128], F32)
        identf = sb("identf", [128, 128], F32); identb = sb("identb", [128, 128], BF16)
        tl = {n: [sb(f"{n}{i}", [128, 2, 128], BF16) for i in range(2)] for n in ("rt", "kkt", "kh", "kka", "v")}
        tz = {n: [sb(f"z{n}{i}", [128, 2, 2, 128], BF16) for i in range(2)] for n in ("kkt", "kh", "kka")}
        L = sb("L", [128, G, 128], BF16); LT = sb("LT", [128, G, 128], BF16)
        om = {n: [sb(f"{n}{i}", [128, G, 128], BF16) for i in range(2)] for n in ("akv", "bkv", "nbab", "tinv")}
        tk = {n: [sb(f"tk_{n}{i}", [128, 2, 128], BF16) for i in range(2)] for n in ("v", "kh", "kka")}
        W = {"identb": identb,
             "nL": [sb(f"nL{i}", [128, G, 128], BF16) for i in range(2)],
             "nLT": [sb(f"nLT{i}", [128, G, 128], BF16) for i in range(2)],
             "Pk": [sb(f"Pk{i}", [128, G, 128], BF16) for i in range(2)],
             "pa": pst("pa", [128, G, 128], F32), "pb": pst("pb", [128, G, 128], F32), "pp": pst("pp", [128, G, 128], F32)}
        s_ps = [pst(f"s_ps{i}", [128, G, 128], F32) for i in range(3)]
        tr_ps = pst("tr_ps", [128, 8, 128], BF16)
        for nm, t_, src in (("mls", mls, "mask_ls"), ("mus", mus, "mask_us"), ("mui", mui, "mask_ui"), ("identf", identf, "ident")):
            P.dma("sp", t_[:, :], C[src][:, :], writes=[r[nm]])
        P.op("pool", lambda e: e.tensor_copy(identb[:, :], identf[:, :]), reads=[r["identf"]], writes=[r["identb"]])
        P.op("dve", lambda e: e.tensor_scalar(out=nmui[:, :], in0=mui[:, :], scalar1=-1.0, scalar2=None, op0=ALU.mult),
             reads=[r["mui"]], writes=[r["nmui"]])

        def bcm(m):
            return m[:, :].unsqueeze(1).to_broadcast([128, G, 128])
        it = 0
        srcs = {"rt": "rtT", "kkt": "kktT", "kh": "khT", "kka": "kkaT", "v": "rvT"}
        for n in tz:
            for i in range(2):
                P.op("pool", lambda e, n=n, i=i: e.memset(tz[n][i][:, :, :, :].rearrange("p a q t -> p (a q t)"), 0.0), writes=[r[f"z{n}{i}"]])
        for c in range(NCH):
            t0 = c * 128
            for g in range(4):
                s = it % 2
                it += 1
                for n in tl:
                    P.dma("sp", tl[n][s][:, :, :], S[srcs[n]][g * 256:(g + 1) * 256, t0:t0 + 128].rearrange("(q p) t -> p q t", p=128),
                          writes=[r[f"{n}{s}"]])

                for n in tz:
                    srcv = S[srcs[n]][g * 256:(g + 1) * 256, t0:t0 + 128].rearrange("(q p) t -> p q t", p=128)
                    P.dma("sp", tz[n][s][0:64, 0, :, :], srcv[0:64], writes=[r[f"z{n}{s}"]])
                    P.dma("sp", tz[n][s][64:128, 1, :, :], srcv[64:128], writes=[r[f"z{n}{s}"]])

                def hz(n, h):
                    return tz[n][s][:, h % 2, h // 2, :]

                def hv(n, h):
                    return tl[n][s][:, h // 2, :]
                for h in range(G):
                    P.op("pe", lambda e, h=h, a=hz("kkt", h), b=hv("kka", h): e.matmul(s_ps[0][:, h, :], a, b, start=True, stop=True),
                         reads=[r[f"zkkt{s}"], r[f"kka{s}"]], writes=[r["s_ps0"]])
                for h in range(G):
                    P.op("pe", lambda e, h=h, a=hz("kka", h), b=hv("kkt", h): e.matmul(s_ps[1][:, h, :], a, b, start=True, stop=True),
                         reads=[r[f"zkka{s}"], r[f"kkt{s}"]], writes=[r["s_ps1"]])
                P.op("dve", lambda e: e.tensor_tensor(out=L[:, :, :], in0=s_ps[0][:, :, :], in1=bcm(mls), op=ALU.mult),
                     reads=[r["s_ps0"], r["mls"]], writes=[r["L"]])
                P.op("dve", lambda e: e.tensor_tensor(out=LT[:, :, :], in0=s_ps[1][:, :, :], in1=bcm(mus), op=ALU.mult),
                     reads=[r["s_ps1"], r["mus"]], writes=[r["LT"]])
                for h in range(G):
                    P.op("pe", lambda e, h=h, a=hz("kh", h), b=hv("kkt", h): e.matmul(s_ps[2][:, h, :], a, b, start=True, stop=True),
                         reads=[r[f"zkh{s}"], r[f"kkt{s}"]], writes=[r["s_ps2"]])
                P.op("dve", lambda e, s=s: e.tensor_tensor(out=om["akv"][s][:, :, :], in0=s_ps[2][:, :, :], in1=bcm(mus), op=ALU.mult),
                     reads=[r["s_ps2"], r["mus"]], writes=[r[f"akv{s}"]])
                for h in range(G):
                    P.op("pe", lambda e, h=h, a=hz("kh", h), b=hv("rt", h): e.matmul(s_ps[0][:, h, :], a, b, start=True, stop=True),
                         reads=[r[f"zkh{s}"], r[f"rt{s}"]], writes=[r["s_ps0"]])
                P.op("dve", lambda e, s=s: e.tensor_tensor(out=om["bkv"][s][:, :, :], in0=s_ps[0][:, :, :], in1=bcm(mui), op=ALU.mult),
                     reads=[r["s_ps0"], r["mui"]], writes=[r[f"bkv{s}"]])
                for h in range(G):
                    P.op("pe", lambda e, h=h, a=hz("kka", h), b=hv("rt", h): e.matmul(s_ps[1][:, h, :], a, b, start=True, stop=True),
                         reads=[r[f"zkka{s}"], r[f"rt{s}"]], writes=[r["s_ps1"]])
                P.op("dve", lambda e, s=s: e.tensor_tensor(out=om["nbab"][s][:, :, :], in0=s_ps[1][:, :, :], in1=bcm(mui), op=ALU.mult),
                     reads=[r["s_ps1"], r["mui"]], writes=[r[f"nbab{s}"]])
                Pt, rPt = neumann(P, nc, L, LT, r["L"], r["LT"], W, r)
                P.op("pool", lambda e, s=s, Pt=Pt: e.tensor_copy(om["tinv"][s][:, :, :], Pt[:, :, :]), reads=[rPt], writes=[r[f"tinv{s}"]])
                for n, dst in (("akv", "akvT_d"), ("bkv", "bkvT_d"), ("nbab", "nbabT_d"), ("tinv", "tinvT_d")):
                    P.dma("pool", S[dst][c, :, g * G:(g + 1) * G, :], om[n][s][:, :, :],
                          reads=[r[f"{n}{s}"]], writes=[r[dst]])
                for qi, n in enumerate(("v", "kh", "kka")):
                    for q in range(2):
                        P.op("pe", lambda e, qi=qi, q=q, n=n, s=s: e.transpose(tr_ps[:, qi * 2 + q, :], tl[n][s][:, q, :], identb[:, :]),
                             reads=[r[f"{n}{s}"], r["identb"]], writes=[r["tr_ps"]])
                for qi, (n, dst) in enumerate((("v", "vtk_d"), ("kh", "khtk_d"), ("kka", "kkatk_d"))):
                    P.op("act", lambda e, qi=qi, n=n, s=s: e.copy(tk[n][s][:, :, :], tr_ps[:, qi * 2:qi * 2 + 2, :]),
                         reads=[r["tr_ps"]], writes=[r[f"tk_{n}{s}"]])
                    P.dma("pool", S[dst][t0:t0 + 128, g * 256:(g + 1) * 256], tk[n][s][:, :, :].rearrange("p q d -> p (q d)"),
                          reads=[r[f"tk_{n}{s}"]], writes=[r[dst]])
        P.barrier()
        P.flush()


def phase_rwkv_r2(P, nc, S, C, yT, prm, T, kc0=8, eps=64e-5):
    NCH = T // 128
    H = 16
    with contextlib.ExitStack() as st:
        def sb(name, shape, dt):
            return st.enter_context(nc.sbuf_tensor(_uid() + "r2_" + name, shape, dt))

        def pst(name, shape, dt):
            return st.enter_context(nc.psum_tensor(_uid() + "r2_" + name, shape, dt))
        r = defaultdict(Res)
        identf = sb("identf", [128, 128], F32); identb = sb("identb", [128, 128], BF16)
        lnw = sb("lnw", [128, 8], F32); lnb = sb("lnb", [128, 8], F32); epsc = sb("epsc", [128, 1], F32); mh = sb("mh", [128, 8], F32)
        pc = sb("pc", [128, 8, NCH], F32)
        kkt = [sb(f"kkt{i}", [128, 2, 8, 128], BF16) for i in range(2)]
        rt = [sb(f"rt{i}", [128, 2, 8, 128], BF16) for i in range(2)]
        tkz = {n: [sb(f"tkz_{n}{i}", [128, 2, 8, 128], BF16) for i in range(2)] for n in ("kh", "kka")}
        mm = {n: [sb(f"{n}{i}", [128, H, 128], BF16) for i in range(2)] for n in ("akv", "bkv", "nbab", "tinv")}
        tk = {n: [sb(f"tk_{n}{i}", [128, 1024], BF16) for i in range(2)] for n in ("v",)}
        bon = [sb(f"bon{i}", [128, 8, 128], F32) for i in range(2)]
        szt = [sb(f"szt{i}", [128, 8, 128], F32) for i in range(2)]
        Tf = sb("Tf", [128, 8, 64], F32); Tb = sb("Tb", [128, 8, 64], BF16)
        yfin = [sb(f"yfin{i}", [128, 4, 128], BF16) for i in range(2)]
        TS = []
        for q in range(2):
            d = {"Ttmp": sb(f"Ttmp{q}", [128, 4, 64], F32), "rhs0": sb(f"rhs0{q}", [128, 8, 64], BF16), "nU": sb(f"nU{q}", [128, 8, 64], BF16),
                 "y_sb": sb(f"y_sb{q}", [128, 8, 64], F32), "ysq": sb(f"ysq{q}", [128, 8, 64], F32),
                 "yc": sb(f"yc{q}", [128, 8, 64], F32), "yn": sb(f"yn{q}", [128, 8, 64], BF16),
                 "t1": sb(f"t1{q}", [128, 4, 128], F32), "t2": sb(f"t2{q}", [128, 4, 128], F32)}
            for n in ("s1", "s2", "mean", "var", "sd", "rstd"):
                d[n] = sb(f"{n}{q}", [128, 8], F32)
            d["r0_ps"] = pst(f"r0_ps{q}", [128, 8, 64], F32)
            d["u_ps"] = d["r0_ps"]
            d["y_ps"] = pst(f"y_ps{q}", [128, 8, 64], F32)
            d["st_ps"] = pst(f"st_ps{q}", [128, 512], F32)[:, 0:256].rearrange("p (q e) -> p q e", e=64)
            d["tr_ps"] = pst(f"tr_ps{q}", [128, 1024], BF16)[:, 0:512].rearrange("p (q t) -> p q t", t=128)
            TS.append(d)
        P.dma("sp", identf[:, :], C["ident"][:, :], writes=[r["identf"]])
        P.dma("sp", lnw[:, :], prm["lnwT"][:, :], writes=[r["lnw"]])
        P.dma("sp", lnb[:, :], prm["lnbT"][:, :], writes=[r["lnb"]])
        for q in range(8):
            P.dma("sp", pc[:, q, :], S["pc_d"][q, :, :], writes=[r["pc"]])
        P.op("pool", lambda e: e.tensor_copy(identb[:, :], identf[:, :]), reads=[r["identf"]], writes=[r["identb"]])
        P.op("pool", lambda e: e.memset(epsc[:, :], eps), writes=[r["eps"]])
        P.op("pool", lambda e: e.memset(mh[:, :], -0.5), writes=[r["mh"]])
        P.op("pool", lambda e: e.memset(Tf[:, :, :].rearrange("p q e -> p (q e)"), 0.0), writes=[r["Tf"]])
        P.op("pool", lambda e: e.memset(Tb[:, :, :].rearrange("p q e -> p (q e)"), 0.0), writes=[r["Tb"]])
        def grp(c, gi, b, s):
            t0 = c * 128
            T_ = TS[b]
            (Ttmp, rhs0, nU, y_sb, ysq, yc, yn, t1, t2, s1, s2, mean, var, sd, rstd, r0_ps, u_ps, y_ps, st_ps, tr_ps) = [T_[n] for n in (
                "Ttmp", "rhs0", "nU", "y_sb", "ysq", "yc", "yn", "t1", "t2", "s1", "s2", "mean", "var", "sd", "rstd", "r0_ps", "u_ps", "y_ps", "st_ps", "tr_ps")]
            rT = r[f"Tf{gi}"]; rTb = r[f"Tb{gi}"]
            for h in range(8):
                hh = gi * 8 + h
                p_ = hh // 2
                ba = (hh % 2) * 64
                P.op("pe", lambda e, h=h, hh=hh, p_=p_, ba=ba, s=s: e.matmul(r0_ps[:, h, :], kkt[s][:, ba // 64, p_, :], Tb[:, p_, :],
                                                                          start=True, stop=False),
                     reads=[r[f"kkt{s}"], rTb, r["Tb"]], writes=[r["ru_ps" + str(b)]])
                P.op("pe", lambda e, h=h, hh=hh, s=s: e.matmul(r0_ps[:, h, :], mm["akv"][s][:, hh, :], tk["v"][s][:, hh * 64:(hh + 1) * 64],
                                                              start=False, stop=True),
                     reads=[r[f"akv{s}"], r[f"tk_v{s}"]], writes=[r["ru_ps" + str(b)]])
            yield
            P.op("act", lambda e: e.copy(rhs0[:, :, :], r0_ps[:, :, :]), reads=[r["ru_ps" + str(b)]], writes=[r["rhs0" + str(b)]])
            yield
            for h in range(8):
                hh = gi * 8 + h
                P.op("pe", lambda e, h=h, hh=hh, s=s: e.matmul(u_ps[:, h, :], mm["tinv"][s][:, hh, :], rhs0[:, h, :], start=True, stop=True),
                     reads=[r[f"tinv{s}"], r["rhs0" + str(b)]], writes=[r["ru_ps" + str(b)]])
            yield
            P.op("dve", lambda e: e.tensor_scalar(out=nU[:, :, :], in0=u_ps[:, :, :], scalar1=-1.0, scalar2=None, op0=ALU.mult),
                 reads=[r["ru_ps" + str(b)]], writes=[r["nU" + str(b)]])
            for h in range(8):
                hh = gi * 8 + h
                p_ = hh // 2
                ba = (hh % 2) * 64
                P.op("pe", lambda e, h=h, p_=p_, ba=ba, s=s: e.matmul(y_ps[:, h, :], rt[s][:, ba // 64, p_, :], Tb[:, p_, :],
                                                                   start=True, stop=False),
                     reads=[r[f"rt{s}"], rTb, r["Tb"]], writes=[r["y_ps" + str(b)]])
                P.op("pe", lambda e, h=h, hh=hh, s=s: e.matmul(y_ps[:, h, :], mm["bkv"][s][:, hh, :], tk["v"][s][:, hh * 64:(hh + 1) * 64],
                                                              start=False, stop=False),
                     reads=[r[f"bkv{s}"], r[f"tk_v{s}"]], writes=[r["y_ps" + str(b)]])
                P.op("pe", lambda e, h=h, hh=hh, s=s: e.matmul(y_ps[:, h, :], mm["nbab"][s][:, hh, :], nU[:, h, :], start=False, stop=True),
                     reads=[r[f"nbab{s}"], r["nU" + str(b)]], writes=[r["y_ps" + str(b)]])
            yield
            for pl in range(4):
                q = gi * 4 + pl
                seq = [("kh", 0, "v"), ("kh", 1, "v"), ("kka", 0, "u"), ("kka", 1, "u")]
                for i_, (n, a, rk_) in enumerate(seq):
                    hh = 2 * q + a
                    if rk_ == "v":
                        P.op("pe", lambda e, pl=pl, q=q, a=a, n=n, hh=hh, s=s, i_=i_: e.matmul(st_ps[:, pl, :], tkz[n][s][:, a, q, :],
                                                                                           tk["v"][s][:, hh * 64:(hh + 1) * 64],
                                                                                           start=(i_ == 0), stop=(i_ == 3)),
                             reads=[r[f"tkz_{n}{s}"], r[f"tk_v{s}"]], writes=[r["st_ps" + str(b)]])
                    else:
                        P.op("pe", lambda e, pl=pl, q=q, a=a, n=n, hh=hh, s=s, i_=i_, gi=gi: e.matmul(st_ps[:, pl, :], tkz[n][s][:, a, q, :],
                                                                                                  nU[:, hh - gi * 8, :],
                                                                                                  start=(i_ == 0), stop=(i_ == 3)),
                             reads=[r[f"tkz_{n}{s}"], r["nU" + str(b)]], writes=[r["st_ps" + str(b)]])
            yield
            qs = slice(gi * 4, gi * 4 + 4)
            P.op("dve", lambda e, qs=qs: e.tensor_tensor(out=Ttmp[:, :, :], in0=st_ps[:, :, :], in1=Tf[:, qs, :], op=ALU.add),
                 reads=[r["st_ps" + str(b)], rT, r["Tf"]], writes=[r["Ttmp" + str(b)]])
            P.op("dve", lambda e, qs=qs, c=c: e.tensor_tensor(out=Tf[:, qs, :], in0=Ttmp[:, :, :],
                                                              in1=pc[:, qs, c:c + 1].to_broadcast([128, 4, 64]), op=ALU.mult),
                 reads=[r["Ttmp" + str(b)], r["pc"]], writes=[rT])
            P.op("act", lambda e, qs=qs: e.copy(Tb[:, qs, :], Tf[:, qs, :]), reads=[rT], writes=[rTb])
            yield
            P.op("act", lambda e: e.copy(y_sb[:, :, :], y_ps[:, :, :]), reads=[r["y_ps" + str(b)]], writes=[r["y_sb" + str(b)]])
            P.op("dve", lambda e: e.tensor_reduce(out=s1[:, :], in_=y_sb[:, :, :], axis=AX.X, op=ALU.add), reads=[r["y_sb" + str(b)]], writes=[r["s1" + str(b)]])
            P.op("pool", lambda e: e.tensor_tensor(out=ysq[:, :, :], in0=y_sb[:, :, :], in1=y_sb[:, :, :], op=ALU.mult),
                 reads=[r["y_sb" + str(b)]], writes=[r["ysq" + str(b)]])
            yield
            P.op("dve", lambda e: e.tensor_reduce(out=s2[:, :], in_=ysq[:, :, :], axis=AX.X, op=ALU.add), reads=[r["ysq" + str(b)]], writes=[r["s2" + str(b)]])
            P.op("dve", lambda e: e.tensor_scalar(out=mean[:, :], in0=s1[:, :], scalar1=1.0 / 64, scalar2=None, op0=ALU.mult),
                 reads=[r["s1" + str(b)]], writes=[r["mean" + str(b)]])
            P.op("dve", lambda e: e.tensor_tensor(out=var[:, :], in0=mean[:, :], in1=mean[:, :], op=ALU.mult), reads=[r["mean" + str(b)]], writes=[r["var" + str(b)]])
            P.op("dve", lambda e: e.scalar_tensor_tensor(out=var[:, :], in0=s2[:, :], scalar=1.0 / 64, in1=var[:, :],
                                                         op0=ALU.mult, op1=ALU.subtract),
                 reads=[r["s2" + str(b)], r["var" + str(b)]], writes=[r["var" + str(b)]])
            P.op("dve", lambda e: e.tensor_scalar(out=sd[:, :], in0=var[:, :], scalar1=1.0, scalar2=eps, op0=ALU.mult, op1=ALU.add),
                 reads=[r["var" + str(b)]], writes=[r["sd" + str(b)]])
            yield
            P.op("pool", lambda e: e.tensor_tensor(out=rstd[:, :], in0=sd[:, :], in1=mh[:, :], op=ALU.pow),
                 reads=[r["sd" + str(b)], r["mh"]], writes=[r["rstd" + str(b)]])
            P.op("dve", lambda e: e.tensor_tensor(out=yc[:, :, :], in0=y_sb[:, :, :], in1=mean[:, :].unsqueeze(2).to_broadcast([128, 8, 64]),
                                                  op=ALU.subtract),
                 reads=[r["y_sb" + str(b)], r["mean" + str(b)]], writes=[r["yc" + str(b)]])
            P.op("dve", lambda e: e.tensor_tensor(out=yn[:, :, :], in0=yc[:, :, :], in1=rstd[:, :].unsqueeze(2).to_broadcast([128, 8, 64]),
                                                  op=ALU.mult),
                 reads=[r["yc" + str(b)], r["rstd" + str(b)]], writes=[r["yn" + str(b)]])
            yield
            ynp = yn[:, :, :].rearrange("p (q a) e -> p q (a e)", a=2)
            for q in range(4):
                P.op("pe", lambda e, q=q: e.transpose(tr_ps[:, q, :], ynp[:, q, :], identb[:, :]),
                     reads=[r["yn" + str(b)], r["identb"]], writes=[r["tr_ps" + str(b)]])
            yield
            P.op("dve", lambda e, qs=qs: e.tensor_tensor(out=t1[:, :, :], in0=tr_ps[:, :, :],
                                                         in1=lnw[:, qs].unsqueeze(2).to_broadcast([128, 4, 128]), op=ALU.mult),
                 reads=[r["tr_ps" + str(b)], r["lnw"]], writes=[r["t1" + str(b)]])
            P.op("pool", lambda e, qs=qs: e.tensor_tensor(out=t2[:, :, :], in0=t1[:, :, :],
                                                          in1=lnb[:, qs].unsqueeze(2).to_broadcast([128, 4, 128]), op=ALU.add),
                 reads=[r["t1" + str(b)], r["lnb"]], writes=[r["t2" + str(b)]])
            P.op("pool", lambda e, qs=qs, s=s: e.tensor_tensor(out=t1[:, :, :], in0=t2[:, :, :], in1=bon[s][:, qs, :], op=ALU.add),
                 reads=[r["t2" + str(b)], r[f"bon{s}"]], writes=[r["t1" + str(b)]])
            P.op("dve", lambda e, qs=qs, s=s, b=b: e.tensor_tensor(out=yfin[b][:, :, :], in0=t1[:, :, :], in1=szt[s][:, qs, :], op=ALU.mult),
                 reads=[r["t1" + str(b)], r[f"szt{s}"]], writes=[r[f"yfin{b}"]])
            P.dma("pool", yT[kc0 + gi * 4:kc0 + gi * 4 + 4, :, t0:t0 + 128].rearrange("k p t -> p k t"), yfin[b][:, :, :],
                  reads=[r[f"yfin{b}"]], writes=[r["y_out"]])

        it = 0
        for i in range(2):
            P.op("pool", lambda e, i=i: e.memset(kkt[i][:, :, :, :].rearrange("p a q t -> p (a q t)"), 0.0), writes=[r[f"kkt{i}"]])
            P.op("pool", lambda e, i=i: e.memset(rt[i][:, :, :, :].rearrange("p a q t -> p (a q t)"), 0.0), writes=[r[f"rt{i}"]])
            for n in tkz:
                P.op("pool", lambda e, i=i, n=n: e.memset(tkz[n][i][:, :, :, :].rearrange("p a q t -> p (a q t)"), 0.0), writes=[r[f"tkz_{n}{i}"]])
        for c in range(NCH):
            t0 = c * 128
            s = c % 2
            for tl_, nm, src in ((kkt, "kkt", "kktT"), (rt, "rt", "rtT")):
                srcv = S[src][0:1024, t0:t0 + 128].rearrange("(q p) t -> p q t", p=128)
                P.dma("sp", tl_[s][0:64, 0, :, :], srcv[0:64], writes=[r[f"{nm}{s}"]])
                P.dma("sp", tl_[s][64:128, 1, :, :], srcv[64:128], writes=[r[f"{nm}{s}"]])
            for n, src in (("kh", "khtk_d"), ("kka", "kkatk_d")):
                srcv = S[src][t0:t0 + 128, :].rearrange("p (q a d) -> p q a d", a=2, d=64)
                for a in range(2):
                    P.dma("sp", tkz[n][s][:, a, :, a * 64:(a + 1) * 64], srcv[:, :, a, :], writes=[r[f"tkz_{n}{s}"]])
            for q0 in range(0, 8, 4):
                P.dma("sp", bon[s][:, q0:q0 + 4, :], S["bonT"][q0 * 128:(q0 + 4) * 128, t0:t0 + 128].rearrange("(q p) t -> p q t", p=128),
                      writes=[r[f"bon{s}"]])
                P.dma("sp", szt[s][:, q0:q0 + 4, :], S["szT"][q0 * 128:(q0 + 4) * 128, t0:t0 + 128].rearrange("(q p) t -> p q t", p=128),
                      writes=[r[f"szt{s}"]])
            for n, dst in (("akv", "akvT_d"), ("bkv", "bkvT_d"), ("nbab", "nbabT_d"), ("tinv", "tinvT_d")):
                for h0 in range(0, 16, 4):
                    P.dma("sp", mm[n][s][:, h0:h0 + 4, :], S[dst][c, :, h0:h0 + 4, :], writes=[r[f"{n}{s}"]])
            for n, dst in (("v", "vtk_d"),):
                P.dma("sp", tk[n][s][:, :], S[dst][t0:t0 + 128, :], writes=[r[f"tk_{n}{s}"]])
            zipper([grp(c, 0, 0, s), grp(c, 1, 1, s)])
        P.barrier()
        P.flush()


D = 4096
KC = 32
EPS = 1e-6
TWO_PI = 2.0 * math.pi
C1 = 6.28125
C2 = float(np.float32(TWO_PI - C1))


def dense_consts():
    c = {}
    j = np.arange(64, dtype=np.float32)
    invf = (np.float32(10000.0) ** (-(j / np.float32(64.0)))).astype(np.float32)
    c["invf"] = np.concatenate([invf, invf]).reshape(128, 1).astype(np.float32)
    c["sgn"] = np.concatenate([-np.ones(64), np.ones(64)]).reshape(128, 1).astype(np.float32)
    sw = np.zeros((128, 128), np.float32)
    for m in range(128):
        sw[(m + 64) % 128, m] = 1.0
    c["swap64"] = sw
    return c


def phase_rope_tables(P, nc, pos_dram, C, cosT, sinT, T):
    with contextlib.ExitStack() as st:
        def sb(name, shape, dt):
            return st.enter_context(nc.sbuf_tensor(_uid() + "rp_" + name, shape, dt))
        r = defaultdict(Res)
        invf = sb("invf", [128, 1], F32); sgn = sb("sgn", [128, 1], F32)
        pi_ = sb("pi", [128, 512], I32)
        ang = sb("ang", [128, 512], F32); kf = sb("kf", [128, 512], F32); kr = sb("kr", [128, 512], F32)
        rr = sb("rr", [128, 512], F32); rc = sb("rc", [128, 512], F32); m = sb("m", [128, 512], F32)
        so = [sb(f"so{i}", [128, 512], F32) for i in range(2)]
        co = [sb(f"co{i}", [128, 512], F32) for i in range(2)]
        P.dma("sp", invf[:, :], C["invf"][:, :], writes=[r["invf"]])
        P.dma("sp", sgn[:, :], C["sgn"][:, :], writes=[r["sgn"]])
        MAGIC = 12582912.0
        for tb in range(T // 512):
            s = tb % 2
            t0 = tb * 512
            P.dma("sp", pi_[:, :], pos_dram[:, t0:t0 + 512], writes=[r["pi"]])
            P.op("dve", lambda e: e.tensor_copy(ang[:, :], pi_[:, :]), reads=[r["pi"]], writes=[r["ang"]])
            P.op("dve", lambda e: e.tensor_scalar(out=ang[:, :], in0=ang[:, :], scalar1=invf[:, 0:1], scalar2=None, op0=ALU.mult),
                 reads=[r["ang"], r["invf"]], writes=[r["ang"]])
            P.op("dve", lambda e: e.tensor_scalar(out=kf[:, :], in0=ang[:, :], scalar1=1.0 / TWO_PI, scalar2=None, op0=ALU.mult),
                 reads=[r["ang"]], writes=[r["kf"]])
            P.op("dve", lambda e: e.tensor_scalar(out=kr[:, :], in0=kf[:, :], scalar1=MAGIC, scalar2=None, op0=ALU.add),
                 reads=[r["kf"]], writes=[r["kr"]])
            P.op("dve", lambda e: e.tensor_scalar(out=kf[:, :], in0=kr[:, :], scalar1=MAGIC, scalar2=None, op0=ALU.subtract),
                 reads=[r["kr"]], writes=[r["kf"]])
            P.op("dve", lambda e: e.scalar_tensor_tensor(out=rr[:, :], in0=kf[:, :], scalar=-C1, in1=ang[:, :], op0=ALU.mult, op1=ALU.add),
                 reads=[r["kf"], r["ang"]], writes=[r["rr"]])
            P.op("dve", lambda e: e.scalar_tensor_tensor(out=rr[:, :], in0=kf[:, :], scalar=-C2, in1=rr[:, :], op0=ALU.mult, op1=ALU.add),
                 reads=[r["kf"], r["rr"]], writes=[r["rr"]])
            P.op("dve", lambda e: e.tensor_scalar(out=rr[:, :], in0=rr[:, :], scalar1=math.pi, scalar2=-math.pi, op0=ALU.min, op1=ALU.max),
                 reads=[r["rr"]], writes=[r["rr"]])
            P.op("dve", lambda e: e.tensor_scalar(out=m[:, :], in0=rr[:, :], scalar1=math.pi / 2, scalar2=-TWO_PI, op0=ALU.is_gt, op1=ALU.mult),
                 reads=[r["rr"]], writes=[r["m"]])
            P.op("dve", lambda e: e.scalar_tensor_tensor(out=rc[:, :], in0=rr[:, :], scalar=math.pi / 2, in1=m[:, :], op0=ALU.add, op1=ALU.add),
                 reads=[r["rr"], r["m"]], writes=[r["rc"]])
            P.op("dve", lambda e: e.tensor_scalar(out=rc[:, :], in0=rc[:, :], scalar1=math.pi, scalar2=-math.pi, op0=ALU.min, op1=ALU.max),
                 reads=[r["rc"]], writes=[r["rc"]])
            P.op("act", lambda e, s=s: e.activation(out=so[s][:, :], in_=rr[:, :], func=AF.Sin), reads=[r["rr"]], writes=[r[f"so{s}"]])
            P.op("act", lambda e, s=s: e.activation(out=co[s][:, :], in_=rc[:, :], func=AF.Sin), reads=[r["rc"]], writes=[r[f"co{s}"]])
            P.op("dve", lambda e, s=s: e.tensor_scalar(out=so[s][:, :], in0=so[s][:, :], scalar1=sgn[:, 0:1], scalar2=None, op0=ALU.mult),
                 reads=[r[f"so{s}"], r["sgn"]], writes=[r[f"so{s}"]])
            P.dma("pool", sinT[:, t0:t0 + 512], so[s][:, :], reads=[r[f"so{s}"]], writes=[r["sinT"]])
            P.dma("pool", cosT[:, t0:t0 + 512], co[s][:, :], reads=[r[f"co{s}"]], writes=[r["cosT"]])
        P.barrier()
        P.flush()


def phase_norm(P, nc, x_dram, normw_bc_dram, hT_dram, T, out_dram=None):
    NT = T // 128
    with contextlib.ExitStack() as st:
        def sb(name, shape, dt):
            return st.enter_context(nc.sbuf_tensor(_uid() + "pn_" + name, shape, dt))
        r = defaultdict(Res)
        xt = [sb(f"x{i}", [128, D], F32) for i in range(2)]
        junk = sb("junk", [128, D], BF16)
        nw = sb("nw", [128, D], F32)
        ss = sb("ss", [128, 2], F32); sd = sb("sd", [128, 2], F32); rs = sb("rs", [128, 2], F32)
        epsc = sb("eps", [128, 1], F32)
        P.dma("sp", nw[:, :], normw_bc_dram[:, :], writes=[r["nw"]])
        P.op("pool", lambda e: e.memset(epsc[:, :], EPS), writes=[r["eps"]])
        if out_dram is None:
            hb = [sb(f"h{i}", [128, D], BF16) for i in range(2)]
            ident = sb("id", [128, 128], BF16)
            stg = [sb(f"stg{i}", [128, KC, 256], BF16) for i in range(2)]
            tp = [st.enter_context(nc.psum_tensor(_uid() + f"pn_tp{i}", [128, 1024], BF16)) for i in range(4)]
            P.op("pool", lambda e: e.memset(ident[:, :], 0.0), writes=[r["id"]])
            P.op("pool", lambda e: e.affine_select(out=ident[:, :], in_=ident[:, :], pattern=[[-1, 128]],
                                                   compare_op=ALU.not_equal, fill=1.0, base=0, channel_multiplier=1),
                 reads=[r["id"]], writes=[r["id"]])
        else:
            of = [sb(f"of{i}", [128, D], F32) for i in range(2)]
        for tt in range(NT):
            s = tt % 2
            P.dma("sp", xt[s][:, :], x_dram[tt * 128:(tt + 1) * 128, :], writes=[r[f"xt{s}"]])
            P.op("dve", lambda e, s=s: e.scalar_tensor_tensor(out=junk[:, :], in0=xt[s][:, :], scalar=1.0, in1=xt[s][:, :],
                                                              op0=ALU.mult, op1=ALU.mult, accum_out=ss[:, s:s + 1]),
                 reads=[r[f"xt{s}"]], writes=[r["junk"], r[f"ss{s}"]])
            P.op("act", lambda e, s=s: e.activation(out=sd[:, s:s + 1], in_=ss[:, s:s + 1], func=AF.Sqrt,
                                                    bias=epsc[:, 0:1], scale=1.0 / D),
                 reads=[r[f"ss{s}"], r["eps"]], writes=[r[f"sd{s}"]])
            P.op("dve", lambda e, s=s: e.reciprocal(rs[:, s:s + 1], sd[:, s:s + 1]), reads=[r[f"sd{s}"]], writes=[r[f"rs{s}"]])
            if out_dram is not None:
                P.op("dve", lambda e, s=s: e.scalar_tensor_tensor(out=of[s][:, :], in0=xt[s][:, :], scalar=rs[:, s:s + 1],
                                                                  in1=nw[:, :], op0=ALU.mult, op1=ALU.mult),
                     reads=[r[f"xt{s}"], r[f"rs{s}"], r["nw"]], writes=[r[f"of{s}"]])
                P.dma("pool", out_dram[tt * 128:(tt + 1) * 128, :], of[s][:, :], reads=[r[f"of{s}"]], writes=[r["out"]])
                continue
            P.op("dve", lambda e, s=s: e.scalar_tensor_tensor(out=hb[s][:, :], in0=xt[s][:, :], scalar=rs[:, s:s + 1],
                                                              in1=nw[:, :], op0=ALU.mult, op1=ALU.mult),
                 reads=[r[f"xt{s}"], r[f"rs{s}"], r["nw"]], writes=[r[f"hb{s}"]])
            sg = (tt // 2) % 2
            off = (tt % 2) * 128
            for b in range(4):
                for j in range(8):
                    kc = b * 8 + j
                    P.op("pe", lambda e, s=s, b=b, j=j, kc=kc: e.transpose(tp[b][:, j * 128:(j + 1) * 128],
                                                                         hb[s][:, kc * 128:(kc + 1) * 128], ident[:, :]),
                         reads=[r[f"hb{s}"], r["id"]], writes=[r[f"tp{b}"]])
                P.op("act", lambda e, b=b, sg=sg, off=off: e.copy(
                    stg[sg][:, b * 8:(b + 1) * 8, off:off + 128],
                    tp[b][:, :].rearrange("p (j t) -> p j t", j=8)),
                     reads=[r[f"tp{b}"]], writes=[r[f"stg{sg}"]])
            if tt % 2 == 1:
                t0 = (tt - 1) * 128
                P.dma("pool", hT_dram[:, :, t0:t0 + 256].rearrange("k p t -> p k t"), stg[sg][:, :, :],
                      reads=[r[f"stg{sg}"]], writes=[r["hT_out"]])
        P.barrier()
        P.flush()


def phase_proj(P, nc, hT_dram, w_tiles_dram, projT_dram, cosT, sinT, T, NCT, n_rope=16, TH=2048, swap_dram=None):
    TH = min(TH, T)
    NTB = TH // 512
    with contextlib.ExitStack() as st:
        def sb(name, shape, dt):
            return st.enter_context(nc.sbuf_tensor(_uid() + "pp_" + name, shape, dt))
        r = defaultdict(Res)
        hT = sb("hT", [128, KC, TH], BF16)
        wf = [sb(f"wf{i}", [128, KC * 128], F32) for i in range(2)]
        wb = [sb(f"wb{i}", [128, KC, 128], BF16) for i in range(2)]
        swp = sb("swp", [128, 128], F32)
        asb = [sb(f"asb{i}", [128, 512], F32) for i in range(2)]
        cs_ = [sb(f"cs{i}", [128, 512], F32) for i in range(2)]
        sn_ = [sb(f"sn{i}", [128, 512], F32) for i in range(2)]
        t1 = sb("t1", [128, 512], F32); t2 = sb("t2", [128, 512], F32)
        ob = [sb(f"ob{i}", [128, 512], F32) for i in range(4)]
        ps = [[st.enter_context(nc.psum_tensor(_uid() + f"pp_ps{a}_{i}", [128, 512], F32)) for i in range(NTB)] for a in range(2)]
        cnt = 0
        rc = 0
        if n_rope:
            P.dma("sp", swp[:, :], swap_dram[:, :], writes=[r["swp"]])
        for t0 in range(0, T, TH):
            for kc in range(KC):
                P.dma("sp", hT[:, kc, :], hT_dram[kc, :, t0:t0 + TH], writes=[r["hT"]])
            def prep(ct):
                s = ct % 2
                P.dma("sp", wf[s][:, :], w_tiles_dram[ct, :, :], writes=[r[f"wf{s}"]])
                if ct % 2 == 0:
                    P.op("act", lambda e, s=s: e.copy(wb[s][:, :, :].rearrange("p k c -> p (k c)"), wf[s][:, :]),
                         reads=[r[f"wf{s}"]], writes=[r[f"wb{s}"]])
                else:
                    P.op("dve", lambda e, s=s: e.tensor_copy(wb[s][:, :, :].rearrange("p k c -> p (k c)"), wf[s][:, :]),
                         reads=[r[f"wf{s}"]], writes=[r[f"wb{s}"]])
            prep(0)
            for ct in range(NCT):
                s = ct % 2
                rope = ct < n_rope
                if ct + 1 < NCT:
                    prep(ct + 1)
                for kc in range(KC):
                    for tb in range(NTB):
                        P.op("pe", lambda e, s=s, kc=kc, tb=tb: e.matmul(ps[s][tb][:, :], wb[s][:, kc, :], hT[:, kc, tb * 512:(tb + 1) * 512],
                                                                        start=(kc == 0), stop=(kc == KC - 1)),
                             reads=[r[f"wb{s}"], r["hT"]], writes=[r[f"ps{s}_{tb}"]])
                for tb in range(NTB):
                    b = cnt % 4
                    cnt += 1
                    tg = t0 + tb * 512
                    if rope:
                        q = rc % 2
                        rc += 1
                        P.dma("sp", cs_[q][:, :], cosT[:, tg:tg + 512], writes=[r[f"cs{q}"]])
                        P.dma("sp", sn_[q][:, :], sinT[:, tg:tg + 512], writes=[r[f"sn{q}"]])
                        P.op("act", lambda e, s=s, tb=tb, q=q: e.copy(asb[q][:, :], ps[s][tb][:, :]), reads=[r[f"ps{s}_{tb}"]], writes=[r[f"asb{q}"]])
                        P.op("pe", lambda e, s=s, tb=tb, q=q: e.matmul(ps[1 - s][tb][:, :], swp[:, :], asb[q][:, :], start=True, stop=True),
                             reads=[r["swp"], r[f"asb{q}"]], writes=[r[f"ps{1 - s}_{tb}"]])
                        P.op("pool", lambda e, q=q: e.tensor_tensor(out=t1[:, :], in0=asb[q][:, :], in1=cs_[q][:, :], op=ALU.mult),
                             reads=[r[f"asb{q}"], r[f"cs{q}"]], writes=[r["t1"]])
                        P.op("dve", lambda e, s=s, tb=tb, q=q: e.tensor_tensor(out=t2[:, :], in0=ps[1 - s][tb][:, :], in1=sn_[q][:, :], op=ALU.mult),
                             reads=[r[f"ps{1 - s}_{tb}"], r[f"sn{q}"]], writes=[r["t2"]])
                        P.op("pool", lambda e, b=b: e.tensor_tensor(out=ob[b][:, :], in0=t1[:, :], in1=t2[:, :], op=ALU.add),
                             reads=[r["t1"], r["t2"]], writes=[r[f"ob{b}"]])
                    else:
                        if cnt % 2 == 0:
                            P.op("dve", lambda e, b=b, s=s, tb=tb: e.tensor_copy(ob[b][:, :], ps[s][tb][:, :]), reads=[r[f"ps{s}_{tb}"]], writes=[r[f"ob{b}"]])
                        else:
                            P.op("act", lambda e, b=b, s=s, tb=tb: e.copy(ob[b][:, :], ps[s][tb][:, :]), reads=[r[f"ps{s}_{tb}"]], writes=[r[f"ob{b}"]])
                    pd_, pr_ = projT_dram(ct)
                    P.dma("pool", pd_[pr_:pr_ + 128, tg:tg + 512], ob[b][:, :], reads=[r[f"ob{b}"]], writes=[r["out"]])
        P.barrier()
        P.flush()


def _load_wblk(P, r, wf, wb, s, w_dram, cb, wcnt):
    for q4 in range(4):
        ws = wcnt[0] % 2
        wcnt[0] += 1
        P.dma("sp", wf[ws][:, :], w_dram[cb * 4 + q4, :, :], writes=[r[f"wf{ws}"]])
        src = wf[ws][:, :].rearrange("p (k c) -> p k c", c=128)
        if q4 % 2 == 0:
            P.op("act", lambda e, s=s, q4=q4, src=src: e.copy(wb[s][:, :, q4 * 128:(q4 + 1) * 128], src),
                 reads=[r[f"wf{ws}"]], writes=[r[f"wb{s}"]])
        else:
            P.op("dve", lambda e, s=s, q4=q4, src=src: e.tensor_copy(wb[s][:, :, q4 * 128:(q4 + 1) * 128], src),
                 reads=[r[f"wf{ws}"]], writes=[r[f"wb{s}"]])


def phase_out(P, nc, yT_dram, w_blk_dram, x_dram, x1_dram, T, TQ=1024):
    NB = D // 512
    TQ = min(TQ, T)
    with contextlib.ExitStack() as st:
        def sb(name, shape, dt):
            return st.enter_context(nc.sbuf_tensor(_uid() + "po_" + name, shape, dt))
        r = defaultdict(Res)
        yT = sb("yT", [128, KC, TQ], BF16)
        wf = [sb(f"wf{i}", [128, KC * 128], F32) for i in range(2)]
        wb = [sb(f"wb{i}", [128, KC, 512], BF16) for i in range(2)]
        xt = [sb(f"xt{i}", [128, 512], F32) for i in range(4)]
        ot = [sb(f"ot{i}", [128, 512], F32) for i in range(4)]
        ps = [st.enter_context(nc.psum_tensor(_uid() + f"po_ps{i}", [128, 512], F32)) for i in range(4)]
        cnt = 0
        wcnt = [0]
        for t0 in range(0, T, TQ):
            for kc in range(KC):
                P.dma("sp", yT[:, kc, :], yT_dram[kc, :, t0:t0 + TQ], writes=[r["yT"]])
            if t0 == 0:
                _load_wblk(P, r, wf, wb, 0, w_blk_dram, 0, wcnt)
            for cb in range(NB):
                s = cb % 2
                if cb + 1 < NB:
                    _load_wblk(P, r, wf, wb, 1 - s, w_blk_dram, cb + 1, wcnt)
                elif t0 + TQ < T:
                    _load_wblk(P, r, wf, wb, 1 - s, w_blk_dram, 0, wcnt)
                for tt in range(TQ // 128):
                    b = cnt % 4
                    cnt += 1
                    tok = t0 + tt * 128
                    P.dma("sp", xt[b][:, :], x_dram[tok:tok + 128, cb * 512:(cb + 1) * 512], writes=[r[f"xt{b}"]])
                    for kc in range(KC):
                        P.op("pe", lambda e, s=s, b=b, kc=kc, tt=tt: e.matmul(ps[b][:, :], yT[:, kc, tt * 128:(tt + 1) * 128], wb[s][:, kc, :],
                                                                             start=(kc == 0), stop=(kc == KC - 1)),
                             reads=[r[f"wb{s}"], r["yT"]], writes=[r[f"ps{b}"]])
                    P.op("dve", lambda e, b=b: e.tensor_tensor(out=ot[b][:, :], in0=ps[b][:, :], in1=xt[b][:, :], op=ALU.add),
                         reads=[r[f"ps{b}"], r[f"xt{b}"]], writes=[r[f"ot{b}"]])
                    P.dma("pool", x1_dram[tok:tok + 128, cb * 512:(cb + 1) * 512], ot[b][:, :], reads=[r[f"ot{b}"]], writes=[r["out"]])
        P.barrier()
        P.flush()


def phase_gate(P, nc, h2T_dram, wg_blk_dram, p_dram, wple_dram, x1_dram, x2_dram, T, TQ=1024):
    NB = D // 512
    TQ = min(TQ, T)
    with contextlib.ExitStack() as st:
        def sb(name, shape, dt):
            return st.enter_context(nc.sbuf_tensor(_uid() + "pg_" + name, shape, dt))
        r = defaultdict(Res)
        hT = sb("hT", [128, KC, TQ], BF16)
        wf = [sb(f"wf{i}", [128, KC * 128], F32) for i in range(2)]
        wb = [sb(f"wb{i}", [128, KC, 512], BF16) for i in range(2)]
        wpb = sb("wpb", [128, 2, D], BF16)
        ident = sb("ident", [128, 128], BF16)
        pt = [sb(f"pt{i}", [128, 256], F32) for i in range(2)]
        pb = [sb(f"pb{i}", [128, 256], BF16) for i in range(2)]
        pT = sb("pT", [128, 2, TQ], BF16)
        xt = [sb(f"xt{i}", [128, 512], F32) for i in range(2)]
        gt = sb("gt", [128, 512], F32)
        tm = sb("tm", [128, 512], F32)
        ot = [sb(f"ot{i}", [128, 512], F32) for i in range(2)]
        ps = [st.enter_context(nc.psum_tensor(_uid() + f"pg_ps{i}", [128, 512], F32)) for i in range(3)]
        pp = [st.enter_context(nc.psum_tensor(_uid() + f"pg_pp{i}", [128, 512], F32)) for i in range(3)]
        tp = st.enter_context(nc.psum_tensor(_uid() + "pg_tp", [128, 1024], BF16))
        for q2 in range(2):
            P.dma("sp", wf[q2][:, :], wple_dram[:, q2 * D:(q2 + 1) * D], writes=[r[f"wf{q2}"]])
            P.op("act", lambda e, q2=q2: e.copy(wpb[:, q2, :], wf[q2][:, :]), reads=[r[f"wf{q2}"]], writes=[r["wpb"]])
        P.op("pool", lambda e: e.memset(ident[:, :], 0.0), writes=[r["id"]])
        P.op("pool", lambda e: e.affine_select(out=ident[:, :], in_=ident[:, :], pattern=[[-1, 128]],
                                               compare_op=ALU.not_equal, fill=1.0, base=0, channel_multiplier=1),
             reads=[r["id"]], writes=[r["id"]])
        cnt = 0
        wcnt = [0]
        for t0 in range(0, T, TQ):
            for kc in range(KC):
                P.dma("sp", hT[:, kc, :], h2T_dram[kc, :, t0:t0 + TQ], writes=[r["hT"]])
            for tt in range(TQ // 128):
                s = tt % 2
                tok = t0 + tt * 128
                P.dma("sp", pt[s][:, :], p_dram[tok:tok + 128, :], writes=[r[f"pt{s}"]])
                P.op("dve", lambda e, s=s: e.tensor_copy(pb[s][:, :], pt[s][:, :]), reads=[r[f"pt{s}"]], writes=[r[f"pb{s}"]])
                for k2 in range(2):
                    P.op("pe", lambda e, s=s, k2=k2: e.transpose(tp[:, k2 * 128:(k2 + 1) * 128], pb[s][:, k2 * 128:(k2 + 1) * 128], ident[:, :]),
                         reads=[r[f"pb{s}"], r["id"]], writes=[r["tp"]])
                P.op("act", lambda e, tt=tt: e.copy(pT[:, :, tt * 128:(tt + 1) * 128], tp[:, 0:256].rearrange("p (k t) -> p k t", k=2)),
                     reads=[r["tp"]], writes=[r["pT"]])
            if t0 == 0:
                _load_wblk(P, r, wf, wb, 0, wg_blk_dram, 0, wcnt)
            for cb in range(NB):
                s = cb % 2
                if cb + 1 < NB:
                    _load_wblk(P, r, wf, wb, 1 - s, wg_blk_dram, cb + 1, wcnt)
                elif t0 + TQ < T:
                    _load_wblk(P, r, wf, wb, 1 - s, wg_blk_dram, 0, wcnt)
                for tt in range(TQ // 128):
                    b3 = cnt % 3
                    b2 = cnt % 2
                    cnt += 1
                    tok = t0 + tt * 128
                    P.dma("sp", xt[b2][:, :], x1_dram[tok:tok + 128, cb * 512:(cb + 1) * 512], writes=[r[f"xt{b2}"]])
                    for kc in range(KC):
                        P.op("pe", lambda e, s=s, b3=b3, kc=kc, tt=tt: e.matmul(ps[b3][:, :], hT[:, kc, tt * 128:(tt + 1) * 128], wb[s][:, kc, :],
                                                                               start=(kc == 0), stop=(kc == KC - 1)),
                             reads=[r[f"wb{s}"], r["hT"]], writes=[r[f"ps{b3}"]])
                    for k2 in range(2):
                        P.op("pe", lambda e, b3=b3, k2=k2, tt=tt, cb=cb: e.matmul(pp[b3][:, :], pT[:, k2, tt * 128:(tt + 1) * 128],
                                                                                 wpb[:, k2, cb * 512:(cb + 1) * 512], start=(k2 == 0), stop=(k2 == 1)),
                             reads=[r["wpb"], r["pT"]], writes=[r[f"pp{b3}"]])
                    P.op("act", lambda e, b3=b3: e.activation(out=gt[:, :], in_=ps[b3][:, :], func=AF.Sigmoid),
                         reads=[r[f"ps{b3}"]], writes=[r["gt"]])
                    P.op("dve", lambda e, b3=b3: e.tensor_tensor(out=tm[:, :], in0=pp[b3][:, :], in1=gt[:, :], op=ALU.mult),
                         reads=[r[f"pp{b3}"], r["gt"]], writes=[r["tm"]])
                    P.op("pool", lambda e, b2=b2: e.tensor_tensor(out=ot[b2][:, :], in0=tm[:, :], in1=xt[b2][:, :], op=ALU.add),
                         reads=[r["tm"], r[f"xt{b2}"]], writes=[r[f"ot{b2}"]])
                    P.dma("pool", x2_dram[tok:tok + 128, cb * 512:(cb + 1) * 512], ot[b2][:, :], reads=[r[f"ot{b2}"]], writes=[r["out"]])
        P.barrier()
        P.flush()


SEQ = 4096
NLAYER = 2
NCT = 130


def make_consts():
    c = {}
    c.update(ret_consts())
    c.update(gdn_consts())
    c.update(rwkv_consts())
    c.update(dense_consts())
    return c


_UIDC = [0]


def build_program(T=SEQ, L=NLAYER):
    nc = bass.Bass("TRN2", target_bir_lowering=False)
    NCH = T // 128
    cs = make_consts()
    ext = lambda n, shp, dt=F32: nc.dram_tensor(n, list(shp), dt, kind="ExternalInput")
    x = ext("x", [T, D])
    pos = ext("pos", [128, T], I32)
    C = {k: ext("c_" + k, v.shape) for k, v in cs.items()}
    Lp = []
    for l in range(L):
        d = {}
        d["nwb"] = ext(f"nwb{l}", [128, D]); d["win"] = ext(f"win{l}", [NCT, 128, KC * 128])
        d["gnwT"] = ext(f"gnwT{l}", [128, 8])
        for n in ("w0T", "a0T", "kkT", "kaT", "rkT", "lnwT", "lnbT"):
            d[n] = ext(f"{n}{l}", [128, 8])
        d["muT"] = ext(f"muT{l}", [128, 33]); d["lw2"] = ext(f"lw2{l}", [128, 1024])
        d["convT"] = ext(f"convT{l}", [128, 192]); d["alog"] = ext(f"alog{l}", [16, 1]); d["dtb"] = ext(f"dtb{l}", [16, 1])
        d["nrm"] = ext(f"nrm{l}", [128, 1])
        d["wout"] = ext(f"wout{l}", [32, 128, KC * 128]); d["wgate"] = ext(f"wgate{l}", [32, 128, KC * 128])
        d["wple"] = ext(f"wple{l}", [128, 2 * D]); d["plnb"] = ext(f"plnb{l}", [128, D]); d["p"] = ext(f"p{l}", [T, 256])
        Lp.append(d)
    fnb = ext("fnb", [128, D])
    out = nc.dram_tensor("out", [T, D], F32, kind="ExternalOutput")
    scr = lambda n, shp, dt: nc.dram_tensor(n, list(shp), dt)
    hT = scr("hT", [KC, 128, T], BF16); yT = scr("yT", [KC, 128, T], BF16)
    projA = scr("projA", [65 * 128, T], F32); projB = scr("projB", [65 * 128, T], F32)
    projT = lambda ct: (projA, ct * 128) if ct < 65 else (projB, (ct - 65) * 128)
    cosT = scr("cosT", [128, T], F32); sinT = scr("sinT", [128, T], F32)
    x1 = scr("x1", [T, D], F32); x2 = scr("x2", [T, D], F32)
    S = {n: scr(n, [2048, T], BF16) for n in ("gqT", "gqdT", "gkT", "gvT", "wkT_d")}
    S["gbt"] = scr("gbt", [T, 32], F32)
    S["attnT_d"] = scr("attnT_d", [NCH, 128, 16, 128], BF16); S["ktl_d"] = scr("ktl_d", [T, 2048], BF16)
    S["u_d"] = scr("u_d", [T, 2048], F32); S["els_d"] = scr("els_d", [NCH, 128, 16], F32)
    S.update({n: scr(n, [1024, T], BF16) for n in ("rtT", "kktT", "khT", "kkaT", "rvT")})
    S.update({n: scr(n, [1024, T], F32) for n in ("bonT", "szT")})
    S["pc_d"] = scr("pc_d", [8, 128, NCH], F32)
    S.update({n: scr(n, [NCH, 128, 16, 128], BF16) for n in ("tinvT_d", "akvT_d", "bkvT_d", "nbabT_d")})
    S.update({n: scr(n, [T, 1024], BF16) for n in ("vtk_d", "khtk_d", "kkatk_d")})
    grows = dict(q=0, k=2048, v=4096, z=6144, a=8192, b=8208)
    import os
    only = os.environ.get("MK_PH")
    only = set(only.split(",")) if only else None

    def on(n):
        return only is None or n in only
    with contextlib.ExitStack() as stack:
        P = Prog(nc, stack)
        if on("rope"):
            phase_rope_tables(P, nc, pos, C, cosT, sinT, T)
        xin = x
        for l in range(L):
            d = Lp[l]
            if on("norm"):
                phase_norm(P, nc, xin, d["nwb"], hT, T)
            if on("proj"):
                phase_proj(P, nc, hT, d["win"], projT, cosT, sinT, T, NCT, n_rope=int(os.environ.get("MK_NROPE", "16")), swap_dram=C["swap64"])
            if on("ret"):
                phase_ret(P, nc, projA, yT, C, d["gnwT"], T)
            if on("rwkv"):
                phase_rwkv_pre(P, nc, projA, S, C, d, T, 4096)
            if on("rwkv"):
                phase_rwkv_r1(P, nc, S, C, T)
            if on("rwkv"):
                phase_rwkv_r2(P, nc, S, C, yT, d, T, kc0=8)
            if on("gdn"):
                phase_gdn_pre(P, nc, projB, S, C, d, T, grows)
            if on("gdn"):
                phase_gdn_g1(P, nc, S, C, T)
            if on("gdn"):
                phase_gdn_g2(P, nc, projB, S, C, yT, d["nrm"], T, grows["z"], kc0=16)
            if on("out"):
                phase_out(P, nc, yT, d["wout"], xin, x1, T)
            if on("norm2"):
                phase_norm(P, nc, x1, d["plnb"], hT, T)
            if on("gate"):
                phase_gate(P, nc, hT, d["wgate"], d["p"], d["wple"], x1, x2, T)
            xin = x2
        if on("fin"):
            phase_norm(P, nc, xin, fnb, None, T, out_dram=out)
        n_ops = P.n_ops
    return nc, cs, n_ops


def _tiles(w, ncols_pad=None):
    K, N = w.shape
    if ncols_pad is not None and ncols_pad > N:
        w = np.concatenate([w, np.zeros((K, ncols_pad - N), w.dtype)], axis=1)
        N = ncols_pad
    return np.ascontiguousarray(w.reshape(K // 128, 128, N // 128, 128).transpose(2, 1, 0, 3).reshape(N // 128, 128, (K // 128) * 128))


def prep_shared(inp, L=NLAYER):
    f = np.float32
    sh = {}
    bc = lambda v: np.ascontiguousarray(np.broadcast_to(np.asarray(v, f), (128, v.shape[-1])))
    col8 = lambda v: np.ascontiguousarray(np.asarray(v, f).reshape(8, 128).T)
    for l in range(L):
        sh[f"nwb{l}"] = bc(inp["norm_w"][l])
        sh[f"win{l}"] = _tiles(np.asarray(inp["w_in"][l], f), NCT * 128)
        sh[f"gnwT{l}"] = col8(inp["ret_gn"][l])
        sh[f"w0T{l}"] = col8(inp["rwkv_w0"][l]); sh[f"a0T{l}"] = col8(inp["rwkv_a0"][l])
        sh[f"kkT{l}"] = col8(inp["rwkv_k_k"][l]); sh[f"kaT{l}"] = col8(inp["rwkv_k_a"][l]); sh[f"rkT{l}"] = col8(inp["rwkv_r_k"][l])
        sh[f"lnwT{l}"] = col8(inp["rwkv_ln_w"][l]); sh[f"lnbT{l}"] = col8(inp["rwkv_ln_b"][l])
        sh[f"muT{l}"] = np.ascontiguousarray(np.asarray(inp["rwkv_mu"][l], f).reshape(33, 128).T)
        sh[f"lw2{l}"] = np.ascontiguousarray(np.concatenate([np.asarray(inp["rwkv_w2"][l], f), np.asarray(inp["rwkv_a2"][l], f)], 0))
        sh[f"convT{l}"] = np.ascontiguousarray(np.asarray(inp["gdn_conv"][l], f).reshape(4, 48, 128).transpose(2, 1, 0).reshape(128, 192))
        sh[f"alog{l}"] = np.asarray(inp["gdn_a_log"][l], f).reshape(16, 1).copy()
        sh[f"dtb{l}"] = np.asarray(inp["gdn_dt_bias"][l], f).reshape(16, 1).copy()
        sh[f"nrm{l}"] = np.asarray(inp["gdn_norm"][l], f).reshape(128, 1).copy()
        sh[f"wout{l}"] = _tiles(np.asarray(inp["w_out"][l], f))
        sh[f"wgate{l}"] = _tiles(np.asarray(inp["w_ple_gate"][l], f))
        sh[f"wple{l}"] = np.ascontiguousarray(np.asarray(inp["w_ple"][l], f).reshape(2, 128, D).transpose(1, 0, 2).reshape(128, 2 * D))
        sh[f"plnb{l}"] = bc(inp["ple_norm"][l])
    sh["fnb"] = bc(inp["final_norm"])
    return sh


def kernel(**inp):
    B = inp["x"].shape[0]
    T = inp["x"].shape[1]
    nc, cs, n_ops = build_program(T, NLAYER)
    sh = prep_shared(inp)
    for k, v in cs.items():
        sh["c_" + k] = v
    in_maps = []
    for b in range(B):
        m = dict(sh)
        m["x"] = np.ascontiguousarray(np.asarray(inp["x"][b], np.float32))
        m["pos"] = np.ascontiguousarray(np.broadcast_to(np.asarray(inp["positions"][b], np.int32), (128, T)))
        for l in range(NLAYER):
            m[f"p{l}"] = np.ascontiguousarray(np.asarray(inp["p"][l, b], np.float32))
        in_maps.append(m)
    res = run_bass_kernel_spmd(nc, in_maps, core_ids=list(range(B)))
    return np.stack([np.asarray(r["out"], np.float32) for r in res.results], axis=0)
```
